# Optimizing a Trainium2 kernel written in Bass

```python
import math
import jax
import jax.numpy as jnp
from jax import lax
import numpy as np

D_MODEL = 1024
BATCH = 8
SEQ = 2048
DEPTH = 1
DEC_BATCH = 128
DEC_SEQ = 1
PAST_LEN = 16384
PAGE_SIZE = 128

D_LRU = D_MODEL
LRU_BLOCKS = 16
LRU_BLOCK_W = D_LRU // LRU_BLOCKS
LRU_C = 8.0
SSD_EXPAND = 2
D_SSD = SSD_EXPAND * D_MODEL
SSD_HEAD_DIM = 64
SSD_HEADS = D_SSD // SSD_HEAD_DIM
SSD_GROUPS = 4
SSD_HPG = SSD_HEADS // SSD_GROUPS
SSD_STATE = 128
SSD_CHUNK = 128
SSD_CONV_DIM = D_SSD + 2 * SSD_GROUPS * SSD_STATE
CONV_W = 4
N_IN = 2 * D_LRU + D_SSD + SSD_CONV_DIM + SSD_HEADS + 2 * D_MODEL
LN_EPS = 1e-5
RMS_EPS = 1e-5
DEEPNORM_ALPHA = (2.0 * DEPTH) ** 0.25
DEEPNORM_BETA = (8.0 * DEPTH) ** -0.25

kernel_name = "hawk_ssd_parallel_gated_deepnorm_adaln_step"


def _layer_norm(x, g, b):
    xf = x.astype(jnp.float32)
    mu = jnp.mean(xf, -1, keepdims=True)
    var = jnp.mean(jnp.square(xf - mu), -1, keepdims=True)
    return ((xf - mu) * lax.rsqrt(var + LN_EPS) * g + b).astype(x.dtype)


def _causal_conv(u, buf, w, b):
    T = u.shape[1]
    up = jnp.concatenate([buf.astype(u.dtype), u], axis=1)
    out = b + w[0] * up[:, 0:T]
    for k in range(1, CONV_W):
        out = out + w[k] * up[:, k:k + T]
    return out, up[:, -(CONV_W - 1):]


def _linear_scan(a, b, h0):
    b = b.at[:, 0].add(a[:, 0] * h0)
    def comb(l, r):
        return (l[0] * r[0], r[0] * l[1] + r[1])
    _, h = lax.associative_scan(comb, (a, b), axis=1)
    return h


def _rg_lru(u, h0, wa, ba, wx, bx, lam, seq_start):
    Bn, T, _ = u.shape
    uf = u.astype(jnp.float32)
    ub = uf.reshape(Bn, T, LRU_BLOCKS, LRU_BLOCK_W)
    r = jax.nn.sigmoid(jnp.einsum('btnk,nkj->btnj', ub, wa).reshape(Bn, T, D_LRU) + ba)
    i = jax.nn.sigmoid(jnp.einsum('btnk,nkj->btnj', ub, wx).reshape(Bn, T, D_LRU) + bx)
    log_a = -LRU_C * r * jax.nn.softplus(-lam.astype(jnp.float32))
    a = jnp.exp(log_a)
    mult = jnp.sqrt(-jnp.expm1(2.0 * log_a))
    if seq_start:
        mult = mult.at[:, 0].set(1.0)
    h = _linear_scan(a, mult * i * uf, h0.astype(jnp.float32))
    return h, h[:, -1]


def _ssd(xh, dt, A, Bm, Cm, h0):
    Bn, T = xh.shape[0], xh.shape[1]
    L = min(SSD_CHUNK, T)
    nc = -(-T // L)
    pad = nc * L - T
    if pad:
        pw = lambda t: jnp.pad(t, [(0, 0), (0, pad)] + [(0, 0)] * (t.ndim - 2))
        xh, dt, Bm, Cm = pw(xh), pw(dt), pw(Bm), pw(Cm)
    xs = (xh.astype(jnp.float32) * dt[..., None]).reshape(Bn, nc, L, SSD_GROUPS, SSD_HPG, SSD_HEAD_DIM)
    a = (dt * A).reshape(Bn, nc, L, SSD_GROUPS, SSD_HPG)
    acs = jnp.transpose(jnp.cumsum(a, axis=2), (0, 1, 3, 4, 2))
    Bc = Bm.astype(jnp.float32).reshape(Bn, nc, L, SSD_GROUPS, SSD_STATE)
    Cc = Cm.astype(jnp.float32).reshape(Bn, nc, L, SSD_GROUPS, SSD_STATE)
    mask = jnp.tril(jnp.ones((L, L), dtype=bool))
    seg = acs[..., :, None] - acs[..., None, :]
    decay = jnp.exp(jnp.where(mask, seg, -jnp.inf))
    cb = jnp.einsum('bclgn,bcsgn->bcgls', Cc, Bc)
    y_diag = jnp.einsum('bcgls,bcghls,bcsghp->bclghp', cb, decay, xs)
    decay_end = jnp.exp(acs[..., -1:] - acs)
    states = jnp.einsum('bclgn,bcghl,bclghp->bcghpn', Bc, decay_end, xs)
    chunk_decay = jnp.exp(acs[..., -1])
    h0g = h0.astype(jnp.float32).reshape(Bn, SSD_GROUPS, SSD_HPG, SSD_HEAD_DIM, SSD_STATE)
    def step(h, inp):
        dec, s = inp
        return dec[..., None, None] * h + s, h
    h_last, h_in = lax.scan(step, h0g, (jnp.moveaxis(chunk_decay, 1, 0), jnp.moveaxis(states, 1, 0)))
    h_in = jnp.moveaxis(h_in, 0, 1)
    y_off = jnp.einsum('bclgn,bcghpn,bcghl->bclghp', Cc, h_in, jnp.exp(acs))
    y = (y_diag + y_off).reshape(Bn, nc * L, SSD_HEADS, SSD_HEAD_DIM)[:, :T]
    return y, h_last.reshape(Bn, SSD_HEADS, SSD_HEAD_DIM, SSD_STATE)


def _gated_rmsnorm(y, z, w):
    g = y.astype(jnp.float32) * jax.nn.silu(z.astype(jnp.float32))
    sh = g.shape
    gg = g.reshape(sh[:-1] + (SSD_GROUPS, D_SSD // SSD_GROUPS))
    gg = gg * lax.rsqrt(jnp.mean(gg * gg, -1, keepdims=True) + RMS_EPS)
    return gg.reshape(sh) * w


def _layer(x, c, lru_h0, lru_buf, ssd_h0, ssd_buf, seq_start,
           w_cond, b_cond, w_in, lru_conv_w, lru_conv_b, lru_wa, lru_ba, lru_wx, lru_bx, lru_lambda,
           ssd_conv_w, ssd_conv_b, ssd_dt_bias, ssd_a_log, ssd_d, ssd_norm_w,
           w_lru_proj, w_ssd_proj, w_out, ln_g, ln_b):
    Bn, T, _ = x.shape
    mod = c @ w_cond + b_cond
    shift, scale, gate = mod[:, :D_MODEL], mod[:, D_MODEL:2 * D_MODEL], mod[:, 2 * D_MODEL:]
    h = x * (1.0 + scale[:, None]) + shift[:, None]
    proj = h @ w_in
    o = 0
    lru_x = proj[..., o:o + D_LRU]; o += D_LRU
    lru_z = proj[..., o:o + D_LRU]; o += D_LRU
    ssd_z = proj[..., o:o + D_SSD]; o += D_SSD
    ssd_xbc = proj[..., o:o + SSD_CONV_DIM]; o += SSD_CONV_DIM
    ssd_dt = proj[..., o:o + SSD_HEADS]; o += SSD_HEADS
    merge_logits = proj[..., o:o + 2 * D_MODEL]
    u, lru_buf_new = _causal_conv(lru_x, lru_buf, lru_conv_w, lru_conv_b)
    hs, lru_h_new = _rg_lru(u, lru_h0, lru_wa, lru_ba, lru_wx, lru_bx, lru_lambda, seq_start)
    y_lru = hs * jax.nn.silu(lru_z.astype(jnp.float32))
    xbc, ssd_buf_new = _causal_conv(ssd_xbc, ssd_buf, ssd_conv_w, ssd_conv_b)
    xbc = jax.nn.silu(xbc)
    xs = xbc[..., :D_SSD].reshape(Bn, T, SSD_HEADS, SSD_HEAD_DIM)
    Bm = xbc[..., D_SSD:D_SSD + SSD_GROUPS * SSD_STATE].reshape(Bn, T, SSD_GROUPS, SSD_STATE)
    Cm = xbc[..., D_SSD + SSD_GROUPS * SSD_STATE:].reshape(Bn, T, SSD_GROUPS, SSD_STATE)
    dt = jax.nn.softplus((ssd_dt + ssd_dt_bias).astype(jnp.float32))
    A = -jnp.exp(ssd_a_log.astype(jnp.float32))
    y, ssd_h_new = _ssd(xs, dt, A, Bm, Cm, ssd_h0)
    y = y + ssd_d[:, None] * xs
    y_ssd = _gated_rmsnorm(y.reshape(Bn, T, D_SSD), ssd_z, ssd_norm_w)
    g = jax.nn.sigmoid(merge_logits.astype(jnp.float32))
    merged = g[..., :D_MODEL] * (y_lru @ w_lru_proj) + g[..., D_MODEL:] * (y_ssd @ w_ssd_proj)
    out = merged @ w_out
    x_new = _layer_norm(DEEPNORM_ALPHA * x + gate[:, None] * out, ln_g, ln_b).astype(x.dtype)
    return x_new, lru_h_new, lru_buf_new, ssd_h_new, ssd_buf_new


def setup_inputs(seed: int = 0) -> dict:
    key = jax.random.key(seed)
    ks = jax.random.split(key, 32)
    f32 = jnp.float32
    nrm = lambda k, shape, s: (jax.random.normal(k, shape, f32) * s)
    Dp = DEPTH
    a_c = jax.random.uniform(ks[20], (Dp, D_LRU), f32, 0.9, 0.999)
    s_l = a_c ** (1.0 / LRU_C)
    lru_lambda = jnp.log(s_l) - jnp.log1p(-s_l)
    dt0 = jnp.exp(jax.random.uniform(ks[21], (Dp, SSD_HEADS), f32, math.log(1e-3), math.log(1e-1)))
    ssd_dt_bias = dt0 + jnp.log(-jnp.expm1(-dt0))
    return {
        "x_prompt": nrm(ks[0], (BATCH, SEQ, D_MODEL), 1.0),
        "x_sample": nrm(ks[1], (DEC_BATCH, DEC_SEQ, D_MODEL), 1.0),
        "state_lru_h": nrm(ks[2], (Dp, DEC_BATCH, D_LRU), 0.5),
        "state_lru_conv": nrm(ks[3], (Dp, DEC_BATCH, CONV_W - 1, D_LRU), 1.0),
        "state_ssd_h": nrm(ks[4], (Dp, DEC_BATCH, SSD_HEADS, SSD_HEAD_DIM, SSD_STATE), 0.1),
        "state_ssd_conv": nrm(ks[5], (Dp, DEC_BATCH, CONV_W - 1, SSD_CONV_DIM), 1.0),
        "c_prompt": nrm(ks[6], (BATCH, D_MODEL), 1.0),
        "c_sample": nrm(ks[7], (DEC_BATCH, D_MODEL), 1.0),
        "w_cond": nrm(ks[8], (Dp, D_MODEL, 3 * D_MODEL), 0.2 * D_MODEL ** -0.5),
        "b_cond": nrm(ks[9], (Dp, 3 * D_MODEL), 0.02),
        "w_in": nrm(ks[10], (Dp, D_MODEL, N_IN), D_MODEL ** -0.5),
        "lru_conv_w": nrm(ks[11], (Dp, CONV_W, D_LRU), CONV_W ** -0.5),
        "lru_conv_b": nrm(ks[12], (Dp, D_LRU), 0.02),
        "lru_wa": nrm(ks[13], (Dp, LRU_BLOCKS, LRU_BLOCK_W, LRU_BLOCK_W), LRU_BLOCK_W ** -0.5),
        "lru_ba": nrm(ks[14], (Dp, D_LRU), 0.02),
        "lru_wx": nrm(ks[15], (Dp, LRU_BLOCKS, LRU_BLOCK_W, LRU_BLOCK_W), LRU_BLOCK_W ** -0.5),
        "lru_bx": nrm(ks[16], (Dp, D_LRU), 0.02),
        "lru_lambda": lru_lambda,
        "ssd_conv_w": nrm(ks[17], (Dp, CONV_W, SSD_CONV_DIM), CONV_W ** -0.5),
        "ssd_conv_b": nrm(ks[18], (Dp, SSD_CONV_DIM), 0.02),
        "ssd_dt_bias": ssd_dt_bias,
        "ssd_a_log": jnp.log(jax.random.uniform(ks[22], (Dp, SSD_HEADS), f32, 1.0, 16.0)),
        "ssd_d": 1.0 + nrm(ks[23], (Dp, SSD_HEADS), 0.1),
        "ssd_norm_w": 1.0 + nrm(ks[24], (Dp, D_SSD), 0.1),
        "w_lru_proj": nrm(ks[25], (Dp, D_LRU, D_MODEL), DEEPNORM_BETA * D_LRU ** -0.5),
        "w_ssd_proj": nrm(ks[26], (Dp, D_SSD, D_MODEL), DEEPNORM_BETA * D_SSD ** -0.5),
        "w_out": nrm(ks[27], (Dp, D_MODEL, D_MODEL), DEEPNORM_BETA * D_MODEL ** -0.5),
        "ln_g": 1.0 + nrm(ks[28], (Dp, D_MODEL), 0.1),
        "ln_b": nrm(ks[29], (Dp, D_MODEL), 0.02),
    }


def reference(x_prompt, x_sample, state_lru_h, state_lru_conv, state_ssd_h, state_ssd_conv,
              c_prompt, c_sample, w_cond, b_cond, w_in, lru_conv_w, lru_conv_b, lru_wa, lru_ba,
              lru_wx, lru_bx, lru_lambda, ssd_conv_w, ssd_conv_b, ssd_dt_bias, ssd_a_log, ssd_d,
              ssd_norm_w, w_lru_proj, w_ssd_proj, w_out, ln_g, ln_b):
    xp, xs = x_prompt, x_sample
    Bp = x_prompt.shape[0]
    dt_ = x_prompt.dtype
    p_lh, p_lc, p_sh, p_sc = [], [], [], []
    s_lh, s_lc, s_sh, s_sc = [], [], [], []
    for l in range(DEPTH):
        weights = (w_cond[l], b_cond[l], w_in[l], lru_conv_w[l], lru_conv_b[l], lru_wa[l], lru_ba[l],
                   lru_wx[l], lru_bx[l], lru_lambda[l], ssd_conv_w[l], ssd_conv_b[l], ssd_dt_bias[l],
                   ssd_a_log[l], ssd_d[l], ssd_norm_w[l], w_lru_proj[l], w_ssd_proj[l], w_out[l],
                   ln_g[l], ln_b[l])
        xp, a1, a2, a3, a4 = _layer(
            xp, c_prompt,
            jnp.zeros((Bp, D_LRU), jnp.float32),
            jnp.zeros((Bp, CONV_W - 1, D_LRU), dt_),
            jnp.zeros((Bp, SSD_HEADS, SSD_HEAD_DIM, SSD_STATE), jnp.float32),
            jnp.zeros((Bp, CONV_W - 1, SSD_CONV_DIM), dt_),
            True, *weights)
        p_lh.append(a1); p_lc.append(a2); p_sh.append(a3); p_sc.append(a4)
        xs, b1, b2, b3, b4 = _layer(
            xs, c_sample, state_lru_h[l], state_lru_conv[l], state_ssd_h[l], state_ssd_conv[l],
            False, *weights)
        s_lh.append(b1); s_lc.append(b2); s_sh.append(b3); s_sc.append(b4)
    return (xp, xs,
            jnp.stack(p_lh), jnp.stack(p_lc), jnp.stack(p_sh), jnp.stack(p_sc),
            jnp.stack(s_lh), jnp.stack(s_lc), jnp.stack(s_sh), jnp.stack(s_sc))
```

```python
import numpy as np
import concourse.bass as bass
import concourse.mybir as mybir
from concourse.bass_utils import run_bass_kernel_spmd

F32 = mybir.dt.float32
BF16 = mybir.dt.bfloat16
AF = mybir.ActivationFunctionType
ALU = mybir.AluOpType
AX = mybir.AxisListType

NCORES = 8
D = 1024
T = 2048
NS = 16
TP = T + NS
KC = 8
NCH = 16
L = 128
N_IN = 9248
C_LX, C_LZ, C_SZ, C_SX, C_SB, C_SC, C_DT, C_MA, C_MB = 0, 1024, 2048, 4096, 6144, 6656, 7168, 7200, 8224
ALPHA = 2.0 ** 0.25
LN_EPS = 1e-5
RMS_EPS = 1e-5
SAME_ENGINE_SYNC = True
import os as _os
KSTOP = float(_os.environ.get('KSTOP', '99'))


class _StopRec(Exception):
    pass


def _stop_at(n):
    if KSTOP <= n:
        raise _StopRec()


class Buf:
    __slots__ = ("name", "last_w", "readers")

    def __init__(self, name):
        self.name = name
        self.last_w = None
        self.readers = []


class Op:
    __slots__ = ("eng", "fn", "deps", "is_dma", "sem", "semval", "prevval", "signal", "sig")

    def __init__(self, eng, fn, deps, is_dma):
        self.eng, self.fn, self.deps, self.is_dma = eng, fn, deps, is_dma
        self.sem = None
        self.semval = 0
        self.prevval = 0
        self.signal = False
        self.sig = 0


class Sched:
    ENGS = ("sp", "pool", "act", "dve", "pe")

    def __init__(self, nc, esems, dma_sems):
        self.nc = nc
        self.ops = []
        self.esems = esems
        self.dma_pool = dma_sems
        self.dma_idx = {q: 0 for q in dma_sems}
        self.dma_val = {}
        self.store_ops = []

    def _deps(self, reads, writes):
        deps = set()
        for b in list(reads) + list(writes):
            if b.last_w is not None:
                deps.add(b.last_w)
        for b in writes:
            for r in b.readers:
                deps.add(r)
        return deps

    def _update(self, idx, reads, writes):
        for b in reads:
            b.readers.append(idx)
        for b in writes:
            b.last_w = idx
            b.readers = []

    def op(self, eng, fn, reads=(), writes=()):
        idx = len(self.ops)
        o = Op(eng, fn, self._deps(reads, writes), False)
        self.ops.append(o)
        self._update(idx, reads, writes)
        return idx

    def dma(self, q, fn, n, reads=(), writes=(), store=False):
        idx = len(self.ops)
        o = Op(q, fn, self._deps(reads, writes), True)
        pool = self.dma_pool[q]
        sem = pool[self.dma_idx[q] % len(pool)]
        self.dma_idx[q] += 1
        o.sem = sem
        o.prevval = self.dma_val.get(id(sem), 0)
        o.semval = o.prevval + 16 * n
        self.dma_val[id(sem)] = o.semval
        self.ops.append(o)
        self._update(idx, reads, writes)
        if store:
            self.store_ops.append(idx)
        return idx

    def barrier(self, bufs):
        allidx = len(self.ops)
        last = {}
        for i, o in enumerate(self.ops):
            if o.fn is not None:
                last[o.eng] = i
        dmas = [i for i, o in enumerate(self.ops) if o.is_dma]
        deps = set(last.values()) | set(dmas[-64:])
        for e in self.ENGS:
            o = Op(e, None, set(deps), False)
            self.ops.append(o)
        for b in bufs:
            b.last_w = None
            b.readers = []

    def finalize(self):
        for i, o in enumerate(self.ops):
            for d in o.deps:
                p = self.ops[d]
                if p.is_dma or p.fn is None:
                    continue
                if p.eng != o.eng or (SAME_ENGINE_SYNC and p.eng != "pe") or o.is_dma:
                    p.signal = True
        cnt = {e: 0 for e in self.ENGS}
        for o in self.ops:
            if o.is_dma:
                continue
            if o.fn is None:
                continue
            if o.signal:
                cnt[o.eng] += 1
                o.sig = cnt[o.eng]

    def emit(self, eng_name, e):
        water = {}
        esems = self.esems

        def wait(sem, val):
            k = id(sem)
            if water.get(k, 0) >= val:
                return
            water[k] = val
            e.wait_ge(sem, val)

        for i, o in enumerate(self.ops):
            if o.eng != eng_name:
                continue
            for d in sorted(o.deps):
                p = self.ops[d]
                if p.is_dma:
                    wait(p.sem, p.semval)
                elif p.fn is None:
                    continue
                elif p.eng != eng_name or (SAME_ENGINE_SYNC and eng_name != "pe") or o.is_dma:
                    wait(esems[p.eng], p.sig)
            if o.fn is None:
                continue
            if o.is_dma:
                if o.prevval > 0:
                    wait(o.sem, o.prevval)
                o.fn(e, o.sem)
            else:
                ins = o.fn(e)
                if o.signal:
                    ins.then_inc(esems[eng_name], 1)
        if eng_name == "sp":
            for idx in self.store_ops:
                o = self.ops[idx]
                wait(o.sem, o.semval)


def build_program():
    nc = bass.Bass("TRN2", target_bir_lowering=False)
    din, dout = {}, {}

    def inp(name, shape):
        din[name] = nc.dram_tensor(name, list(shape), F32, kind="ExternalInput").ap()
        return din[name]

    def outp(name, shape):
        dout[name] = nc.dram_tensor(name, list(shape), F32, kind="ExternalOutput").ap()
        return dout[name]

    xT = inp("xT", [D, T]); x_tok = inp("x_tok", [T, D])
    xsT = inp("xsT", [D, NS]); xs_tok = inp("xs_tok", [NS, D])
    cT = inp("cT", [D, 17])
    lru_h0T = inp("lru_h0T", [D, NS]); lru_cvT = inp("lru_cvT", [D, NS, 3])
    ssd_h0 = inp("ssd_h0", [NS, 32, 64, 128]); ssd_cvT = inp("ssd_cvT", [3072, NS, 3])
    w_cond = inp("w_cond", [D, 3072]); b_condT = inp("b_condT", [128, 24]); b_gate = inp("b_gate", [1, 1024])
    w_in = inp("w_in", [D, N_IN])
    lcw = inp("lcw", [128, 8, 4]); lcb = inp("lcb", [128, 8])
    lwa = inp("lwa", [16, 64, 64]); lwx = inp("lwx", [16, 64, 64])
    lba = inp("lba", [128, 8]); lbx = inp("lbx", [128, 8]); llam = inp("llam", [128, 8])
    scw = inp("scw", [128, 24, 4]); scb = inp("scb", [128, 24])
    dtb_row = inp("dtb_row", [1, 32]); alog_row = inp("alog_row", [1, 32]); d_row = inp("d_row", [1, 32])
    dtb_col = inp("dtb_col", [32, 1]); alog_col = inp("alog_col", [32, 1])
    d_x = inp("d_x", [128, 16]); normw_row = inp("normw_row", [1, 2048]); normwT = inp("normwT", [128, 16])
    w_lp = inp("w_lp", [D, D]); w_sp = inp("w_sp", [2048, D]); w_out = inp("w_out", [D, D])
    lng_row = inp("lng_row", [1, D]); lnb_row = inp("lnb_row", [1, D])
    c_ident = inp("c_ident", [128, 128]); c_tri = inp("c_tri", [128, 128]); c_esel = inp("c_esel", [128, 2048])

    y_p = outp("y_p", [T, D]); y_s = outp("y_s", [NS, D])
    o_lh_p = outp("o_lh_p", [128, 8]); o_lc_p = outp("o_lc_p", [128, 8, 3])
    o_sh_p = outp("o_sh_p", [128, 2048]); o_sc_p = outp("o_sc_p", [128, 24, 3])
    o_lh_s = outp("o_lh_s", [128, 8, NS]); o_lc_s = outp("o_lc_s", [128, 8, NS, 3])
    o_sh_s = outp("o_sh_s", [NS, 32, 64, 128]); o_sc_s = outp("o_sc_s", [128, 24, NS, 3])

    SCRN = 23040
    from contextlib import ExitStack
    with ExitStack() as _es:
        _en = _es.enter_context
        hT = _en(nc.sbuf_tensor("hT", [128, KC, TP], BF16))
        yssd = _en(nc.sbuf_tensor("yssd", [128, 16, TP], BF16))
        wt = _en(nc.sbuf_tensor("wt", [128, 8192], BF16))
        scr = _en(nc.sbuf_tensor("scr", [128, SCRN], F32))
        identF = _en(nc.sbuf_tensor("identF", [128, 128], F32))
        identB = _en(nc.sbuf_tensor("identB", [128, 128], BF16))
        triF = _en(nc.sbuf_tensor("triF", [128, 128], F32))
        onesF = _en(nc.sbuf_tensor("onesF", [128, 128], F32))
        smallv = _en(nc.sbuf_tensor("smallv", [128, 512], F32))
        modT = _en(nc.sbuf_tensor("modT", [128, 16, 17], F32))
        pF0, pF1, pF2, pF3, pF4, pF5 = [_en(nc.psum_tensor("pF%d" % i, [128, 512], F32)) for i in range(6)]
        pB0 = _en(nc.psum_tensor("pB0", [128, 1024], BF16))
        pB1 = _en(nc.psum_tensor("pB1", [128, 1024], BF16))
        s_pool, s_act, s_dve, s_pe, s_sp = [_en(nc.semaphore(n)) for n in ("s_pool", "s_act", "s_dve", "s_pe", "s_sp")]
        dq0, dq1, dq2, dq3, dq4, dq5, dq6, dq7 = [_en(nc.semaphore("dq%d" % i)) for i in range(8)]
        dg0, dg1, dg2, dg3, dg4, dg5 = [_en(nc.semaphore("dg%d" % i)) for i in range(6)]
        block = _en(nc.Block())
        S = Sched(nc, {"sp": s_sp, "pool": s_pool, "act": s_act, "dve": s_dve, "pe": s_pe},
                  {"sp": [dq0, dq1, dq2, dq3, dq4, dq5, dq6, dq7], "pool": [dg0, dg1, dg2, dg3, dg4, dg5]})
        PF = [pF0, pF1, pF2, pF3, pF4, pF5]
        bPF = [Buf("pF%d" % i) for i in range(6)]
        bPB = [Buf("pB0"), Buf("pB1")]

        class Carver:
            def __init__(self, start=0):
                self.pos = start

            def f32(self, n):
                a = scr[:, self.pos:self.pos + n]
                self.pos += n
                assert self.pos <= SCRN, self.pos
                return a

            def bf16(self, n):
                n32 = (n + 1) // 2
                a = scr[:, self.pos:self.pos + n32].bitcast(BF16)
                self.pos += n32
                assert self.pos <= SCRN, self.pos
                return a

        def v3(ap, a):
            return ap.rearrange("p (a b) -> p a b", a=a)

        sv = [0]

        def svec(n):
            a = smallv[:, sv[0]:sv[0] + n]
            sv[0] += n
            assert sv[0] <= 512
            return a

        lcw_t = svec(32); lcb_t = svec(8); lba_t = svec(8); lbx_t = svec(8); cvec_t = svec(8); cvec2_t = svec(8)
        scw_t = svec(96); scb_t = svec(24); bcond_t = svec(24); dx_t = svec(16); nwT_t = svec(16)
        dtbc_t = svec(1); alogc_t = svec(1)
        bSV = Buf("smallv")
        bCONST = Buf("consts")
        bMOD = Buf("modT")
        bHT = [Buf("hT%d" % k) for k in range(KC)]
        bHTs = Buf("hTs")
        bYS = [Buf("yssd%d" % k) for k in range(16)]
        bYSs = Buf("yssd_s")
        bWTbig = [Buf("wtbig0"), Buf("wtbig1")]
        bWTsm = [Buf("wtsm%d" % i) for i in range(8)]
        wt_big = [wt[:, i * 4096:(i + 1) * 4096].rearrange("p (k c) -> p k c", k=8) for i in range(2)]
        wt_sm = [wt[:, i * 1024:(i + 1) * 1024].rearrange("p (k c) -> p k c", k=8) for i in range(8)]
        allbufs = []

        def mk(name):
            b = Buf(name)
            allbufs.append(b)
            return b

        def dma_in(q, out_ap, in_ap, reads=(), writes=(), n=1):
            def fn(e, sem, out_ap=out_ap, in_ap=in_ap):
                e.dma_start(out=out_ap, in_=in_ap).then_inc(sem, 16)
            return S.dma(q, fn, 1, reads, writes)

        def dma_out(out_ap, in_ap, reads=()):
            def fn(e, sem, out_ap=out_ap, in_ap=in_ap):
                e.dma_start(out=out_ap, in_=in_ap).then_inc(sem, 16)
            return S.dma("sp", fn, 1, reads, (), store=True)

        wbig_i = [0]

        def load_wbig(src, r0, c0, ncols):
            s = wbig_i[0] % 2
            wbig_i[0] += 1
            dst = wt_big[s][:, :, 0:ncols]
            srcv = src[r0:r0 + 1024, c0:c0 + ncols].rearrange("(k p) n -> p k n", p=128)
            dma_in("pool", dst, srcv, (), (bWTbig[s],))
            return wt_big[s], bWTbig[s]

        wsm_i = [0]

        def load_wsm(src, r0, c0):
            s = wsm_i[0] % 8
            wsm_i[0] += 1
            srcv = src[r0:r0 + 1024, c0:c0 + 128].rearrange("(k p) n -> p k n", p=128)
            dma_in("pool", wt_sm[s], srcv, (), (bWTsm[s],))
            return wt_sm[s], bWTsm[s]

        pf_i = [0]

        def next_pf():
            i = pf_i[0] % 6
            pf_i[0] += 1
            return PF[i], bPF[i]

        def mm_group(out_ap, pairs, reads, wbuf):
            n = len(pairs)

            def fn(e, out_ap=out_ap, pairs=pairs):
                ins = None
                for i, (l, r) in enumerate(pairs):
                    ins = e.matmul(out_ap, lhsT=l, rhs=r, start=(i == 0), stop=(i == n - 1))
                return ins
            return S.op("pe", fn, reads, (wbuf,))

        def act(out, in_, func, reads, writes, bias=None, scale=None, accum_out=None):
            kw = {}
            if bias is not None:
                kw["bias"] = bias
            if scale is not None:
                kw["scale"] = scale
            if accum_out is not None:
                kw["accum_out"] = accum_out
            return S.op("act", lambda e, kw=kw: e.activation(out=out, in_=in_, func=func, **kw), reads, writes)

        def tt(eng, out, in0, in1, op, reads, writes):
            return S.op(eng, lambda e: e.tensor_tensor(out=out, in0=in0, in1=in1, op=op), reads, writes)

        def ts(eng, out, in0, s1, s2, op0, op1, reads, writes):
            if s2 is None:
                return S.op(eng, lambda e: e.tensor_scalar(out=out, in0=in0, scalar1=s1, scalar2=None, op0=op0), reads, writes)
            return S.op(eng, lambda e: e.tensor_scalar(out=out, in0=in0, scalar1=s1, scalar2=s2, op0=op0, op1=op1), reads, writes)

        def stt(out, in0, scalar, in1, op0, op1, reads, writes):
            return S.op("dve", lambda e: e.scalar_tensor_tensor(out=out, in0=in0, scalar=scalar, in1=in1, op0=op0, op1=op1), reads, writes)

        def cp(eng, out, in_, reads, writes):
            if eng == "act":
                return act(out, in_, AF.Copy, reads, writes)
            return S.op(eng, lambda e: e.tensor_copy(out=out, in_=in_), reads, writes)

        try:
            S.op("dve", lambda e: e.memset(smallv[:], 0.0), (), (bSV,))
            dma_in("sp", identF[:], c_ident[:, :], (), (bCONST,))
            dma_in("sp", triF[:], c_tri[:, :], (), (bCONST,))
            dma_in("pool", identB[:], c_ident[:, :], (), (bCONST,))
            S.op("pool", lambda e: e.memset(onesF[:], 1.0), (), (bCONST,))
            for (t_, src_) in ((lcw_t, lcw.rearrange("p a b -> p (a b)")), (lcb_t, lcb), (lba_t, lba), (lbx_t, lbx), (cvec_t, llam),
                               (scw_t, scw.rearrange("p a b -> p (a b)")), (scb_t, scb), (bcond_t, b_condT), (dx_t, d_x), (nwT_t, normwT)):
                dma_in("sp", t_, src_, (), (bSV,))
            dma_in("sp", dtbc_t[0:32, :], dtb_col[:, :], (), (bSV,))
            dma_in("sp", alogc_t[0:32, :], alog_col[:, :], (), (bSV,))
            _stop_at(-3)
            act(cvec_t, cvec_t, AF.Exp, (bSV,), (bSV,), scale=-1.0)
            act(cvec_t, cvec_t, AF.Ln, (bSV,), (bSV,), bias=1.0)
            ts("dve", cvec2_t, cvec_t, -16.0, None, ALU.mult, None, (bSV,), (bSV,))
            ts("dve", cvec_t, cvec_t, -8.0, None, ALU.mult, None, (bSV,), (bSV,))
            act(alogc_t, alogc_t, AF.Exp, (bSV,), (bSV,))
            ts("dve", alogc_t, alogc_t, -1.0, None, ALU.mult, None, (bSV,), (bSV,))

            _stop_at(-2)
            P0 = Carver(0)
            cf = P0.f32(8 * 17); cb_ = P0.bf16(8 * 17)
            xin = [P0.f32(T), P0.f32(T)]
            xs_f = P0.f32(8 * NS); hs_f = P0.f32(8 * NS)
            b_cf, b_cb = mk("cf"), mk("cb")
            b_xin = [mk("xin0"), mk("xin1")]
            b_xs = mk("xs_f")
            cf3 = v3(cf, 8); cb3 = v3(cb_, 8)
            dma_in("sp", cf3, cT.rearrange("(k p) n -> p k n", p=128), (), (b_cf,))
            cp("dve", cb_, cf, (b_cf,), (b_cb,))
            modps, bmodps = PF[0], bPF[0]
            for pc in range(4):
                wtile, wb = load_wbig(w_cond, 0, pc * 512, 512)
                for i in range(4):
                    mc = pc * 4 + i
                    pairs = [(wtile[:, k, i * 128:(i + 1) * 128], cb3[:, k, :]) for k in range(KC)]
                    mm_group(modps[:, mc * 17:(mc + 1) * 17], pairs, (wb, b_cb), bmodps)
            tt("dve", modT[:], v3(modps[:, 0:16 * 17], 16), bcond_t[:, 0:16].unsqueeze(2).to_broadcast([128, 16, 17]),
               ALU.add, (bmodps, bSV), (bMOD,))
            ts("dve", modT[:, 8:16, :], modT[:, 8:16, :], 1.0, None, ALU.add, None, (bMOD,), (bMOD,))
            _stop_at(-1)
            xTv = xT.rearrange("(k p) t -> p k t", p=128)
            for k in range(KC):
                s = k % 2
                dma_in("sp", xin[s], xTv[:, k, :], (), (b_xin[s],))
                act(hT[:, k, 0:T], xin[s], AF.Identity, (b_xin[s], bMOD), (bHT[k],), bias=modT[:, k, 0:1], scale=modT[:, 8 + k, 0:1])
            _stop_at(-0.5)
            dma_in("sp", v3(xs_f, 8), xsT.rearrange("(k p) n -> p k n", p=128), (), (b_xs,))
            _stop_at(-0.4)
            tt("dve", v3(hs_f, 8), v3(xs_f, 8), modT[:, 8:16, 1:17], ALU.mult, (b_xs, bMOD), (b_xs,))
            _stop_at(-0.3)
            tt("dve", v3(hs_f, 8), v3(hs_f, 8), modT[:, 0:8, 1:17], ALU.add, (b_xs, bMOD), (b_xs,))
            _stop_at(-0.2)
            cp("act", hT[:, :, T:TP], v3(hs_f, 8), (b_xs,), (bHTs,))
            bHTall = bHT + [bHTs]
            _stop_at(0)

            S.barrier(allbufs)
            P1 = Carver(0)
            xTg = v3(P1.bf16(4 * T), 4); BTg = P1.bf16(T); CTg = P1.bf16(T)
            xbuf = P1.bf16(T + 8); dg = v3(P1.bf16(512), 4); tail3 = P1.f32(4)
            wsmB = [v3(P1.bf16(1024), 8), v3(P1.bf16(1024), 8)]; wsmC = [v3(P1.bf16(1024), 8), v3(P1.bf16(1024), 8)]
            dt_t = P1.f32(512); adt_t = P1.f32(512); acs_t = P1.f32(512); nacs_t = P1.f32(512)
            e_t = P1.f32(512); w1_t = P1.f32(512); cdec_t = P1.f32(512); tmp_t = P1.f32(512)
            negI = P1.bf16(128); Lmask = P1.bf16(512)
            persist_start = P1.pos
            nw_b = P1.f32(512); D_b = P1.f32(32); dtb_b = P1.f32(32); nA_b = P1.f32(32)
            xsT_all = v3(P1.f32(16 * NS), 16)
            BsT = v3(P1.f32(4 * NS), 4); CsT = v3(P1.f32(4 * NS), 4)
            szs = v3(P1.f32(16 * NS), 16)
            xsb = v3(P1.f32(NS * 4), NS); us_t = P1.f32(NS)
            dtT_s = P1.f32(NS); adtT_s = P1.f32(NS); dAT_s = P1.f32(NS)
            persist_end = P1.pos
            x_tok_d = [P1.bf16(512), P1.bf16(512)]; xs_d = [None, None]; xsc_d = [P1.bf16(512), P1.bf16(512)]
            B_tok_d = [P1.bf16(128), P1.bf16(128)]; MT_d = [v3(P1.bf16(1024), 8), v3(P1.bf16(1024), 8)]
            sz_d = [P1.f32(512), P1.f32(512)]
            CBm = P1.f32(128); _ex = P1.f32(512); ex_t = [_ex, P1.f32(512)]
            wdt = v3(_ex.bitcast(BF16), 8)
            t_a_d = [P1.f32(512), P1.f32(512)]; t_b = tmp_t; junk = P1.bf16(512); mhalf = P1.f32(1)
            yn_t = P1.bf16(512); hst = P1.f32(512); hst_bf = P1.bf16(512); st8_d = [P1.f32(8), P1.f32(8)]
            print("P1 end", P1.pos, "of", SCRN)
            b_xTg = [mk("xTg%d" % i) for i in range(4)]; b_BT = mk("BTg"); b_CT = mk("CTg")
            b_wsmB = [mk("wsmB0"), mk("wsmB1")]; b_wsmC = [mk("wsmC0"), mk("wsmC1")]
            b_xbuf = mk("xbuf"); b_xbufq = [mk("xbufq%d" % q) for q in range(4)]; b_dg = mk("dg"); b_tail = mk("tail3"); b_dtf = mk("dtfam"); b_bc = mk("bcasts")
            b_xsT = mk("xsT_all"); b_BsT = mk("BsT"); b_CsT = mk("CsT"); b_szs = mk("szs"); b_xsb = mk("xsb"); b_us = mk("us")
            b_msk = mk("maskconsts")
            b_dts = mk("dts"); b_wdt = mk("wdt"); b_wz = mk("wzs")
            bd_xtok = [mk("x_tok0"), mk("x_tok1")]; bd_xs = [mk("xs0"), mk("xs1")]; bd_xsc = [mk("xsc0"), mk("xsc1")]
            bd_Btok = [mk("B_tok0"), mk("B_tok1")]; bd_MT = [mk("MT0"), mk("MT1")]; bd_sz = [mk("sz0"), mk("sz1")]
            b_CBm = mk("CBm"); b_ex = [mk("ex0"), mk("ex1")]; bd_ta = [mk("t_a0"), mk("t_a1")]; b_tb = mk("t_b"); b_junk = mk("junk")
            b_yn = mk("yn"); b_hst = mk("hst"); b_hbf = mk("hst_bf"); bd_st8 = [mk("st8a"), mk("st8b")]
            S.op("dve", lambda e: e.memset(xbuf[:, 0:3], 0.0), (), (b_xbuf,))
            S.op("dve", lambda e: e.memset(mhalf, -0.5), (), (b_bc,))
            ts("pool", negI, identF[:], -32768.0, None, ALU.mult, None, (bCONST,), (b_msk,))
            ts("dve", v3(Lmask, 4), triF[:].unsqueeze(1).to_broadcast([128, 4, 128]), -1.0, 1.0, ALU.mult, ALU.add, (bCONST,), (b_msk,))
            dma_in("sp", dtb_b, dtb_row.partition_broadcast(128), (), (b_bc,))
            dma_in("sp", nA_b, alog_row.partition_broadcast(128), (), (b_bc,))
            dma_in("sp", D_b, d_row.partition_broadcast(128), (), (b_bc,))
            act(nA_b, nA_b, AF.Exp, (b_bc,), (b_bc,))
            ts("dve", nA_b, nA_b, -1.0, None, ALU.mult, None, (b_bc,), (b_bc,))
            S.op("pool", lambda e: e.memset(wdt, 0.0), (), (b_wdt,))
            dma_in("pool", wdt[:, :, 0:32], w_in[:, C_DT:C_DT + 32].rearrange("(k p) n -> p k n", p=128), (), (b_wdt,))
            dtps, bdtps = PF[1], bPF[1]
            for c in range(NCH):
                pairs = [(hT[:, k, c * L:(c + 1) * L], wdt[:, k, 0:32]) for k in range(KC)]
                mm_group(dtps[:, c * 32:(c + 1) * 32], pairs, tuple(bHT) + (b_wdt,), bdtps)
            dsps, bdsps = PF[2], bPF[2]
            mm_group(dsps[:, 0:NS], [(wdt[:, k, :], hT[:, k, T:TP]) for k in range(KC)], (bHTs, b_wdt), bdsps)
            act(dtT_s, dsps[:, 0:NS], AF.Exp, (bdsps, bSV), (b_dts,), bias=dtbc_t)
            act(dtT_s, dtT_s, AF.Ln, (b_dts,), (b_dts,), bias=1.0)
            ts("dve", adtT_s, dtT_s, alogc_t, None, ALU.mult, None, (b_dts, bSV), (b_dts,))
            act(dAT_s, adtT_s, AF.Exp, (b_dts,), (b_dts,))
            tt("dve", v3(tmp_t, 16), v3(dtps[:, :], 16), dtb_b.unsqueeze(1).to_broadcast([128, 16, 32]), ALU.add, (bdtps, b_bc), (b_dtf,))
            act(tmp_t, tmp_t, AF.Exp, (b_dtf,), (b_dtf,))
            act(dt_t, tmp_t, AF.Ln, (b_dtf,), (b_dtf,), bias=1.0)
            tt("dve", v3(adt_t, 16), v3(dt_t, 16), nA_b.unsqueeze(1).to_broadcast([128, 16, 32]), ALU.mult, (b_dtf, b_bc), (b_dtf,))
            acsps, bacsps = PF[3], bPF[3]
            totps, btotps = PF[4], bPF[4]
            mm_group(acsps[:, :], [(triF[:], adt_t)], (b_dtf, bCONST), bacsps)
            mm_group(totps[:, :], [(onesF[:], adt_t)], (b_dtf, bCONST), btotps)
            cp("act", acs_t, acsps[:, :], (bacsps,), (b_dtf,))
            act(e_t, acs_t, AF.Exp, (b_dtf,), (b_dtf,))
            tt("dve", tmp_t, totps[:, :], acs_t, ALU.subtract, (btotps, b_dtf), (b_dtf,))
            act(tmp_t, tmp_t, AF.Exp, (b_dtf,), (b_dtf,))
            tt("dve", w1_t, tmp_t, dt_t, ALU.mult, (b_dtf,), (b_dtf,))
            act(tmp_t, dt_t, AF.Ln, (b_dtf,), (b_dtf,))
            tt("dve", nacs_t, tmp_t, acs_t, ALU.subtract, (b_dtf,), (b_dtf,))
            act(cdec_t, totps[:, :], AF.Exp, (btotps,), (b_dtf,))
            _stop_at(1)

            def conv_chunk(wtile, wb, coff, cw4, cbias, out_bf, b_out, tail_dst, s_state_src, s_out, b_s_out, s_state_dst):
                tt("pool", dg, identF[:].unsqueeze(1).to_broadcast([128, 4, 128]), cw4.unsqueeze(2).to_broadcast([128, 4, 128]), ALU.mult,
                   (bCONST, bSV), (b_dg,))
                def proj_q(q):
                    ps, bps = next_pf()
                    pairs = [(wtile[:, k, coff:coff + 128], hT[:, k, q * 512:(q + 1) * 512]) for k in range(KC)]
                    mm_group(ps[:, :], pairs, tuple(bHT) + (wb,), bps)
                    cp("act", xbuf[:, 3 + q * 512:3 + (q + 1) * 512], ps[:, :], (bps,), (b_xbufq[q],))
                    if q == 3:
                        cp("act", tail3[:, 0:3], ps[:, 509:512], (bps,), (b_tail,))
                        dma_out(tail_dst, tail3[:, 0:3], (b_tail,))

                def conv_q(q):
                    ps2, bps2 = next_pf()
                    rd = (b_dg, b_xbufq[q]) + ((b_xbufq[q - 1],) if q > 0 else (b_xbuf,))
                    mm_group(ps2[:, :], [(dg[:, k, :], xbuf[:, q * 512 + k:q * 512 + k + 512]) for k in range(4)], rd, bps2)
                    act(out_bf[:, q * 512:(q + 1) * 512], ps2[:, :], AF.Silu, (bps2, bSV), (b_out,), bias=cbias)
                proj_q(0); proj_q(1); conv_q(0); proj_q(2); conv_q(1); proj_q(3); conv_q(2); conv_q(3)
                ps, bps = next_pf()
                mm_group(ps[:, 0:NS], [(wtile[:, k, coff:coff + 128], hT[:, k, T:TP]) for k in range(KC)], (bHTs, wb), bps)
                dma_in("sp", xsb[:, :, 0:3], s_state_src, (), (b_xsb,))
                cp("act", xsb[:, :, 3], ps[:, 0:NS], (bps,), (b_xsb,))
                dma_out(s_state_dst, xsb[:, :, 1:4], (b_xsb,))
                ts("dve", us_t, xsb[:, :, 0], cw4[:, 0:1], cbias, ALU.mult, ALU.add, (b_xsb, bSV), (b_us,))
                for k in range(1, 4):
                    stt(us_t, xsb[:, :, k], cw4[:, k:k + 1], us_t, ALU.mult, ALU.add, (b_xsb, bSV, b_us), (b_us,))
                act(s_out, us_t, AF.Silu, (b_us,), (b_s_out,))

            def load_big_slot(slot, src, c0, ncols=512):
                srcv = src[0:1024, c0:c0 + ncols].rearrange("(k p) n -> p k n", p=128)
                dma_in("pool", wt_big[slot][:, :, 0:ncols], srcv, (), (bWTbig[slot],))
                return wt_big[slot], bWTbig[slot]

            def load_bc(g):
                par = g % 2
                for (tile_, buf_, c0) in ((wsmB[par], b_wsmB[par], C_SB + g * 128), (wsmC[par], b_wsmC[par], C_SC + g * 128)):
                    dma_in("pool", tile_, w_in[0:1024, c0:c0 + 128].rearrange("(k p) n -> p k n", p=128), (), (buf_,))

            wX, wXb_ = load_big_slot(0, w_in, C_SX)
            load_bc(0)
            for g in range(4):
                wzs, b_wz = load_big_slot(1, w_in, C_SZ + g * 512)
                if g + 1 < 4:
                    load_bc(g + 1)
                par = g % 2
                ch = 16 + g
                conv_chunk(wsmB[par], b_wsmB[par], 0, scw_t[:, ch * 4:(ch + 1) * 4], scb_t[:, ch:ch + 1], BTg, b_BT,
                           o_sc_p[:, ch, :], ssd_cvT[ch * 128:(ch + 1) * 128, :, :], BsT[:, g, :], b_BsT, o_sc_s[:, ch, :, :])
                ch = 20 + g
                conv_chunk(wsmC[par], b_wsmC[par], 0, scw_t[:, ch * 4:(ch + 1) * 4], scb_t[:, ch:ch + 1], CTg, b_CT,
                           o_sc_p[:, ch, :], ssd_cvT[ch * 128:(ch + 1) * 128, :, :], CsT[:, g, :], b_CsT, o_sc_s[:, ch, :, :])
                wtile, wb = wt_big[0], bWTbig[0]
                for i in range(4):
                    ch = g * 4 + i
                    conv_chunk(wtile, wb, i * 128, scw_t[:, ch * 4:(ch + 1) * 4], scb_t[:, ch:ch + 1], xTg[:, i, :], b_xTg[i],
                               o_sc_p[:, ch, :], ssd_cvT[ch * 128:(ch + 1) * 128, :, :], xsT_all[:, ch, :], b_xsT, o_sc_s[:, ch, :, :])
                if g + 1 < 4:
                    load_big_slot(0, w_in, C_SX + (g + 1) * 512)
                dma_in("sp", nw_b, normw_row[:, g * 512:(g + 1) * 512].partition_broadcast(128), (), (b_bc,))
                for i in range(4):
                    ps, bps = next_pf()
                    mm_group(ps[:, 0:NS], [(wzs[:, k, i * 128:(i + 1) * 128], hT[:, k, T:TP]) for k in range(KC)], (bHTs, b_wz), bps)
                    act(szs[:, g * 4 + i, :], ps[:, 0:NS], AF.Silu, (bps,), (b_szs,))
                def bufs(c):
                    par = c % 2
                    return (x_tok_d[par], xs_d[par], xsc_d[par], B_tok_d[par], MT_d[par], sz_d[par],
                            bd_xtok[par], bd_xs[par], bd_xsc[par], bd_Btok[par], bd_MT[par], bd_sz[par],
                            t_a_d[par], bd_ta[par], st8_d[par], bd_st8[par])

                def S4(c, g=g):
                    (x_tok_t, xs_t, xsc_t, B_tok, MT, sz_t, b_xtok, b_xs_, b_xsc, b_Btok, b_MT, b_sz, t_a, b_ta, st8, b_st8) = bufs(c)
                    cs = slice(c * L, (c + 1) * L)

                    def fnY(e):
                        ins = None
                        for h8 in range(8):
                            ins = e.matmul(PF[3][:, h8 * 64:(h8 + 1) * 64], lhsT=MT[:, h8, :], rhs=x_tok_t[:, h8 * 64:(h8 + 1) * 64],
                                           start=True, stop=True)
                        return ins
                    if c > 0:
                        mm_group(PF[4][:, :], [(CTg[:, cs], hst_bf)], (b_CT, b_hbf), bPF[4])
                    mm_group(PF[5][:, :], [(B_tok, xsc_t)], (b_Btok, b_xsc), bPF[5])
                    S.op("pe", fnY, (b_MT, b_xtok), (bPF[3],))

                def Dskip(c, g=g):
                    (x_tok_t, xs_t, xsc_t, B_tok, MT, sz_t, b_xtok, b_xs_, b_xsc, b_Btok, b_MT, b_sz, t_a, b_ta, st8, b_st8) = bufs(c)
                    tt("pool", v3(t_b, 8), v3(x_tok_t, 8), D_b[:, g * 8:(g + 1) * 8].unsqueeze(2).to_broadcast([128, 8, 64]), ALU.mult,
                       (b_xtok, b_bc), (b_tb,))

                def S7(c, g=g):
                    (x_tok_t, xs_t, xsc_t, B_tok, MT, sz_t, b_xtok, b_xs_, b_xsc, b_Btok, b_MT, b_sz, t_a, b_ta, st8, b_st8) = bufs(c)
                    stt(yn_t, t_a, st8[:, 3:4], nw_b, ALU.mult, ALU.mult, (b_ta, b_st8, b_bc), (b_yn,))

                def S5a(c, g=g):
                    hb = c * 32 + g * 8
                    if c > 0:
                        tt("dve", v3(hst, 8), v3(hst, 8), cdec_t[:, hb:hb + 8].unsqueeze(2).to_broadcast([128, 8, 64]), ALU.mult,
                           (b_hst, b_dtf), (b_hst,))
                        tt("dve", hst, hst, PF[5][:, :], ALU.add, (b_hst, bPF[5]), (b_hst,))
                    else:
                        cp("dve", hst, PF[5][:, :], (bPF[5],), (b_hst,))
                    if c == NCH - 1:
                        dma_out(o_sh_p[:, g * 512:(g + 1) * 512], hst, (b_hst,))

                def S5b(c, g=g):
                    (x_tok_t, xs_t, xsc_t, B_tok, MT, sz_t, b_xtok, b_xs_, b_xsc, b_Btok, b_MT, b_sz, t_a, b_ta, st8, b_st8) = bufs(c)
                    hb = c * 32 + g * 8
                    if c > 0:
                        tt("dve", v3(t_a, 8), v3(PF[4][:, :], 8), e_t[:, hb:hb + 8].unsqueeze(2).to_broadcast([128, 8, 64]), ALU.mult,
                           (bPF[4], b_dtf), (b_ta,))
                        tt("dve", t_a, t_a, PF[3][:, :], ALU.add, (b_ta, bPF[3]), (b_ta,))
                    else:
                        cp("dve", t_a, PF[3][:, :], (bPF[3],), (b_ta,))
                    tt("dve", t_a, t_a, t_b, ALU.add, (b_ta, b_tb), (b_ta,))
                    tt("dve", t_a, t_a, sz_t, ALU.mult, (b_ta, b_sz), (b_ta,))

                def S1(c, g=g):
                    cs = slice(c * L, (c + 1) * L)

                    def fnT(e, cs=cs):
                        ins = None
                        for i in range(4):
                            ins = e.transpose(pB0[:, i * 128:(i + 1) * 128], xTg[:, i, cs], identB[:])
                        ins = e.transpose(pB0[:, 512:640], BTg[:, cs], identB[:])
                        return ins
                    S.op("pe", fnT, tuple(b_xTg) + (b_BT, bCONST), (bPB[0],))

                def S1b(c, g=g):
                    cs = slice(c * L, (c + 1) * L)
                    mm_group(PF[0][:, 0:128], [(BTg[:, cs], CTg[:, cs])], (b_BT, b_CT), bPF[0])

                def S1c(c, g=g):
                    cs = slice(c * L, (c + 1) * L)
                    mm_group(PF[0][:, :], [(hT[:, k, cs], wzs[:, k, :]) for k in range(KC)], tuple(bHT) + (b_wz,), bPF[0])

                def S1d(c, half, g=g):
                    hb = c * 32 + g * 8
                    accp, baccp = PF[1 + half], bPF[1 + half]

                    def fnA(e, half=half, hb=hb, accp=accp):
                        ins = e.matmul(accp[:, :], lhsT=negI, rhs=Lmask, start=True, stop=False)
                        for hh in range(4):
                            col = hb + half * 4 + hh
                            ins = e.matmul(accp[:, hh * 128:(hh + 1) * 128], lhsT=adt_t[:, col:col + 1].to_broadcast([128, 128]),
                                           rhs=triF[:], start=False, stop=(hh == 3))
                        return ins
                    S.op("pe", fnA, (b_dtf, bCONST, b_msk), (baccp,))

                def copies(c, g=g):
                    (x_tok_t, xs_t, xsc_t, B_tok, MT, sz_t, b_xtok, b_xs_, b_xsc, b_Btok, b_MT, b_sz, t_a, b_ta, st8, b_st8) = bufs(c)
                    cp("act", x_tok_t, pB0[:, 0:512], (bPB[0],), (b_xtok,))
                    cp("act", B_tok, pB0[:, 512:640], (bPB[0],), (b_Btok,))

                def poolx(c, g=g):
                    (x_tok_t, xs_t, xsc_t, B_tok, MT, sz_t, b_xtok, b_xs_, b_xsc, b_Btok, b_MT, b_sz, t_a, b_ta, st8, b_st8) = bufs(c)
                    hb = c * 32 + g * 8
                    tt("pool", v3(xsc_t, 8), v3(x_tok_t, 8), w1_t[:, hb:hb + 8].unsqueeze(2).to_broadcast([128, 8, 64]), ALU.mult,
                       (b_xtok, b_dtf), (b_xsc,))

                def tanhz(c, g=g):
                    (x_tok_t, xs_t, xsc_t, B_tok, MT, sz_t, b_xtok, b_xs_, b_xsc, b_Btok, b_MT, b_sz, t_a, b_ta, st8, b_st8) = bufs(c)
                    act(sz_t, PF[0][:, :], AF.Tanh, (bPF[0],), (b_sz,), scale=0.5)

                def exps(c, half, g=g):
                    hb = c * 32 + g * 8
                    accp, baccp = PF[1 + half], bPF[1 + half]
                    for hh in range(4):
                        col = hb + half * 4 + hh
                        act(ex_t[half][:, hh * 128:(hh + 1) * 128], accp[:, hh * 128:(hh + 1) * 128], AF.Exp,
                            (baccp, b_dtf), (b_ex[half],), bias=nacs_t[:, col:col + 1])

                def S3a(c, g=g):
                    tt("dve", CBm, PF[0][:, 0:128], triF[:], ALU.mult, (bPF[0], bCONST), (b_CBm,))

                def S3b(c, g=g):
                    (x_tok_t, xs_t, xsc_t, B_tok, MT, sz_t, b_xtok, b_xs_, b_xsc, b_Btok, b_MT, b_sz, t_a, b_ta, st8, b_st8) = bufs(c)
                    stt(sz_t, sz_t, 1.0, PF[0][:, :], ALU.add, ALU.mult, (b_sz, bPF[0]), (b_sz,))

                def S3c(c, half, g=g):
                    (x_tok_t, xs_t, xsc_t, B_tok, MT, sz_t, b_xtok, b_xs_, b_xsc, b_Btok, b_MT, b_sz, t_a, b_ta, st8, b_st8) = bufs(c)
                    stt(MT[:, half * 4:(half + 1) * 4, :], v3(ex_t[half], 4), 1.0e30, CBm.unsqueeze(1).to_broadcast([128, 4, 128]),
                        ALU.min, ALU.mult, (b_ex[half], b_CBm), (b_MT,))

                def S6(c, g=g):
                    (x_tok_t, xs_t, xsc_t, B_tok, MT, sz_t, b_xtok, b_xs_, b_xsc, b_Btok, b_MT, b_sz, t_a, b_ta, st8, b_st8) = bufs(c)
                    act(junk, t_a, AF.Square, (b_ta,), (b_junk, b_st8), accum_out=st8[:, 0:1])
                    ts("pool", st8[:, 1:2], st8[:, 0:1], 1.0 / 512.0, 4.0 * RMS_EPS, ALU.mult, ALU.add, (b_st8,), (b_st8,))
                    tt("pool", st8[:, 3:4], st8[:, 1:2], mhalf[:, 0:1], ALU.pow, (b_st8, b_bc), (b_st8,))

                def S8(c, g=g):
                    cs = slice(c * L, (c + 1) * L)

                    def fnT2(e):
                        ins = None
                        for i in range(4):
                            ins = e.transpose(pB1[:, i * 128:(i + 1) * 128], yn_t[:, i * 128:(i + 1) * 128], identB[:])
                        return ins
                    S.op("pe", fnT2, (b_yn, bCONST), (bPB[1],))
                    cp("act", yssd[:, g * 4:(g + 1) * 4, cs], v3(pB1[:, 0:512], 4), (bPB[1],), tuple(bYS[g * 4:(g + 1) * 4]))

                for s_ in range(-1, NCH + 1):
                    cur, nxt, prv = s_, s_ + 1, s_ - 1
                    hc = 0 <= cur < NCH
                    hn = 0 <= nxt < NCH
                    hp = 0 <= prv < NCH
                    if hc:
                        S4(cur)
                        Dskip(cur)
                    if hp:
                        S7(prv)
                    if hn:
                        S1(nxt)
                        S1d(nxt, 0)
                        S1d(nxt, 1)
                        S1b(nxt)
                        S3a(nxt)
                        copies(nxt)
                        poolx(nxt)
                        exps(nxt, 0)
                    if hc:
                        S5a(cur)
                    if hn:
                        S1c(nxt)
                        exps(nxt, 1)
                    if hc and cur < NCH - 1:
                        cp("act", hst_bf, hst, (b_hst,), (b_hbf,))
                    if hc:
                        S5b(cur)
                    if hn:
                        S3c(nxt, 0)
                        tanhz(nxt)
                        S3c(nxt, 1)
                        S3b(nxt)
                    if hc:
                        S6(cur)
                    if hp:
                        S8(prv)

            _stop_at(2)
            S.barrier(allbufs)
            P1s = Carver(0)
            H0 = [v3(P1s.f32(2048), 16), v3(P1s.f32(2048), 16)]
            HN = [v3(P1s.f32(2048), 16), v3(P1s.f32(2048), 16)]
            T1 = v3(P1s.f32(2048), 16); T2 = v3(P1s.f32(2048), 16)
            assert P1s.pos <= persist_start, (P1s.pos, persist_start)
            P1s2 = Carver(persist_end)
            Bb = v3(P1s2.f32(512), 4); Cb = v3(P1s2.f32(512), 4)
            T3 = v3(P1s2.f32(2048), 16)
            esel = P1s2.f32(2048)
            dAx = v3(P1s2.f32(256), 16); xdt = v3(P1s2.f32(256), 16); ysT = v3(P1s2.f32(256), 16)
            gys = v3(P1s2.f32(256), 16); sqs = v3(P1s2.f32(256), 16); rs_t = v3(P1s2.f32(64), 4)
            b_H0 = [mk("H0a"), mk("H0b")]; b_HN = [mk("HNa"), mk("HNb")]; b_T1 = mk("T1"); b_T2 = mk("T2"); b_T3 = mk("T3")
            b_Bb = mk("Bb"); b_Cb = mk("Cb"); b_esel = mk("esel"); b_dAx = mk("dAx"); b_xdt = mk("xdt"); b_ysT = mk("ysT")
            b_gys = mk("gys"); b_sqs = mk("sqs"); b_rs = mk("rs")
            dma_in("sp", esel, c_esel[:, :], (), (b_esel,))
            exps, bexps = PF[0], bPF[0]
            _stop_at(2.05)

            def fnE(e):
                ins = None
                for j in range(16):
                    ins = e.matmul(exps[:, j * 16:(j + 1) * 16], lhsT=esel[:, j * 128:(j + 1) * 128], rhs=dAT_s, start=True, stop=True)
                for j in range(16):
                    ins = e.matmul(exps[:, 256 + j * 16:256 + (j + 1) * 16], lhsT=esel[:, j * 128:(j + 1) * 128], rhs=dtT_s,
                                   start=True, stop=True)
                return ins
            S.op("pe", fnE, (b_esel, b_dts), (bexps,))
            _stop_at(2.07)
            cp("act", dAx, v3(exps[:, 0:256], 16), (bexps,), (b_dAx,))
            _stop_at(2.08)
            cp("act", xdt, v3(exps[:, 256:512], 16), (bexps,), (b_xdt,))
            tt("dve", xdt, xdt, xsT_all, ALU.mult, (b_xdt, b_xsT), (b_xdt,))
            _stop_at(2.1)
            h0v = ssd_h0.rearrange("s (j e) p n -> s (e p) j n", e=2)
            ohv = o_sh_s.rearrange("s (j e) p n -> s (e p) j n", e=2)
            for s in range(NS):
                sl = s % 2
                dma_in("sp", H0[sl], h0v[s], (), (b_H0[sl],))
                bps_, bbps_ = PF[1], bPF[1]
                cps_, bcps_ = PF[2], bPF[2]

                def fnB(e, s=s):
                    ins = None
                    for g in range(4):
                        ins = e.matmul(PF[1][:, g * 128:(g + 1) * 128], lhsT=BsT[:, g, s:s + 1].to_broadcast([128, 128]), rhs=identF[:],
                                       start=True, stop=True)
                    return ins

                def fnC(e, s=s):
                    ins = None
                    for g in range(4):
                        ins = e.matmul(PF[2][:, g * 128:(g + 1) * 128], lhsT=CsT[:, g, s:s + 1].to_broadcast([128, 128]), rhs=identF[:],
                                       start=True, stop=True)
                    return ins
                S.op("pe", fnB, (b_BsT, bCONST), (bbps_,))
                S.op("pe", fnC, (b_CsT, bCONST), (bcps_,))
                cp("act", Bb, v3(PF[1][:, :], 4), (bbps_,), (b_Bb,))
                cp("act", Cb, v3(PF[2][:, :], 4), (bcps_,), (b_Cb,))
                _stop_at(2.2)
                tt("dve", T1, H0[sl], dAx[:, :, s:s + 1].to_broadcast([128, 16, 128]), ALU.mult, (b_H0[sl], b_dAx), (b_T1,))
                _stop_at(2.3)
                xdt4 = xdt[:, :, s:s + 1].rearrange("p (g j) o -> p g j o", g=4).to_broadcast([128, 4, 4, 128])
                Bb4 = Bb.unsqueeze(2).to_broadcast([128, 4, 4, 128])
                Cb4 = Cb.unsqueeze(2).to_broadcast([128, 4, 4, 128])
                tt("pool", T2.rearrange("p (g j) n -> p g j n", g=4), xdt4, Bb4, ALU.mult, (b_xdt, b_Bb), (b_T2,))
                _stop_at(2.4)
                tt("dve", HN[sl], T1, T2, ALU.add, (b_T1, b_T2), (b_HN[sl],))
                dma_out(ohv[s], HN[sl], (b_HN[sl],))
                tt("pool", T3.rearrange("p (g j) n -> p g j n", g=4), HN[sl].rearrange("p (g j) n -> p g j n", g=4), Cb4, ALU.mult,
                   (b_HN[sl], b_Cb), (b_T3,))
                _stop_at(2.5)
                S.op("dve", lambda e, s=s: e.tensor_reduce(out=ysT[:, :, s], in_=T3, axis=AX.X, op=ALU.add), (b_T3,), (b_ysT,))
                _stop_at(2.6)
            _stop_at(2.7)
            tt("dve", gys, xsT_all, dx_t.unsqueeze(2).to_broadcast([128, 16, NS]), ALU.mult, (b_xsT, bSV), (b_gys,))
            tt("dve", ysT, ysT, gys, ALU.add, (b_ysT, b_gys), (b_ysT,))
            tt("dve", gys, ysT, szs, ALU.mult, (b_ysT, b_szs), (b_gys,))
            tt("dve", sqs, gys, gys, ALU.mult, (b_gys,), (b_sqs,))
            ssp, bssp = PF[3], bPF[3]

            def fnS(e):
                ins = None
                for g in range(4):
                    for i in range(4):
                        ins = e.matmul(ssp[:, g * 16:(g + 1) * 16], lhsT=onesF[:], rhs=sqs[:, g * 4 + i, :], start=(i == 0), stop=(i == 3))
                return ins
            S.op("pe", fnS, (b_sqs, bCONST), (bssp,))
            ts("dve", rs_t, v3(ssp[:, 0:64], 4), 1.0 / 512.0, RMS_EPS, ALU.mult, ALU.add, (bssp,), (b_rs,))
            act(rs_t, rs_t, AF.Sqrt, (b_rs,), (b_rs,))
            S.op("dve", lambda e: e.reciprocal(out=rs_t, in_=rs_t), (b_rs,), (b_rs,))
            tt("dve", gys.rearrange("p (g j) s -> p g j s", g=4), gys.rearrange("p (g j) s -> p g j s", g=4),
               rs_t.unsqueeze(2).to_broadcast([128, 4, 4, NS]), ALU.mult, (b_gys, b_rs), (b_gys,))
            tt("dve", gys, gys, nwT_t.unsqueeze(2).to_broadcast([128, 16, NS]), ALU.mult, (b_gys, bSV), (b_gys,))
            cp("act", yssd[:, :, T:TP], gys, (b_gys,), (bYSs,))

            _stop_at(3)
            S.barrier(allbufs)
            P2 = Carver(0)
            ylru = v3(P2.bf16(8 * TP), 8)
            xbuf2 = P2.bf16(T + 8); dg2 = v3(P2.bf16(512), 4); tail2 = P2.f32(4)
            u2 = P2.f32(T); ubf = P2.bf16(T)
            r_t = P2.f32(T); i_t = P2.f32(T); a_t = P2.f32(T); m_t = P2.f32(T)
            lxs = v3(P2.f32(NS * 4), NS); lus = P2.f32(NS); lusb = P2.bf16(NS); lr = P2.f32(NS); li = P2.f32(NS); la = P2.f32(NS)
            lm = P2.f32(NS); lh0 = v3(P2.f32(8 * NS), 8); lhn = v3(P2.f32(8 * NS), 8); lhp = P2.f32(8)
            hba = P2.f32(8); hbx = P2.f32(8); hcv = P2.f32(8); q25 = P2.f32(1)
            wab = v3(P2.bf16(8 * 128), 8); wxb = v3(P2.bf16(8 * 128), 8)
            bYL = [mk("ylru%d" % k) for k in range(8)]; bYLs = mk("ylru_s")
            b_x2 = mk("xbuf2"); b_x2q = [mk("xbuf2q%d" % q) for q in range(4)]; b_dg2 = mk("dg2"); b_tail2 = mk("tail2")
            b_u2 = mk("u2"); b_ubf = mk("ubf"); b_r = mk("r"); b_i = mk("i"); b_a = mk("a"); b_m = mk("m")
            b_lxs = mk("lxs"); b_lus = mk("lus"); b_lsm = mk("lsm"); b_lh0 = mk("lh0"); b_lhn = mk("lhn"); b_lhp = mk("lhp"); b_wab = mk("wab")
            b_hv = mk("halfvecs")
            S.op("dve", lambda e: e.memset(xbuf2[:, 0:3], 0.0), (), (b_x2,))
            ts("dve", hba, lba_t, 0.5, None, ALU.mult, None, (bSV,), (b_hv,))
            ts("dve", hbx, lbx_t, 0.5, None, ALU.mult, None, (bSV,), (b_hv,))
            ts("dve", hcv, cvec_t, 0.5, None, ALU.mult, None, (bSV,), (b_hv,))
            S.op("dve", lambda e: e.memset(q25, 0.25), (), (b_hv,))
            S.op("pool", lambda e: e.memset(wab, 0.0), (), (b_wab,))
            S.op("pool", lambda e: e.memset(wxb, 0.0), (), (b_wab,))
            for (dst_, src_) in ((wab, lwa), (wxb, lwx)):
                sv_ = src_.rearrange("(j e) k m -> e k j m", e=2)
                for e_ in range(2):
                    dma_in("pool", dst_[e_ * 64:(e_ + 1) * 64, :, e_ * 64:(e_ + 1) * 64], sv_[e_], (), (b_wab,))
            dma_in("sp", lh0, lru_h0T.rearrange("(k p) n -> p k n", p=128), (), (b_lh0,))
            _stop_at(3.05)
            for pp in range(2):
                wX, wXb = load_wbig(w_in, 0, C_LX + pp * 512, 512)
                wZ, wZb = load_wbig(w_in, 0, C_LZ + pp * 512, 512)
                for j4 in range(4):
                    j = pp * 4 + j4
                    co = j4 * 128
                    cw4 = lcw_t[:, j * 4:(j + 1) * 4]
                    tt("pool", dg2, identF[:].unsqueeze(1).to_broadcast([128, 4, 128]), cw4.unsqueeze(2).to_broadcast([128, 4, 128]), ALU.mult,
                       (bCONST, bSV), (b_dg2,))

                    def proj_q(q, j=j, co=co, wX=wX, wXb=wXb):
                        ps, bps = next_pf()
                        mm_group(ps[:, :], [(wX[:, k, co:co + 128], hT[:, k, q * 512:(q + 1) * 512]) for k in range(KC)], tuple(bHT) + (wXb,), bps)
                        cp("act", xbuf2[:, 3 + q * 512:3 + (q + 1) * 512], ps[:, :], (bps,), (b_x2q[q],))
                        if q == 3:
                            cp("act", tail2[:, 0:3], ps[:, 509:512], (bps,), (b_tail2,))
                            dma_out(o_lc_p[:, j, :], tail2[:, 0:3], (b_tail2,))

                    def conv_q(q, j=j):
                        qs = slice(q * 512, (q + 1) * 512)
                        ps2, bps2 = next_pf()
                        rd = (b_dg2, b_x2q[q]) + ((b_x2q[q - 1],) if q > 0 else (b_x2,))
                        mm_group(ps2[:, :], [(dg2[:, k, :], xbuf2[:, q * 512 + k:q * 512 + k + 512]) for k in range(4)], rd, bps2)
                        _stop_at(3.06)
                        act(u2[:, qs], ps2[:, :], AF.Identity, (bps2, bSV), (b_u2,), bias=lcb_t[:, j:j + 1])
                        _stop_at(3.07)
                        cp("dve", ubf[:, qs], u2[:, qs], (b_u2,), (b_ubf,))
                        _stop_at(3.08)
                        ps, bps = next_pf()
                        mm_group(ps[:, :], [(wab[:, j, :], ubf[:, qs])], (b_wab, b_ubf), bps)
                        act(r_t[:, qs], ps[:, :], AF.Tanh, (bps, b_hv), (b_r,), bias=hba[:, j:j + 1], scale=0.5)
                        ps, bps = next_pf()
                        mm_group(ps[:, :], [(wxb[:, j, :], ubf[:, qs])], (b_wab, b_ubf), bps)
                        act(i_t[:, qs], ps[:, :], AF.Tanh, (bps, b_hv), (b_i,), bias=hbx[:, j:j + 1], scale=0.5)
                    proj_q(0); proj_q(1); conv_q(0); proj_q(2); conv_q(1); proj_q(3); conv_q(2); conv_q(3)
                    _stop_at(3.1)
                    ps, bps = next_pf()
                    mm_group(ps[:, 0:NS], [(wX[:, k, co:co + 128], hT[:, k, T:TP]) for k in range(KC)], (bHTs, wXb), bps)
                    dma_in("sp", lxs[:, :, 0:3], lru_cvT[j * 128:(j + 1) * 128, :, :], (), (b_lxs,))
                    cp("act", lxs[:, :, 3], ps[:, 0:NS], (bps,), (b_lxs,))
                    dma_out(o_lc_s[:, j, :, :], lxs[:, :, 1:4], (b_lxs,))
                    ts("dve", lus, lxs[:, :, 0], cw4[:, 0:1], lcb_t[:, j:j + 1], ALU.mult, ALU.add, (b_lxs, bSV), (b_lus,))
                    for k in range(1, 4):
                        stt(lus, lxs[:, :, k], cw4[:, k:k + 1], lus, ALU.mult, ALU.add, (b_lxs, bSV, b_lus), (b_lus,))
                    cp("dve", lusb, lus, (b_lus,), (b_lus,))
                    ps, bps = next_pf()
                    mm_group(ps[:, 0:NS], [(wab[:, j, :], lusb)], (b_wab, b_lus), bps)
                    mm_group(ps[:, 32:32 + NS], [(wxb[:, j, :], lusb)], (b_wab, b_lus), bps)
                    act(lr, ps[:, 0:NS], AF.Tanh, (bps, b_hv), (b_lsm,), bias=hba[:, j:j + 1], scale=0.5)
                    act(li, ps[:, 32:32 + NS], AF.Tanh, (bps, b_hv), (b_lsm,), bias=hbx[:, j:j + 1], scale=0.5)
                    _stop_at(3.2)
                    act(a_t, r_t, AF.Exp, (b_r, b_hv), (b_a,), scale=hcv[:, j:j + 1], bias=hcv[:, j:j + 1])
                    act(m_t, r_t, AF.Exp, (b_r, bSV), (b_m,), scale=cvec_t[:, j:j + 1], bias=cvec_t[:, j:j + 1])
                    act(la, lr, AF.Exp, (b_lsm, b_hv), (b_lsm,), scale=hcv[:, j:j + 1], bias=hcv[:, j:j + 1])
                    act(lm, lr, AF.Exp, (b_lsm, bSV), (b_lsm,), scale=cvec_t[:, j:j + 1], bias=cvec_t[:, j:j + 1])
                    act(m_t, m_t, AF.Sqrt, (b_m, b_hv), (b_m,), scale=-0.25, bias=q25[:, 0:1])
                    act(lm, lm, AF.Sqrt, (b_lsm, b_hv), (b_lsm,), scale=-0.25, bias=q25[:, 0:1])
                    _stop_at(3.3)
                    stt(i_t, i_t, 1.0, u2, ALU.add, ALU.mult, (b_i, b_u2), (b_i,))
                    tt("dve", m_t[:, 1:T], m_t[:, 1:T], i_t[:, 1:T], ALU.mult, (b_m, b_i), (b_m,))
                    ts("dve", m_t[:, 0:1], i_t[:, 0:1], 0.5, None, ALU.mult, None, (b_m, b_i), (b_m,))
                    S.op("dve", lambda e: e.tensor_tensor_scan(out=r_t, data0=a_t, data1=m_t, initial=0.0, op0=ALU.mult, op1=ALU.add),
                         (b_a, b_m, b_r), (b_r,))
                    cp("dve", lhp[:, j:j + 1], r_t[:, T - 1:T], (b_r,), (b_lhp,))
                    _stop_at(3.4)
                    stt(li, li, 1.0, lus, ALU.add, ALU.mult, (b_lsm, b_lus), (b_lsm,))
                    tt("dve", lm, lm, li, ALU.mult, (b_lsm,), (b_lsm,))
                    tt("dve", la, la, lh0[:, j, :], ALU.mult, (b_lsm, b_lh0), (b_lsm,))
                    tt("dve", lhn[:, j, :], la, lm, ALU.add, (b_lsm,), (b_lhn,))
                    _stop_at(3.5)
                    for q in range(4):
                        qs = slice(q * 512, (q + 1) * 512)
                        ps, bps = next_pf()
                        mm_group(ps[:, :], [(wZ[:, k, co:co + 128], hT[:, k, qs]) for k in range(KC)], tuple(bHT) + (wZb,), bps)
                        act(a_t[:, qs], ps[:, :], AF.Tanh, (bps, b_a), (b_a,), scale=0.5)
                        stt(a_t[:, qs], a_t[:, qs], 1.0, ps[:, :], ALU.add, ALU.mult, (b_a, bps), (b_a,))
                    stt(ylru[:, j, 0:T], a_t, 0.5, r_t, ALU.mult, ALU.mult, (b_r, b_a), (bYL[j],))
                    ps, bps = next_pf()
                    mm_group(ps[:, 0:NS], [(wZ[:, k, co:co + 128], hT[:, k, T:TP]) for k in range(KC)], (bHTs, wZb), bps)
                    act(lr, ps[:, 0:NS], AF.Tanh, (bps,), (b_lsm,), scale=0.5)
                    stt(lr, lr, 1.0, ps[:, 0:NS], ALU.add, ALU.mult, (b_lsm, bps), (b_lsm,))
                    stt(ylru[:, j, T:TP], lr, 0.5, lhn[:, j, :], ALU.mult, ALU.mult, (b_lsm, b_lhn), (bYLs,))
            dma_out(o_lh_p[:, :], lhp, (b_lhp,))
            dma_out(o_lh_s[:, :, :], lhn, (b_lhn,))

            _stop_at(4)
            S.barrier([b for b in allbufs if b not in bYL and b is not bYLs] + [bWTbig[0], bWTbig[1]] + bWTsm)
            P3 = Carver((8 * TP) // 2)
            merged = v3(P3.bf16(8 * TP), 8)
            sgA = P3.f32(512); sgB = P3.f32(512); tA = P3.f32(512); tB = P3.f32(512)
            yo_t = [P3.f32(1024), P3.f32(1024)]; gts = P3.f32(1024)
            bnst_d = [P3.f32(16), P3.f32(16)]; mh3 = P3.f32(1); junk3 = P3.bf16(1024); cbb = v3(P3.bf16(8 * 128), 8); cf2 = v3(P3.f32(8 * 17), 8); cb2 = v3(P3.bf16(8 * 17), 8)
            P3b = Carver(0)
            gate_b = P3b.f32(1024); lng_b = P3b.f32(1024); lnb_b = P3b.f32(1024); bg_b = P3b.f32(1024)
            xtk = [P3b.f32(1024), P3b.f32(1024)]; resid_d = [P3b.f32(1024), scr[:, (8 * TP) // 2 + 8 * TP // 2:(8 * TP) // 2 + 8 * TP // 2 + 1024]]; xn_d = [P3b.f32(1024), scr[:, (8 * TP) // 2 + 8 * TP // 2 + 1024:(8 * TP) // 2 + 8 * TP // 2 + 2048]]
            assert P3b.pos <= (8 * TP) // 2
            bMG = [mk("merged%d" % k) for k in range(8)]; bMGs = mk("merged_s")
            b_sgA = mk("sgA"); b_sgB = mk("sgB"); b_tA = mk("tA"); b_tB = mk("tB"); b_gateb = mk("gate_b"); b_ln = mk("lnbc")
            b_xtk = [mk("xtk0"), mk("xtk1")]; b_res_d = [mk("resid0"), mk("resid1")]; b_xn_d = [mk("xn0"), mk("xn1")]; b_bn_d = [mk("bn0"), mk("bn1")]; b_yo = [mk("yo0"), mk("yo1")]
            b_cbb = mk("cbb"); b_c2 = mk("c2"); b_gts = mk("gts"); b_j3 = mk("junk3")
            colsets = [(slice(q * 512, (q + 1) * 512), 512) for q in range(4)] + [(slice(T, TP), NS)]
            for jo in range(8):
                wA, wAb = load_wsm(w_lp, 0, jo * 128)
                wB0, wB0b = load_wsm(w_sp, 0, jo * 128)
                wB1, wB1b = load_wsm(w_sp, 1024, jo * 128)
                wgA, wgAb = load_wsm(w_in, 0, C_MA + jo * 128)
                wgB, wgBb = load_wsm(w_in, 0, C_MB + jo * 128)
                for qi, (cs, n) in enumerate(colsets):
                    smp = qi == 4
                    rH = (bHTs,) if smp else tuple(bHT)
                    rYL = (bYLs,) if smp else tuple(bYL)
                    rYS = (bYSs,) if smp else tuple(bYS)
                    pA, bpA = next_pf()
                    mm_group(pA[:, 0:n], [(wA[:, k, :], ylru[:, k, cs]) for k in range(8)], rYL + (wAb,), bpA)
                    pB, bpB = next_pf()
                    mm_group(pB[:, 0:n], [(wB0[:, k, :], yssd[:, k, cs]) for k in range(8)] + [(wB1[:, k, :], yssd[:, 8 + k, cs]) for k in range(8)],
                             rYS + (wB0b, wB1b), bpB)
                    pgA, bpgA = next_pf()
                    mm_group(pgA[:, 0:n], [(wgA[:, k, :], hT[:, k, cs]) for k in range(8)], rH + (wgAb,), bpgA)
                    pgB, bpgB = next_pf()
                    mm_group(pgB[:, 0:n], [(wgB[:, k, :], hT[:, k, cs]) for k in range(8)], rH + (wgBb,), bpgB)
                    act(sgA[:, 0:n], pgA[:, 0:n], AF.Sigmoid, (bpgA,), (b_sgA,))
                    act(sgB[:, 0:n], pgB[:, 0:n], AF.Sigmoid, (bpgB,), (b_sgB,))
                    tt("dve", tA[:, 0:n], sgA[:, 0:n], pA[:, 0:n], ALU.mult, (b_sgA, bpA), (b_tA,))
                    tt("dve", tB[:, 0:n], sgB[:, 0:n], pB[:, 0:n], ALU.mult, (b_sgB, bpB), (b_tB,))
                    tt("dve", merged[:, jo, cs], tA[:, 0:n], tB[:, 0:n], ALU.add, (b_tA, b_tB), (bMGs if smp else bMG[jo],))
            S.barrier(bWTsm + bWTbig + bYL + [bYLs, b_sgA, b_sgB, b_tA, b_tB])
            dma_in("sp", cf2, cT.rearrange("(k p) n -> p k n", p=128), (), (b_c2,))
            cp("dve", cb2, cf2, (b_c2,), (b_c2,))
            cp("dve", cbb, cf2[:, :, 0:1].to_broadcast([128, 8, 128]), (b_c2,), (b_cbb,))
            S.op("dve", lambda e: e.memset(mh3, -0.5), (), (b_ln,))
            dma_in("sp", bg_b, b_gate.partition_broadcast(128), (), (b_ln,))
            dma_in("sp", lng_b, lng_row.partition_broadcast(128), (), (b_ln,))
            dma_in("sp", lnb_b, lnb_row.partition_broadcast(128), (), (b_ln,))
            for hf in range(2):
                wg, wgb = load_wbig(w_cond, 0, 2048 + hf * 512, 512)
                ps, bps = next_pf()
                mm_group(ps[:, :], [(cbb[:, k, :], wg[:, k, :]) for k in range(8)], (b_cbb, wgb), bps)
                tt("dve", gate_b[:, hf * 512:(hf + 1) * 512], ps[:, :], bg_b[:, hf * 512:(hf + 1) * 512], ALU.add, (bps, b_ln), (b_gateb,))
                ps, bps = next_pf()
                mm_group(ps[0:NS, :], [(cb2[:, k, 1:17], wg[:, k, :]) for k in range(8)], (b_c2, wgb), bps)
                tt("dve", gts[0:NS, hf * 512:(hf + 1) * 512], ps[0:NS, :], bg_b[0:NS, hf * 512:(hf + 1) * 512], ALU.add, (bps, b_ln), (b_gts,))
            wo0, wo0b = load_wbig(w_out, 0, 0, 512)
            wo1, wo1b = load_wbig(w_out, 0, 512, 512)
            for ti in range(NCH + 1):
                smp = ti == NCH
                np_ = NS if smp else 128
                cs = slice(T, TP) if smp else slice(ti * 128, (ti + 1) * 128)
                sl = ti % 2
                resid, xn, bnst, b_res, b_xn, b_bn = resid_d[sl], xn_d[sl], bnst_d[sl], b_res_d[sl], b_xn_d[sl], b_bn_d[sl]
                rM = (bMGs,) if smp else tuple(bMG)
                dma_in("sp", xtk[sl][0:np_, :], xs_tok[:, :] if smp else x_tok[ti * 128:(ti + 1) * 128, :], (), (b_xtk[sl],))
                gsrc = gts if smp else gate_b
                bgs = b_gts if smp else b_gateb
                for hf, (wo, wob) in enumerate(((wo0, wo0b), (wo1, wo1b))):
                    ps, bps = next_pf()
                    mm_group(ps[0:np_, :], [(merged[:, k, cs], wo[:, k, :]) for k in range(8)], rM + (wob,), bps)
                    tt("dve", resid[0:np_, hf * 512:(hf + 1) * 512], ps[0:np_, :], gsrc[0:np_, hf * 512:(hf + 1) * 512], ALU.mult,
                       (bps, bgs), (b_res,))
                stt(resid[0:np_, :], xtk[sl][0:np_, :], ALPHA, resid[0:np_, :], ALU.mult, ALU.add, (b_xtk[sl], b_res), (b_res,))
                act(junk3[0:np_, :], resid[0:np_, :], AF.Copy, (b_res,), (b_j3, b_bn), accum_out=bnst[0:np_, 0:1])
                act(junk3[0:np_, :], resid[0:np_, :], AF.Square, (b_res,), (b_j3, b_bn), accum_out=bnst[0:np_, 1:2])
                ts("pool", bnst[0:np_, 12:13], bnst[0:np_, 0:1], 1.0 / D, None, ALU.mult, None, (b_bn,), (b_bn,))
                tt("pool", bnst[0:np_, 2:3], bnst[0:np_, 12:13], bnst[0:np_, 12:13], ALU.mult, (b_bn,), (b_bn,))
                ts("pool", bnst[0:np_, 3:4], bnst[0:np_, 1:2], 1.0 / D, LN_EPS, ALU.mult, ALU.add, (b_bn,), (b_bn,))
                tt("pool", bnst[0:np_, 14:15], bnst[0:np_, 3:4], bnst[0:np_, 2:3], ALU.subtract, (b_bn,), (b_bn,))
                tt("pool", bnst[0:np_, 15:16], bnst[0:np_, 14:15], mh3[0:np_, 0:1], ALU.pow, (b_bn, b_ln), (b_bn,))
                stt(xn[0:np_, :], resid[0:np_, :], bnst[0:np_, 12:13], lng_b[0:np_, :], ALU.subtract, ALU.mult, (b_res, b_bn, b_ln), (b_xn,))
                stt(yo_t[sl][0:np_, :], xn[0:np_, :], bnst[0:np_, 15:16], lnb_b[0:np_, :], ALU.mult, ALU.add, (b_xn, b_bn, b_ln), (b_yo[sl],))
                dma_out(y_s[:, :] if smp else y_p[ti * 128:(ti + 1) * 128, :], yo_t[sl][0:np_, :], (b_yo[sl],))

        except _StopRec:
            pass
        S.finalize()

        @block.sync
        def _(e):
            S.emit("sp", e)

        @block.gpsimd
        def _(e):
            S.emit("pool", e)

        @block.scalar
        def _(e):
            S.emit("act", e)

        @block.vector
        def _(e):
            S.emit("dve", e)

        @block.tensor
        def _(e):
            S.emit("pe", e)
    return nc


_NC_CACHE = {}


def _vec_pk(v, k):
    return np.ascontiguousarray(np.asarray(v, np.float32).reshape(k, 128).T)


def kernel(x_prompt, x_sample, state_lru_h, state_lru_conv, state_ssd_h, state_ssd_conv,
           c_prompt, c_sample, w_cond, b_cond, w_in, lru_conv_w, lru_conv_b, lru_wa, lru_ba,
           lru_wx, lru_bx, lru_lambda, ssd_conv_w, ssd_conv_b, ssd_dt_bias, ssd_a_log, ssd_d,
           ssd_norm_w, w_lru_proj, w_ssd_proj, w_out, ln_g, ln_b):
    f = lambda a: np.ascontiguousarray(np.asarray(a, np.float32))
    x_prompt, x_sample = f(x_prompt), f(x_sample)
    if "nc" not in _NC_CACHE:
        _NC_CACHE["nc"] = build_program()
    nc = _NC_CACHE["nc"]
    shared = {
        "w_cond": f(w_cond[0]), "b_condT": _vec_pk(b_cond[0], 24), "b_gate": f(np.asarray(b_cond)[0:1, 2048:3072]),
        "w_in": f(w_in[0]),
        "lcw": f(np.asarray(lru_conv_w)[0].reshape(4, 8, 128).transpose(2, 1, 0)), "lcb": _vec_pk(lru_conv_b[0], 8),
        "lwa": f(lru_wa[0]), "lwx": f(lru_wx[0]),
        "lba": _vec_pk(lru_ba[0], 8), "lbx": _vec_pk(lru_bx[0], 8), "llam": _vec_pk(lru_lambda[0], 8),
        "scw": f(np.asarray(ssd_conv_w)[0].reshape(4, 24, 128).transpose(2, 1, 0)), "scb": _vec_pk(ssd_conv_b[0], 24),
        "dtb_row": f(np.asarray(ssd_dt_bias)[0:1]), "alog_row": f(np.asarray(ssd_a_log)[0:1]), "d_row": f(np.asarray(ssd_d)[0:1]),
        "dtb_col": f(np.asarray(ssd_dt_bias)[0].reshape(32, 1)), "alog_col": f(np.asarray(ssd_a_log)[0].reshape(32, 1)),
        "d_x": _vec_pk(np.repeat(np.asarray(ssd_d, np.float32)[0], 64), 16),
        "normw_row": f(np.asarray(ssd_norm_w)[0:1]), "normwT": _vec_pk(ssd_norm_w[0], 16),
        "w_lp": f(w_lru_proj[0]), "w_sp": f(w_ssd_proj[0]), "w_out": f(w_out[0]),
        "lng_row": f(np.asarray(ln_g)[0:1]), "lnb_row": f(np.asarray(ln_b)[0:1]),
        "c_ident": np.eye(128, dtype=np.float32), "c_tri": np.triu(np.ones((128, 128), np.float32)),
        "c_esel": f((np.arange(128)[:, None] == (np.arange(2048)[None, :] // 64)).astype(np.float32)),
    }
    in_maps = []
    for i in range(NCORES):
        ss = slice(NS * i, NS * (i + 1))
        cT = np.concatenate([np.asarray(c_prompt, np.float32)[i][:, None], np.asarray(c_sample, np.float32)[ss].T], axis=1)
        m = dict(shared)
        m.update({
            "xT": f(x_prompt[i].T), "x_tok": f(x_prompt[i]),
            "xsT": f(x_sample[ss, 0, :].T), "xs_tok": f(x_sample[ss, 0, :]),
            "cT": f(cT),
            "lru_h0T": f(np.asarray(state_lru_h)[0, ss].T),
            "lru_cvT": f(np.asarray(state_lru_conv)[0, ss].transpose(2, 0, 1)),
            "ssd_h0": f(np.asarray(state_ssd_h)[0, ss]),
            "ssd_cvT": f(np.asarray(state_ssd_conv)[0, ss].transpose(2, 0, 1)),
        })
        in_maps.append(m)
    res = run_bass_kernel_spmd(nc, in_maps, core_ids=list(range(NCORES)))
    R = res.results
    y_prompt = np.stack([R[i]["y_p"] for i in range(NCORES)])
    y_sample = np.concatenate([R[i]["y_s"] for i in range(NCORES)])[:, None, :]
    lh_p = np.stack([R[i]["o_lh_p"].T.reshape(1024) for i in range(NCORES)])[None]
    lc_p = np.stack([R[i]["o_lc_p"].transpose(2, 1, 0).reshape(3, 1024) for i in range(NCORES)])[None]
    sh_p = np.stack([R[i]["o_sh_p"].T.reshape(32, 64, 128) for i in range(NCORES)])[None]
    sc_p = np.stack([R[i]["o_sc_p"].transpose(2, 1, 0).reshape(3, 3072) for i in range(NCORES)])[None]
    lh_s = np.concatenate([R[i]["o_lh_s"].transpose(2, 1, 0).reshape(NS, 1024) for i in range(NCORES)])[None]
    lc_s = np.concatenate([R[i]["o_lc_s"].transpose(2, 3, 1, 0).reshape(NS, 3, 1024) for i in range(NCORES)])[None]
    sh_s = np.concatenate([R[i]["o_sh_s"] for i in range(NCORES)])[None]
    sc_s = np.concatenate([R[i]["o_sc_s"].transpose(2, 3, 1, 0).reshape(NS, 3, 3072) for i in range(NCORES)])[None]
    c32 = lambda a: np.ascontiguousarray(a, dtype=np.float32)
    return (c32(y_prompt), c32(y_sample), c32(lh_p), c32(lc_p), c32(sh_p), c32(sc_p),
            c32(lh_s), c32(lc_s), c32(sh_s), c32(sc_s))
```

```python
import numpy as np
import concourse.bass as bass
import concourse.mybir as mybir
from concourse.bass_utils import run_bass_kernel_spmd

F32 = mybir.dt.float32
BF16 = mybir.dt.bfloat16
AF = mybir.ActivationFunctionType
ALU = mybir.AluOpType
AX = mybir.AxisListType

NCORES = 8
D = 1024
T = 2048
NS = 16
TP = T + NS
KC = 8
NCH = 16
L = 128
N_IN = 9248
C_LX, C_LZ, C_SZ, C_SX, C_SB, C_SC, C_DT, C_MA, C_MB = 0, 1024, 2048, 4096, 6144, 6656, 7168, 7200, 8224
ALPHA = 2.0 ** 0.25
LN_EPS = 1e-5
RMS_EPS = 1e-5
SAME_ENGINE_SYNC = True
import os as _os
KSTOP = float(_os.environ.get('KSTOP', '99'))


class _StopRec(Exception):
    pass


def _stop_at(n):
    if KSTOP <= n:
        raise _StopRec()


class Buf:
    __slots__ = ("name", "last_w", "readers")

    def __init__(self, name):
        self.name = name
        self.last_w = None
        self.readers = []


class Op:
    __slots__ = ("eng", "fn", "deps", "is_dma", "sem", "semval", "prevval", "signal", "sig")

    def __init__(self, eng, fn, deps, is_dma):
        self.eng, self.fn, self.deps, self.is_dma = eng, fn, deps, is_dma
        self.sem = None
        self.semval = 0
        self.prevval = 0
        self.signal = False
        self.sig = 0


class Sched:
    ENGS = ("sp", "pool", "act", "dve", "pe")

    def __init__(self, nc, esems, dma_sems):
        self.nc = nc
        self.ops = []
        self.esems = esems
        self.dma_pool = dma_sems
        self.dma_idx = {q: 0 for q in dma_sems}
        self.dma_val = {}
        self.store_ops = []

    def _deps(self, reads, writes):
        deps = set()
        for b in list(reads) + list(writes):
            if b.last_w is not None:
                deps.add(b.last_w)
        for b in writes:
            for r in b.readers:
                deps.add(r)
        return deps

    def _update(self, idx, reads, writes):
        for b in reads:
            b.readers.append(idx)
        for b in writes:
            b.last_w = idx
            b.readers = []

    def op(self, eng, fn, reads=(), writes=()):
        idx = len(self.ops)
        o = Op(eng, fn, self._deps(reads, writes), False)
        self.ops.append(o)
        self._update(idx, reads, writes)
        return idx

    def dma(self, q, fn, n, reads=(), writes=(), store=False):
        idx = len(self.ops)
        o = Op(q, fn, self._deps(reads, writes), True)
        pool = self.dma_pool[q]
        sem = pool[self.dma_idx[q] % len(pool)]
        self.dma_idx[q] += 1
        o.sem = sem
        o.prevval = self.dma_val.get(id(sem), 0)
        o.semval = o.prevval + 16 * n
        self.dma_val[id(sem)] = o.semval
        self.ops.append(o)
        self._update(idx, reads, writes)
        if store:
            self.store_ops.append(idx)
        return idx

    def barrier(self, bufs):
        allidx = len(self.ops)
        last = {}
        for i, o in enumerate(self.ops):
            if o.fn is not None:
                last[o.eng] = i
        dmas = [i for i, o in enumerate(self.ops) if o.is_dma]
        deps = set(last.values()) | set(dmas[-64:])
        for e in self.ENGS:
            o = Op(e, None, set(deps), False)
            self.ops.append(o)
        for b in bufs:
            b.last_w = None
            b.readers = []

    def finalize(self):
        for i, o in enumerate(self.ops):
            for d in o.deps:
                p = self.ops[d]
                if p.is_dma or p.fn is None:
                    continue
                if p.eng != o.eng or (SAME_ENGINE_SYNC and p.eng != "pe") or o.is_dma:
                    p.signal = True
        cnt = {e: 0 for e in self.ENGS}
        for o in self.ops:
            if o.is_dma:
                continue
            if o.fn is None:
                continue
            if o.signal:
                cnt[o.eng] += 1
                o.sig = cnt[o.eng]

    def emit(self, eng_name, e):
        water = {}
        esems = self.esems

        def wait(sem, val):
            k = id(sem)
            if water.get(k, 0) >= val:
                return
            water[k] = val
            e.wait_ge(sem, val)

        for i, o in enumerate(self.ops):
            if o.eng != eng_name:
                continue
            for d in sorted(o.deps):
                p = self.ops[d]
                if p.is_dma:
                    wait(p.sem, p.semval)
                elif p.fn is None:
                    continue
                elif p.eng != eng_name or (SAME_ENGINE_SYNC and eng_name != "pe") or o.is_dma:
                    wait(esems[p.eng], p.sig)
            if o.fn is None:
                continue
            if o.is_dma:
                if o.prevval > 0:
                    wait(o.sem, o.prevval)
                o.fn(e, o.sem)
            else:
                ins = o.fn(e)
                if o.signal:
                    ins.then_inc(esems[eng_name], 1)
        if eng_name == "sp":
            for idx in self.store_ops:
                o = self.ops[idx]
                wait(o.sem, o.semval)


def build_program():
    nc = bass.Bass("TRN2", target_bir_lowering=False)
    din, dout = {}, {}

    def inp(name, shape):
        din[name] = nc.dram_tensor(name, list(shape), F32, kind="ExternalInput").ap()
        return din[name]

    def outp(name, shape):
        dout[name] = nc.dram_tensor(name, list(shape), F32, kind="ExternalOutput").ap()
        return dout[name]

    xT = inp("xT", [D, T]); x_tok = inp("x_tok", [T, D])
    xsT = inp("xsT", [D, NS]); xs_tok = inp("xs_tok", [NS, D])
    cT = inp("cT", [D, 17])
    lru_h0T = inp("lru_h0T", [D, NS]); lru_cvT = inp("lru_cvT", [D, NS, 3])
    ssd_h0 = inp("ssd_h0", [NS, 32, 64, 128]); ssd_cvT = inp("ssd_cvT", [3072, NS, 3])
    w_cond = inp("w_cond", [D, 3072]); b_condT = inp("b_condT", [128, 24]); b_gate = inp("b_gate", [1, 1024])
    w_in = inp("w_in", [D, N_IN])
    lcw = inp("lcw", [128, 8, 4]); lcb = inp("lcb", [128, 8])
    lwa = inp("lwa", [16, 64, 64]); lwx = inp("lwx", [16, 64, 64])
    lba = inp("lba", [128, 8]); lbx = inp("lbx", [128, 8]); llam = inp("llam", [128, 8])
    scw = inp("scw", [128, 24, 4]); scb = inp("scb", [128, 24])
    dtb_row = inp("dtb_row", [1, 32]); alog_row = inp("alog_row", [1, 32]); d_row = inp("d_row", [1, 32])
    dtb_col = inp("dtb_col", [32, 1]); alog_col = inp("alog_col", [32, 1])
    d_x = inp("d_x", [128, 16]); normw_row = inp("normw_row", [1, 2048]); normwT = inp("normwT", [128, 16])
    w_lp = inp("w_lp", [D, D]); w_sp = inp("w_sp", [2048, D]); w_out = inp("w_out", [D, D])
    lng_row = inp("lng_row", [1, D]); lnb_row = inp("lnb_row", [1, D])
    c_ident = inp("c_ident", [128, 128]); c_tri = inp("c_tri", [128, 128]); c_esel = inp("c_esel", [128, 2048])

    y_p = outp("y_p", [T, D]); y_s = outp("y_s", [NS, D])
    o_lh_p = outp("o_lh_p", [128, 8]); o_lc_p = outp("o_lc_p", [128, 8, 3])
    o_sh_p = outp("o_sh_p", [128, 2048]); o_sc_p = outp("o_sc_p", [128, 24, 3])
    o_lh_s = outp("o_lh_s", [128, 8, NS]); o_lc_s = outp("o_lc_s", [128, 8, NS, 3])
    o_sh_s = outp("o_sh_s", [NS, 32, 64, 128]); o_sc_s = outp("o_sc_s", [128, 24, NS, 3])

    SCRN = 23040
    from contextlib import ExitStack
    with ExitStack() as _es:
        _en = _es.enter_context
        hT = _en(nc.sbuf_tensor("hT", [128, KC, TP], BF16))
        yssd = _en(nc.sbuf_tensor("yssd", [128, 16, TP], BF16))
        wt = _en(nc.sbuf_tensor("wt", [128, 8192], BF16))
        scr = _en(nc.sbuf_tensor("scr", [128, SCRN], F32))
        identF = _en(nc.sbuf_tensor("identF", [128, 128], F32))
        identB = _en(nc.sbuf_tensor("identB", [128, 128], BF16))
        triF = _en(nc.sbuf_tensor("triF", [128, 128], F32))
        onesF = _en(nc.sbuf_tensor("onesF", [128, 128], F32))
        smallv = _en(nc.sbuf_tensor("smallv", [128, 512], F32))
        modT = _en(nc.sbuf_tensor("modT", [128, 16, 17], F32))
        pF0, pF1, pF2, pF3, pF4, pF5 = [_en(nc.psum_tensor("pF%d" % i, [128, 512], F32)) for i in range(6)]
        pB0 = _en(nc.psum_tensor("pB0", [128, 1024], BF16))
        pB1 = _en(nc.psum_tensor("pB1", [128, 1024], BF16))
        s_pool, s_act, s_dve, s_pe, s_sp = [_en(nc.semaphore(n)) for n in ("s_pool", "s_act", "s_dve", "s_pe", "s_sp")]
        dq0, dq1, dq2, dq3, dq4, dq5, dq6, dq7 = [_en(nc.semaphore("dq%d" % i)) for i in range(8)]
        dg0, dg1, dg2, dg3, dg4, dg5 = [_en(nc.semaphore("dg%d" % i)) for i in range(6)]
        da0, da1, da2, da3 = [_en(nc.semaphore("da%d" % i)) for i in range(4)]
        block = _en(nc.Block())
        S = Sched(nc, {"sp": s_sp, "pool": s_pool, "act": s_act, "dve": s_dve, "pe": s_pe},
                  {"sp": [dq0, dq1, dq2, dq3, dq4, dq5, dq6, dq7], "pool": [dg0, dg1, dg2, dg3, dg4, dg5], "act": [da0, da1, da2, da3]})
        PF = [pF0, pF1, pF2, pF3, pF4, pF5]
        bPF = [Buf("pF%d" % i) for i in range(6)]
        bPB = [Buf("pB0"), Buf("pB1")]

        class Carver:
            def __init__(self, start=0):
                self.pos = start

            def f32(self, n):
                a = scr[:, self.pos:self.pos + n]
                self.pos += n
                assert self.pos <= SCRN, self.pos
                return a

            def bf16(self, n):
                n32 = (n + 1) // 2
                a = scr[:, self.pos:self.pos + n32].bitcast(BF16)
                self.pos += n32
                assert self.pos <= SCRN, self.pos
                return a

        def v3(ap, a):
            return ap.rearrange("p (a b) -> p a b", a=a)

        sv = [0]

        def svec(n):
            a = smallv[:, sv[0]:sv[0] + n]
            sv[0] += n
            assert sv[0] <= 512
            return a

        lcw_t = svec(32); lcb_t = svec(8); lba_t = svec(8); lbx_t = svec(8); cvec_t = svec(8); cvec2_t = svec(8)
        scw_t = svec(96); scb_t = svec(24); bcond_t = svec(24); dx_t = svec(16); nwT_t = svec(16)
        dtbc_t = svec(1); alogc_t = svec(1)
        bSV = Buf("smallv")
        bCONST = Buf("consts")
        bMOD = Buf("modT")
        bHT = [Buf("hT%d" % k) for k in range(KC)]
        bHTs = Buf("hTs")
        bYS = [Buf("yssd%d" % k) for k in range(16)]
        bYSs = Buf("yssd_s")
        bWTbig = [Buf("wtbig0"), Buf("wtbig1")]
        bWTsm = [Buf("wtsm%d" % i) for i in range(8)]
        wt_big = [wt[:, i * 4096:(i + 1) * 4096].rearrange("p (k c) -> p k c", k=8) for i in range(2)]
        wt_sm = [wt[:, i * 1024:(i + 1) * 1024].rearrange("p (k c) -> p k c", k=8) for i in range(8)]
        allbufs = []

        def mk(name):
            b = Buf(name)
            allbufs.append(b)
            return b

        def dma_in(q, out_ap, in_ap, reads=(), writes=(), n=1):
            def fn(e, sem, out_ap=out_ap, in_ap=in_ap):
                e.dma_start(out=out_ap, in_=in_ap).then_inc(sem, 16)
            return S.dma(q, fn, 1, reads, writes)

        def dma_out(out_ap, in_ap, reads=()):
            def fn(e, sem, out_ap=out_ap, in_ap=in_ap):
                e.dma_start(out=out_ap, in_=in_ap).then_inc(sem, 16)
            return S.dma("sp", fn, 1, reads, (), store=True)

        wbig_i = [0]

        def load_wbig(src, r0, c0, ncols):
            s = wbig_i[0] % 2
            wbig_i[0] += 1
            dst = wt_big[s][:, :, 0:ncols]
            srcv = src[r0:r0 + 1024, c0:c0 + ncols].rearrange("(k p) n -> p k n", p=128)
            dma_in("pool", dst, srcv, (), (bWTbig[s],))
            return wt_big[s], bWTbig[s]

        wsm_i = [0]

        def load_wsm(src, r0, c0):
            s = wsm_i[0] % 8
            wsm_i[0] += 1
            srcv = src[r0:r0 + 1024, c0:c0 + 128].rearrange("(k p) n -> p k n", p=128)
            dma_in("pool", wt_sm[s], srcv, (), (bWTsm[s],))
            return wt_sm[s], bWTsm[s]

        pf_i = [0]

        def next_pf():
            i = pf_i[0] % 6
            pf_i[0] += 1
            return PF[i], bPF[i]

        def mm_group(out_ap, pairs, reads, wbuf):
            n = len(pairs)

            def fn(e, out_ap=out_ap, pairs=pairs):
                ins = None
                for i, (l, r) in enumerate(pairs):
                    ins = e.matmul(out_ap, lhsT=l, rhs=r, start=(i == 0), stop=(i == n - 1))
                return ins
            return S.op("pe", fn, reads, (wbuf,))

        def act(out, in_, func, reads, writes, bias=None, scale=None, accum_out=None):
            kw = {}
            if bias is not None:
                kw["bias"] = bias
            if scale is not None:
                kw["scale"] = scale
            if accum_out is not None:
                kw["accum_out"] = accum_out
            return S.op("act", lambda e, kw=kw: e.activation(out=out, in_=in_, func=func, **kw), reads, writes)

        def tt(eng, out, in0, in1, op, reads, writes):
            return S.op(eng, lambda e: e.tensor_tensor(out=out, in0=in0, in1=in1, op=op), reads, writes)

        def ts(eng, out, in0, s1, s2, op0, op1, reads, writes):
            if s2 is None:
                return S.op(eng, lambda e: e.tensor_scalar(out=out, in0=in0, scalar1=s1, scalar2=None, op0=op0), reads, writes)
            return S.op(eng, lambda e: e.tensor_scalar(out=out, in0=in0, scalar1=s1, scalar2=s2, op0=op0, op1=op1), reads, writes)

        def stt(out, in0, scalar, in1, op0, op1, reads, writes):
            return S.op("dve", lambda e: e.scalar_tensor_tensor(out=out, in0=in0, scalar=scalar, in1=in1, op0=op0, op1=op1), reads, writes)

        def cp(eng, out, in_, reads, writes):
            if eng == "act":
                return act(out, in_, AF.Copy, reads, writes)
            return S.op(eng, lambda e: e.tensor_copy(out=out, in_=in_), reads, writes)

        try:
            S.op("dve", lambda e: e.memset(smallv[:], 0.0), (), (bSV,))
            dma_in("sp", identF[:], c_ident[:, :], (), (bCONST,))
            dma_in("sp", triF[:], c_tri[:, :], (), (bCONST,))
            dma_in("pool", identB[:], c_ident[:, :], (), (bCONST,))
            S.op("pool", lambda e: e.memset(onesF[:], 1.0), (), (bCONST,))
            for (t_, src_) in ((lcw_t, lcw.rearrange("p a b -> p (a b)")), (lcb_t, lcb), (lba_t, lba), (lbx_t, lbx), (cvec_t, llam),
                               (scw_t, scw.rearrange("p a b -> p (a b)")), (scb_t, scb), (bcond_t, b_condT), (dx_t, d_x), (nwT_t, normwT)):
                dma_in("sp", t_, src_, (), (bSV,))
            dma_in("sp", dtbc_t[0:32, :], dtb_col[:, :], (), (bSV,))
            dma_in("sp", alogc_t[0:32, :], alog_col[:, :], (), (bSV,))
            _stop_at(-3)
            act(cvec_t, cvec_t, AF.Exp, (bSV,), (bSV,), scale=-1.0)
            act(cvec_t, cvec_t, AF.Ln, (bSV,), (bSV,), bias=1.0)
            ts("dve", cvec2_t, cvec_t, -16.0, None, ALU.mult, None, (bSV,), (bSV,))
            ts("dve", cvec_t, cvec_t, -8.0, None, ALU.mult, None, (bSV,), (bSV,))
            act(alogc_t, alogc_t, AF.Exp, (bSV,), (bSV,))
            ts("dve", alogc_t, alogc_t, -1.0, None, ALU.mult, None, (bSV,), (bSV,))

            _stop_at(-2)
            P0 = Carver(0)
            cf = P0.f32(8 * 17); cb_ = P0.bf16(8 * 17)
            xin = [P0.f32(T), P0.f32(T)]
            xs_f = P0.f32(8 * NS); hs_f = P0.f32(8 * NS)
            b_cf, b_cb = mk("cf"), mk("cb")
            b_xin = [mk("xin0"), mk("xin1")]
            b_xs = mk("xs_f")
            cf3 = v3(cf, 8); cb3 = v3(cb_, 8)
            dma_in("sp", cf3, cT.rearrange("(k p) n -> p k n", p=128), (), (b_cf,))
            cp("dve", cb_, cf, (b_cf,), (b_cb,))
            modps, bmodps = PF[0], bPF[0]
            for pc in range(4):
                wtile, wb = load_wbig(w_cond, 0, pc * 512, 512)
                for i in range(4):
                    mc = pc * 4 + i
                    pairs = [(wtile[:, k, i * 128:(i + 1) * 128], cb3[:, k, :]) for k in range(KC)]
                    mm_group(modps[:, mc * 17:(mc + 1) * 17], pairs, (wb, b_cb), bmodps)
            tt("dve", modT[:], v3(modps[:, 0:16 * 17], 16), bcond_t[:, 0:16].unsqueeze(2).to_broadcast([128, 16, 17]),
               ALU.add, (bmodps, bSV), (bMOD,))
            ts("dve", modT[:, 8:16, :], modT[:, 8:16, :], 1.0, None, ALU.add, None, (bMOD,), (bMOD,))
            _stop_at(-1)
            xTv = xT.rearrange("(k p) t -> p k t", p=128)
            for k in range(KC):
                s = k % 2
                dma_in("sp", xin[s], xTv[:, k, :], (), (b_xin[s],))
                act(hT[:, k, 0:T], xin[s], AF.Identity, (b_xin[s], bMOD), (bHT[k],), bias=modT[:, k, 0:1], scale=modT[:, 8 + k, 0:1])
            _stop_at(-0.5)
            dma_in("sp", v3(xs_f, 8), xsT.rearrange("(k p) n -> p k n", p=128), (), (b_xs,))
            _stop_at(-0.4)
            tt("dve", v3(hs_f, 8), v3(xs_f, 8), modT[:, 8:16, 1:17], ALU.mult, (b_xs, bMOD), (b_xs,))
            _stop_at(-0.3)
            tt("dve", v3(hs_f, 8), v3(hs_f, 8), modT[:, 0:8, 1:17], ALU.add, (b_xs, bMOD), (b_xs,))
            _stop_at(-0.2)
            cp("act", hT[:, :, T:TP], v3(hs_f, 8), (b_xs,), (bHTs,))
            bHTall = bHT + [bHTs]
            _stop_at(0)

            S.barrier(allbufs)
            P1 = Carver(0)
            xTg = v3(P1.bf16(4 * T), 4); BTg = P1.bf16(T); CTg = P1.bf16(T)
            xbuf = P1.bf16(T + 8); dg = v3(P1.bf16(512), 4); tail3 = P1.f32(4)
            wsmB = [v3(P1.bf16(1024), 8), v3(P1.bf16(1024), 8)]; wsmC = [v3(P1.bf16(1024), 8), v3(P1.bf16(1024), 8)]
            dt_t = P1.f32(512); adt_t = P1.f32(512); acs_t = P1.f32(512); nacs_t = P1.f32(512)
            e_t = P1.f32(512); w1_t = P1.f32(512); cdec_t = P1.f32(512); tmp_t = P1.f32(512)
            negI = P1.bf16(128); Lmask = P1.bf16(512)
            persist_start = P1.pos
            nw_b = P1.f32(512); D_b = P1.f32(32); dtb_b = P1.f32(32); nA_b = P1.f32(32)
            xsT_all = v3(P1.f32(16 * NS), 16)
            BsT = v3(P1.f32(4 * NS), 4); CsT = v3(P1.f32(4 * NS), 4)
            szs = v3(P1.f32(16 * NS), 16)
            xsb = v3(P1.f32(NS * 4), NS); us_t = P1.f32(NS)
            dtT_s = P1.f32(NS); adtT_s = P1.f32(NS); dAT_s = P1.f32(NS)
            persist_end = P1.pos
            x_tok_d = [P1.bf16(512), P1.bf16(512)]; xs_d = [None, None]; xsc_d = [P1.bf16(512), P1.bf16(512)]
            B_tok_d = [P1.bf16(128), P1.bf16(128)]; MT_d = [v3(P1.bf16(1024), 8), v3(P1.bf16(1024), 8)]
            sz_d = [P1.f32(512), P1.f32(512)]
            CBm = P1.f32(128); _ex = P1.f32(512); ex_t = [_ex, P1.f32(512)]
            wdt = v3(_ex.bitcast(BF16), 8)
            t_a_d = [P1.f32(512), P1.f32(512)]; t_b = tmp_t; junk = P1.bf16(512); mhalf = P1.f32(1)
            yn_t = P1.bf16(512); hst = P1.f32(512); hst_bf = P1.bf16(512); st8_d = [P1.f32(8), P1.f32(8)]
            print("P1 end", P1.pos, "of", SCRN)
            b_xTg = [mk("xTg%d" % i) for i in range(4)]; b_BT = mk("BTg"); b_CT = mk("CTg")
            b_wsmB = [mk("wsmB0"), mk("wsmB1")]; b_wsmC = [mk("wsmC0"), mk("wsmC1")]
            b_xbuf = mk("xbuf"); b_xbufq = [mk("xbufq%d" % q) for q in range(4)]; b_dg = mk("dg"); b_tail = mk("tail3"); b_dtf = mk("dtfam"); b_bc = mk("bcasts")
            b_xsT = mk("xsT_all"); b_BsT = mk("BsT"); b_CsT = mk("CsT"); b_szs = mk("szs"); b_xsb = mk("xsb"); b_us = mk("us")
            b_msk = mk("maskconsts")
            b_dts = mk("dts"); b_wdt = mk("wdt"); b_wz = mk("wzs")
            bd_xtok = [mk("x_tok0"), mk("x_tok1")]; bd_xs = [mk("xs0"), mk("xs1")]; bd_xsc = [mk("xsc0"), mk("xsc1")]
            bd_Btok = [mk("B_tok0"), mk("B_tok1")]; bd_MT = [mk("MT0"), mk("MT1")]; bd_sz = [mk("sz0"), mk("sz1")]
            b_CBm = mk("CBm"); b_ex = [mk("ex0"), mk("ex1")]; bd_ta = [mk("t_a0"), mk("t_a1")]; b_tb = mk("t_b"); b_junk = mk("junk")
            b_yn = mk("yn"); b_hst = mk("hst"); b_hbf = mk("hst_bf"); bd_st8 = [mk("st8a"), mk("st8b")]
            S.op("dve", lambda e: e.memset(xbuf[:, 0:3], 0.0), (), (b_xbuf,))
            S.op("dve", lambda e: e.memset(mhalf, -0.5), (), (b_bc,))
            ts("pool", negI, identF[:], -32768.0, None, ALU.mult, None, (bCONST,), (b_msk,))
            ts("dve", v3(Lmask, 4), triF[:].unsqueeze(1).to_broadcast([128, 4, 128]), -1.0, 1.0, ALU.mult, ALU.add, (bCONST,), (b_msk,))
            dma_in("sp", dtb_b, dtb_row.partition_broadcast(128), (), (b_bc,))
            dma_in("sp", nA_b, alog_row.partition_broadcast(128), (), (b_bc,))
            dma_in("sp", D_b, d_row.partition_broadcast(128), (), (b_bc,))
            act(nA_b, nA_b, AF.Exp, (b_bc,), (b_bc,))
            ts("dve", nA_b, nA_b, -1.0, None, ALU.mult, None, (b_bc,), (b_bc,))
            S.op("pool", lambda e: e.memset(wdt, 0.0), (), (b_wdt,))
            dma_in("pool", wdt[:, :, 0:32], w_in[:, C_DT:C_DT + 32].rearrange("(k p) n -> p k n", p=128), (), (b_wdt,))
            dtps, bdtps = PF[1], bPF[1]
            for c in range(NCH):
                pairs = [(hT[:, k, c * L:(c + 1) * L], wdt[:, k, 0:32]) for k in range(KC)]
                mm_group(dtps[:, c * 32:(c + 1) * 32], pairs, tuple(bHT) + (b_wdt,), bdtps)
            dsps, bdsps = PF[2], bPF[2]
            mm_group(dsps[:, 0:NS], [(wdt[:, k, :], hT[:, k, T:TP]) for k in range(KC)], (bHTs, b_wdt), bdsps)
            act(dtT_s, dsps[:, 0:NS], AF.Exp, (bdsps, bSV), (b_dts,), bias=dtbc_t)
            act(dtT_s, dtT_s, AF.Ln, (b_dts,), (b_dts,), bias=1.0)
            ts("dve", adtT_s, dtT_s, alogc_t, None, ALU.mult, None, (b_dts, bSV), (b_dts,))
            act(dAT_s, adtT_s, AF.Exp, (b_dts,), (b_dts,))
            tt("dve", v3(tmp_t, 16), v3(dtps[:, :], 16), dtb_b.unsqueeze(1).to_broadcast([128, 16, 32]), ALU.add, (bdtps, b_bc), (b_dtf,))
            act(tmp_t, tmp_t, AF.Exp, (b_dtf,), (b_dtf,))
            act(dt_t, tmp_t, AF.Ln, (b_dtf,), (b_dtf,), bias=1.0)
            tt("dve", v3(adt_t, 16), v3(dt_t, 16), nA_b.unsqueeze(1).to_broadcast([128, 16, 32]), ALU.mult, (b_dtf, b_bc), (b_dtf,))
            acsps, bacsps = PF[3], bPF[3]
            totps, btotps = PF[4], bPF[4]
            mm_group(acsps[:, :], [(triF[:], adt_t)], (b_dtf, bCONST), bacsps)
            mm_group(totps[:, :], [(onesF[:], adt_t)], (b_dtf, bCONST), btotps)
            cp("act", acs_t, acsps[:, :], (bacsps,), (b_dtf,))
            act(e_t, acs_t, AF.Exp, (b_dtf,), (b_dtf,))
            tt("dve", tmp_t, totps[:, :], acs_t, ALU.subtract, (btotps, b_dtf), (b_dtf,))
            act(tmp_t, tmp_t, AF.Exp, (b_dtf,), (b_dtf,))
            tt("dve", w1_t, tmp_t, dt_t, ALU.mult, (b_dtf,), (b_dtf,))
            act(tmp_t, dt_t, AF.Ln, (b_dtf,), (b_dtf,))
            tt("dve", nacs_t, tmp_t, acs_t, ALU.subtract, (b_dtf,), (b_dtf,))
            act(cdec_t, totps[:, :], AF.Exp, (btotps,), (b_dtf,))
            _stop_at(1)

            def conv_chunk(wtile, wb, coff, cw4, cbias, out_bf, b_out, tail_dst, s_state_src, s_out, b_s_out, s_state_dst):
                tt("pool", dg, identF[:].unsqueeze(1).to_broadcast([128, 4, 128]), cw4.unsqueeze(2).to_broadcast([128, 4, 128]), ALU.mult,
                   (bCONST, bSV), (b_dg,))
                def proj_q(q):
                    ps, bps = next_pf()
                    pairs = [(wtile[:, k, coff:coff + 128], hT[:, k, q * 512:(q + 1) * 512]) for k in range(KC)]
                    mm_group(ps[:, :], pairs, tuple(bHT) + (wb,), bps)
                    cp("act", xbuf[:, 3 + q * 512:3 + (q + 1) * 512], ps[:, :], (bps,), (b_xbufq[q],))
                    if q == 3:
                        cp("act", tail3[:, 0:3], ps[:, 509:512], (bps,), (b_tail,))
                        dma_out(tail_dst, tail3[:, 0:3], (b_tail,))

                def conv_q(q):
                    ps2, bps2 = next_pf()
                    rd = (b_dg, b_xbufq[q]) + ((b_xbufq[q - 1],) if q > 0 else (b_xbuf,))
                    mm_group(ps2[:, :], [(dg[:, k, :], xbuf[:, q * 512 + k:q * 512 + k + 512]) for k in range(4)], rd, bps2)
                    act(out_bf[:, q * 512:(q + 1) * 512], ps2[:, :], AF.Silu, (bps2, bSV), (b_out,), bias=cbias)
                proj_q(0); proj_q(1); conv_q(0); proj_q(2); conv_q(1); proj_q(3); conv_q(2); conv_q(3)
                ps, bps = next_pf()
                mm_group(ps[:, 0:NS], [(wtile[:, k, coff:coff + 128], hT[:, k, T:TP]) for k in range(KC)], (bHTs, wb), bps)
                dma_in("sp", xsb[:, :, 0:3], s_state_src, (), (b_xsb,))
                cp("act", xsb[:, :, 3], ps[:, 0:NS], (bps,), (b_xsb,))
                dma_out(s_state_dst, xsb[:, :, 1:4], (b_xsb,))
                ts("dve", us_t, xsb[:, :, 0], cw4[:, 0:1], cbias, ALU.mult, ALU.add, (b_xsb, bSV), (b_us,))
                for k in range(1, 4):
                    stt(us_t, xsb[:, :, k], cw4[:, k:k + 1], us_t, ALU.mult, ALU.add, (b_xsb, bSV, b_us), (b_us,))
                act(s_out, us_t, AF.Silu, (b_us,), (b_s_out,))

            def load_big_slot(slot, src, c0, ncols=512):
                srcv = src[0:1024, c0:c0 + ncols].rearrange("(k p) n -> p k n", p=128)
                dma_in("pool", wt_big[slot][:, :, 0:ncols], srcv, (), (bWTbig[slot],))
                return wt_big[slot], bWTbig[slot]

            def load_bc(g):
                par = g % 2
                for (tile_, buf_, c0) in ((wsmB[par], b_wsmB[par], C_SB + g * 128), (wsmC[par], b_wsmC[par], C_SC + g * 128)):
                    dma_in("pool", tile_, w_in[0:1024, c0:c0 + 128].rearrange("(k p) n -> p k n", p=128), (), (buf_,))

            wX, wXb_ = load_big_slot(0, w_in, C_SX)
            load_bc(0)
            for g in range(4):
                wzs, b_wz = load_big_slot(1, w_in, C_SZ + g * 512)
                if g + 1 < 4:
                    load_bc(g + 1)
                par = g % 2
                ch = 16 + g
                conv_chunk(wsmB[par], b_wsmB[par], 0, scw_t[:, ch * 4:(ch + 1) * 4], scb_t[:, ch:ch + 1], BTg, b_BT,
                           o_sc_p[:, ch, :], ssd_cvT[ch * 128:(ch + 1) * 128, :, :], BsT[:, g, :], b_BsT, o_sc_s[:, ch, :, :])
                ch = 20 + g
                conv_chunk(wsmC[par], b_wsmC[par], 0, scw_t[:, ch * 4:(ch + 1) * 4], scb_t[:, ch:ch + 1], CTg, b_CT,
                           o_sc_p[:, ch, :], ssd_cvT[ch * 128:(ch + 1) * 128, :, :], CsT[:, g, :], b_CsT, o_sc_s[:, ch, :, :])
                wtile, wb = wt_big[0], bWTbig[0]
                for i in range(4):
                    ch = g * 4 + i
                    conv_chunk(wtile, wb, i * 128, scw_t[:, ch * 4:(ch + 1) * 4], scb_t[:, ch:ch + 1], xTg[:, i, :], b_xTg[i],
                               o_sc_p[:, ch, :], ssd_cvT[ch * 128:(ch + 1) * 128, :, :], xsT_all[:, ch, :], b_xsT, o_sc_s[:, ch, :, :])
                if g + 1 < 4:
                    load_big_slot(0, w_in, C_SX + (g + 1) * 512)
                dma_in("sp", nw_b, normw_row[:, g * 512:(g + 1) * 512].partition_broadcast(128), (), (b_bc,))
                for i in range(4):
                    ps, bps = next_pf()
                    mm_group(ps[:, 0:NS], [(wzs[:, k, i * 128:(i + 1) * 128], hT[:, k, T:TP]) for k in range(KC)], (bHTs, b_wz), bps)
                    act(szs[:, g * 4 + i, :], ps[:, 0:NS], AF.Silu, (bps,), (b_szs,))
                def bufs(c):
                    par = c % 2
                    return (x_tok_d[par], xs_d[par], xsc_d[par], B_tok_d[par], MT_d[par], sz_d[par],
                            bd_xtok[par], bd_xs[par], bd_xsc[par], bd_Btok[par], bd_MT[par], bd_sz[par],
                            t_a_d[par], bd_ta[par], st8_d[par], bd_st8[par])

                def S4(c, g=g):
                    (x_tok_t, xs_t, xsc_t, B_tok, MT, sz_t, b_xtok, b_xs_, b_xsc, b_Btok, b_MT, b_sz, t_a, b_ta, st8, b_st8) = bufs(c)
                    cs = slice(c * L, (c + 1) * L)

                    def fnY(e):
                        ins = None
                        for h8 in range(8):
                            ins = e.matmul(PF[3][:, h8 * 64:(h8 + 1) * 64], lhsT=MT[:, h8, :], rhs=x_tok_t[:, h8 * 64:(h8 + 1) * 64],
                                           start=True, stop=True)
                        return ins
                    if c > 0:
                        mm_group(PF[4][:, :], [(CTg[:, cs], hst_bf)], (b_CT, b_hbf), bPF[4])
                    mm_group(PF[5][:, :], [(B_tok, xsc_t)], (b_Btok, b_xsc), bPF[5])
                    S.op("pe", fnY, (b_MT, b_xtok), (bPF[3],))

                def Dskip(c, g=g):
                    (x_tok_t, xs_t, xsc_t, B_tok, MT, sz_t, b_xtok, b_xs_, b_xsc, b_Btok, b_MT, b_sz, t_a, b_ta, st8, b_st8) = bufs(c)
                    tt("pool", v3(t_b, 8), v3(x_tok_t, 8), D_b[:, g * 8:(g + 1) * 8].unsqueeze(2).to_broadcast([128, 8, 64]), ALU.mult,
                       (b_xtok, b_bc), (b_tb,))

                def S7(c, g=g):
                    (x_tok_t, xs_t, xsc_t, B_tok, MT, sz_t, b_xtok, b_xs_, b_xsc, b_Btok, b_MT, b_sz, t_a, b_ta, st8, b_st8) = bufs(c)
                    stt(yn_t, t_a, st8[:, 3:4], nw_b, ALU.mult, ALU.mult, (b_ta, b_st8, b_bc), (b_yn,))

                def S5a(c, g=g):
                    hb = c * 32 + g * 8
                    if c > 0:
                        tt("dve", v3(hst, 8), v3(hst, 8), cdec_t[:, hb:hb + 8].unsqueeze(2).to_broadcast([128, 8, 64]), ALU.mult,
                           (b_hst, b_dtf), (b_hst,))
                        tt("dve", hst, hst, PF[5][:, :], ALU.add, (b_hst, bPF[5]), (b_hst,))
                    else:
                        cp("dve", hst, PF[5][:, :], (bPF[5],), (b_hst,))
                    if c == NCH - 1:
                        dma_out(o_sh_p[:, g * 512:(g + 1) * 512], hst, (b_hst,))

                def S5b(c, g=g):
                    (x_tok_t, xs_t, xsc_t, B_tok, MT, sz_t, b_xtok, b_xs_, b_xsc, b_Btok, b_MT, b_sz, t_a, b_ta, st8, b_st8) = bufs(c)
                    hb = c * 32 + g * 8
                    if c > 0:
                        tt("dve", v3(t_a, 8), v3(PF[4][:, :], 8), e_t[:, hb:hb + 8].unsqueeze(2).to_broadcast([128, 8, 64]), ALU.mult,
                           (bPF[4], b_dtf), (b_ta,))
                        tt("dve", t_a, t_a, PF[3][:, :], ALU.add, (b_ta, bPF[3]), (b_ta,))
                    else:
                        cp("dve", t_a, PF[3][:, :], (bPF[3],), (b_ta,))
                    tt("dve", t_a, t_a, t_b, ALU.add, (b_ta, b_tb), (b_ta,))
                    tt("dve", t_a, t_a, sz_t, ALU.mult, (b_ta, b_sz), (b_ta,))

                def S1(c, g=g):
                    cs = slice(c * L, (c + 1) * L)

                    def fnT(e, cs=cs):
                        ins = None
                        for i in range(4):
                            ins = e.transpose(pB0[:, i * 128:(i + 1) * 128], xTg[:, i, cs], identB[:])
                        ins = e.transpose(pB0[:, 512:640], BTg[:, cs], identB[:])
                        return ins
                    S.op("pe", fnT, tuple(b_xTg) + (b_BT, bCONST), (bPB[0],))

                def S1b(c, g=g):
                    cs = slice(c * L, (c + 1) * L)
                    mm_group(PF[0][:, 0:128], [(BTg[:, cs], CTg[:, cs])], (b_BT, b_CT), bPF[0])

                def S1c(c, g=g):
                    cs = slice(c * L, (c + 1) * L)
                    mm_group(PF[0][:, :], [(hT[:, k, cs], wzs[:, k, :]) for k in range(KC)], tuple(bHT) + (b_wz,), bPF[0])

                def S1d(c, half, g=g):
                    hb = c * 32 + g * 8
                    accp, baccp = PF[1 + half], bPF[1 + half]

                    def fnA(e, half=half, hb=hb, accp=accp):
                        ins = e.matmul(accp[:, :], lhsT=negI, rhs=Lmask, start=True, stop=False)
                        for hh in range(4):
                            col = hb + half * 4 + hh
                            ins = e.matmul(accp[:, hh * 128:(hh + 1) * 128], lhsT=adt_t[:, col:col + 1].to_broadcast([128, 128]),
                                           rhs=triF[:], start=False, stop=(hh == 3))
                        return ins
                    S.op("pe", fnA, (b_dtf, bCONST, b_msk), (baccp,))

                def copies(c, g=g):
                    (x_tok_t, xs_t, xsc_t, B_tok, MT, sz_t, b_xtok, b_xs_, b_xsc, b_Btok, b_MT, b_sz, t_a, b_ta, st8, b_st8) = bufs(c)
                    cp("act", x_tok_t, pB0[:, 0:512], (bPB[0],), (b_xtok,))
                    cp("act", B_tok, pB0[:, 512:640], (bPB[0],), (b_Btok,))

                def poolx(c, g=g):
                    (x_tok_t, xs_t, xsc_t, B_tok, MT, sz_t, b_xtok, b_xs_, b_xsc, b_Btok, b_MT, b_sz, t_a, b_ta, st8, b_st8) = bufs(c)
                    hb = c * 32 + g * 8
                    tt("pool", v3(xsc_t, 8), v3(x_tok_t, 8), w1_t[:, hb:hb + 8].unsqueeze(2).to_broadcast([128, 8, 64]), ALU.mult,
                       (b_xtok, b_dtf), (b_xsc,))

                def tanhz(c, g=g):
                    (x_tok_t, xs_t, xsc_t, B_tok, MT, sz_t, b_xtok, b_xs_, b_xsc, b_Btok, b_MT, b_sz, t_a, b_ta, st8, b_st8) = bufs(c)
                    act(sz_t, PF[0][:, :], AF.Tanh, (bPF[0],), (b_sz,), scale=0.5)

                def exps(c, half, g=g):
                    hb = c * 32 + g * 8
                    accp, baccp = PF[1 + half], bPF[1 + half]
                    for hh in range(4):
                        col = hb + half * 4 + hh
                        act(ex_t[half][:, hh * 128:(hh + 1) * 128], accp[:, hh * 128:(hh + 1) * 128], AF.Exp,
                            (baccp, b_dtf), (b_ex[half],), bias=nacs_t[:, col:col + 1])

                def S3a(c, g=g):
                    tt("dve", CBm, PF[0][:, 0:128], triF[:], ALU.mult, (bPF[0], bCONST), (b_CBm,))

                def S3b(c, g=g):
                    (x_tok_t, xs_t, xsc_t, B_tok, MT, sz_t, b_xtok, b_xs_, b_xsc, b_Btok, b_MT, b_sz, t_a, b_ta, st8, b_st8) = bufs(c)
                    stt(sz_t, sz_t, 1.0, PF[0][:, :], ALU.add, ALU.mult, (b_sz, bPF[0]), (b_sz,))

                def S3c(c, half, g=g):
                    (x_tok_t, xs_t, xsc_t, B_tok, MT, sz_t, b_xtok, b_xs_, b_xsc, b_Btok, b_MT, b_sz, t_a, b_ta, st8, b_st8) = bufs(c)
                    stt(MT[:, half * 4:(half + 1) * 4, :], v3(ex_t[half], 4), 1.0e30, CBm.unsqueeze(1).to_broadcast([128, 4, 128]),
                        ALU.min, ALU.mult, (b_ex[half], b_CBm), (b_MT,))

                def S6(c, g=g):
                    (x_tok_t, xs_t, xsc_t, B_tok, MT, sz_t, b_xtok, b_xs_, b_xsc, b_Btok, b_MT, b_sz, t_a, b_ta, st8, b_st8) = bufs(c)
                    act(junk, t_a, AF.Square, (b_ta,), (b_junk, b_st8), accum_out=st8[:, 0:1])
                    ts("pool", st8[:, 1:2], st8[:, 0:1], 1.0 / 512.0, 4.0 * RMS_EPS, ALU.mult, ALU.add, (b_st8,), (b_st8,))
                    tt("pool", st8[:, 3:4], st8[:, 1:2], mhalf[:, 0:1], ALU.pow, (b_st8, b_bc), (b_st8,))

                def S8(c, g=g):
                    cs = slice(c * L, (c + 1) * L)

                    def fnT2(e):
                        ins = None
                        for i in range(4):
                            ins = e.transpose(pB1[:, i * 128:(i + 1) * 128], yn_t[:, i * 128:(i + 1) * 128], identB[:])
                        return ins
                    S.op("pe", fnT2, (b_yn, bCONST), (bPB[1],))
                    cp("act", yssd[:, g * 4:(g + 1) * 4, cs], v3(pB1[:, 0:512], 4), (bPB[1],), tuple(bYS[g * 4:(g + 1) * 4]))

                for s_ in range(-1, NCH + 1):
                    cur, nxt, prv = s_, s_ + 1, s_ - 1
                    hc = 0 <= cur < NCH
                    hn = 0 <= nxt < NCH
                    hp = 0 <= prv < NCH
                    if hc:
                        S4(cur)
                        Dskip(cur)
                    if hp:
                        S7(prv)
                    if hn:
                        S1(nxt)
                        S1d(nxt, 0)
                        S1d(nxt, 1)
                        S1b(nxt)
                        S3a(nxt)
                        copies(nxt)
                        poolx(nxt)
                        exps(nxt, 0)
                    if hc:
                        S5a(cur)
                    if hn:
                        S1c(nxt)
                        exps(nxt, 1)
                    if hc and cur < NCH - 1:
                        cp("act", hst_bf, hst, (b_hst,), (b_hbf,))
                    if hc:
                        S5b(cur)
                    if hn:
                        S3c(nxt, 0)
                        tanhz(nxt)
                        S3c(nxt, 1)
                        S3b(nxt)
                    if hc:
                        S6(cur)
                    if hp:
                        S8(prv)

            _stop_at(2)
            S.barrier(allbufs)
            P1s = Carver(0)
            H0 = [v3(P1s.f32(2048), 16), v3(P1s.f32(2048), 16)]
            HN = [v3(P1s.f32(2048), 16), v3(P1s.f32(2048), 16)]
            T1 = v3(P1s.f32(2048), 16); T2 = v3(P1s.f32(2048), 16)
            assert P1s.pos <= persist_start, (P1s.pos, persist_start)
            P1s2 = Carver(persist_end)
            Bb = v3(P1s2.f32(512), 4); Cb = v3(P1s2.f32(512), 4)
            T3 = v3(P1s2.f32(2048), 16)
            esel = P1s2.f32(2048)
            dAx = v3(P1s2.f32(256), 16); xdt = v3(P1s2.f32(256), 16); ysT = v3(P1s2.f32(256), 16)
            gys = v3(P1s2.f32(256), 16); sqs = v3(P1s2.f32(256), 16); rs_t = v3(P1s2.f32(64), 4)
            b_H0 = [mk("H0a"), mk("H0b")]; b_HN = [mk("HNa"), mk("HNb")]; b_T1 = mk("T1"); b_T2 = mk("T2"); b_T3 = mk("T3")
            b_Bb = mk("Bb"); b_Cb = mk("Cb"); b_esel = mk("esel"); b_dAx = mk("dAx"); b_xdt = mk("xdt"); b_ysT = mk("ysT")
            b_gys = mk("gys"); b_sqs = mk("sqs"); b_rs = mk("rs")
            dma_in("sp", esel, c_esel[:, :], (), (b_esel,))
            exps, bexps = PF[0], bPF[0]
            _stop_at(2.05)

            def fnE(e):
                ins = None
                for j in range(16):
                    ins = e.matmul(exps[:, j * 16:(j + 1) * 16], lhsT=esel[:, j * 128:(j + 1) * 128], rhs=dAT_s, start=True, stop=True)
                for j in range(16):
                    ins = e.matmul(exps[:, 256 + j * 16:256 + (j + 1) * 16], lhsT=esel[:, j * 128:(j + 1) * 128], rhs=dtT_s,
                                   start=True, stop=True)
                return ins
            S.op("pe", fnE, (b_esel, b_dts), (bexps,))
            _stop_at(2.07)
            cp("act", dAx, v3(exps[:, 0:256], 16), (bexps,), (b_dAx,))
            _stop_at(2.08)
            cp("act", xdt, v3(exps[:, 256:512], 16), (bexps,), (b_xdt,))
            tt("dve", xdt, xdt, xsT_all, ALU.mult, (b_xdt, b_xsT), (b_xdt,))
            _stop_at(2.1)
            h0v = ssd_h0.rearrange("s (j e) p n -> s (e p) j n", e=2)
            ohv = o_sh_s.rearrange("s (j e) p n -> s (e p) j n", e=2)
            for s in range(NS):
                sl = s % 2
                dma_in("act", H0[sl], h0v[s], (), (b_H0[sl],))
                bps_, bbps_ = PF[1], bPF[1]
                cps_, bcps_ = PF[2], bPF[2]

                def fnB(e, s=s):
                    ins = None
                    for g in range(4):
                        ins = e.matmul(PF[1][:, g * 128:(g + 1) * 128], lhsT=BsT[:, g, s:s + 1].to_broadcast([128, 128]), rhs=identF[:],
                                       start=True, stop=True)
                    return ins

                def fnC(e, s=s):
                    ins = None
                    for g in range(4):
                        ins = e.matmul(PF[2][:, g * 128:(g + 1) * 128], lhsT=CsT[:, g, s:s + 1].to_broadcast([128, 128]), rhs=identF[:],
                                       start=True, stop=True)
                    return ins
                S.op("pe", fnB, (b_BsT, bCONST), (bbps_,))
                S.op("pe", fnC, (b_CsT, bCONST), (bcps_,))
                cp("act", Bb, v3(PF[1][:, :], 4), (bbps_,), (b_Bb,))
                cp("act", Cb, v3(PF[2][:, :], 4), (bcps_,), (b_Cb,))
                _stop_at(2.2)
                tt("dve", T1, H0[sl], dAx[:, :, s:s + 1].to_broadcast([128, 16, 128]), ALU.mult, (b_H0[sl], b_dAx), (b_T1,))
                _stop_at(2.3)
                xdt4 = xdt[:, :, s:s + 1].rearrange("p (g j) o -> p g j o", g=4).to_broadcast([128, 4, 4, 128])
                Bb4 = Bb.unsqueeze(2).to_broadcast([128, 4, 4, 128])
                Cb4 = Cb.unsqueeze(2).to_broadcast([128, 4, 4, 128])
                tt("pool", T2.rearrange("p (g j) n -> p g j n", g=4), xdt4, Bb4, ALU.mult, (b_xdt, b_Bb), (b_T2,))
                _stop_at(2.4)
                tt("dve", HN[sl], T1, T2, ALU.add, (b_T1, b_T2), (b_HN[sl],))
                dma_out(ohv[s], HN[sl], (b_HN[sl],))
                tt("pool", T3.rearrange("p (g j) n -> p g j n", g=4), HN[sl].rearrange("p (g j) n -> p g j n", g=4), Cb4, ALU.mult,
                   (b_HN[sl], b_Cb), (b_T3,))
                _stop_at(2.5)
                S.op("dve", lambda e, s=s: e.tensor_reduce(out=ysT[:, :, s], in_=T3, axis=AX.X, op=ALU.add), (b_T3,), (b_ysT,))
                _stop_at(2.6)
            _stop_at(2.7)
            tt("dve", gys, xsT_all, dx_t.unsqueeze(2).to_broadcast([128, 16, NS]), ALU.mult, (b_xsT, bSV), (b_gys,))
            tt("dve", ysT, ysT, gys, ALU.add, (b_ysT, b_gys), (b_ysT,))
            tt("dve", gys, ysT, szs, ALU.mult, (b_ysT, b_szs), (b_gys,))
            tt("dve", sqs, gys, gys, ALU.mult, (b_gys,), (b_sqs,))
            ssp, bssp = PF[3], bPF[3]

            def fnS(e):
                ins = None
                for g in range(4):
                    for i in range(4):
                        ins = e.matmul(ssp[:, g * 16:(g + 1) * 16], lhsT=onesF[:], rhs=sqs[:, g * 4 + i, :], start=(i == 0), stop=(i == 3))
                return ins
            S.op("pe", fnS, (b_sqs, bCONST), (bssp,))
            ts("dve", rs_t, v3(ssp[:, 0:64], 4), 1.0 / 512.0, RMS_EPS, ALU.mult, ALU.add, (bssp,), (b_rs,))
            act(rs_t, rs_t, AF.Sqrt, (b_rs,), (b_rs,))
            S.op("dve", lambda e: e.reciprocal(out=rs_t, in_=rs_t), (b_rs,), (b_rs,))
            tt("dve", gys.rearrange("p (g j) s -> p g j s", g=4), gys.rearrange("p (g j) s -> p g j s", g=4),
               rs_t.unsqueeze(2).to_broadcast([128, 4, 4, NS]), ALU.mult, (b_gys, b_rs), (b_gys,))
            tt("dve", gys, gys, nwT_t.unsqueeze(2).to_broadcast([128, 16, NS]), ALU.mult, (b_gys, bSV), (b_gys,))
            cp("act", yssd[:, :, T:TP], gys, (b_gys,), (bYSs,))

            _stop_at(3)
            S.barrier(allbufs)
            P2 = Carver(0)
            ylru = v3(P2.bf16(8 * TP), 8)
            xbuf2 = P2.bf16(T + 8); dg2 = v3(P2.bf16(512), 4); tail2 = P2.f32(4)
            u2 = P2.f32(T); ubf = P2.bf16(T)
            r_t = P2.f32(T); i_t = P2.f32(T); a_t = P2.f32(T); m_t = P2.f32(T)
            lxs = v3(P2.f32(NS * 4), NS); lus = P2.f32(NS); lusb = P2.bf16(NS); lr = P2.f32(NS); li = P2.f32(NS); la = P2.f32(NS)
            lm = P2.f32(NS); lh0 = v3(P2.f32(8 * NS), 8); lhn = v3(P2.f32(8 * NS), 8); lhp = P2.f32(8)
            hba = P2.f32(8); hbx = P2.f32(8); hcv = P2.f32(8); q25 = P2.f32(1)
            wab = v3(P2.bf16(8 * 128), 8); wxb = v3(P2.bf16(8 * 128), 8)
            bYL = [mk("ylru%d" % k) for k in range(8)]; bYLs = mk("ylru_s")
            b_x2 = mk("xbuf2"); b_x2q = [mk("xbuf2q%d" % q) for q in range(4)]; b_dg2 = mk("dg2"); b_tail2 = mk("tail2")
            b_u2 = mk("u2"); b_ubf = mk("ubf"); b_r = mk("r"); b_i = mk("i"); b_a = mk("a"); b_m = mk("m")
            b_lxs = mk("lxs"); b_lus = mk("lus"); b_lsm = mk("lsm"); b_lh0 = mk("lh0"); b_lhn = mk("lhn"); b_lhp = mk("lhp"); b_wab = mk("wab")
            b_hv = mk("halfvecs")
            S.op("dve", lambda e: e.memset(xbuf2[:, 0:3], 0.0), (), (b_x2,))
            ts("dve", hba, lba_t, 0.5, None, ALU.mult, None, (bSV,), (b_hv,))
            ts("dve", hbx, lbx_t, 0.5, None, ALU.mult, None, (bSV,), (b_hv,))
            ts("dve", hcv, cvec_t, 0.5, None, ALU.mult, None, (bSV,), (b_hv,))
            S.op("dve", lambda e: e.memset(q25, 0.25), (), (b_hv,))
            S.op("pool", lambda e: e.memset(wab, 0.0), (), (b_wab,))
            S.op("pool", lambda e: e.memset(wxb, 0.0), (), (b_wab,))
            for (dst_, src_) in ((wab, lwa), (wxb, lwx)):
                sv_ = src_.rearrange("(j e) k m -> e k j m", e=2)
                for e_ in range(2):
                    dma_in("pool", dst_[e_ * 64:(e_ + 1) * 64, :, e_ * 64:(e_ + 1) * 64], sv_[e_], (), (b_wab,))
            dma_in("sp", lh0, lru_h0T.rearrange("(k p) n -> p k n", p=128), (), (b_lh0,))
            _stop_at(3.05)
            for pp in range(2):
                wX, wXb = load_wbig(w_in, 0, C_LX + pp * 512, 512)
                wZ, wZb = load_wbig(w_in, 0, C_LZ + pp * 512, 512)
                for j4 in range(4):
                    j = pp * 4 + j4
                    co = j4 * 128
                    cw4 = lcw_t[:, j * 4:(j + 1) * 4]
                    tt("pool", dg2, identF[:].unsqueeze(1).to_broadcast([128, 4, 128]), cw4.unsqueeze(2).to_broadcast([128, 4, 128]), ALU.mult,
                       (bCONST, bSV), (b_dg2,))

                    def proj_q(q, j=j, co=co, wX=wX, wXb=wXb):
                        ps, bps = next_pf()
                        mm_group(ps[:, :], [(wX[:, k, co:co + 128], hT[:, k, q * 512:(q + 1) * 512]) for k in range(KC)], tuple(bHT) + (wXb,), bps)
                        cp("act", xbuf2[:, 3 + q * 512:3 + (q + 1) * 512], ps[:, :], (bps,), (b_x2q[q],))
                        if q == 3:
                            cp("act", tail2[:, 0:3], ps[:, 509:512], (bps,), (b_tail2,))
                            dma_out(o_lc_p[:, j, :], tail2[:, 0:3], (b_tail2,))

                    def conv_q(q, j=j):
                        qs = slice(q * 512, (q + 1) * 512)
                        ps2, bps2 = next_pf()
                        rd = (b_dg2, b_x2q[q]) + ((b_x2q[q - 1],) if q > 0 else (b_x2,))
                        mm_group(ps2[:, :], [(dg2[:, k, :], xbuf2[:, q * 512 + k:q * 512 + k + 512]) for k in range(4)], rd, bps2)
                        _stop_at(3.06)
                        act(u2[:, qs], ps2[:, :], AF.Identity, (bps2, bSV), (b_u2,), bias=lcb_t[:, j:j + 1])
                        _stop_at(3.07)
                        cp("dve", ubf[:, qs], u2[:, qs], (b_u2,), (b_ubf,))
                        _stop_at(3.08)
                        ps, bps = next_pf()
                        mm_group(ps[:, :], [(wab[:, j, :], ubf[:, qs])], (b_wab, b_ubf), bps)
                        act(r_t[:, qs], ps[:, :], AF.Tanh, (bps, b_hv), (b_r,), bias=hba[:, j:j + 1], scale=0.5)
                        ps, bps = next_pf()
                        mm_group(ps[:, :], [(wxb[:, j, :], ubf[:, qs])], (b_wab, b_ubf), bps)
                        act(i_t[:, qs], ps[:, :], AF.Tanh, (bps, b_hv), (b_i,), bias=hbx[:, j:j + 1], scale=0.5)
                    proj_q(0); proj_q(1); conv_q(0); proj_q(2); conv_q(1); proj_q(3); conv_q(2); conv_q(3)
                    _stop_at(3.1)
                    ps, bps = next_pf()
                    mm_group(ps[:, 0:NS], [(wX[:, k, co:co + 128], hT[:, k, T:TP]) for k in range(KC)], (bHTs, wXb), bps)
                    dma_in("sp", lxs[:, :, 0:3], lru_cvT[j * 128:(j + 1) * 128, :, :], (), (b_lxs,))
                    cp("act", lxs[:, :, 3], ps[:, 0:NS], (bps,), (b_lxs,))
                    dma_out(o_lc_s[:, j, :, :], lxs[:, :, 1:4], (b_lxs,))
                    ts("dve", lus, lxs[:, :, 0], cw4[:, 0:1], lcb_t[:, j:j + 1], ALU.mult, ALU.add, (b_lxs, bSV), (b_lus,))
                    for k in range(1, 4):
                        stt(lus, lxs[:, :, k], cw4[:, k:k + 1], lus, ALU.mult, ALU.add, (b_lxs, bSV, b_lus), (b_lus,))
                    cp("dve", lusb, lus, (b_lus,), (b_lus,))
                    ps, bps = next_pf()
                    mm_group(ps[:, 0:NS], [(wab[:, j, :], lusb)], (b_wab, b_lus), bps)
                    mm_group(ps[:, 32:32 + NS], [(wxb[:, j, :], lusb)], (b_wab, b_lus), bps)
                    act(lr, ps[:, 0:NS], AF.Tanh, (bps, b_hv), (b_lsm,), bias=hba[:, j:j + 1], scale=0.5)
                    act(li, ps[:, 32:32 + NS], AF.Tanh, (bps, b_hv), (b_lsm,), bias=hbx[:, j:j + 1], scale=0.5)
                    _stop_at(3.2)
                    act(a_t, r_t, AF.Exp, (b_r, b_hv), (b_a,), scale=hcv[:, j:j + 1], bias=hcv[:, j:j + 1])
                    act(m_t, r_t, AF.Exp, (b_r, bSV), (b_m,), scale=cvec_t[:, j:j + 1], bias=cvec_t[:, j:j + 1])
                    act(la, lr, AF.Exp, (b_lsm, b_hv), (b_lsm,), scale=hcv[:, j:j + 1], bias=hcv[:, j:j + 1])
                    act(lm, lr, AF.Exp, (b_lsm, bSV), (b_lsm,), scale=cvec_t[:, j:j + 1], bias=cvec_t[:, j:j + 1])
                    act(m_t, m_t, AF.Sqrt, (b_m, b_hv), (b_m,), scale=-0.25, bias=q25[:, 0:1])
                    act(lm, lm, AF.Sqrt, (b_lsm, b_hv), (b_lsm,), scale=-0.25, bias=q25[:, 0:1])
                    _stop_at(3.3)
                    stt(i_t, i_t, 1.0, u2, ALU.add, ALU.mult, (b_i, b_u2), (b_i,))
                    tt("dve", m_t[:, 1:T], m_t[:, 1:T], i_t[:, 1:T], ALU.mult, (b_m, b_i), (b_m,))
                    ts("dve", m_t[:, 0:1], i_t[:, 0:1], 0.5, None, ALU.mult, None, (b_m, b_i), (b_m,))
                    S.op("dve", lambda e: e.tensor_tensor_scan(out=r_t, data0=a_t, data1=m_t, initial=0.0, op0=ALU.mult, op1=ALU.add),
                         (b_a, b_m, b_r), (b_r,))
                    cp("dve", lhp[:, j:j + 1], r_t[:, T - 1:T], (b_r,), (b_lhp,))
                    _stop_at(3.4)
                    stt(li, li, 1.0, lus, ALU.add, ALU.mult, (b_lsm, b_lus), (b_lsm,))
                    tt("dve", lm, lm, li, ALU.mult, (b_lsm,), (b_lsm,))
                    tt("dve", la, la, lh0[:, j, :], ALU.mult, (b_lsm, b_lh0), (b_lsm,))
                    tt("dve", lhn[:, j, :], la, lm, ALU.add, (b_lsm,), (b_lhn,))
                    _stop_at(3.5)
                    for q in range(4):
                        qs = slice(q * 512, (q + 1) * 512)
                        ps, bps = next_pf()
                        mm_group(ps[:, :], [(wZ[:, k, co:co + 128], hT[:, k, qs]) for k in range(KC)], tuple(bHT) + (wZb,), bps)
                        act(a_t[:, qs], ps[:, :], AF.Tanh, (bps, b_a), (b_a,), scale=0.5)
                        stt(a_t[:, qs], a_t[:, qs], 1.0, ps[:, :], ALU.add, ALU.mult, (b_a, bps), (b_a,))
                    stt(ylru[:, j, 0:T], a_t, 0.5, r_t, ALU.mult, ALU.mult, (b_r, b_a), (bYL[j],))
                    ps, bps = next_pf()
                    mm_group(ps[:, 0:NS], [(wZ[:, k, co:co + 128], hT[:, k, T:TP]) for k in range(KC)], (bHTs, wZb), bps)
                    act(lr, ps[:, 0:NS], AF.Tanh, (bps,), (b_lsm,), scale=0.5)
                    stt(lr, lr, 1.0, ps[:, 0:NS], ALU.add, ALU.mult, (b_lsm, bps), (b_lsm,))
                    stt(ylru[:, j, T:TP], lr, 0.5, lhn[:, j, :], ALU.mult, ALU.mult, (b_lsm, b_lhn), (bYLs,))
            dma_out(o_lh_p[:, :], lhp, (b_lhp,))
            dma_out(o_lh_s[:, :, :], lhn, (b_lhn,))

            _stop_at(4)
            S.barrier([b for b in allbufs if b not in bYL and b is not bYLs] + [bWTbig[0], bWTbig[1]] + bWTsm)
            P3 = Carver((8 * TP) // 2)
            merged = v3(P3.bf16(8 * TP), 8)
            sgA = P3.f32(512); sgB = P3.f32(512); tA = P3.f32(512); tB = P3.f32(512)
            yo_t = [P3.f32(1024), P3.f32(1024)]; gts = P3.f32(1024)
            bnst_d = [P3.f32(16), P3.f32(16)]; mh3 = P3.f32(1); junk3 = P3.bf16(1024); cbb = v3(P3.bf16(8 * 128), 8); cf2 = v3(P3.f32(8 * 17), 8); cb2 = v3(P3.bf16(8 * 17), 8)
            P3b = Carver(0)
            gate_b = P3b.f32(1024); lng_b = P3b.f32(1024); lnb_b = P3b.f32(1024); bg_b = P3b.f32(1024)
            xtk = [P3b.f32(1024), P3b.f32(1024)]; resid_d = [P3b.f32(1024), scr[:, (8 * TP) // 2 + 8 * TP // 2:(8 * TP) // 2 + 8 * TP // 2 + 1024]]; xn_d = [P3b.f32(1024), scr[:, (8 * TP) // 2 + 8 * TP // 2 + 1024:(8 * TP) // 2 + 8 * TP // 2 + 2048]]
            assert P3b.pos <= (8 * TP) // 2
            bMG = [mk("merged%d" % k) for k in range(8)]; bMGs = mk("merged_s")
            b_sgA = mk("sgA"); b_sgB = mk("sgB"); b_tA = mk("tA"); b_tB = mk("tB"); b_gateb = mk("gate_b"); b_ln = mk("lnbc")
            b_xtk = [mk("xtk0"), mk("xtk1")]; b_res_d = [mk("resid0"), mk("resid1")]; b_xn_d = [mk("xn0"), mk("xn1")]; b_bn_d = [mk("bn0"), mk("bn1")]; b_yo = [mk("yo0"), mk("yo1")]
            b_cbb = mk("cbb"); b_c2 = mk("c2"); b_gts = mk("gts"); b_j3 = mk("junk3")
            colsets = [(slice(q * 512, (q + 1) * 512), 512) for q in range(4)] + [(slice(T, TP), NS)]
            for jo in range(8):
                wA, wAb = load_wsm(w_lp, 0, jo * 128)
                wB0, wB0b = load_wsm(w_sp, 0, jo * 128)
                wB1, wB1b = load_wsm(w_sp, 1024, jo * 128)
                wgA, wgAb = load_wsm(w_in, 0, C_MA + jo * 128)
                wgB, wgBb = load_wsm(w_in, 0, C_MB + jo * 128)
                for qi, (cs, n) in enumerate(colsets):
                    smp = qi == 4
                    rH = (bHTs,) if smp else tuple(bHT)
                    rYL = (bYLs,) if smp else tuple(bYL)
                    rYS = (bYSs,) if smp else tuple(bYS)
                    pA, bpA = next_pf()
                    mm_group(pA[:, 0:n], [(wA[:, k, :], ylru[:, k, cs]) for k in range(8)], rYL + (wAb,), bpA)
                    pB, bpB = next_pf()
                    mm_group(pB[:, 0:n], [(wB0[:, k, :], yssd[:, k, cs]) for k in range(8)] + [(wB1[:, k, :], yssd[:, 8 + k, cs]) for k in range(8)],
                             rYS + (wB0b, wB1b), bpB)
                    pgA, bpgA = next_pf()
                    mm_group(pgA[:, 0:n], [(wgA[:, k, :], hT[:, k, cs]) for k in range(8)], rH + (wgAb,), bpgA)
                    pgB, bpgB = next_pf()
                    mm_group(pgB[:, 0:n], [(wgB[:, k, :], hT[:, k, cs]) for k in range(8)], rH + (wgBb,), bpgB)
                    act(sgA[:, 0:n], pgA[:, 0:n], AF.Sigmoid, (bpgA,), (b_sgA,))
                    act(sgB[:, 0:n], pgB[:, 0:n], AF.Sigmoid, (bpgB,), (b_sgB,))
                    tt("dve", tA[:, 0:n], sgA[:, 0:n], pA[:, 0:n], ALU.mult, (b_sgA, bpA), (b_tA,))
                    tt("dve", tB[:, 0:n], sgB[:, 0:n], pB[:, 0:n], ALU.mult, (b_sgB, bpB), (b_tB,))
                    tt("dve", merged[:, jo, cs], tA[:, 0:n], tB[:, 0:n], ALU.add, (b_tA, b_tB), (bMGs if smp else bMG[jo],))
            S.barrier(bWTsm + bWTbig + bYL + [bYLs, b_sgA, b_sgB, b_tA, b_tB])
            dma_in("sp", cf2, cT.rearrange("(k p) n -> p k n", p=128), (), (b_c2,))
            cp("dve", cb2, cf2, (b_c2,), (b_c2,))
            cp("dve", cbb, cf2[:, :, 0:1].to_broadcast([128, 8, 128]), (b_c2,), (b_cbb,))
            S.op("dve", lambda e: e.memset(mh3, -0.5), (), (b_ln,))
            dma_in("sp", bg_b, b_gate.partition_broadcast(128), (), (b_ln,))
            dma_in("sp", lng_b, lng_row.partition_broadcast(128), (), (b_ln,))
            dma_in("sp", lnb_b, lnb_row.partition_broadcast(128), (), (b_ln,))
            for hf in range(2):
                wg, wgb = load_wbig(w_cond, 0, 2048 + hf * 512, 512)
                ps, bps = next_pf()
                mm_group(ps[:, :], [(cbb[:, k, :], wg[:, k, :]) for k in range(8)], (b_cbb, wgb), bps)
                tt("dve", gate_b[:, hf * 512:(hf + 1) * 512], ps[:, :], bg_b[:, hf * 512:(hf + 1) * 512], ALU.add, (bps, b_ln), (b_gateb,))
                ps, bps = next_pf()
                mm_group(ps[0:NS, :], [(cb2[:, k, 1:17], wg[:, k, :]) for k in range(8)], (b_c2, wgb), bps)
                tt("dve", gts[0:NS, hf * 512:(hf + 1) * 512], ps[0:NS, :], bg_b[0:NS, hf * 512:(hf + 1) * 512], ALU.add, (bps, b_ln), (b_gts,))
            wo0, wo0b = load_wbig(w_out, 0, 0, 512)
            wo1, wo1b = load_wbig(w_out, 0, 512, 512)
            for ti in range(NCH + 1):
                smp = ti == NCH
                np_ = NS if smp else 128
                cs = slice(T, TP) if smp else slice(ti * 128, (ti + 1) * 128)
                sl = ti % 2
                resid, xn, bnst, b_res, b_xn, b_bn = resid_d[sl], xn_d[sl], bnst_d[sl], b_res_d[sl], b_xn_d[sl], b_bn_d[sl]
                rM = (bMGs,) if smp else tuple(bMG)
                dma_in("act", xtk[sl][0:np_, :], xs_tok[:, :] if smp else x_tok[ti * 128:(ti + 1) * 128, :], (), (b_xtk[sl],))
                gsrc = gts if smp else gate_b
                bgs = b_gts if smp else b_gateb
                for hf, (wo, wob) in enumerate(((wo0, wo0b), (wo1, wo1b))):
                    ps, bps = next_pf()
                    mm_group(ps[0:np_, :], [(merged[:, k, cs], wo[:, k, :]) for k in range(8)], rM + (wob,), bps)
                    tt("dve", resid[0:np_, hf * 512:(hf + 1) * 512], ps[0:np_, :], gsrc[0:np_, hf * 512:(hf + 1) * 512], ALU.mult,
                       (bps, bgs), (b_res,))
                stt(resid[0:np_, :], xtk[sl][0:np_, :], ALPHA, resid[0:np_, :], ALU.mult, ALU.add, (b_xtk[sl], b_res), (b_res,))
                act(junk3[0:np_, :], resid[0:np_, :], AF.Copy, (b_res,), (b_j3, b_bn), accum_out=bnst[0:np_, 0:1])
                act(junk3[0:np_, :], resid[0:np_, :], AF.Square, (b_res,), (b_j3, b_bn), accum_out=bnst[0:np_, 1:2])
                ts("pool", bnst[0:np_, 12:13], bnst[0:np_, 0:1], 1.0 / D, None, ALU.mult, None, (b_bn,), (b_bn,))
                tt("pool", bnst[0:np_, 2:3], bnst[0:np_, 12:13], bnst[0:np_, 12:13], ALU.mult, (b_bn,), (b_bn,))
                ts("pool", bnst[0:np_, 3:4], bnst[0:np_, 1:2], 1.0 / D, LN_EPS, ALU.mult, ALU.add, (b_bn,), (b_bn,))
                tt("pool", bnst[0:np_, 14:15], bnst[0:np_, 3:4], bnst[0:np_, 2:3], ALU.subtract, (b_bn,), (b_bn,))
                tt("pool", bnst[0:np_, 15:16], bnst[0:np_, 14:15], mh3[0:np_, 0:1], ALU.pow, (b_bn, b_ln), (b_bn,))
                stt(xn[0:np_, :], resid[0:np_, :], bnst[0:np_, 12:13], lng_b[0:np_, :], ALU.subtract, ALU.mult, (b_res, b_bn, b_ln), (b_xn,))
                stt(yo_t[sl][0:np_, :], xn[0:np_, :], bnst[0:np_, 15:16], lnb_b[0:np_, :], ALU.mult, ALU.add, (b_xn, b_bn, b_ln), (b_yo[sl],))
                dma_out(y_s[:, :] if smp else y_p[ti * 128:(ti + 1) * 128, :], yo_t[sl][0:np_, :], (b_yo[sl],))

        except _StopRec:
            pass
        S.finalize()

        @block.sync
        def _(e):
            S.emit("sp", e)

        @block.gpsimd
        def _(e):
            S.emit("pool", e)

        @block.scalar
        def _(e):
            S.emit("act", e)

        @block.vector
        def _(e):
            S.emit("dve", e)

        @block.tensor
        def _(e):
            S.emit("pe", e)
    return nc


_NC_CACHE = {}


def _vec_pk(v, k):
    return np.ascontiguousarray(np.asarray(v, np.float32).reshape(k, 128).T)


def kernel(x_prompt, x_sample, state_lru_h, state_lru_conv, state_ssd_h, state_ssd_conv,
           c_prompt, c_sample, w_cond, b_cond, w_in, lru_conv_w, lru_conv_b, lru_wa, lru_ba,
           lru_wx, lru_bx, lru_lambda, ssd_conv_w, ssd_conv_b, ssd_dt_bias, ssd_a_log, ssd_d,
           ssd_norm_w, w_lru_proj, w_ssd_proj, w_out, ln_g, ln_b):
    f = lambda a: np.ascontiguousarray(np.asarray(a, np.float32))
    x_prompt, x_sample = f(x_prompt), f(x_sample)
    if "nc" not in _NC_CACHE:
        _NC_CACHE["nc"] = build_program()
    nc = _NC_CACHE["nc"]
    shared = {
        "w_cond": f(w_cond[0]), "b_condT": _vec_pk(b_cond[0], 24), "b_gate": f(np.asarray(b_cond)[0:1, 2048:3072]),
        "w_in": f(w_in[0]),
        "lcw": f(np.asarray(lru_conv_w)[0].reshape(4, 8, 128).transpose(2, 1, 0)), "lcb": _vec_pk(lru_conv_b[0], 8),
        "lwa": f(lru_wa[0]), "lwx": f(lru_wx[0]),
        "lba": _vec_pk(lru_ba[0], 8), "lbx": _vec_pk(lru_bx[0], 8), "llam": _vec_pk(lru_lambda[0], 8),
        "scw": f(np.asarray(ssd_conv_w)[0].reshape(4, 24, 128).transpose(2, 1, 0)), "scb": _vec_pk(ssd_conv_b[0], 24),
        "dtb_row": f(np.asarray(ssd_dt_bias)[0:1]), "alog_row": f(np.asarray(ssd_a_log)[0:1]), "d_row": f(np.asarray(ssd_d)[0:1]),
        "dtb_col": f(np.asarray(ssd_dt_bias)[0].reshape(32, 1)), "alog_col": f(np.asarray(ssd_a_log)[0].reshape(32, 1)),
        "d_x": _vec_pk(np.repeat(np.asarray(ssd_d, np.float32)[0], 64), 16),
        "normw_row": f(np.asarray(ssd_norm_w)[0:1]), "normwT": _vec_pk(ssd_norm_w[0], 16),
        "w_lp": f(w_lru_proj[0]), "w_sp": f(w_ssd_proj[0]), "w_out": f(w_out[0]),
        "lng_row": f(np.asarray(ln_g)[0:1]), "lnb_row": f(np.asarray(ln_b)[0:1]),
        "c_ident": np.eye(128, dtype=np.float32), "c_tri": np.triu(np.ones((128, 128), np.float32)),
        "c_esel": f((np.arange(128)[:, None] == (np.arange(2048)[None, :] // 64)).astype(np.float32)),
    }
    in_maps = []
    for i in range(NCORES):
        ss = slice(NS * i, NS * (i + 1))
        cT = np.concatenate([np.asarray(c_prompt, np.float32)[i][:, None], np.asarray(c_sample, np.float32)[ss].T], axis=1)
        m = dict(shared)
        m.update({
            "xT": f(x_prompt[i].T), "x_tok": f(x_prompt[i]),
            "xsT": f(x_sample[ss, 0, :].T), "xs_tok": f(x_sample[ss, 0, :]),
            "cT": f(cT),
            "lru_h0T": f(np.asarray(state_lru_h)[0, ss].T),
            "lru_cvT": f(np.asarray(state_lru_conv)[0, ss].transpose(2, 0, 1)),
            "ssd_h0": f(np.asarray(state_ssd_h)[0, ss]),
            "ssd_cvT": f(np.asarray(state_ssd_conv)[0, ss].transpose(2, 0, 1)),
        })
        in_maps.append(m)
    res = run_bass_kernel_spmd(nc, in_maps, core_ids=list(range(NCORES)))
    R = res.results
    y_prompt = np.stack([R[i]["y_p"] for i in range(NCORES)])
    y_sample = np.concatenate([R[i]["y_s"] for i in range(NCORES)])[:, None, :]
    lh_p = np.stack([R[i]["o_lh_p"].T.reshape(1024) for i in range(NCORES)])[None]
    lc_p = np.stack([R[i]["o_lc_p"].transpose(2, 1, 0).reshape(3, 1024) for i in range(NCORES)])[None]
    sh_p = np.stack([R[i]["o_sh_p"].T.reshape(32, 64, 128) for i in range(NCORES)])[None]
    sc_p = np.stack([R[i]["o_sc_p"].transpose(2, 1, 0).reshape(3, 3072) for i in range(NCORES)])[None]
    lh_s = np.concatenate([R[i]["o_lh_s"].transpose(2, 1, 0).reshape(NS, 1024) for i in range(NCORES)])[None]
    lc_s = np.concatenate([R[i]["o_lc_s"].transpose(2, 3, 1, 0).reshape(NS, 3, 1024) for i in range(NCORES)])[None]
    sh_s = np.concatenate([R[i]["o_sh_s"] for i in range(NCORES)])[None]
    sc_s = np.concatenate([R[i]["o_sc_s"].transpose(2, 3, 1, 0).reshape(NS, 3, 3072) for i in range(NCORES)])[None]
    c32 = lambda a: np.ascontiguousarray(a, dtype=np.float32)
    return (c32(y_prompt), c32(y_sample), c32(lh_p), c32(lc_p), c32(sh_p), c32(sc_p),
            c32(lh_s), c32(lc_s), c32(sh_s), c32(sc_s))
```

```python
import numpy as np
import concourse.bass as bass
import concourse.mybir as mybir
from concourse.bass_utils import run_bass_kernel_spmd

F32 = mybir.dt.float32
BF16 = mybir.dt.bfloat16
AF = mybir.ActivationFunctionType
ALU = mybir.AluOpType
AX = mybir.AxisListType

NCORES = 8
D = 1024
T = 2048
NS = 16
TP = T + NS
KC = 8
NCH = 16
L = 128
N_IN = 9248
C_LX, C_LZ, C_SZ, C_SX, C_SB, C_SC, C_DT, C_MA, C_MB = 0, 1024, 2048, 4096, 6144, 6656, 7168, 7200, 8224
ALPHA = 2.0 ** 0.25
LN_EPS = 1e-5
RMS_EPS = 1e-5
SAME_ENGINE_SYNC = True
import os as _os
KSTOP = float(_os.environ.get('KSTOP', '99'))


class _StopRec(Exception):
    pass


def _stop_at(n):
    if KSTOP <= n:
        raise _StopRec()


class Buf:
    __slots__ = ("name", "last_w", "readers")

    def __init__(self, name):
        self.name = name
        self.last_w = None
        self.readers = []


class Op:
    __slots__ = ("eng", "fn", "deps", "is_dma", "sem", "semval", "prevval", "signal", "sig")

    def __init__(self, eng, fn, deps, is_dma):
        self.eng, self.fn, self.deps, self.is_dma = eng, fn, deps, is_dma
        self.sem = None
        self.semval = 0
        self.prevval = 0
        self.signal = False
        self.sig = 0


class Sched:
    ENGS = ("sp", "pool", "act", "dve", "pe")

    def __init__(self, nc, esems, dma_sems):
        self.nc = nc
        self.ops = []
        self.esems = esems
        self.dma_pool = dma_sems
        self.dma_idx = {q: 0 for q in dma_sems}
        self.dma_val = {}
        self.store_ops = []

    def _deps(self, reads, writes):
        deps = set()
        for b in list(reads) + list(writes):
            if b.last_w is not None:
                deps.add(b.last_w)
        for b in writes:
            for r in b.readers:
                deps.add(r)
        return deps

    def _update(self, idx, reads, writes):
        for b in reads:
            b.readers.append(idx)
        for b in writes:
            b.last_w = idx
            b.readers = []

    def op(self, eng, fn, reads=(), writes=()):
        idx = len(self.ops)
        o = Op(eng, fn, self._deps(reads, writes), False)
        self.ops.append(o)
        self._update(idx, reads, writes)
        return idx

    def dma(self, q, fn, n, reads=(), writes=(), store=False):
        idx = len(self.ops)
        o = Op(q, fn, self._deps(reads, writes), True)
        pool = self.dma_pool[q]
        sem = pool[self.dma_idx[q] % len(pool)]
        self.dma_idx[q] += 1
        o.sem = sem
        o.prevval = self.dma_val.get(id(sem), 0)
        o.semval = o.prevval + 16 * n
        self.dma_val[id(sem)] = o.semval
        self.ops.append(o)
        self._update(idx, reads, writes)
        if store:
            self.store_ops.append(idx)
        return idx

    def barrier(self, bufs):
        allidx = len(self.ops)
        last = {}
        for i, o in enumerate(self.ops):
            if o.fn is not None:
                last[o.eng] = i
        dmas = [i for i, o in enumerate(self.ops) if o.is_dma]
        deps = set(last.values()) | set(dmas[-64:])
        for e in self.ENGS:
            o = Op(e, None, set(deps), False)
            self.ops.append(o)
        for b in bufs:
            b.last_w = None
            b.readers = []

    def finalize(self):
        for i, o in enumerate(self.ops):
            for d in o.deps:
                p = self.ops[d]
                if p.is_dma or p.fn is None:
                    continue
                if p.eng != o.eng or (SAME_ENGINE_SYNC and p.eng != "pe") or o.is_dma:
                    p.signal = True
        cnt = {e: 0 for e in self.ENGS}
        for o in self.ops:
            if o.is_dma:
                continue
            if o.fn is None:
                continue
            if o.signal:
                cnt[o.eng] += 1
                o.sig = cnt[o.eng]

    def emit(self, eng_name, e):
        water = {}
        esems = self.esems

        def wait(sem, val):
            k = id(sem)
            if water.get(k, 0) >= val:
                return
            water[k] = val
            e.wait_ge(sem, val)

        for i, o in enumerate(self.ops):
            if o.eng != eng_name:
                continue
            for d in sorted(o.deps):
                p = self.ops[d]
                if p.is_dma:
                    wait(p.sem, p.semval)
                elif p.fn is None:
                    continue
                elif p.eng != eng_name or (SAME_ENGINE_SYNC and eng_name != "pe") or o.is_dma:
                    wait(esems[p.eng], p.sig)
            if o.fn is None:
                continue
            if o.is_dma:
                if o.prevval > 0:
                    wait(o.sem, o.prevval)
                o.fn(e, o.sem)
            else:
                ins = o.fn(e)
                if o.signal:
                    ins.then_inc(esems[eng_name], 1)
        if eng_name == "sp":
            for idx in self.store_ops:
                o = self.ops[idx]
                wait(o.sem, o.semval)


def build_program():
    nc = bass.Bass("TRN2", target_bir_lowering=False)
    din, dout = {}, {}

    def inp(name, shape):
        din[name] = nc.dram_tensor(name, list(shape), F32, kind="ExternalInput").ap()
        return din[name]

    def outp(name, shape):
        dout[name] = nc.dram_tensor(name, list(shape), F32, kind="ExternalOutput").ap()
        return dout[name]

    xT = inp("xT", [D, T]); x_tok = inp("x_tok", [T, D])
    xsT = inp("xsT", [D, NS]); xs_tok = inp("xs_tok", [NS, D])
    cT = inp("cT", [D, 17])
    lru_h0T = inp("lru_h0T", [D, NS]); lru_cvT = inp("lru_cvT", [D, NS, 3])
    ssd_h0 = inp("ssd_h0", [NS, 32, 64, 128]); ssd_cvT = inp("ssd_cvT", [3072, NS, 3])
    w_cond = inp("w_cond", [D, 3072]); b_condT = inp("b_condT", [128, 24]); b_gate = inp("b_gate", [1, 1024])
    w_in = inp("w_in", [D, N_IN])
    lcw = inp("lcw", [128, 8, 4]); lcb = inp("lcb", [128, 8])
    lwa = inp("lwa", [16, 64, 64]); lwx = inp("lwx", [16, 64, 64])
    lba = inp("lba", [128, 8]); lbx = inp("lbx", [128, 8]); llam = inp("llam", [128, 8])
    scw = inp("scw", [128, 24, 4]); scb = inp("scb", [128, 24])
    dtb_row = inp("dtb_row", [1, 32]); alog_row = inp("alog_row", [1, 32]); d_row = inp("d_row", [1, 32])
    dtb_col = inp("dtb_col", [32, 1]); alog_col = inp("alog_col", [32, 1])
    d_x = inp("d_x", [128, 16]); normw_row = inp("normw_row", [1, 2048]); normwT = inp("normwT", [128, 16])
    w_lp = inp("w_lp", [D, D]); w_sp = inp("w_sp", [2048, D]); w_out = inp("w_out", [D, D])
    lng_row = inp("lng_row", [1, D]); lnb_row = inp("lnb_row", [1, D])
    c_ident = inp("c_ident", [128, 128]); c_tri = inp("c_tri", [128, 128]); c_esel = inp("c_esel", [128, 2048])

    y_p = outp("y_p", [T, D]); y_s = outp("y_s", [NS, D])
    o_lh_p = outp("o_lh_p", [128, 8]); o_lc_p = outp("o_lc_p", [128, 8, 3])
    o_sh_p = outp("o_sh_p", [128, 2048]); o_sc_p = outp("o_sc_p", [128, 24, 3])
    o_lh_s = outp("o_lh_s", [128, 8, NS]); o_lc_s = outp("o_lc_s", [128, 8, NS, 3])
    o_sh_s = outp("o_sh_s", [NS, 32, 64, 128]); o_sc_s = outp("o_sc_s", [128, 24, NS, 3])

    SCRN = 23040
    from contextlib import ExitStack
    with ExitStack() as _es:
        _en = _es.enter_context
        hT = _en(nc.sbuf_tensor("hT", [128, KC, TP], BF16))
        yssd = _en(nc.sbuf_tensor("yssd", [128, 16, TP], BF16))
        wt = _en(nc.sbuf_tensor("wt", [128, 8192], BF16))
        scr = _en(nc.sbuf_tensor("scr", [128, SCRN], F32))
        identF = _en(nc.sbuf_tensor("identF", [128, 128], F32))
        identB = _en(nc.sbuf_tensor("identB", [128, 128], BF16))
        triF = _en(nc.sbuf_tensor("triF", [128, 128], F32))
        onesF = _en(nc.sbuf_tensor("onesF", [128, 128], F32))
        smallv = _en(nc.sbuf_tensor("smallv", [128, 512], F32))
        modT = _en(nc.sbuf_tensor("modT", [128, 16, 17], F32))
        pF0, pF1, pF2, pF3, pF4, pF5 = [_en(nc.psum_tensor("pF%d" % i, [128, 512], F32)) for i in range(6)]
        pB0 = _en(nc.psum_tensor("pB0", [128, 1024], BF16))
        pB1 = _en(nc.psum_tensor("pB1", [128, 1024], BF16))
        s_pool, s_act, s_dve, s_pe, s_sp = [_en(nc.semaphore(n)) for n in ("s_pool", "s_act", "s_dve", "s_pe", "s_sp")]
        dq0, dq1, dq2, dq3, dq4, dq5, dq6, dq7 = [_en(nc.semaphore("dq%d" % i)) for i in range(8)]
        dg0, dg1, dg2, dg3, dg4, dg5 = [_en(nc.semaphore("dg%d" % i)) for i in range(6)]
        da0, da1, da2, da3 = [_en(nc.semaphore("da%d" % i)) for i in range(4)]
        block = _en(nc.Block())
        S = Sched(nc, {"sp": s_sp, "pool": s_pool, "act": s_act, "dve": s_dve, "pe": s_pe},
                  {"sp": [dq0, dq1, dq2, dq3, dq4, dq5, dq6, dq7], "pool": [dg0, dg1, dg2, dg3, dg4, dg5], "act": [da0, da1, da2, da3]})
        PF = [pF0, pF1, pF2, pF3, pF4, pF5]
        bPF = [Buf("pF%d" % i) for i in range(6)]
        bPB = [Buf("pB0"), Buf("pB1")]

        class Carver:
            def __init__(self, start=0):
                self.pos = start

            def f32(self, n):
                a = scr[:, self.pos:self.pos + n]
                self.pos += n
                assert self.pos <= SCRN, self.pos
                return a

            def bf16(self, n):
                n32 = (n + 1) // 2
                a = scr[:, self.pos:self.pos + n32].bitcast(BF16)
                self.pos += n32
                assert self.pos <= SCRN, self.pos
                return a

        def v3(ap, a):
            return ap.rearrange("p (a b) -> p a b", a=a)

        sv = [0]

        def svec(n):
            a = smallv[:, sv[0]:sv[0] + n]
            sv[0] += n
            assert sv[0] <= 512
            return a

        lcw_t = svec(32); lcb_t = svec(8); lba_t = svec(8); lbx_t = svec(8); cvec_t = svec(8); cvec2_t = svec(8)
        scw_t = svec(96); scb_t = svec(24); bcond_t = svec(24); dx_t = svec(16); nwT_t = svec(16)
        dtbc_t = svec(1); alogc_t = svec(1)
        bSV = Buf("smallv")
        bCONST = Buf("consts")
        bMOD = Buf("modT")
        bHT = [Buf("hT%d" % k) for k in range(KC)]
        bHTs = Buf("hTs")
        bYS = [Buf("yssd%d" % k) for k in range(16)]
        bYSs = Buf("yssd_s")
        bWTbig = [Buf("wtbig0"), Buf("wtbig1")]
        bWTsm = [Buf("wtsm%d" % i) for i in range(8)]
        wt_big = [wt[:, i * 4096:(i + 1) * 4096].rearrange("p (k c) -> p k c", k=8) for i in range(2)]
        wt_sm = [wt[:, i * 1024:(i + 1) * 1024].rearrange("p (k c) -> p k c", k=8) for i in range(8)]
        allbufs = []

        def mk(name):
            b = Buf(name)
            allbufs.append(b)
            return b

        def dma_in(q, out_ap, in_ap, reads=(), writes=(), n=1):
            def fn(e, sem, out_ap=out_ap, in_ap=in_ap):
                e.dma_start(out=out_ap, in_=in_ap).then_inc(sem, 16)
            return S.dma(q, fn, 1, reads, writes)

        def dma_out(out_ap, in_ap, reads=()):
            def fn(e, sem, out_ap=out_ap, in_ap=in_ap):
                e.dma_start(out=out_ap, in_=in_ap).then_inc(sem, 16)
            return S.dma("sp", fn, 1, reads, (), store=True)

        wbig_i = [0]

        def load_wbig(src, r0, c0, ncols):
            s = wbig_i[0] % 2
            wbig_i[0] += 1
            dst = wt_big[s][:, :, 0:ncols]
            srcv = src[r0:r0 + 1024, c0:c0 + ncols].rearrange("(k p) n -> p k n", p=128)
            dma_in("pool", dst, srcv, (), (bWTbig[s],))
            return wt_big[s], bWTbig[s]

        wsm_i = [0]

        def load_wsm(src, r0, c0):
            s = wsm_i[0] % 8
            wsm_i[0] += 1
            srcv = src[r0:r0 + 1024, c0:c0 + 128].rearrange("(k p) n -> p k n", p=128)
            dma_in("pool", wt_sm[s], srcv, (), (bWTsm[s],))
            return wt_sm[s], bWTsm[s]

        pf_i = [0]

        def next_pf():
            i = pf_i[0] % 6
            pf_i[0] += 1
            return PF[i], bPF[i]

        def mm_group(out_ap, pairs, reads, wbuf):
            n = len(pairs)

            def fn(e, out_ap=out_ap, pairs=pairs):
                ins = None
                for i, (l, r) in enumerate(pairs):
                    ins = e.matmul(out_ap, lhsT=l, rhs=r, start=(i == 0), stop=(i == n - 1))
                return ins
            return S.op("pe", fn, reads, (wbuf,))

        def act(out, in_, func, reads, writes, bias=None, scale=None, accum_out=None):
            kw = {}
            if bias is not None:
                kw["bias"] = bias
            if scale is not None:
                kw["scale"] = scale
            if accum_out is not None:
                kw["accum_out"] = accum_out
            return S.op("act", lambda e, kw=kw: e.activation(out=out, in_=in_, func=func, **kw), reads, writes)

        def tt(eng, out, in0, in1, op, reads, writes):
            return S.op(eng, lambda e: e.tensor_tensor(out=out, in0=in0, in1=in1, op=op), reads, writes)

        def ts(eng, out, in0, s1, s2, op0, op1, reads, writes):
            if s2 is None:
                return S.op(eng, lambda e: e.tensor_scalar(out=out, in0=in0, scalar1=s1, scalar2=None, op0=op0), reads, writes)
            return S.op(eng, lambda e: e.tensor_scalar(out=out, in0=in0, scalar1=s1, scalar2=s2, op0=op0, op1=op1), reads, writes)

        def stt(out, in0, scalar, in1, op0, op1, reads, writes):
            return S.op("dve", lambda e: e.scalar_tensor_tensor(out=out, in0=in0, scalar=scalar, in1=in1, op0=op0, op1=op1), reads, writes)

        def cp(eng, out, in_, reads, writes):
            if eng == "act":
                return act(out, in_, AF.Copy, reads, writes)
            return S.op(eng, lambda e: e.tensor_copy(out=out, in_=in_), reads, writes)

        try:
            S.op("dve", lambda e: e.memset(smallv[:], 0.0), (), (bSV,))
            dma_in("sp", identF[:], c_ident[:, :], (), (bCONST,))
            dma_in("sp", triF[:], c_tri[:, :], (), (bCONST,))
            dma_in("pool", identB[:], c_ident[:, :], (), (bCONST,))
            S.op("pool", lambda e: e.memset(onesF[:], 1.0), (), (bCONST,))
            for (t_, src_) in ((lcw_t, lcw.rearrange("p a b -> p (a b)")), (lcb_t, lcb), (lba_t, lba), (lbx_t, lbx), (cvec_t, llam),
                               (scw_t, scw.rearrange("p a b -> p (a b)")), (scb_t, scb), (bcond_t, b_condT), (dx_t, d_x), (nwT_t, normwT)):
                dma_in("sp", t_, src_, (), (bSV,))
            dma_in("sp", dtbc_t[0:32, :], dtb_col[:, :], (), (bSV,))
            dma_in("sp", alogc_t[0:32, :], alog_col[:, :], (), (bSV,))
            _stop_at(-3)
            act(cvec_t, cvec_t, AF.Exp, (bSV,), (bSV,), scale=-1.0)
            act(cvec_t, cvec_t, AF.Ln, (bSV,), (bSV,), bias=1.0)
            ts("dve", cvec2_t, cvec_t, -16.0, None, ALU.mult, None, (bSV,), (bSV,))
            ts("dve", cvec_t, cvec_t, -8.0, None, ALU.mult, None, (bSV,), (bSV,))
            act(alogc_t, alogc_t, AF.Exp, (bSV,), (bSV,))
            ts("dve", alogc_t, alogc_t, -1.0, None, ALU.mult, None, (bSV,), (bSV,))

            _stop_at(-2)
            P0 = Carver(0)
            cf = P0.f32(8 * 17); cb_ = P0.bf16(8 * 17)
            xin = [P0.f32(T), P0.f32(T)]
            xs_f = P0.f32(8 * NS); hs_f = P0.f32(8 * NS)
            b_cf, b_cb = mk("cf"), mk("cb")
            b_xin = [mk("xin0"), mk("xin1")]
            b_xs = mk("xs_f")
            cf3 = v3(cf, 8); cb3 = v3(cb_, 8)
            dma_in("sp", cf3, cT.rearrange("(k p) n -> p k n", p=128), (), (b_cf,))
            cp("dve", cb_, cf, (b_cf,), (b_cb,))
            modps, bmodps = PF[0], bPF[0]
            for pc in range(4):
                wtile, wb = load_wbig(w_cond, 0, pc * 512, 512)
                for i in range(4):
                    mc = pc * 4 + i
                    pairs = [(wtile[:, k, i * 128:(i + 1) * 128], cb3[:, k, :]) for k in range(KC)]
                    mm_group(modps[:, mc * 17:(mc + 1) * 17], pairs, (wb, b_cb), bmodps)
            tt("dve", modT[:], v3(modps[:, 0:16 * 17], 16), bcond_t[:, 0:16].unsqueeze(2).to_broadcast([128, 16, 17]),
               ALU.add, (bmodps, bSV), (bMOD,))
            ts("dve", modT[:, 8:16, :], modT[:, 8:16, :], 1.0, None, ALU.add, None, (bMOD,), (bMOD,))
            _stop_at(-1)
            xTv = xT.rearrange("(k p) t -> p k t", p=128)
            for k in range(KC):
                s = k % 2
                dma_in("sp", xin[s], xTv[:, k, :], (), (b_xin[s],))
                act(hT[:, k, 0:T], xin[s], AF.Identity, (b_xin[s], bMOD), (bHT[k],), bias=modT[:, k, 0:1], scale=modT[:, 8 + k, 0:1])
            _stop_at(-0.5)
            dma_in("sp", v3(xs_f, 8), xsT.rearrange("(k p) n -> p k n", p=128), (), (b_xs,))
            _stop_at(-0.4)
            tt("dve", v3(hs_f, 8), v3(xs_f, 8), modT[:, 8:16, 1:17], ALU.mult, (b_xs, bMOD), (b_xs,))
            _stop_at(-0.3)
            tt("dve", v3(hs_f, 8), v3(hs_f, 8), modT[:, 0:8, 1:17], ALU.add, (b_xs, bMOD), (b_xs,))
            _stop_at(-0.2)
            cp("act", hT[:, :, T:TP], v3(hs_f, 8), (b_xs,), (bHTs,))
            bHTall = bHT + [bHTs]
            _stop_at(0)

            S.barrier(allbufs)
            P1 = Carver(0)
            xTg = v3(P1.bf16(4 * T), 4); BTg = P1.bf16(T); CTg = P1.bf16(T)
            xbuf = P1.bf16(T + 8); dg = v3(P1.bf16(512), 4); tail3 = P1.f32(4)
            wsmB = [v3(P1.bf16(1024), 8), v3(P1.bf16(1024), 8)]; wsmC = [v3(P1.bf16(1024), 8), v3(P1.bf16(1024), 8)]
            dt_t = P1.f32(512); adt_t = P1.f32(512); acs_t = P1.f32(512); nacs_t = P1.f32(512)
            e_t = P1.f32(512); w1_t = P1.f32(512); cdec_t = P1.f32(512); tmp_t = P1.f32(512)
            negI = P1.bf16(128); Lmask = P1.bf16(512)
            persist_start = P1.pos
            nw_b = P1.f32(512); D_b = P1.f32(32); dtb_b = P1.f32(32); nA_b = P1.f32(32)
            xsT_all = v3(P1.f32(16 * NS), 16)
            BsT = v3(P1.f32(4 * NS), 4); CsT = v3(P1.f32(4 * NS), 4)
            szs = v3(P1.f32(16 * NS), 16)
            xsb = v3(P1.f32(NS * 4), NS); us_t = P1.f32(NS)
            dtT_s = P1.f32(NS); adtT_s = P1.f32(NS); dAT_s = P1.f32(NS)
            persist_end = P1.pos
            x_tok_d = [P1.bf16(512), P1.bf16(512)]; xs_d = [None, None]; xsc_d = [P1.bf16(512), P1.bf16(512)]
            B_tok_d = [P1.bf16(128), P1.bf16(128)]; MT_d = [v3(P1.bf16(1024), 8), v3(P1.bf16(1024), 8)]
            sz_d = [P1.f32(512), P1.f32(512)]
            CBm = P1.f32(128); _ex = P1.f32(512); ex_t = [_ex, P1.f32(512)]
            wdt = v3(_ex.bitcast(BF16), 8)
            t_a_d = [P1.f32(512), P1.f32(512)]; t_b = tmp_t; junk = P1.bf16(512); mhalf = P1.f32(1)
            yn_t = P1.bf16(512); hst = P1.f32(512); hst_bf = P1.bf16(512); st8_d = [P1.f32(8), P1.f32(8)]
            print("P1 end", P1.pos, "of", SCRN)
            b_xTg = [mk("xTg%d" % i) for i in range(4)]; b_BT = mk("BTg"); b_CT = mk("CTg")
            b_wsmB = [mk("wsmB0"), mk("wsmB1")]; b_wsmC = [mk("wsmC0"), mk("wsmC1")]
            b_xbuf = mk("xbuf"); b_xbufq = [mk("xbufq%d" % q) for q in range(4)]; b_dg = mk("dg"); b_tail = mk("tail3"); b_dtf = mk("dtfam"); b_bc = mk("bcasts")
            b_xsT = mk("xsT_all"); b_BsT = mk("BsT"); b_CsT = mk("CsT"); b_szs = mk("szs"); b_xsb = mk("xsb"); b_us = mk("us")
            b_msk = mk("maskconsts")
            b_dts = mk("dts"); b_wdt = mk("wdt"); b_wz = mk("wzs")
            bd_xtok = [mk("x_tok0"), mk("x_tok1")]; bd_xs = [mk("xs0"), mk("xs1")]; bd_xsc = [mk("xsc0"), mk("xsc1")]
            bd_Btok = [mk("B_tok0"), mk("B_tok1")]; bd_MT = [mk("MT0"), mk("MT1")]; bd_sz = [mk("sz0"), mk("sz1")]
            b_CBm = mk("CBm"); b_ex = [mk("ex0"), mk("ex1")]; bd_ta = [mk("t_a0"), mk("t_a1")]; b_tb = mk("t_b"); b_junk = mk("junk")
            b_yn = mk("yn"); b_hst = mk("hst"); b_hbf = mk("hst_bf"); bd_st8 = [mk("st8a"), mk("st8b")]
            S.op("dve", lambda e: e.memset(xbuf[:, 0:3], 0.0), (), (b_xbuf,))
            S.op("dve", lambda e: e.memset(mhalf, -0.5), (), (b_bc,))
            ts("pool", negI, identF[:], -32768.0, None, ALU.mult, None, (bCONST,), (b_msk,))
            ts("dve", v3(Lmask, 4), triF[:].unsqueeze(1).to_broadcast([128, 4, 128]), -1.0, 1.0, ALU.mult, ALU.add, (bCONST,), (b_msk,))
            dma_in("sp", dtb_b, dtb_row.partition_broadcast(128), (), (b_bc,))
            dma_in("sp", nA_b, alog_row.partition_broadcast(128), (), (b_bc,))
            dma_in("sp", D_b, d_row.partition_broadcast(128), (), (b_bc,))
            act(nA_b, nA_b, AF.Exp, (b_bc,), (b_bc,))
            ts("dve", nA_b, nA_b, -1.0, None, ALU.mult, None, (b_bc,), (b_bc,))
            S.op("pool", lambda e: e.memset(wdt, 0.0), (), (b_wdt,))
            dma_in("pool", wdt[:, :, 0:32], w_in[:, C_DT:C_DT + 32].rearrange("(k p) n -> p k n", p=128), (), (b_wdt,))
            dtps, bdtps = PF[1], bPF[1]
            for c in range(NCH):
                pairs = [(hT[:, k, c * L:(c + 1) * L], wdt[:, k, 0:32]) for k in range(KC)]
                mm_group(dtps[:, c * 32:(c + 1) * 32], pairs, tuple(bHT) + (b_wdt,), bdtps)
            dsps, bdsps = PF[2], bPF[2]
            mm_group(dsps[:, 0:NS], [(wdt[:, k, :], hT[:, k, T:TP]) for k in range(KC)], (bHTs, b_wdt), bdsps)
            act(dtT_s, dsps[:, 0:NS], AF.Exp, (bdsps, bSV), (b_dts,), bias=dtbc_t)
            act(dtT_s, dtT_s, AF.Ln, (b_dts,), (b_dts,), bias=1.0)
            ts("dve", adtT_s, dtT_s, alogc_t, None, ALU.mult, None, (b_dts, bSV), (b_dts,))
            act(dAT_s, adtT_s, AF.Exp, (b_dts,), (b_dts,))
            tt("dve", v3(tmp_t, 16), v3(dtps[:, :], 16), dtb_b.unsqueeze(1).to_broadcast([128, 16, 32]), ALU.add, (bdtps, b_bc), (b_dtf,))
            act(tmp_t, tmp_t, AF.Exp, (b_dtf,), (b_dtf,))
            act(dt_t, tmp_t, AF.Ln, (b_dtf,), (b_dtf,), bias=1.0)
            tt("dve", v3(adt_t, 16), v3(dt_t, 16), nA_b.unsqueeze(1).to_broadcast([128, 16, 32]), ALU.mult, (b_dtf, b_bc), (b_dtf,))
            acsps, bacsps = PF[3], bPF[3]
            totps, btotps = PF[4], bPF[4]
            mm_group(acsps[:, :], [(triF[:], adt_t)], (b_dtf, bCONST), bacsps)
            mm_group(totps[:, :], [(onesF[:], adt_t)], (b_dtf, bCONST), btotps)
            cp("act", acs_t, acsps[:, :], (bacsps,), (b_dtf,))
            act(e_t, acs_t, AF.Exp, (b_dtf,), (b_dtf,))
            tt("dve", tmp_t, totps[:, :], acs_t, ALU.subtract, (btotps, b_dtf), (b_dtf,))
            act(tmp_t, tmp_t, AF.Exp, (b_dtf,), (b_dtf,))
            tt("dve", w1_t, tmp_t, dt_t, ALU.mult, (b_dtf,), (b_dtf,))
            act(tmp_t, dt_t, AF.Ln, (b_dtf,), (b_dtf,))
            tt("dve", nacs_t, tmp_t, acs_t, ALU.subtract, (b_dtf,), (b_dtf,))
            act(cdec_t, totps[:, :], AF.Exp, (btotps,), (b_dtf,))
            _stop_at(1)

            def conv_chunk(wtile, wb, coff, cw4, cbias, out_bf, b_out, tail_dst, s_state_src, s_out, b_s_out, s_state_dst):
                tt("pool", dg, identF[:].unsqueeze(1).to_broadcast([128, 4, 128]), cw4.unsqueeze(2).to_broadcast([128, 4, 128]), ALU.mult,
                   (bCONST, bSV), (b_dg,))
                def proj_q(q):
                    ps, bps = next_pf()
                    pairs = [(wtile[:, k, coff:coff + 128], hT[:, k, q * 512:(q + 1) * 512]) for k in range(KC)]
                    mm_group(ps[:, :], pairs, tuple(bHT) + (wb,), bps)
                    cp("act", xbuf[:, 3 + q * 512:3 + (q + 1) * 512], ps[:, :], (bps,), (b_xbufq[q],))
                    if q == 3:
                        cp("act", tail3[:, 0:3], ps[:, 509:512], (bps,), (b_tail,))
                        dma_out(tail_dst, tail3[:, 0:3], (b_tail,))

                def conv_q(q):
                    ps2, bps2 = next_pf()
                    rd = (b_dg, b_xbufq[q]) + ((b_xbufq[q - 1],) if q > 0 else (b_xbuf,))
                    mm_group(ps2[:, :], [(dg[:, k, :], xbuf[:, q * 512 + k:q * 512 + k + 512]) for k in range(4)], rd, bps2)
                    act(out_bf[:, q * 512:(q + 1) * 512], ps2[:, :], AF.Silu, (bps2, bSV), (b_out,), bias=cbias)
                proj_q(0); proj_q(1); conv_q(0); proj_q(2); conv_q(1); proj_q(3); conv_q(2); conv_q(3)
                ps, bps = next_pf()
                mm_group(ps[:, 0:NS], [(wtile[:, k, coff:coff + 128], hT[:, k, T:TP]) for k in range(KC)], (bHTs, wb), bps)
                dma_in("sp", xsb[:, :, 0:3], s_state_src, (), (b_xsb,))
                cp("act", xsb[:, :, 3], ps[:, 0:NS], (bps,), (b_xsb,))
                dma_out(s_state_dst, xsb[:, :, 1:4], (b_xsb,))
                ts("dve", us_t, xsb[:, :, 0], cw4[:, 0:1], cbias, ALU.mult, ALU.add, (b_xsb, bSV), (b_us,))
                for k in range(1, 4):
                    stt(us_t, xsb[:, :, k], cw4[:, k:k + 1], us_t, ALU.mult, ALU.add, (b_xsb, bSV, b_us), (b_us,))
                act(s_out, us_t, AF.Silu, (b_us,), (b_s_out,))

            def load_big_slot(slot, src, c0, ncols=512):
                srcv = src[0:1024, c0:c0 + ncols].rearrange("(k p) n -> p k n", p=128)
                dma_in("pool", wt_big[slot][:, :, 0:ncols], srcv, (), (bWTbig[slot],))
                return wt_big[slot], bWTbig[slot]

            def load_bc(g):
                par = g % 2
                for (tile_, buf_, c0) in ((wsmB[par], b_wsmB[par], C_SB + g * 128), (wsmC[par], b_wsmC[par], C_SC + g * 128)):
                    dma_in("pool", tile_, w_in[0:1024, c0:c0 + 128].rearrange("(k p) n -> p k n", p=128), (), (buf_,))

            wX, wXb_ = load_big_slot(0, w_in, C_SX)
            load_bc(0)
            for g in range(4):
                wzs, b_wz = load_big_slot(1, w_in, C_SZ + g * 512)
                if g + 1 < 4:
                    load_bc(g + 1)
                par = g % 2
                ch = 16 + g
                conv_chunk(wsmB[par], b_wsmB[par], 0, scw_t[:, ch * 4:(ch + 1) * 4], scb_t[:, ch:ch + 1], BTg, b_BT,
                           o_sc_p[:, ch, :], ssd_cvT[ch * 128:(ch + 1) * 128, :, :], BsT[:, g, :], b_BsT, o_sc_s[:, ch, :, :])
                ch = 20 + g
                conv_chunk(wsmC[par], b_wsmC[par], 0, scw_t[:, ch * 4:(ch + 1) * 4], scb_t[:, ch:ch + 1], CTg, b_CT,
                           o_sc_p[:, ch, :], ssd_cvT[ch * 128:(ch + 1) * 128, :, :], CsT[:, g, :], b_CsT, o_sc_s[:, ch, :, :])
                wtile, wb = wt_big[0], bWTbig[0]
                for i in range(4):
                    ch = g * 4 + i
                    conv_chunk(wtile, wb, i * 128, scw_t[:, ch * 4:(ch + 1) * 4], scb_t[:, ch:ch + 1], xTg[:, i, :], b_xTg[i],
                               o_sc_p[:, ch, :], ssd_cvT[ch * 128:(ch + 1) * 128, :, :], xsT_all[:, ch, :], b_xsT, o_sc_s[:, ch, :, :])
                if g + 1 < 4:
                    load_big_slot(0, w_in, C_SX + (g + 1) * 512)
                dma_in("sp", nw_b, normw_row[:, g * 512:(g + 1) * 512].partition_broadcast(128), (), (b_bc,))
                for i in range(4):
                    ps, bps = next_pf()
                    mm_group(ps[:, 0:NS], [(wzs[:, k, i * 128:(i + 1) * 128], hT[:, k, T:TP]) for k in range(KC)], (bHTs, b_wz), bps)
                    act(szs[:, g * 4 + i, :], ps[:, 0:NS], AF.Silu, (bps,), (b_szs,))
                def bufs(c):
                    par = c % 2
                    return (x_tok_d[par], xs_d[par], xsc_d[par], B_tok_d[par], MT_d[par], sz_d[par],
                            bd_xtok[par], bd_xs[par], bd_xsc[par], bd_Btok[par], bd_MT[par], bd_sz[par],
                            t_a_d[par], bd_ta[par], st8_d[par], bd_st8[par])

                def S4(c, g=g):
                    (x_tok_t, xs_t, xsc_t, B_tok, MT, sz_t, b_xtok, b_xs_, b_xsc, b_Btok, b_MT, b_sz, t_a, b_ta, st8, b_st8) = bufs(c)
                    cs = slice(c * L, (c + 1) * L)

                    def fnY(e):
                        ins = None
                        for h8 in range(8):
                            ins = e.matmul(PF[3][:, h8 * 64:(h8 + 1) * 64], lhsT=MT[:, h8, :], rhs=x_tok_t[:, h8 * 64:(h8 + 1) * 64],
                                           start=True, stop=True)
                        return ins
                    if c > 0:
                        mm_group(PF[4][:, :], [(CTg[:, cs], hst_bf)], (b_CT, b_hbf), bPF[4])
                    mm_group(PF[5][:, :], [(B_tok, xsc_t)], (b_Btok, b_xsc), bPF[5])
                    S.op("pe", fnY, (b_MT, b_xtok), (bPF[3],))

                def Dskip(c, g=g):
                    (x_tok_t, xs_t, xsc_t, B_tok, MT, sz_t, b_xtok, b_xs_, b_xsc, b_Btok, b_MT, b_sz, t_a, b_ta, st8, b_st8) = bufs(c)
                    tt("pool", v3(t_b, 8), v3(x_tok_t, 8), D_b[:, g * 8:(g + 1) * 8].unsqueeze(2).to_broadcast([128, 8, 64]), ALU.mult,
                       (b_xtok, b_bc), (b_tb,))

                def S7(c, g=g):
                    (x_tok_t, xs_t, xsc_t, B_tok, MT, sz_t, b_xtok, b_xs_, b_xsc, b_Btok, b_MT, b_sz, t_a, b_ta, st8, b_st8) = bufs(c)
                    stt(yn_t, t_a, st8[:, 3:4], nw_b, ALU.mult, ALU.mult, (b_ta, b_st8, b_bc), (b_yn,))

                def S5a(c, g=g):
                    hb = c * 32 + g * 8
                    if c > 0:
                        tt("dve", v3(hst, 8), v3(hst, 8), cdec_t[:, hb:hb + 8].unsqueeze(2).to_broadcast([128, 8, 64]), ALU.mult,
                           (b_hst, b_dtf), (b_hst,))
                        tt("dve", hst, hst, PF[5][:, :], ALU.add, (b_hst, bPF[5]), (b_hst,))
                    else:
                        cp("dve", hst, PF[5][:, :], (bPF[5],), (b_hst,))
                    if c == NCH - 1:
                        dma_out(o_sh_p[:, g * 512:(g + 1) * 512], hst, (b_hst,))

                def S5b(c, g=g):
                    (x_tok_t, xs_t, xsc_t, B_tok, MT, sz_t, b_xtok, b_xs_, b_xsc, b_Btok, b_MT, b_sz, t_a, b_ta, st8, b_st8) = bufs(c)
                    hb = c * 32 + g * 8
                    if c > 0:
                        tt("dve", v3(t_a, 8), v3(PF[4][:, :], 8), e_t[:, hb:hb + 8].unsqueeze(2).to_broadcast([128, 8, 64]), ALU.mult,
                           (bPF[4], b_dtf), (b_ta,))
                        tt("dve", t_a, t_a, PF[3][:, :], ALU.add, (b_ta, bPF[3]), (b_ta,))
                    else:
                        cp("dve", t_a, PF[3][:, :], (bPF[3],), (b_ta,))
                    tt("dve", t_a, t_a, t_b, ALU.add, (b_ta, b_tb), (b_ta,))
                    tt("dve", t_a, t_a, sz_t, ALU.mult, (b_ta, b_sz), (b_ta,))

                def S1(c, g=g):
                    cs = slice(c * L, (c + 1) * L)

                    def fnT(e, cs=cs):
                        ins = None
                        for i in range(4):
                            ins = e.transpose(pB0[:, i * 128:(i + 1) * 128], xTg[:, i, cs], identB[:])
                        ins = e.transpose(pB0[:, 512:640], BTg[:, cs], identB[:])
                        return ins
                    S.op("pe", fnT, tuple(b_xTg) + (b_BT, bCONST), (bPB[0],))

                def S1b(c, g=g):
                    cs = slice(c * L, (c + 1) * L)
                    mm_group(PF[0][:, 0:128], [(BTg[:, cs], CTg[:, cs])], (b_BT, b_CT), bPF[0])

                def S1c(c, g=g):
                    cs = slice(c * L, (c + 1) * L)
                    mm_group(PF[0][:, :], [(hT[:, k, cs], wzs[:, k, :]) for k in range(KC)], tuple(bHT) + (b_wz,), bPF[0])

                def S1d(c, half, g=g):
                    hb = c * 32 + g * 8
                    accp, baccp = PF[1 + half], bPF[1 + half]

                    def fnA(e, half=half, hb=hb, accp=accp):
                        ins = e.matmul(accp[:, :], lhsT=negI, rhs=Lmask, start=True, stop=False)
                        for hh in range(4):
                            col = hb + half * 4 + hh
                            ins = e.matmul(accp[:, hh * 128:(hh + 1) * 128], lhsT=adt_t[:, col:col + 1].to_broadcast([128, 128]),
                                           rhs=triF[:], start=False, stop=(hh == 3))
                        return ins
                    S.op("pe", fnA, (b_dtf, bCONST, b_msk), (baccp,))

                def copies(c, g=g):
                    (x_tok_t, xs_t, xsc_t, B_tok, MT, sz_t, b_xtok, b_xs_, b_xsc, b_Btok, b_MT, b_sz, t_a, b_ta, st8, b_st8) = bufs(c)
                    cp("act", x_tok_t, pB0[:, 0:512], (bPB[0],), (b_xtok,))
                    cp("act", B_tok, pB0[:, 512:640], (bPB[0],), (b_Btok,))

                def poolx(c, g=g):
                    (x_tok_t, xs_t, xsc_t, B_tok, MT, sz_t, b_xtok, b_xs_, b_xsc, b_Btok, b_MT, b_sz, t_a, b_ta, st8, b_st8) = bufs(c)
                    hb = c * 32 + g * 8
                    tt("pool", v3(xsc_t, 8), v3(x_tok_t, 8), w1_t[:, hb:hb + 8].unsqueeze(2).to_broadcast([128, 8, 64]), ALU.mult,
                       (b_xtok, b_dtf), (b_xsc,))

                def tanhz(c, g=g):
                    (x_tok_t, xs_t, xsc_t, B_tok, MT, sz_t, b_xtok, b_xs_, b_xsc, b_Btok, b_MT, b_sz, t_a, b_ta, st8, b_st8) = bufs(c)
                    act(sz_t, PF[0][:, :], AF.Tanh, (bPF[0],), (b_sz,), scale=0.5)

                def exps(c, half, g=g):
                    hb = c * 32 + g * 8
                    accp, baccp = PF[1 + half], bPF[1 + half]
                    for hh in range(4):
                        col = hb + half * 4 + hh
                        act(ex_t[half][:, hh * 128:(hh + 1) * 128], accp[:, hh * 128:(hh + 1) * 128], AF.Exp,
                            (baccp, b_dtf), (b_ex[half],), bias=nacs_t[:, col:col + 1])

                def S3a(c, g=g):
                    tt("dve", CBm, PF[0][:, 0:128], triF[:], ALU.mult, (bPF[0], bCONST), (b_CBm,))

                def S3b(c, g=g):
                    (x_tok_t, xs_t, xsc_t, B_tok, MT, sz_t, b_xtok, b_xs_, b_xsc, b_Btok, b_MT, b_sz, t_a, b_ta, st8, b_st8) = bufs(c)
                    stt(sz_t, sz_t, 1.0, PF[0][:, :], ALU.add, ALU.mult, (b_sz, bPF[0]), (b_sz,))

                def S3c(c, half, g=g):
                    (x_tok_t, xs_t, xsc_t, B_tok, MT, sz_t, b_xtok, b_xs_, b_xsc, b_Btok, b_MT, b_sz, t_a, b_ta, st8, b_st8) = bufs(c)
                    stt(MT[:, half * 4:(half + 1) * 4, :], v3(ex_t[half], 4), 1.0e30, CBm.unsqueeze(1).to_broadcast([128, 4, 128]),
                        ALU.min, ALU.mult, (b_ex[half], b_CBm), (b_MT,))

                def S6(c, g=g):
                    (x_tok_t, xs_t, xsc_t, B_tok, MT, sz_t, b_xtok, b_xs_, b_xsc, b_Btok, b_MT, b_sz, t_a, b_ta, st8, b_st8) = bufs(c)
                    act(junk, t_a, AF.Square, (b_ta,), (b_junk, b_st8), accum_out=st8[:, 0:1])
                    ts("pool", st8[:, 1:2], st8[:, 0:1], 1.0 / 512.0, 4.0 * RMS_EPS, ALU.mult, ALU.add, (b_st8,), (b_st8,))
                    tt("pool", st8[:, 3:4], st8[:, 1:2], mhalf[:, 0:1], ALU.pow, (b_st8, b_bc), (b_st8,))

                def S8(c, g=g):
                    cs = slice(c * L, (c + 1) * L)

                    def fnT2(e):
                        ins = None
                        for i in range(4):
                            ins = e.transpose(pB1[:, i * 128:(i + 1) * 128], yn_t[:, i * 128:(i + 1) * 128], identB[:])
                        return ins
                    S.op("pe", fnT2, (b_yn, bCONST), (bPB[1],))
                    cp("act", yssd[:, g * 4:(g + 1) * 4, cs], v3(pB1[:, 0:512], 4), (bPB[1],), tuple(bYS[g * 4:(g + 1) * 4]))

                for s_ in range(-1, NCH + 1):
                    cur, nxt, prv = s_, s_ + 1, s_ - 1
                    hc = 0 <= cur < NCH
                    hn = 0 <= nxt < NCH
                    hp = 0 <= prv < NCH
                    if hc:
                        S4(cur)
                        Dskip(cur)
                    if hp:
                        S7(prv)
                    if hn:
                        S1(nxt)
                        S1d(nxt, 0)
                        S1d(nxt, 1)
                        S1b(nxt)
                        S3a(nxt)
                        copies(nxt)
                        poolx(nxt)
                        exps(nxt, 0)
                    if hc:
                        S5a(cur)
                    if hn:
                        S1c(nxt)
                        exps(nxt, 1)
                    if hc and cur < NCH - 1:
                        cp("act", hst_bf, hst, (b_hst,), (b_hbf,))
                    if hc:
                        S5b(cur)
                    if hn:
                        S3c(nxt, 0)
                        tanhz(nxt)
                        S3c(nxt, 1)
                        S3b(nxt)
                    if hc:
                        S6(cur)
                    if hp:
                        S8(prv)

            _stop_at(2)
            S.barrier(allbufs)
            P1s = Carver(0)
            H0 = [v3(P1s.f32(2048), 16), v3(P1s.f32(2048), 16)]
            HN = [v3(P1s.f32(2048), 16), v3(P1s.f32(2048), 16)]
            T1 = v3(P1s.f32(2048), 16); T2 = v3(P1s.f32(2048), 16)
            assert P1s.pos <= persist_start, (P1s.pos, persist_start)
            P1s2 = Carver(persist_end)
            Bb = v3(P1s2.f32(512), 4); Cb = v3(P1s2.f32(512), 4)
            T3 = v3(P1s2.f32(2048), 16)
            esel = P1s2.f32(2048)
            dAx = v3(P1s2.f32(256), 16); xdt = v3(P1s2.f32(256), 16); ysT = v3(P1s2.f32(256), 16)
            gys = v3(P1s2.f32(256), 16); sqs = v3(P1s2.f32(256), 16); rs_t = v3(P1s2.f32(64), 4)
            b_H0 = [mk("H0a"), mk("H0b")]; b_HN = [mk("HNa"), mk("HNb")]; b_T1 = mk("T1"); b_T2 = mk("T2"); b_T3 = mk("T3")
            b_Bb = mk("Bb"); b_Cb = mk("Cb"); b_esel = mk("esel"); b_dAx = mk("dAx"); b_xdt = mk("xdt"); b_ysT = mk("ysT")
            b_gys = mk("gys"); b_sqs = mk("sqs"); b_rs = mk("rs")
            dma_in("sp", esel, c_esel[:, :], (), (b_esel,))
            exps, bexps = PF[0], bPF[0]
            _stop_at(2.05)

            def fnE(e):
                ins = None
                for j in range(16):
                    ins = e.matmul(exps[:, j * 16:(j + 1) * 16], lhsT=esel[:, j * 128:(j + 1) * 128], rhs=dAT_s, start=True, stop=True)
                for j in range(16):
                    ins = e.matmul(exps[:, 256 + j * 16:256 + (j + 1) * 16], lhsT=esel[:, j * 128:(j + 1) * 128], rhs=dtT_s,
                                   start=True, stop=True)
                return ins
            S.op("pe", fnE, (b_esel, b_dts), (bexps,))
            _stop_at(2.07)
            cp("act", dAx, v3(exps[:, 0:256], 16), (bexps,), (b_dAx,))
            _stop_at(2.08)
            cp("act", xdt, v3(exps[:, 256:512], 16), (bexps,), (b_xdt,))
            tt("dve", xdt, xdt, xsT_all, ALU.mult, (b_xdt, b_xsT), (b_xdt,))
            _stop_at(2.1)
            h0v = ssd_h0.rearrange("s (j e) p n -> s (e p) j n", e=2)
            ohv = o_sh_s.rearrange("s (j e) p n -> s (e p) j n", e=2)
            for s in range(NS):
                sl = s % 2
                dma_in("act", H0[sl], h0v[s], (), (b_H0[sl],))
                bps_, bbps_ = PF[1], bPF[1]
                cps_, bcps_ = PF[2], bPF[2]

                def fnB(e, s=s):
                    ins = None
                    for g in range(4):
                        ins = e.matmul(PF[1][:, g * 128:(g + 1) * 128], lhsT=BsT[:, g, s:s + 1].to_broadcast([128, 128]), rhs=identF[:],
                                       start=True, stop=True)
                    return ins

                def fnC(e, s=s):
                    ins = None
                    for g in range(4):
                        ins = e.matmul(PF[2][:, g * 128:(g + 1) * 128], lhsT=CsT[:, g, s:s + 1].to_broadcast([128, 128]), rhs=identF[:],
                                       start=True, stop=True)
                    return ins
                S.op("pe", fnB, (b_BsT, bCONST), (bbps_,))
                S.op("pe", fnC, (b_CsT, bCONST), (bcps_,))
                cp("act", Bb, v3(PF[1][:, :], 4), (bbps_,), (b_Bb,))
                cp("act", Cb, v3(PF[2][:, :], 4), (bcps_,), (b_Cb,))
                _stop_at(2.2)
                tt("dve", T1, H0[sl], dAx[:, :, s:s + 1].to_broadcast([128, 16, 128]), ALU.mult, (b_H0[sl], b_dAx), (b_T1,))
                _stop_at(2.3)
                xdt4 = xdt[:, :, s:s + 1].rearrange("p (g j) o -> p g j o", g=4).to_broadcast([128, 4, 4, 128])
                Bb4 = Bb.unsqueeze(2).to_broadcast([128, 4, 4, 128])
                Cb4 = Cb.unsqueeze(2).to_broadcast([128, 4, 4, 128])
                tt("pool", T2.rearrange("p (g j) n -> p g j n", g=4), xdt4, Bb4, ALU.mult, (b_xdt, b_Bb), (b_T2,))
                _stop_at(2.4)
                tt("dve", HN[sl], T1, T2, ALU.add, (b_T1, b_T2), (b_HN[sl],))
                dma_out(ohv[s], HN[sl], (b_HN[sl],))
                tt("dve", T3.rearrange("p (g j) n -> p g j n", g=4), HN[sl].rearrange("p (g j) n -> p g j n", g=4), Cb4, ALU.mult,
                   (b_HN[sl], b_Cb), (b_T3,))
                _stop_at(2.5)
                S.op("dve", lambda e, s=s: e.tensor_reduce(out=ysT[:, :, s], in_=T3, axis=AX.X, op=ALU.add), (b_T3,), (b_ysT,))
                _stop_at(2.6)
            _stop_at(2.7)
            tt("dve", gys, xsT_all, dx_t.unsqueeze(2).to_broadcast([128, 16, NS]), ALU.mult, (b_xsT, bSV), (b_gys,))
            tt("dve", ysT, ysT, gys, ALU.add, (b_ysT, b_gys), (b_ysT,))
            tt("dve", gys, ysT, szs, ALU.mult, (b_ysT, b_szs), (b_gys,))
            tt("dve", sqs, gys, gys, ALU.mult, (b_gys,), (b_sqs,))
            ssp, bssp = PF[3], bPF[3]

            def fnS(e):
                ins = None
                for g in range(4):
                    for i in range(4):
                        ins = e.matmul(ssp[:, g * 16:(g + 1) * 16], lhsT=onesF[:], rhs=sqs[:, g * 4 + i, :], start=(i == 0), stop=(i == 3))
                return ins
            S.op("pe", fnS, (b_sqs, bCONST), (bssp,))
            ts("dve", rs_t, v3(ssp[:, 0:64], 4), 1.0 / 512.0, RMS_EPS, ALU.mult, ALU.add, (bssp,), (b_rs,))
            act(rs_t, rs_t, AF.Sqrt, (b_rs,), (b_rs,))
            S.op("dve", lambda e: e.reciprocal(out=rs_t, in_=rs_t), (b_rs,), (b_rs,))
            tt("dve", gys.rearrange("p (g j) s -> p g j s", g=4), gys.rearrange("p (g j) s -> p g j s", g=4),
               rs_t.unsqueeze(2).to_broadcast([128, 4, 4, NS]), ALU.mult, (b_gys, b_rs), (b_gys,))
            tt("dve", gys, gys, nwT_t.unsqueeze(2).to_broadcast([128, 16, NS]), ALU.mult, (b_gys, bSV), (b_gys,))
            cp("act", yssd[:, :, T:TP], gys, (b_gys,), (bYSs,))

            _stop_at(3)
            S.barrier(allbufs)
            P2 = Carver(0)
            ylru = v3(P2.bf16(8 * TP), 8)
            xbuf2 = P2.bf16(T + 8); dg2 = v3(P2.bf16(512), 4); tail2 = P2.f32(4)
            u2 = P2.f32(T); ubf = P2.bf16(T)
            r_t = P2.f32(T); i_t = P2.f32(T); a_t = P2.f32(T); m_t = P2.f32(T)
            lxs = v3(P2.f32(NS * 4), NS); lus = P2.f32(NS); lusb = P2.bf16(NS); lr = P2.f32(NS); li = P2.f32(NS); la = P2.f32(NS)
            lm = P2.f32(NS); lh0 = v3(P2.f32(8 * NS), 8); lhn = v3(P2.f32(8 * NS), 8); lhp = P2.f32(8)
            hba = P2.f32(8); hbx = P2.f32(8); hcv = P2.f32(8); q25 = P2.f32(1)
            wab = v3(P2.bf16(8 * 128), 8); wxb = v3(P2.bf16(8 * 128), 8)
            bYL = [mk("ylru%d" % k) for k in range(8)]; bYLs = mk("ylru_s")
            b_x2 = mk("xbuf2"); b_x2q = [mk("xbuf2q%d" % q) for q in range(4)]; b_dg2 = mk("dg2"); b_tail2 = mk("tail2")
            b_u2 = mk("u2"); b_ubf = mk("ubf"); b_r = mk("r"); b_i = mk("i"); b_a = mk("a"); b_m = mk("m")
            b_lxs = mk("lxs"); b_lus = mk("lus"); b_lsm = mk("lsm"); b_lh0 = mk("lh0"); b_lhn = mk("lhn"); b_lhp = mk("lhp"); b_wab = mk("wab")
            b_hv = mk("halfvecs")
            S.op("dve", lambda e: e.memset(xbuf2[:, 0:3], 0.0), (), (b_x2,))
            ts("dve", hba, lba_t, 0.5, None, ALU.mult, None, (bSV,), (b_hv,))
            ts("dve", hbx, lbx_t, 0.5, None, ALU.mult, None, (bSV,), (b_hv,))
            ts("dve", hcv, cvec_t, 0.5, None, ALU.mult, None, (bSV,), (b_hv,))
            S.op("dve", lambda e: e.memset(q25, 0.25), (), (b_hv,))
            S.op("pool", lambda e: e.memset(wab, 0.0), (), (b_wab,))
            S.op("pool", lambda e: e.memset(wxb, 0.0), (), (b_wab,))
            for (dst_, src_) in ((wab, lwa), (wxb, lwx)):
                sv_ = src_.rearrange("(j e) k m -> e k j m", e=2)
                for e_ in range(2):
                    dma_in("pool", dst_[e_ * 64:(e_ + 1) * 64, :, e_ * 64:(e_ + 1) * 64], sv_[e_], (), (b_wab,))
            dma_in("sp", lh0, lru_h0T.rearrange("(k p) n -> p k n", p=128), (), (b_lh0,))
            _stop_at(3.05)
            for pp in range(2):
                wX, wXb = load_wbig(w_in, 0, C_LX + pp * 512, 512)
                wZ, wZb = load_wbig(w_in, 0, C_LZ + pp * 512, 512)
                for j4 in range(4):
                    j = pp * 4 + j4
                    co = j4 * 128
                    cw4 = lcw_t[:, j * 4:(j + 1) * 4]
                    tt("pool", dg2, identF[:].unsqueeze(1).to_broadcast([128, 4, 128]), cw4.unsqueeze(2).to_broadcast([128, 4, 128]), ALU.mult,
                       (bCONST, bSV), (b_dg2,))

                    def proj_q(q, j=j, co=co, wX=wX, wXb=wXb):
                        ps, bps = next_pf()
                        mm_group(ps[:, :], [(wX[:, k, co:co + 128], hT[:, k, q * 512:(q + 1) * 512]) for k in range(KC)], tuple(bHT) + (wXb,), bps)
                        cp("act", xbuf2[:, 3 + q * 512:3 + (q + 1) * 512], ps[:, :], (bps,), (b_x2q[q],))
                        if q == 3:
                            cp("act", tail2[:, 0:3], ps[:, 509:512], (bps,), (b_tail2,))
                            dma_out(o_lc_p[:, j, :], tail2[:, 0:3], (b_tail2,))

                    def conv_q(q, j=j):
                        qs = slice(q * 512, (q + 1) * 512)
                        ps2, bps2 = next_pf()
                        rd = (b_dg2, b_x2q[q]) + ((b_x2q[q - 1],) if q > 0 else (b_x2,))
                        mm_group(ps2[:, :], [(dg2[:, k, :], xbuf2[:, q * 512 + k:q * 512 + k + 512]) for k in range(4)], rd, bps2)
                        _stop_at(3.06)
                        act(u2[:, qs], ps2[:, :], AF.Identity, (bps2, bSV), (b_u2,), bias=lcb_t[:, j:j + 1])
                        _stop_at(3.07)
                        cp("dve", ubf[:, qs], u2[:, qs], (b_u2,), (b_ubf,))
                        _stop_at(3.08)
                        ps, bps = next_pf()
                        mm_group(ps[:, :], [(wab[:, j, :], ubf[:, qs])], (b_wab, b_ubf), bps)
                        act(r_t[:, qs], ps[:, :], AF.Tanh, (bps, b_hv), (b_r,), bias=hba[:, j:j + 1], scale=0.5)
                        ps, bps = next_pf()
                        mm_group(ps[:, :], [(wxb[:, j, :], ubf[:, qs])], (b_wab, b_ubf), bps)
                        act(i_t[:, qs], ps[:, :], AF.Tanh, (bps, b_hv), (b_i,), bias=hbx[:, j:j + 1], scale=0.5)
                    proj_q(0); proj_q(1); conv_q(0); proj_q(2); conv_q(1); proj_q(3); conv_q(2); conv_q(3)
                    _stop_at(3.1)
                    ps, bps = next_pf()
                    mm_group(ps[:, 0:NS], [(wX[:, k, co:co + 128], hT[:, k, T:TP]) for k in range(KC)], (bHTs, wXb), bps)
                    dma_in("sp", lxs[:, :, 0:3], lru_cvT[j * 128:(j + 1) * 128, :, :], (), (b_lxs,))
                    cp("act", lxs[:, :, 3], ps[:, 0:NS], (bps,), (b_lxs,))
                    dma_out(o_lc_s[:, j, :, :], lxs[:, :, 1:4], (b_lxs,))
                    ts("dve", lus, lxs[:, :, 0], cw4[:, 0:1], lcb_t[:, j:j + 1], ALU.mult, ALU.add, (b_lxs, bSV), (b_lus,))
                    for k in range(1, 4):
                        stt(lus, lxs[:, :, k], cw4[:, k:k + 1], lus, ALU.mult, ALU.add, (b_lxs, bSV, b_lus), (b_lus,))
                    cp("dve", lusb, lus, (b_lus,), (b_lus,))
                    ps, bps = next_pf()
                    mm_group(ps[:, 0:NS], [(wab[:, j, :], lusb)], (b_wab, b_lus), bps)
                    mm_group(ps[:, 32:32 + NS], [(wxb[:, j, :], lusb)], (b_wab, b_lus), bps)
                    act(lr, ps[:, 0:NS], AF.Tanh, (bps, b_hv), (b_lsm,), bias=hba[:, j:j + 1], scale=0.5)
                    act(li, ps[:, 32:32 + NS], AF.Tanh, (bps, b_hv), (b_lsm,), bias=hbx[:, j:j + 1], scale=0.5)
                    _stop_at(3.2)
                    act(a_t, r_t, AF.Exp, (b_r, b_hv), (b_a,), scale=hcv[:, j:j + 1], bias=hcv[:, j:j + 1])
                    act(m_t, r_t, AF.Exp, (b_r, bSV), (b_m,), scale=cvec_t[:, j:j + 1], bias=cvec_t[:, j:j + 1])
                    act(la, lr, AF.Exp, (b_lsm, b_hv), (b_lsm,), scale=hcv[:, j:j + 1], bias=hcv[:, j:j + 1])
                    act(lm, lr, AF.Exp, (b_lsm, bSV), (b_lsm,), scale=cvec_t[:, j:j + 1], bias=cvec_t[:, j:j + 1])
                    act(m_t, m_t, AF.Sqrt, (b_m, b_hv), (b_m,), scale=-0.25, bias=q25[:, 0:1])
                    act(lm, lm, AF.Sqrt, (b_lsm, b_hv), (b_lsm,), scale=-0.25, bias=q25[:, 0:1])
                    _stop_at(3.3)
                    stt(i_t, i_t, 1.0, u2, ALU.add, ALU.mult, (b_i, b_u2), (b_i,))
                    tt("dve", m_t[:, 1:T], m_t[:, 1:T], i_t[:, 1:T], ALU.mult, (b_m, b_i), (b_m,))
                    ts("dve", m_t[:, 0:1], i_t[:, 0:1], 0.5, None, ALU.mult, None, (b_m, b_i), (b_m,))
                    S.op("dve", lambda e: e.tensor_tensor_scan(out=r_t, data0=a_t, data1=m_t, initial=0.0, op0=ALU.mult, op1=ALU.add),
                         (b_a, b_m, b_r), (b_r,))
                    cp("dve", lhp[:, j:j + 1], r_t[:, T - 1:T], (b_r,), (b_lhp,))
                    _stop_at(3.4)
                    stt(li, li, 1.0, lus, ALU.add, ALU.mult, (b_lsm, b_lus), (b_lsm,))
                    tt("dve", lm, lm, li, ALU.mult, (b_lsm,), (b_lsm,))
                    tt("dve", la, la, lh0[:, j, :], ALU.mult, (b_lsm, b_lh0), (b_lsm,))
                    tt("dve", lhn[:, j, :], la, lm, ALU.add, (b_lsm,), (b_lhn,))
                    _stop_at(3.5)
                    for q in range(4):
                        qs = slice(q * 512, (q + 1) * 512)
                        ps, bps = next_pf()
                        mm_group(ps[:, :], [(wZ[:, k, co:co + 128], hT[:, k, qs]) for k in range(KC)], tuple(bHT) + (wZb,), bps)
                        act(a_t[:, qs], ps[:, :], AF.Tanh, (bps, b_a), (b_a,), scale=0.5)
                        stt(a_t[:, qs], a_t[:, qs], 1.0, ps[:, :], ALU.add, ALU.mult, (b_a, bps), (b_a,))
                    stt(ylru[:, j, 0:T], a_t, 0.5, r_t, ALU.mult, ALU.mult, (b_r, b_a), (bYL[j],))
                    ps, bps = next_pf()
                    mm_group(ps[:, 0:NS], [(wZ[:, k, co:co + 128], hT[:, k, T:TP]) for k in range(KC)], (bHTs, wZb), bps)
                    act(lr, ps[:, 0:NS], AF.Tanh, (bps,), (b_lsm,), scale=0.5)
                    stt(lr, lr, 1.0, ps[:, 0:NS], ALU.add, ALU.mult, (b_lsm, bps), (b_lsm,))
                    stt(ylru[:, j, T:TP], lr, 0.5, lhn[:, j, :], ALU.mult, ALU.mult, (b_lsm, b_lhn), (bYLs,))
            dma_out(o_lh_p[:, :], lhp, (b_lhp,))
            dma_out(o_lh_s[:, :, :], lhn, (b_lhn,))

            _stop_at(4)
            S.barrier([b for b in allbufs if b not in bYL and b is not bYLs] + [bWTbig[0], bWTbig[1]] + bWTsm)
            P3 = Carver((8 * TP) // 2)
            merged = v3(P3.bf16(8 * TP), 8)
            sgA = P3.f32(512); sgB = P3.f32(512); tA = P3.f32(512); tB = P3.f32(512)
            yo_t = [P3.f32(1024), P3.f32(1024)]; gts = P3.f32(1024)
            bnst_d = [P3.f32(16), P3.f32(16)]; mh3 = P3.f32(1); junk3 = P3.bf16(1024); cbb = v3(P3.bf16(8 * 128), 8); cf2 = v3(P3.f32(8 * 17), 8); cb2 = v3(P3.bf16(8 * 17), 8)
            P3b = Carver(0)
            gate_b = P3b.f32(1024); lng_b = P3b.f32(1024); lnb_b = P3b.f32(1024); bg_b = P3b.f32(1024)
            xtk = [P3b.f32(1024), P3b.f32(1024)]; resid_d = [P3b.f32(1024), scr[:, (8 * TP) // 2 + 8 * TP // 2:(8 * TP) // 2 + 8 * TP // 2 + 1024]]; xn_d = [P3b.f32(1024), scr[:, (8 * TP) // 2 + 8 * TP // 2 + 1024:(8 * TP) // 2 + 8 * TP // 2 + 2048]]
            assert P3b.pos <= (8 * TP) // 2
            bMG = [mk("merged%d" % k) for k in range(8)]; bMGs = mk("merged_s")
            b_sgA = mk("sgA"); b_sgB = mk("sgB"); b_tA = mk("tA"); b_tB = mk("tB"); b_gateb = mk("gate_b"); b_ln = mk("lnbc")
            b_xtk = [mk("xtk0"), mk("xtk1")]; b_res_d = [mk("resid0"), mk("resid1")]; b_xn_d = [mk("xn0"), mk("xn1")]; b_bn_d = [mk("bn0"), mk("bn1")]; b_yo = [mk("yo0"), mk("yo1")]
            b_cbb = mk("cbb"); b_c2 = mk("c2"); b_gts = mk("gts"); b_j3 = mk("junk3")
            colsets = [(slice(q * 512, (q + 1) * 512), 512) for q in range(4)] + [(slice(T, TP), NS)]
            for jo in range(8):
                wA, wAb = load_wsm(w_lp, 0, jo * 128)
                wB0, wB0b = load_wsm(w_sp, 0, jo * 128)
                wB1, wB1b = load_wsm(w_sp, 1024, jo * 128)
                wgA, wgAb = load_wsm(w_in, 0, C_MA + jo * 128)
                wgB, wgBb = load_wsm(w_in, 0, C_MB + jo * 128)
                for qi, (cs, n) in enumerate(colsets):
                    smp = qi == 4
                    rH = (bHTs,) if smp else tuple(bHT)
                    rYL = (bYLs,) if smp else tuple(bYL)
                    rYS = (bYSs,) if smp else tuple(bYS)
                    pA, bpA = next_pf()
                    mm_group(pA[:, 0:n], [(wA[:, k, :], ylru[:, k, cs]) for k in range(8)], rYL + (wAb,), bpA)
                    pB, bpB = next_pf()
                    mm_group(pB[:, 0:n], [(wB0[:, k, :], yssd[:, k, cs]) for k in range(8)] + [(wB1[:, k, :], yssd[:, 8 + k, cs]) for k in range(8)],
                             rYS + (wB0b, wB1b), bpB)
                    pgA, bpgA = next_pf()
                    mm_group(pgA[:, 0:n], [(wgA[:, k, :], hT[:, k, cs]) for k in range(8)], rH + (wgAb,), bpgA)
                    pgB, bpgB = next_pf()
                    mm_group(pgB[:, 0:n], [(wgB[:, k, :], hT[:, k, cs]) for k in range(8)], rH + (wgBb,), bpgB)
                    act(sgA[:, 0:n], pgA[:, 0:n], AF.Sigmoid, (bpgA,), (b_sgA,))
                    act(sgB[:, 0:n], pgB[:, 0:n], AF.Sigmoid, (bpgB,), (b_sgB,))
                    tt("dve", tA[:, 0:n], sgA[:, 0:n], pA[:, 0:n], ALU.mult, (b_sgA, bpA), (b_tA,))
                    tt("dve", tB[:, 0:n], sgB[:, 0:n], pB[:, 0:n], ALU.mult, (b_sgB, bpB), (b_tB,))
                    tt("dve", merged[:, jo, cs], tA[:, 0:n], tB[:, 0:n], ALU.add, (b_tA, b_tB), (bMGs if smp else bMG[jo],))
            S.barrier(bWTsm + bWTbig + bYL + [bYLs, b_sgA, b_sgB, b_tA, b_tB])
            dma_in("sp", cf2, cT.rearrange("(k p) n -> p k n", p=128), (), (b_c2,))
            cp("dve", cb2, cf2, (b_c2,), (b_c2,))
            cp("dve", cbb, cf2[:, :, 0:1].to_broadcast([128, 8, 128]), (b_c2,), (b_cbb,))
            S.op("dve", lambda e: e.memset(mh3, -0.5), (), (b_ln,))
            dma_in("sp", bg_b, b_gate.partition_broadcast(128), (), (b_ln,))
            dma_in("sp", lng_b, lng_row.partition_broadcast(128), (), (b_ln,))
            dma_in("sp", lnb_b, lnb_row.partition_broadcast(128), (), (b_ln,))
            for hf in range(2):
                wg, wgb = load_wbig(w_cond, 0, 2048 + hf * 512, 512)
                ps, bps = next_pf()
                mm_group(ps[:, :], [(cbb[:, k, :], wg[:, k, :]) for k in range(8)], (b_cbb, wgb), bps)
                tt("dve", gate_b[:, hf * 512:(hf + 1) * 512], ps[:, :], bg_b[:, hf * 512:(hf + 1) * 512], ALU.add, (bps, b_ln), (b_gateb,))
                ps, bps = next_pf()
                mm_group(ps[0:NS, :], [(cb2[:, k, 1:17], wg[:, k, :]) for k in range(8)], (b_c2, wgb), bps)
                tt("dve", gts[0:NS, hf * 512:(hf + 1) * 512], ps[0:NS, :], bg_b[0:NS, hf * 512:(hf + 1) * 512], ALU.add, (bps, b_ln), (b_gts,))
            wo0, wo0b = load_wbig(w_out, 0, 0, 512)
            wo1, wo1b = load_wbig(w_out, 0, 512, 512)
            def tile_ctx(ti):
                smp = ti == NCH
                np_ = NS if smp else 128
                cs = slice(T, TP) if smp else slice(ti * 128, (ti + 1) * 128)
                sl = ti % 2
                return smp, np_, cs, sl

            def ln_stageP(ti):
                smp, np_, cs, sl = tile_ctx(ti)
                resid, bnst, b_res, b_bn = resid_d[sl], bnst_d[sl], b_res_d[sl], b_bn_d[sl]
                rM = (bMGs,) if smp else tuple(bMG)
                dma_in("act", xtk[sl][0:np_, :], xs_tok[:, :] if smp else x_tok[ti * 128:(ti + 1) * 128, :], (), (b_xtk[sl],))
                gsrc = gts if smp else gate_b
                bgs = b_gts if smp else b_gateb
                for hf, (wo, wob) in enumerate(((wo0, wo0b), (wo1, wo1b))):
                    ps, bps = next_pf()
                    mm_group(ps[0:np_, :], [(merged[:, k, cs], wo[:, k, :]) for k in range(8)], rM + (wob,), bps)
                    tt("dve", resid[0:np_, hf * 512:(hf + 1) * 512], ps[0:np_, :], gsrc[0:np_, hf * 512:(hf + 1) * 512], ALU.mult,
                       (bps, bgs), (b_res,))
                stt(resid[0:np_, :], xtk[sl][0:np_, :], ALPHA, resid[0:np_, :], ALU.mult, ALU.add, (b_xtk[sl], b_res), (b_res,))
                act(junk3[0:np_, :], resid[0:np_, :], AF.Copy, (b_res,), (b_j3, b_bn), accum_out=bnst[0:np_, 0:1])
                act(junk3[0:np_, :], resid[0:np_, :], AF.Square, (b_res,), (b_j3, b_bn), accum_out=bnst[0:np_, 1:2])
                ts("pool", bnst[0:np_, 12:13], bnst[0:np_, 0:1], 1.0 / D, None, ALU.mult, None, (b_bn,), (b_bn,))
                tt("pool", bnst[0:np_, 2:3], bnst[0:np_, 12:13], bnst[0:np_, 12:13], ALU.mult, (b_bn,), (b_bn,))
                ts("pool", bnst[0:np_, 3:4], bnst[0:np_, 1:2], 1.0 / D, LN_EPS, ALU.mult, ALU.add, (b_bn,), (b_bn,))
                tt("pool", bnst[0:np_, 14:15], bnst[0:np_, 3:4], bnst[0:np_, 2:3], ALU.subtract, (b_bn,), (b_bn,))
                tt("pool", bnst[0:np_, 15:16], bnst[0:np_, 14:15], mh3[0:np_, 0:1], ALU.pow, (b_bn, b_ln), (b_bn,))

            def ln_stageQ(ti):
                smp, np_, cs, sl = tile_ctx(ti)
                resid, xn, bnst, b_res, b_xn, b_bn = resid_d[sl], xn_d[sl], bnst_d[sl], b_res_d[sl], b_xn_d[sl], b_bn_d[sl]
                stt(xn[0:np_, :], resid[0:np_, :], bnst[0:np_, 12:13], lng_b[0:np_, :], ALU.subtract, ALU.mult, (b_res, b_bn, b_ln), (b_xn,))
                stt(yo_t[sl][0:np_, :], xn[0:np_, :], bnst[0:np_, 15:16], lnb_b[0:np_, :], ALU.mult, ALU.add, (b_xn, b_bn, b_ln), (b_yo[sl],))
                dma_out(y_s[:, :] if smp else y_p[ti * 128:(ti + 1) * 128, :], yo_t[sl][0:np_, :], (b_yo[sl],))

            for ti in range(NCH + 2):
                if ti <= NCH:
                    ln_stageP(ti)
                if ti >= 1:
                    ln_stageQ(ti - 1)

        except _StopRec:
            pass
        S.finalize()

        @block.sync
        def _(e):
            S.emit("sp", e)

        @block.gpsimd
        def _(e):
            S.emit("pool", e)

        @block.scalar
        def _(e):
            S.emit("act", e)

        @block.vector
        def _(e):
            S.emit("dve", e)

        @block.tensor
        def _(e):
            S.emit("pe", e)
    return nc


_NC_CACHE = {}


def _vec_pk(v, k):
    return np.ascontiguousarray(np.asarray(v, np.float32).reshape(k, 128).T)


def kernel(x_prompt, x_sample, state_lru_h, state_lru_conv, state_ssd_h, state_ssd_conv,
           c_prompt, c_sample, w_cond, b_cond, w_in, lru_conv_w, lru_conv_b, lru_wa, lru_ba,
           lru_wx, lru_bx, lru_lambda, ssd_conv_w, ssd_conv_b, ssd_dt_bias, ssd_a_log, ssd_d,
           ssd_norm_w, w_lru_proj, w_ssd_proj, w_out, ln_g, ln_b):
    f = lambda a: np.ascontiguousarray(np.asarray(a, np.float32))
    x_prompt, x_sample = f(x_prompt), f(x_sample)
    if "nc" not in _NC_CACHE:
        _NC_CACHE["nc"] = build_program()
    nc = _NC_CACHE["nc"]
    shared = {
        "w_cond": f(w_cond[0]), "b_condT": _vec_pk(b_cond[0], 24), "b_gate": f(np.asarray(b_cond)[0:1, 2048:3072]),
        "w_in": f(w_in[0]),
        "lcw": f(np.asarray(lru_conv_w)[0].reshape(4, 8, 128).transpose(2, 1, 0)), "lcb": _vec_pk(lru_conv_b[0], 8),
        "lwa": f(lru_wa[0]), "lwx": f(lru_wx[0]),
        "lba": _vec_pk(lru_ba[0], 8), "lbx": _vec_pk(lru_bx[0], 8), "llam": _vec_pk(lru_lambda[0], 8),
        "scw": f(np.asarray(ssd_conv_w)[0].reshape(4, 24, 128).transpose(2, 1, 0)), "scb": _vec_pk(ssd_conv_b[0], 24),
        "dtb_row": f(np.asarray(ssd_dt_bias)[0:1]), "alog_row": f(np.asarray(ssd_a_log)[0:1]), "d_row": f(np.asarray(ssd_d)[0:1]),
        "dtb_col": f(np.asarray(ssd_dt_bias)[0].reshape(32, 1)), "alog_col": f(np.asarray(ssd_a_log)[0].reshape(32, 1)),
        "d_x": _vec_pk(np.repeat(np.asarray(ssd_d, np.float32)[0], 64), 16),
        "normw_row": f(np.asarray(ssd_norm_w)[0:1]), "normwT": _vec_pk(ssd_norm_w[0], 16),
        "w_lp": f(w_lru_proj[0]), "w_sp": f(w_ssd_proj[0]), "w_out": f(w_out[0]),
        "lng_row": f(np.asarray(ln_g)[0:1]), "lnb_row": f(np.asarray(ln_b)[0:1]),
        "c_ident": np.eye(128, dtype=np.float32), "c_tri": np.triu(np.ones((128, 128), np.float32)),
        "c_esel": f((np.arange(128)[:, None] == (np.arange(2048)[None, :] // 64)).astype(np.float32)),
    }
    in_maps = []
    for i in range(NCORES):
        ss = slice(NS * i, NS * (i + 1))
        cT = np.concatenate([np.asarray(c_prompt, np.float32)[i][:, None], np.asarray(c_sample, np.float32)[ss].T], axis=1)
        m = dict(shared)
        m.update({
            "xT": f(x_prompt[i].T), "x_tok": f(x_prompt[i]),
            "xsT": f(x_sample[ss, 0, :].T), "xs_tok": f(x_sample[ss, 0, :]),
            "cT": f(cT),
            "lru_h0T": f(np.asarray(state_lru_h)[0, ss].T),
            "lru_cvT": f(np.asarray(state_lru_conv)[0, ss].transpose(2, 0, 1)),
            "ssd_h0": f(np.asarray(state_ssd_h)[0, ss]),
            "ssd_cvT": f(np.asarray(state_ssd_conv)[0, ss].transpose(2, 0, 1)),
        })
        in_maps.append(m)
    res = run_bass_kernel_spmd(nc, in_maps, core_ids=list(range(NCORES)))
    R = res.results
    y_prompt = np.stack([R[i]["y_p"] for i in range(NCORES)])
    y_sample = np.concatenate([R[i]["y_s"] for i in range(NCORES)])[:, None, :]
    lh_p = np.stack([R[i]["o_lh_p"].T.reshape(1024) for i in range(NCORES)])[None]
    lc_p = np.stack([R[i]["o_lc_p"].transpose(2, 1, 0).reshape(3, 1024) for i in range(NCORES)])[None]
    sh_p = np.stack([R[i]["o_sh_p"].T.reshape(32, 64, 128) for i in range(NCORES)])[None]
    sc_p = np.stack([R[i]["o_sc_p"].transpose(2, 1, 0).reshape(3, 3072) for i in range(NCORES)])[None]
    lh_s = np.concatenate([R[i]["o_lh_s"].transpose(2, 1, 0).reshape(NS, 1024) for i in range(NCORES)])[None]
    lc_s = np.concatenate([R[i]["o_lc_s"].transpose(2, 3, 1, 0).reshape(NS, 3, 1024) for i in range(NCORES)])[None]
    sh_s = np.concatenate([R[i]["o_sh_s"] for i in range(NCORES)])[None]
    sc_s = np.concatenate([R[i]["o_sc_s"].transpose(2, 3, 1, 0).reshape(NS, 3, 3072) for i in range(NCORES)])[None]
    c32 = lambda a: np.ascontiguousarray(a, dtype=np.float32)
    return (c32(y_prompt), c32(y_sample), c32(lh_p), c32(lc_p), c32(sh_p), c32(sc_p),
            c32(lh_s), c32(lc_s), c32(sh_s), c32(sc_s))
```

```python
import numpy as np
import concourse.bass as bass
import concourse.mybir as mybir
from concourse.bass_utils import run_bass_kernel_spmd

F32 = mybir.dt.float32
BF16 = mybir.dt.bfloat16
AF = mybir.ActivationFunctionType
ALU = mybir.AluOpType
AX = mybir.AxisListType

NCORES = 8
D = 1024
T = 2048
NS = 16
TP = T + NS
KC = 8
NCH = 16
L = 128
N_IN = 9248
C_LX, C_LZ, C_SZ, C_SX, C_SB, C_SC, C_DT, C_MA, C_MB = 0, 1024, 2048, 4096, 6144, 6656, 7168, 7200, 8224
ALPHA = 2.0 ** 0.25
LN_EPS = 1e-5
RMS_EPS = 1e-5
SAME_ENGINE_SYNC = True
RAW_ONLY = True
import os as _os
KSTOP = float(_os.environ.get('KSTOP', '99'))


class _StopRec(Exception):
    pass


def _stop_at(n):
    if KSTOP <= n:
        raise _StopRec()


class Buf:
    __slots__ = ("name", "last_w", "readers")

    def __init__(self, name):
        self.name = name
        self.last_w = None
        self.readers = []


class Op:
    __slots__ = ("eng", "fn", "deps", "is_dma", "sem", "semval", "prevval", "signal", "sig", "raw")

    def __init__(self, eng, fn, deps, is_dma):
        self.eng, self.fn, self.deps, self.is_dma = eng, fn, deps, is_dma
        self.sem = None
        self.semval = 0
        self.prevval = 0
        self.signal = False
        self.sig = 0
        self.raw = set()


class Sched:
    ENGS = ("sp", "pool", "act", "dve", "pe")

    def __init__(self, nc, esems, dma_sems):
        self.nc = nc
        self.ops = []
        self.esems = esems
        self.dma_pool = dma_sems
        self.dma_idx = {q: 0 for q in dma_sems}
        self.dma_val = {}
        self.store_ops = []

    def _deps(self, reads, writes):
        deps = set()
        self._last_raw = set()
        for b in list(reads) + list(writes):
            if b.last_w is not None:
                deps.add(b.last_w)
        for b in reads:
            if b.last_w is not None:
                self._last_raw.add(b.last_w)
        for b in writes:
            for r in b.readers:
                deps.add(r)
        return deps

    def _update(self, idx, reads, writes):
        for b in reads:
            b.readers.append(idx)
        for b in writes:
            b.last_w = idx
            b.readers = []

    def op(self, eng, fn, reads=(), writes=()):
        idx = len(self.ops)
        o = Op(eng, fn, self._deps(reads, writes), False)
        o.raw = self._last_raw
        self.ops.append(o)
        self._update(idx, reads, writes)
        return idx

    def dma(self, q, fn, n, reads=(), writes=(), store=False):
        idx = len(self.ops)
        o = Op(q, fn, self._deps(reads, writes), True)
        o.raw = self._last_raw
        pool = self.dma_pool[q]
        sem = pool[self.dma_idx[q] % len(pool)]
        self.dma_idx[q] += 1
        o.sem = sem
        o.prevval = self.dma_val.get(id(sem), 0)
        o.semval = o.prevval + 16 * n
        self.dma_val[id(sem)] = o.semval
        self.ops.append(o)
        self._update(idx, reads, writes)
        if store:
            self.store_ops.append(idx)
        return idx

    def barrier(self, bufs):
        allidx = len(self.ops)
        last = {}
        for i, o in enumerate(self.ops):
            if o.fn is not None:
                last[o.eng] = i
        dmas = [i for i, o in enumerate(self.ops) if o.is_dma]
        deps = set(last.values()) | set(dmas[-64:])
        for e in self.ENGS:
            o = Op(e, None, set(deps), False)
            self.ops.append(o)
        for b in bufs:
            b.last_w = None
            b.readers = []

    def finalize(self):
        for i, o in enumerate(self.ops):
            for d in o.deps:
                p = self.ops[d]
                if p.is_dma or p.fn is None:
                    continue
                same_ok = SAME_ENGINE_SYNC and p.eng != "pe" and (not RAW_ONLY or d in o.raw)
                if p.eng != o.eng or same_ok or o.is_dma:
                    p.signal = True
        cnt = {e: 0 for e in self.ENGS}
        for o in self.ops:
            if o.is_dma:
                continue
            if o.fn is None:
                continue
            if o.signal:
                cnt[o.eng] += 1
                o.sig = cnt[o.eng]

    def emit(self, eng_name, e):
        water = {}
        esems = self.esems

        def wait(sem, val):
            k = id(sem)
            if water.get(k, 0) >= val:
                return
            water[k] = val
            e.wait_ge(sem, val)

        for i, o in enumerate(self.ops):
            if o.eng != eng_name:
                continue
            for d in sorted(o.deps):
                p = self.ops[d]
                if p.is_dma:
                    wait(p.sem, p.semval)
                elif p.fn is None:
                    continue
                elif p.eng != eng_name or (SAME_ENGINE_SYNC and eng_name != "pe" and (not RAW_ONLY or d in o.raw)) or o.is_dma:
                    wait(esems[p.eng], p.sig)
            if o.fn is None:
                continue
            if o.is_dma:
                if o.prevval > 0:
                    wait(o.sem, o.prevval)
                o.fn(e, o.sem)
            else:
                ins = o.fn(e)
                if o.signal:
                    ins.then_inc(esems[eng_name], 1)
        if eng_name == "sp":
            for idx in self.store_ops:
                o = self.ops[idx]
                wait(o.sem, o.semval)


def build_program():
    nc = bass.Bass("TRN2", target_bir_lowering=False)
    din, dout = {}, {}

    def inp(name, shape):
        din[name] = nc.dram_tensor(name, list(shape), F32, kind="ExternalInput").ap()
        return din[name]

    def outp(name, shape):
        dout[name] = nc.dram_tensor(name, list(shape), F32, kind="ExternalOutput").ap()
        return dout[name]

    xT = inp("xT", [D, T]); x_tok = inp("x_tok", [T, D])
    xsT = inp("xsT", [D, NS]); xs_tok = inp("xs_tok", [NS, D])
    cT = inp("cT", [D, 17])
    lru_h0T = inp("lru_h0T", [D, NS]); lru_cvT = inp("lru_cvT", [D, NS, 3])
    ssd_h0 = inp("ssd_h0", [NS, 32, 64, 128]); ssd_cvT = inp("ssd_cvT", [3072, NS, 3])
    w_cond = inp("w_cond", [D, 3072]); b_condT = inp("b_condT", [128, 24]); b_gate = inp("b_gate", [1, 1024])
    w_in = inp("w_in", [D, N_IN])
    lcw = inp("lcw", [128, 8, 4]); lcb = inp("lcb", [128, 8])
    lwa = inp("lwa", [16, 64, 64]); lwx = inp("lwx", [16, 64, 64])
    lba = inp("lba", [128, 8]); lbx = inp("lbx", [128, 8]); llam = inp("llam", [128, 8])
    scw = inp("scw", [128, 24, 4]); scb = inp("scb", [128, 24])
    dtb_row = inp("dtb_row", [1, 32]); alog_row = inp("alog_row", [1, 32]); d_row = inp("d_row", [1, 32])
    dtb_col = inp("dtb_col", [32, 1]); alog_col = inp("alog_col", [32, 1])
    d_x = inp("d_x", [128, 16]); normw_row = inp("normw_row", [1, 2048]); normwT = inp("normwT", [128, 16])
    w_lp = inp("w_lp", [D, D]); w_sp = inp("w_sp", [2048, D]); w_out = inp("w_out", [D, D])
    lng_row = inp("lng_row", [1, D]); lnb_row = inp("lnb_row", [1, D])
    c_ident = inp("c_ident", [128, 128]); c_tri = inp("c_tri", [128, 128]); c_esel = inp("c_esel", [128, 2048])

    y_p = outp("y_p", [T, D]); y_s = outp("y_s", [NS, D])
    o_lh_p = outp("o_lh_p", [128, 8]); o_lc_p = outp("o_lc_p", [128, 8, 3])
    o_sh_p = outp("o_sh_p", [128, 2048]); o_sc_p = outp("o_sc_p", [128, 24, 3])
    o_lh_s = outp("o_lh_s", [128, 8, NS]); o_lc_s = outp("o_lc_s", [128, 8, NS, 3])
    o_sh_s = outp("o_sh_s", [NS, 32, 64, 128]); o_sc_s = outp("o_sc_s", [128, 24, NS, 3])

    SCRN = 23040
    from contextlib import ExitStack
    with ExitStack() as _es:
        _en = _es.enter_context
        hT = _en(nc.sbuf_tensor("hT", [128, KC, TP], BF16))
        yssd = _en(nc.sbuf_tensor("yssd", [128, 16, TP], BF16))
        wt = _en(nc.sbuf_tensor("wt", [128, 8192], BF16))
        scr = _en(nc.sbuf_tensor("scr", [128, SCRN], F32))
        identF = _en(nc.sbuf_tensor("identF", [128, 128], F32))
        identB = _en(nc.sbuf_tensor("identB", [128, 128], BF16))
        triF = _en(nc.sbuf_tensor("triF", [128, 128], F32))
        onesF = _en(nc.sbuf_tensor("onesF", [128, 128], F32))
        smallv = _en(nc.sbuf_tensor("smallv", [128, 512], F32))
        modT = _en(nc.sbuf_tensor("modT", [128, 16, 17], F32))
        pF0, pF1, pF2, pF3, pF4, pF5 = [_en(nc.psum_tensor("pF%d" % i, [128, 512], F32)) for i in range(6)]
        pB0 = _en(nc.psum_tensor("pB0", [128, 1024], BF16))
        pB1 = _en(nc.psum_tensor("pB1", [128, 1024], BF16))
        s_pool, s_act, s_dve, s_pe, s_sp = [_en(nc.semaphore(n)) for n in ("s_pool", "s_act", "s_dve", "s_pe", "s_sp")]
        dq0, dq1, dq2, dq3, dq4, dq5, dq6, dq7 = [_en(nc.semaphore("dq%d" % i)) for i in range(8)]
        dg0, dg1, dg2, dg3, dg4, dg5 = [_en(nc.semaphore("dg%d" % i)) for i in range(6)]
        da0, da1, da2, da3 = [_en(nc.semaphore("da%d" % i)) for i in range(4)]
        block = _en(nc.Block())
        S = Sched(nc, {"sp": s_sp, "pool": s_pool, "act": s_act, "dve": s_dve, "pe": s_pe},
                  {"sp": [dq0, dq1, dq2, dq3, dq4, dq5, dq6, dq7], "pool": [dg0, dg1, dg2, dg3, dg4, dg5], "act": [da0, da1, da2, da3]})
        PF = [pF0, pF1, pF2, pF3, pF4, pF5]
        bPF = [Buf("pF%d" % i) for i in range(6)]
        bPB = [Buf("pB0"), Buf("pB1")]

        class Carver:
            def __init__(self, start=0):
                self.pos = start

            def f32(self, n):
                a = scr[:, self.pos:self.pos + n]
                self.pos += n
                assert self.pos <= SCRN, self.pos
                return a

            def bf16(self, n):
                n32 = (n + 1) // 2
                a = scr[:, self.pos:self.pos + n32].bitcast(BF16)
                self.pos += n32
                assert self.pos <= SCRN, self.pos
                return a

        def v3(ap, a):
            return ap.rearrange("p (a b) -> p a b", a=a)

        sv = [0]

        def svec(n):
            a = smallv[:, sv[0]:sv[0] + n]
            sv[0] += n
            assert sv[0] <= 512
            return a

        lcw_t = svec(32); lcb_t = svec(8); lba_t = svec(8); lbx_t = svec(8); cvec_t = svec(8); cvec2_t = svec(8)
        scw_t = svec(96); scb_t = svec(24); bcond_t = svec(24); dx_t = svec(16); nwT_t = svec(16)
        dtbc_t = svec(1); alogc_t = svec(1)
        bSV = Buf("smallv")
        bCONST = Buf("consts")
        bMOD = Buf("modT")
        bHT = [Buf("hT%d" % k) for k in range(KC)]
        bHTs = Buf("hTs")
        bYS = [Buf("yssd%d" % k) for k in range(16)]
        bYSs = Buf("yssd_s")
        bWTbig = [Buf("wtbig0"), Buf("wtbig1")]
        bWTsm = [Buf("wtsm%d" % i) for i in range(8)]
        wt_big = [wt[:, i * 4096:(i + 1) * 4096].rearrange("p (k c) -> p k c", k=8) for i in range(2)]
        wt_sm = [wt[:, i * 1024:(i + 1) * 1024].rearrange("p (k c) -> p k c", k=8) for i in range(8)]
        allbufs = []

        def mk(name):
            b = Buf(name)
            allbufs.append(b)
            return b

        def dma_in(q, out_ap, in_ap, reads=(), writes=(), n=1):
            def fn(e, sem, out_ap=out_ap, in_ap=in_ap):
                e.dma_start(out=out_ap, in_=in_ap).then_inc(sem, 16)
            return S.dma(q, fn, 1, reads, writes)

        def dma_out(out_ap, in_ap, reads=()):
            def fn(e, sem, out_ap=out_ap, in_ap=in_ap):
                e.dma_start(out=out_ap, in_=in_ap).then_inc(sem, 16)
            return S.dma("sp", fn, 1, reads, (), store=True)

        wbig_i = [0]

        def load_wbig(src, r0, c0, ncols):
            s = wbig_i[0] % 2
            wbig_i[0] += 1
            dst = wt_big[s][:, :, 0:ncols]
            srcv = src[r0:r0 + 1024, c0:c0 + ncols].rearrange("(k p) n -> p k n", p=128)
            dma_in("pool", dst, srcv, (), (bWTbig[s],))
            return wt_big[s], bWTbig[s]

        wsm_i = [0]

        def load_wsm(src, r0, c0):
            s = wsm_i[0] % 8
            wsm_i[0] += 1
            srcv = src[r0:r0 + 1024, c0:c0 + 128].rearrange("(k p) n -> p k n", p=128)
            dma_in("pool", wt_sm[s], srcv, (), (bWTsm[s],))
            return wt_sm[s], bWTsm[s]

        pf_i = [0]

        def next_pf():
            i = pf_i[0] % 6
            pf_i[0] += 1
            return PF[i], bPF[i]

        def mm_group(out_ap, pairs, reads, wbuf):
            n = len(pairs)

            def fn(e, out_ap=out_ap, pairs=pairs):
                ins = None
                for i, (l, r) in enumerate(pairs):
                    ins = e.matmul(out_ap, lhsT=l, rhs=r, start=(i == 0), stop=(i == n - 1))
                return ins
            return S.op("pe", fn, reads, (wbuf,))

        def act(out, in_, func, reads, writes, bias=None, scale=None, accum_out=None):
            kw = {}
            if bias is not None:
                kw["bias"] = bias
            if scale is not None:
                kw["scale"] = scale
            if accum_out is not None:
                kw["accum_out"] = accum_out
            return S.op("act", lambda e, kw=kw: e.activation(out=out, in_=in_, func=func, **kw), reads, writes)

        def tt(eng, out, in0, in1, op, reads, writes):
            return S.op(eng, lambda e: e.tensor_tensor(out=out, in0=in0, in1=in1, op=op), reads, writes)

        def ts(eng, out, in0, s1, s2, op0, op1, reads, writes):
            if s2 is None:
                return S.op(eng, lambda e: e.tensor_scalar(out=out, in0=in0, scalar1=s1, scalar2=None, op0=op0), reads, writes)
            return S.op(eng, lambda e: e.tensor_scalar(out=out, in0=in0, scalar1=s1, scalar2=s2, op0=op0, op1=op1), reads, writes)

        def stt(out, in0, scalar, in1, op0, op1, reads, writes):
            return S.op("dve", lambda e: e.scalar_tensor_tensor(out=out, in0=in0, scalar=scalar, in1=in1, op0=op0, op1=op1), reads, writes)

        def cp(eng, out, in_, reads, writes):
            if eng == "act":
                return act(out, in_, AF.Copy, reads, writes)
            return S.op(eng, lambda e: e.tensor_copy(out=out, in_=in_), reads, writes)

        try:
            S.op("dve", lambda e: e.memset(smallv[:], 0.0), (), (bSV,))
            dma_in("sp", identF[:], c_ident[:, :], (), (bCONST,))
            dma_in("sp", triF[:], c_tri[:, :], (), (bCONST,))
            dma_in("pool", identB[:], c_ident[:, :], (), (bCONST,))
            S.op("pool", lambda e: e.memset(onesF[:], 1.0), (), (bCONST,))
            for (t_, src_) in ((lcw_t, lcw.rearrange("p a b -> p (a b)")), (lcb_t, lcb), (lba_t, lba), (lbx_t, lbx), (cvec_t, llam),
                               (scw_t, scw.rearrange("p a b -> p (a b)")), (scb_t, scb), (bcond_t, b_condT), (dx_t, d_x), (nwT_t, normwT)):
                dma_in("sp", t_, src_, (), (bSV,))
            dma_in("sp", dtbc_t[0:32, :], dtb_col[:, :], (), (bSV,))
            dma_in("sp", alogc_t[0:32, :], alog_col[:, :], (), (bSV,))
            _stop_at(-3)
            act(cvec_t, cvec_t, AF.Exp, (bSV,), (bSV,), scale=-1.0)
            act(cvec_t, cvec_t, AF.Ln, (bSV,), (bSV,), bias=1.0)
            ts("dve", cvec2_t, cvec_t, -16.0, None, ALU.mult, None, (bSV,), (bSV,))
            ts("dve", cvec_t, cvec_t, -8.0, None, ALU.mult, None, (bSV,), (bSV,))
            act(alogc_t, alogc_t, AF.Exp, (bSV,), (bSV,))
            ts("dve", alogc_t, alogc_t, -1.0, None, ALU.mult, None, (bSV,), (bSV,))

            _stop_at(-2)
            P0 = Carver(0)
            cf = P0.f32(8 * 17); cb_ = P0.bf16(8 * 17)
            xin = [P0.f32(T), P0.f32(T)]
            xs_f = P0.f32(8 * NS); hs_f = P0.f32(8 * NS)
            b_cf, b_cb = mk("cf"), mk("cb")
            b_xin = [mk("xin0"), mk("xin1")]
            b_xs = mk("xs_f")
            cf3 = v3(cf, 8); cb3 = v3(cb_, 8)
            dma_in("sp", cf3, cT.rearrange("(k p) n -> p k n", p=128), (), (b_cf,))
            cp("dve", cb_, cf, (b_cf,), (b_cb,))
            modps, bmodps = PF[0], bPF[0]
            for pc in range(4):
                wtile, wb = load_wbig(w_cond, 0, pc * 512, 512)
                for i in range(4):
                    mc = pc * 4 + i
                    pairs = [(wtile[:, k, i * 128:(i + 1) * 128], cb3[:, k, :]) for k in range(KC)]
                    mm_group(modps[:, mc * 17:(mc + 1) * 17], pairs, (wb, b_cb), bmodps)
            tt("dve", modT[:], v3(modps[:, 0:16 * 17], 16), bcond_t[:, 0:16].unsqueeze(2).to_broadcast([128, 16, 17]),
               ALU.add, (bmodps, bSV), (bMOD,))
            ts("dve", modT[:, 8:16, :], modT[:, 8:16, :], 1.0, None, ALU.add, None, (bMOD,), (bMOD,))
            _stop_at(-1)
            xTv = xT.rearrange("(k p) t -> p k t", p=128)
            for k in range(KC):
                s = k % 2
                dma_in("sp", xin[s], xTv[:, k, :], (), (b_xin[s],))
                act(hT[:, k, 0:T], xin[s], AF.Identity, (b_xin[s], bMOD), (bHT[k],), bias=modT[:, k, 0:1], scale=modT[:, 8 + k, 0:1])
            _stop_at(-0.5)
            dma_in("sp", v3(xs_f, 8), xsT.rearrange("(k p) n -> p k n", p=128), (), (b_xs,))
            _stop_at(-0.4)
            tt("dve", v3(hs_f, 8), v3(xs_f, 8), modT[:, 8:16, 1:17], ALU.mult, (b_xs, bMOD), (b_xs,))
            _stop_at(-0.3)
            tt("dve", v3(hs_f, 8), v3(hs_f, 8), modT[:, 0:8, 1:17], ALU.add, (b_xs, bMOD), (b_xs,))
            _stop_at(-0.2)
            cp("act", hT[:, :, T:TP], v3(hs_f, 8), (b_xs,), (bHTs,))
            bHTall = bHT + [bHTs]
            _stop_at(0)

            S.barrier(allbufs)
            P1 = Carver(0)
            xTg = v3(P1.bf16(4 * T), 4); BTg = P1.bf16(T); CTg = P1.bf16(T)
            xbuf = P1.bf16(T + 8); dg = v3(P1.bf16(512), 4); tail3 = P1.f32(4)
            wsmB = [v3(P1.bf16(1024), 8), v3(P1.bf16(1024), 8)]; wsmC = [v3(P1.bf16(1024), 8), v3(P1.bf16(1024), 8)]
            dt_t = P1.f32(512); adt_t = P1.f32(512); acs_t = P1.f32(512); nacs_t = P1.f32(512)
            e_t = P1.f32(512); w1_t = P1.f32(512); cdec_t = P1.f32(512); tmp_t = P1.f32(512)
            negI = P1.bf16(128); Lmask = P1.bf16(512)
            persist_start = P1.pos
            nw_b = P1.f32(512); D_b = P1.f32(32); dtb_b = P1.f32(32); nA_b = P1.f32(32)
            xsT_all = v3(P1.f32(16 * NS), 16)
            BsT = v3(P1.f32(4 * NS), 4); CsT = v3(P1.f32(4 * NS), 4)
            szs = v3(P1.f32(16 * NS), 16)
            xsb = v3(P1.f32(NS * 4), NS); us_t = P1.f32(NS)
            dtT_s = P1.f32(NS); adtT_s = P1.f32(NS); dAT_s = P1.f32(NS)
            persist_end = P1.pos
            x_tok_d = [P1.bf16(512), P1.bf16(512)]; xs_d = [None, None]; xsc_d = [P1.bf16(512), P1.bf16(512)]
            B_tok_d = [P1.bf16(128), P1.bf16(128)]; MT_d = [v3(P1.bf16(1024), 8), v3(P1.bf16(1024), 8)]
            sz_d = [P1.f32(512), P1.f32(512)]
            CBm = P1.f32(128); _ex = P1.f32(512); ex_t = [_ex, P1.f32(512)]
            wdt = v3(_ex.bitcast(BF16), 8)
            t_a_d = [P1.f32(512), P1.f32(512)]; t_b = tmp_t; junk = P1.bf16(512); mhalf = P1.f32(1)
            yn_t = P1.bf16(512); hst = P1.f32(512); hst_bf = P1.bf16(512); st8_d = [P1.f32(8), P1.f32(8)]
            print("P1 end", P1.pos, "of", SCRN)
            b_xTg = [mk("xTg%d" % i) for i in range(4)]; b_BT = mk("BTg"); b_CT = mk("CTg")
            b_wsmB = [mk("wsmB0"), mk("wsmB1")]; b_wsmC = [mk("wsmC0"), mk("wsmC1")]
            b_xbuf = mk("xbuf"); b_xbufq = [mk("xbufq%d" % q) for q in range(4)]; b_dg = mk("dg"); b_tail = mk("tail3"); b_dtf = mk("dtfam"); b_bc = mk("bcasts")
            b_xsT = mk("xsT_all"); b_BsT = mk("BsT"); b_CsT = mk("CsT"); b_szs = mk("szs"); b_xsb = mk("xsb"); b_us = mk("us")
            b_msk = mk("maskconsts")
            b_dts = mk("dts"); b_wdt = mk("wdt"); b_wz = mk("wzs")
            bd_xtok = [mk("x_tok0"), mk("x_tok1")]; bd_xs = [mk("xs0"), mk("xs1")]; bd_xsc = [mk("xsc0"), mk("xsc1")]
            bd_Btok = [mk("B_tok0"), mk("B_tok1")]; bd_MT = [mk("MT0"), mk("MT1")]; bd_sz = [mk("sz0"), mk("sz1")]
            b_CBm = mk("CBm"); b_ex = [mk("ex0"), mk("ex1")]; bd_ta = [mk("t_a0"), mk("t_a1")]; b_tb = mk("t_b"); b_junk = mk("junk")
            b_yn = mk("yn"); b_hst = mk("hst"); b_hbf = mk("hst_bf"); bd_st8 = [mk("st8a"), mk("st8b")]
            S.op("dve", lambda e: e.memset(xbuf[:, 0:3], 0.0), (), (b_xbuf,))
            S.op("dve", lambda e: e.memset(mhalf, -0.5), (), (b_bc,))
            ts("pool", negI, identF[:], -32768.0, None, ALU.mult, None, (bCONST,), (b_msk,))
            ts("dve", v3(Lmask, 4), triF[:].unsqueeze(1).to_broadcast([128, 4, 128]), -1.0, 1.0, ALU.mult, ALU.add, (bCONST,), (b_msk,))
            dma_in("sp", dtb_b, dtb_row.partition_broadcast(128), (), (b_bc,))
            dma_in("sp", nA_b, alog_row.partition_broadcast(128), (), (b_bc,))
            dma_in("sp", D_b, d_row.partition_broadcast(128), (), (b_bc,))
            act(nA_b, nA_b, AF.Exp, (b_bc,), (b_bc,))
            ts("dve", nA_b, nA_b, -1.0, None, ALU.mult, None, (b_bc,), (b_bc,))
            S.op("pool", lambda e: e.memset(wdt, 0.0), (), (b_wdt,))
            dma_in("pool", wdt[:, :, 0:32], w_in[:, C_DT:C_DT + 32].rearrange("(k p) n -> p k n", p=128), (), (b_wdt,))
            dtps, bdtps = PF[1], bPF[1]
            for c in range(NCH):
                pairs = [(hT[:, k, c * L:(c + 1) * L], wdt[:, k, 0:32]) for k in range(KC)]
                mm_group(dtps[:, c * 32:(c + 1) * 32], pairs, tuple(bHT) + (b_wdt,), bdtps)
            dsps, bdsps = PF[2], bPF[2]
            mm_group(dsps[:, 0:NS], [(wdt[:, k, :], hT[:, k, T:TP]) for k in range(KC)], (bHTs, b_wdt), bdsps)
            act(dtT_s, dsps[:, 0:NS], AF.Exp, (bdsps, bSV), (b_dts,), bias=dtbc_t)
            act(dtT_s, dtT_s, AF.Ln, (b_dts,), (b_dts,), bias=1.0)
            ts("dve", adtT_s, dtT_s, alogc_t, None, ALU.mult, None, (b_dts, bSV), (b_dts,))
            act(dAT_s, adtT_s, AF.Exp, (b_dts,), (b_dts,))
            tt("dve", v3(tmp_t, 16), v3(dtps[:, :], 16), dtb_b.unsqueeze(1).to_broadcast([128, 16, 32]), ALU.add, (bdtps, b_bc), (b_dtf,))
            act(tmp_t, tmp_t, AF.Exp, (b_dtf,), (b_dtf,))
            act(dt_t, tmp_t, AF.Ln, (b_dtf,), (b_dtf,), bias=1.0)
            tt("dve", v3(adt_t, 16), v3(dt_t, 16), nA_b.unsqueeze(1).to_broadcast([128, 16, 32]), ALU.mult, (b_dtf, b_bc), (b_dtf,))
            acsps, bacsps = PF[3], bPF[3]
            totps, btotps = PF[4], bPF[4]
            mm_group(acsps[:, :], [(triF[:], adt_t)], (b_dtf, bCONST), bacsps)
            mm_group(totps[:, :], [(onesF[:], adt_t)], (b_dtf, bCONST), btotps)
            cp("act", acs_t, acsps[:, :], (bacsps,), (b_dtf,))
            act(e_t, acs_t, AF.Exp, (b_dtf,), (b_dtf,))
            tt("dve", tmp_t, totps[:, :], acs_t, ALU.subtract, (btotps, b_dtf), (b_dtf,))
            act(tmp_t, tmp_t, AF.Exp, (b_dtf,), (b_dtf,))
            tt("dve", w1_t, tmp_t, dt_t, ALU.mult, (b_dtf,), (b_dtf,))
            act(tmp_t, dt_t, AF.Ln, (b_dtf,), (b_dtf,))
            tt("dve", nacs_t, tmp_t, acs_t, ALU.subtract, (b_dtf,), (b_dtf,))
            act(cdec_t, totps[:, :], AF.Exp, (btotps,), (b_dtf,))
            _stop_at(1)

            def conv_chunk(wtile, wb, coff, cw4, cbias, out_bf, b_out, tail_dst, s_state_src, s_out, b_s_out, s_state_dst):
                tt("pool", dg, identF[:].unsqueeze(1).to_broadcast([128, 4, 128]), cw4.unsqueeze(2).to_broadcast([128, 4, 128]), ALU.mult,
                   (bCONST, bSV), (b_dg,))
                def proj_q(q):
                    ps, bps = next_pf()
                    pairs = [(wtile[:, k, coff:coff + 128], hT[:, k, q * 512:(q + 1) * 512]) for k in range(KC)]
                    mm_group(ps[:, :], pairs, tuple(bHT) + (wb,), bps)
                    cp("act", xbuf[:, 3 + q * 512:3 + (q + 1) * 512], ps[:, :], (bps,), (b_xbufq[q],))
                    if q == 3:
                        cp("act", tail3[:, 0:3], ps[:, 509:512], (bps,), (b_tail,))
                        dma_out(tail_dst, tail3[:, 0:3], (b_tail,))

                def conv_q(q):
                    ps2, bps2 = next_pf()
                    rd = (b_dg, b_xbufq[q]) + ((b_xbufq[q - 1],) if q > 0 else (b_xbuf,))
                    mm_group(ps2[:, :], [(dg[:, k, :], xbuf[:, q * 512 + k:q * 512 + k + 512]) for k in range(4)], rd, bps2)
                    act(out_bf[:, q * 512:(q + 1) * 512], ps2[:, :], AF.Silu, (bps2, bSV), (b_out,), bias=cbias)
                proj_q(0); proj_q(1); conv_q(0); proj_q(2); conv_q(1); proj_q(3); conv_q(2); conv_q(3)
                ps, bps = next_pf()
                mm_group(ps[:, 0:NS], [(wtile[:, k, coff:coff + 128], hT[:, k, T:TP]) for k in range(KC)], (bHTs, wb), bps)
                dma_in("sp", xsb[:, :, 0:3], s_state_src, (), (b_xsb,))
                cp("act", xsb[:, :, 3], ps[:, 0:NS], (bps,), (b_xsb,))
                dma_out(s_state_dst, xsb[:, :, 1:4], (b_xsb,))
                ts("dve", us_t, xsb[:, :, 0], cw4[:, 0:1], cbias, ALU.mult, ALU.add, (b_xsb, bSV), (b_us,))
                for k in range(1, 4):
                    stt(us_t, xsb[:, :, k], cw4[:, k:k + 1], us_t, ALU.mult, ALU.add, (b_xsb, bSV, b_us), (b_us,))
                act(s_out, us_t, AF.Silu, (b_us,), (b_s_out,))

            def load_big_slot(slot, src, c0, ncols=512):
                srcv = src[0:1024, c0:c0 + ncols].rearrange("(k p) n -> p k n", p=128)
                dma_in("pool", wt_big[slot][:, :, 0:ncols], srcv, (), (bWTbig[slot],))
                return wt_big[slot], bWTbig[slot]

            def load_bc(g):
                par = g % 2
                for (tile_, buf_, c0) in ((wsmB[par], b_wsmB[par], C_SB + g * 128), (wsmC[par], b_wsmC[par], C_SC + g * 128)):
                    dma_in("pool", tile_, w_in[0:1024, c0:c0 + 128].rearrange("(k p) n -> p k n", p=128), (), (buf_,))

            wX, wXb_ = load_big_slot(0, w_in, C_SX)
            load_bc(0)
            for g in range(4):
                wzs, b_wz = load_big_slot(1, w_in, C_SZ + g * 512)
                if g + 1 < 4:
                    load_bc(g + 1)
                par = g % 2
                ch = 16 + g
                conv_chunk(wsmB[par], b_wsmB[par], 0, scw_t[:, ch * 4:(ch + 1) * 4], scb_t[:, ch:ch + 1], BTg, b_BT,
                           o_sc_p[:, ch, :], ssd_cvT[ch * 128:(ch + 1) * 128, :, :], BsT[:, g, :], b_BsT, o_sc_s[:, ch, :, :])
                ch = 20 + g
                conv_chunk(wsmC[par], b_wsmC[par], 0, scw_t[:, ch * 4:(ch + 1) * 4], scb_t[:, ch:ch + 1], CTg, b_CT,
                           o_sc_p[:, ch, :], ssd_cvT[ch * 128:(ch + 1) * 128, :, :], CsT[:, g, :], b_CsT, o_sc_s[:, ch, :, :])
                wtile, wb = wt_big[0], bWTbig[0]
                for i in range(4):
                    ch = g * 4 + i
                    conv_chunk(wtile, wb, i * 128, scw_t[:, ch * 4:(ch + 1) * 4], scb_t[:, ch:ch + 1], xTg[:, i, :], b_xTg[i],
                               o_sc_p[:, ch, :], ssd_cvT[ch * 128:(ch + 1) * 128, :, :], xsT_all[:, ch, :], b_xsT, o_sc_s[:, ch, :, :])
                if g + 1 < 4:
                    load_big_slot(0, w_in, C_SX + (g + 1) * 512)
                dma_in("sp", nw_b, normw_row[:, g * 512:(g + 1) * 512].partition_broadcast(128), (), (b_bc,))
                for i in range(4):
                    ps, bps = next_pf()
                    mm_group(ps[:, 0:NS], [(wzs[:, k, i * 128:(i + 1) * 128], hT[:, k, T:TP]) for k in range(KC)], (bHTs, b_wz), bps)
                    act(szs[:, g * 4 + i, :], ps[:, 0:NS], AF.Silu, (bps,), (b_szs,))
                def bufs(c):
                    par = c % 2
                    return (x_tok_d[par], xs_d[par], xsc_d[par], B_tok_d[par], MT_d[par], sz_d[par],
                            bd_xtok[par], bd_xs[par], bd_xsc[par], bd_Btok[par], bd_MT[par], bd_sz[par],
                            t_a_d[par], bd_ta[par], st8_d[par], bd_st8[par])

                def S4(c, g=g):
                    (x_tok_t, xs_t, xsc_t, B_tok, MT, sz_t, b_xtok, b_xs_, b_xsc, b_Btok, b_MT, b_sz, t_a, b_ta, st8, b_st8) = bufs(c)
                    cs = slice(c * L, (c + 1) * L)

                    def fnY(e):
                        ins = None
                        for h8 in range(8):
                            ins = e.matmul(PF[3][:, h8 * 64:(h8 + 1) * 64], lhsT=MT[:, h8, :], rhs=x_tok_t[:, h8 * 64:(h8 + 1) * 64],
                                           start=True, stop=True)
                        return ins
                    if c > 0:
                        mm_group(PF[4][:, :], [(CTg[:, cs], hst_bf)], (b_CT, b_hbf), bPF[4])
                    mm_group(PF[5][:, :], [(B_tok, xsc_t)], (b_Btok, b_xsc), bPF[5])
                    S.op("pe", fnY, (b_MT, b_xtok), (bPF[3],))

                def Dskip(c, g=g):
                    (x_tok_t, xs_t, xsc_t, B_tok, MT, sz_t, b_xtok, b_xs_, b_xsc, b_Btok, b_MT, b_sz, t_a, b_ta, st8, b_st8) = bufs(c)
                    tt("pool", v3(t_b, 8), v3(x_tok_t, 8), D_b[:, g * 8:(g + 1) * 8].unsqueeze(2).to_broadcast([128, 8, 64]), ALU.mult,
                       (b_xtok, b_bc), (b_tb,))

                def S7(c, g=g):
                    (x_tok_t, xs_t, xsc_t, B_tok, MT, sz_t, b_xtok, b_xs_, b_xsc, b_Btok, b_MT, b_sz, t_a, b_ta, st8, b_st8) = bufs(c)
                    stt(yn_t, t_a, st8[:, 3:4], nw_b, ALU.mult, ALU.mult, (b_ta, b_st8, b_bc), (b_yn,))

                def S5a(c, g=g):
                    hb = c * 32 + g * 8
                    if c > 0:
                        tt("dve", v3(hst, 8), v3(hst, 8), cdec_t[:, hb:hb + 8].unsqueeze(2).to_broadcast([128, 8, 64]), ALU.mult,
                           (b_hst, b_dtf), (b_hst,))
                        tt("dve", hst, hst, PF[5][:, :], ALU.add, (b_hst, bPF[5]), (b_hst,))
                    else:
                        cp("dve", hst, PF[5][:, :], (bPF[5],), (b_hst,))
                    if c == NCH - 1:
                        dma_out(o_sh_p[:, g * 512:(g + 1) * 512], hst, (b_hst,))

                def S5b(c, g=g):
                    (x_tok_t, xs_t, xsc_t, B_tok, MT, sz_t, b_xtok, b_xs_, b_xsc, b_Btok, b_MT, b_sz, t_a, b_ta, st8, b_st8) = bufs(c)
                    hb = c * 32 + g * 8
                    if c > 0:
                        tt("dve", v3(t_a, 8), v3(PF[4][:, :], 8), e_t[:, hb:hb + 8].unsqueeze(2).to_broadcast([128, 8, 64]), ALU.mult,
                           (bPF[4], b_dtf), (b_ta,))
                        tt("dve", t_a, t_a, PF[3][:, :], ALU.add, (b_ta, bPF[3]), (b_ta,))
                    else:
                        cp("dve", t_a, PF[3][:, :], (bPF[3],), (b_ta,))
                    tt("dve", t_a, t_a, t_b, ALU.add, (b_ta, b_tb), (b_ta,))
                    tt("dve", t_a, t_a, sz_t, ALU.mult, (b_ta, b_sz), (b_ta,))

                def S1(c, g=g):
                    cs = slice(c * L, (c + 1) * L)

                    def fnT(e, cs=cs):
                        ins = None
                        for i in range(4):
                            ins = e.transpose(pB0[:, i * 128:(i + 1) * 128], xTg[:, i, cs], identB[:])
                        ins = e.transpose(pB0[:, 512:640], BTg[:, cs], identB[:])
                        return ins
                    S.op("pe", fnT, tuple(b_xTg) + (b_BT, bCONST), (bPB[0],))

                def S1b(c, g=g):
                    cs = slice(c * L, (c + 1) * L)
                    mm_group(PF[0][:, 0:128], [(BTg[:, cs], CTg[:, cs])], (b_BT, b_CT), bPF[0])

                def S1c(c, g=g):
                    cs = slice(c * L, (c + 1) * L)
                    mm_group(PF[0][:, :], [(hT[:, k, cs], wzs[:, k, :]) for k in range(KC)], tuple(bHT) + (b_wz,), bPF[0])

                def S1d(c, half, g=g):
                    hb = c * 32 + g * 8
                    accp, baccp = PF[1 + half], bPF[1 + half]

                    def fnA(e, half=half, hb=hb, accp=accp):
                        ins = e.matmul(accp[:, :], lhsT=negI, rhs=Lmask, start=True, stop=False)
                        for hh in range(4):
                            col = hb + half * 4 + hh
                            ins = e.matmul(accp[:, hh * 128:(hh + 1) * 128], lhsT=adt_t[:, col:col + 1].to_broadcast([128, 128]),
                                           rhs=triF[:], start=False, stop=(hh == 3))
                        return ins
                    S.op("pe", fnA, (b_dtf, bCONST, b_msk), (baccp,))

                def copies(c, g=g):
                    (x_tok_t, xs_t, xsc_t, B_tok, MT, sz_t, b_xtok, b_xs_, b_xsc, b_Btok, b_MT, b_sz, t_a, b_ta, st8, b_st8) = bufs(c)
                    cp("act", x_tok_t, pB0[:, 0:512], (bPB[0],), (b_xtok,))
                    cp("act", B_tok, pB0[:, 512:640], (bPB[0],), (b_Btok,))

                def poolx(c, g=g):
                    (x_tok_t, xs_t, xsc_t, B_tok, MT, sz_t, b_xtok, b_xs_, b_xsc, b_Btok, b_MT, b_sz, t_a, b_ta, st8, b_st8) = bufs(c)
                    hb = c * 32 + g * 8
                    tt("pool", v3(xsc_t, 8), v3(x_tok_t, 8), w1_t[:, hb:hb + 8].unsqueeze(2).to_broadcast([128, 8, 64]), ALU.mult,
                       (b_xtok, b_dtf), (b_xsc,))

                def tanhz(c, g=g):
                    (x_tok_t, xs_t, xsc_t, B_tok, MT, sz_t, b_xtok, b_xs_, b_xsc, b_Btok, b_MT, b_sz, t_a, b_ta, st8, b_st8) = bufs(c)
                    act(sz_t, PF[0][:, :], AF.Tanh, (bPF[0],), (b_sz,), scale=0.5)

                def exps(c, half, g=g):
                    hb = c * 32 + g * 8
                    accp, baccp = PF[1 + half], bPF[1 + half]
                    for hh in range(4):
                        col = hb + half * 4 + hh
                        act(ex_t[half][:, hh * 128:(hh + 1) * 128], accp[:, hh * 128:(hh + 1) * 128], AF.Exp,
                            (baccp, b_dtf), (b_ex[half],), bias=nacs_t[:, col:col + 1])

                def S3a(c, g=g):
                    tt("dve", CBm, PF[0][:, 0:128], triF[:], ALU.mult, (bPF[0], bCONST), (b_CBm,))

                def S3b(c, g=g):
                    (x_tok_t, xs_t, xsc_t, B_tok, MT, sz_t, b_xtok, b_xs_, b_xsc, b_Btok, b_MT, b_sz, t_a, b_ta, st8, b_st8) = bufs(c)
                    stt(sz_t, sz_t, 1.0, PF[0][:, :], ALU.add, ALU.mult, (b_sz, bPF[0]), (b_sz,))

                def S3c(c, half, g=g):
                    (x_tok_t, xs_t, xsc_t, B_tok, MT, sz_t, b_xtok, b_xs_, b_xsc, b_Btok, b_MT, b_sz, t_a, b_ta, st8, b_st8) = bufs(c)
                    stt(MT[:, half * 4:(half + 1) * 4, :], v3(ex_t[half], 4), 1.0e30, CBm.unsqueeze(1).to_broadcast([128, 4, 128]),
                        ALU.min, ALU.mult, (b_ex[half], b_CBm), (b_MT,))

                def S6(c, g=g):
                    (x_tok_t, xs_t, xsc_t, B_tok, MT, sz_t, b_xtok, b_xs_, b_xsc, b_Btok, b_MT, b_sz, t_a, b_ta, st8, b_st8) = bufs(c)
                    act(junk, t_a, AF.Square, (b_ta,), (b_junk, b_st8), accum_out=st8[:, 0:1])
                    ts("pool", st8[:, 1:2], st8[:, 0:1], 1.0 / 512.0, 4.0 * RMS_EPS, ALU.mult, ALU.add, (b_st8,), (b_st8,))
                    tt("pool", st8[:, 3:4], st8[:, 1:2], mhalf[:, 0:1], ALU.pow, (b_st8, b_bc), (b_st8,))

                def S8(c, g=g):
                    cs = slice(c * L, (c + 1) * L)

                    def fnT2(e):
                        ins = None
                        for i in range(4):
                            ins = e.transpose(pB1[:, i * 128:(i + 1) * 128], yn_t[:, i * 128:(i + 1) * 128], identB[:])
                        return ins
                    S.op("pe", fnT2, (b_yn, bCONST), (bPB[1],))
                    cp("act", yssd[:, g * 4:(g + 1) * 4, cs], v3(pB1[:, 0:512], 4), (bPB[1],), tuple(bYS[g * 4:(g + 1) * 4]))

                for s_ in range(-1, NCH + 1):
                    cur, nxt, prv = s_, s_ + 1, s_ - 1
                    hc = 0 <= cur < NCH
                    hn = 0 <= nxt < NCH
                    hp = 0 <= prv < NCH
                    if hc:
                        S4(cur)
                        Dskip(cur)
                    if hp:
                        S7(prv)
                    if hn:
                        S1(nxt)
                        S1d(nxt, 0)
                        S1d(nxt, 1)
                        S1b(nxt)
                        S3a(nxt)
                        copies(nxt)
                        poolx(nxt)
                        exps(nxt, 0)
                    if hc:
                        S5a(cur)
                    if hn:
                        S1c(nxt)
                        exps(nxt, 1)
                    if hc and cur < NCH - 1:
                        cp("act", hst_bf, hst, (b_hst,), (b_hbf,))
                    if hc:
                        S5b(cur)
                    if hn:
                        S3c(nxt, 0)
                        tanhz(nxt)
                        S3c(nxt, 1)
                        S3b(nxt)
                    if hc:
                        S6(cur)
                    if hp:
                        S8(prv)

            _stop_at(2)
            S.barrier(allbufs)
            P1s = Carver(0)
            H0 = [v3(P1s.f32(2048), 16), v3(P1s.f32(2048), 16)]
            HN = [v3(P1s.f32(2048), 16), v3(P1s.f32(2048), 16)]
            T1 = v3(P1s.f32(2048), 16); T2 = v3(P1s.f32(2048), 16)
            assert P1s.pos <= persist_start, (P1s.pos, persist_start)
            P1s2 = Carver(persist_end)
            Bb = v3(P1s2.f32(512), 4); Cb = v3(P1s2.f32(512), 4)
            T3 = v3(P1s2.f32(2048), 16)
            esel = P1s2.f32(2048)
            dAx = v3(P1s2.f32(256), 16); xdt = v3(P1s2.f32(256), 16); ysT = v3(P1s2.f32(256), 16)
            gys = v3(P1s2.f32(256), 16); sqs = v3(P1s2.f32(256), 16); rs_t = v3(P1s2.f32(64), 4)
            b_H0 = [mk("H0a"), mk("H0b")]; b_HN = [mk("HNa"), mk("HNb")]; b_T1 = mk("T1"); b_T2 = mk("T2"); b_T3 = mk("T3")
            b_Bb = mk("Bb"); b_Cb = mk("Cb"); b_esel = mk("esel"); b_dAx = mk("dAx"); b_xdt = mk("xdt"); b_ysT = mk("ysT")
            b_gys = mk("gys"); b_sqs = mk("sqs"); b_rs = mk("rs")
            dma_in("sp", esel, c_esel[:, :], (), (b_esel,))
            exps, bexps = PF[0], bPF[0]
            _stop_at(2.05)

            def fnE(e):
                ins = None
                for j in range(16):
                    ins = e.matmul(exps[:, j * 16:(j + 1) * 16], lhsT=esel[:, j * 128:(j + 1) * 128], rhs=dAT_s, start=True, stop=True)
                for j in range(16):
                    ins = e.matmul(exps[:, 256 + j * 16:256 + (j + 1) * 16], lhsT=esel[:, j * 128:(j + 1) * 128], rhs=dtT_s,
                                   start=True, stop=True)
                return ins
            S.op("pe", fnE, (b_esel, b_dts), (bexps,))
            _stop_at(2.07)
            cp("act", dAx, v3(exps[:, 0:256], 16), (bexps,), (b_dAx,))
            _stop_at(2.08)
            cp("act", xdt, v3(exps[:, 256:512], 16), (bexps,), (b_xdt,))
            tt("dve", xdt, xdt, xsT_all, ALU.mult, (b_xdt, b_xsT), (b_xdt,))
            _stop_at(2.1)
            h0v = ssd_h0.rearrange("s (j e) p n -> s (e p) j n", e=2)
            ohv = o_sh_s.rearrange("s (j e) p n -> s (e p) j n", e=2)
            for s in range(NS):
                sl = s % 2
                dma_in("act", H0[sl], h0v[s], (), (b_H0[sl],))
                bps_, bbps_ = PF[1], bPF[1]
                cps_, bcps_ = PF[2], bPF[2]

                def fnB(e, s=s):
                    ins = None
                    for g in range(4):
                        ins = e.matmul(PF[1][:, g * 128:(g + 1) * 128], lhsT=BsT[:, g, s:s + 1].to_broadcast([128, 128]), rhs=identF[:],
                                       start=True, stop=True)
                    return ins

                def fnC(e, s=s):
                    ins = None
                    for g in range(4):
                        ins = e.matmul(PF[2][:, g * 128:(g + 1) * 128], lhsT=CsT[:, g, s:s + 1].to_broadcast([128, 128]), rhs=identF[:],
                                       start=True, stop=True)
                    return ins
                S.op("pe", fnB, (b_BsT, bCONST), (bbps_,))
                S.op("pe", fnC, (b_CsT, bCONST), (bcps_,))
                cp("act", Bb, v3(PF[1][:, :], 4), (bbps_,), (b_Bb,))
                cp("act", Cb, v3(PF[2][:, :], 4), (bcps_,), (b_Cb,))
                _stop_at(2.2)
                tt("dve", T1, H0[sl], dAx[:, :, s:s + 1].to_broadcast([128, 16, 128]), ALU.mult, (b_H0[sl], b_dAx), (b_T1,))
                _stop_at(2.3)
                xdt4 = xdt[:, :, s:s + 1].rearrange("p (g j) o -> p g j o", g=4).to_broadcast([128, 4, 4, 128])
                Bb4 = Bb.unsqueeze(2).to_broadcast([128, 4, 4, 128])
                Cb4 = Cb.unsqueeze(2).to_broadcast([128, 4, 4, 128])
                tt("pool", T2.rearrange("p (g j) n -> p g j n", g=4), xdt4, Bb4, ALU.mult, (b_xdt, b_Bb), (b_T2,))
                _stop_at(2.4)
                tt("dve", HN[sl], T1, T2, ALU.add, (b_T1, b_T2), (b_HN[sl],))
                dma_out(ohv[s], HN[sl], (b_HN[sl],))
                tt("dve", T3.rearrange("p (g j) n -> p g j n", g=4), HN[sl].rearrange("p (g j) n -> p g j n", g=4), Cb4, ALU.mult,
                   (b_HN[sl], b_Cb), (b_T3,))
                _stop_at(2.5)
                S.op("dve", lambda e, s=s: e.tensor_reduce(out=ysT[:, :, s], in_=T3, axis=AX.X, op=ALU.add), (b_T3,), (b_ysT,))
                _stop_at(2.6)
            _stop_at(2.7)
            tt("dve", gys, xsT_all, dx_t.unsqueeze(2).to_broadcast([128, 16, NS]), ALU.mult, (b_xsT, bSV), (b_gys,))
            tt("dve", ysT, ysT, gys, ALU.add, (b_ysT, b_gys), (b_ysT,))
            tt("dve", gys, ysT, szs, ALU.mult, (b_ysT, b_szs), (b_gys,))
            tt("dve", sqs, gys, gys, ALU.mult, (b_gys,), (b_sqs,))
            ssp, bssp = PF[3], bPF[3]

            def fnS(e):
                ins = None
                for g in range(4):
                    for i in range(4):
                        ins = e.matmul(ssp[:, g * 16:(g + 1) * 16], lhsT=onesF[:], rhs=sqs[:, g * 4 + i, :], start=(i == 0), stop=(i == 3))
                return ins
            S.op("pe", fnS, (b_sqs, bCONST), (bssp,))
            ts("dve", rs_t, v3(ssp[:, 0:64], 4), 1.0 / 512.0, RMS_EPS, ALU.mult, ALU.add, (bssp,), (b_rs,))
            act(rs_t, rs_t, AF.Sqrt, (b_rs,), (b_rs,))
            S.op("dve", lambda e: e.reciprocal(out=rs_t, in_=rs_t), (b_rs,), (b_rs,))
            tt("dve", gys.rearrange("p (g j) s -> p g j s", g=4), gys.rearrange("p (g j) s -> p g j s", g=4),
               rs_t.unsqueeze(2).to_broadcast([128, 4, 4, NS]), ALU.mult, (b_gys, b_rs), (b_gys,))
            tt("dve", gys, gys, nwT_t.unsqueeze(2).to_broadcast([128, 16, NS]), ALU.mult, (b_gys, bSV), (b_gys,))
            cp("act", yssd[:, :, T:TP], gys, (b_gys,), (bYSs,))

            _stop_at(3)
            S.barrier(allbufs)
            P2 = Carver(0)
            ylru = v3(P2.bf16(8 * TP), 8)
            xbuf2 = P2.bf16(T + 8); dg2 = v3(P2.bf16(512), 4); tail2 = P2.f32(4)
            u2 = P2.f32(T); ubf = P2.bf16(T)
            r_t = P2.f32(T); i_t = P2.f32(T); a_t = P2.f32(T); m_t = P2.f32(T)
            lxs = v3(P2.f32(NS * 4), NS); lus = P2.f32(NS); lusb = P2.bf16(NS); lr = P2.f32(NS); li = P2.f32(NS); la = P2.f32(NS)
            lm = P2.f32(NS); lh0 = v3(P2.f32(8 * NS), 8); lhn = v3(P2.f32(8 * NS), 8); lhp = P2.f32(8)
            hba = P2.f32(8); hbx = P2.f32(8); hcv = P2.f32(8); q25 = P2.f32(1)
            wab = v3(P2.bf16(8 * 128), 8); wxb = v3(P2.bf16(8 * 128), 8)
            bYL = [mk("ylru%d" % k) for k in range(8)]; bYLs = mk("ylru_s")
            b_x2 = mk("xbuf2"); b_x2q = [mk("xbuf2q%d" % q) for q in range(4)]; b_dg2 = mk("dg2"); b_tail2 = mk("tail2")
            b_u2 = mk("u2"); b_ubf = mk("ubf"); b_r = mk("r"); b_i = mk("i"); b_a = mk("a"); b_m = mk("m")
            b_lxs = mk("lxs"); b_lus = mk("lus"); b_lsm = mk("lsm"); b_lh0 = mk("lh0"); b_lhn = mk("lhn"); b_lhp = mk("lhp"); b_wab = mk("wab")
            b_hv = mk("halfvecs")
            S.op("dve", lambda e: e.memset(xbuf2[:, 0:3], 0.0), (), (b_x2,))
            ts("dve", hba, lba_t, 0.5, None, ALU.mult, None, (bSV,), (b_hv,))
            ts("dve", hbx, lbx_t, 0.5, None, ALU.mult, None, (bSV,), (b_hv,))
            ts("dve", hcv, cvec_t, 0.5, None, ALU.mult, None, (bSV,), (b_hv,))
            S.op("dve", lambda e: e.memset(q25, 0.25), (), (b_hv,))
            S.op("pool", lambda e: e.memset(wab, 0.0), (), (b_wab,))
            S.op("pool", lambda e: e.memset(wxb, 0.0), (), (b_wab,))
            for (dst_, src_) in ((wab, lwa), (wxb, lwx)):
                sv_ = src_.rearrange("(j e) k m -> e k j m", e=2)
                for e_ in range(2):
                    dma_in("pool", dst_[e_ * 64:(e_ + 1) * 64, :, e_ * 64:(e_ + 1) * 64], sv_[e_], (), (b_wab,))
            dma_in("sp", lh0, lru_h0T.rearrange("(k p) n -> p k n", p=128), (), (b_lh0,))
            _stop_at(3.05)
            for pp in range(2):
                wX, wXb = load_wbig(w_in, 0, C_LX + pp * 512, 512)
                wZ, wZb = load_wbig(w_in, 0, C_LZ + pp * 512, 512)
                for j4 in range(4):
                    j = pp * 4 + j4
                    co = j4 * 128
                    cw4 = lcw_t[:, j * 4:(j + 1) * 4]
                    tt("pool", dg2, identF[:].unsqueeze(1).to_broadcast([128, 4, 128]), cw4.unsqueeze(2).to_broadcast([128, 4, 128]), ALU.mult,
                       (bCONST, bSV), (b_dg2,))

                    def proj_q(q, j=j, co=co, wX=wX, wXb=wXb):
                        ps, bps = next_pf()
                        mm_group(ps[:, :], [(wX[:, k, co:co + 128], hT[:, k, q * 512:(q + 1) * 512]) for k in range(KC)], tuple(bHT) + (wXb,), bps)
                        cp("act", xbuf2[:, 3 + q * 512:3 + (q + 1) * 512], ps[:, :], (bps,), (b_x2q[q],))
                        if q == 3:
                            cp("act", tail2[:, 0:3], ps[:, 509:512], (bps,), (b_tail2,))
                            dma_out(o_lc_p[:, j, :], tail2[:, 0:3], (b_tail2,))

                    def conv_q(q, j=j):
                        qs = slice(q * 512, (q + 1) * 512)
                        ps2, bps2 = next_pf()
                        rd = (b_dg2, b_x2q[q]) + ((b_x2q[q - 1],) if q > 0 else (b_x2,))
                        mm_group(ps2[:, :], [(dg2[:, k, :], xbuf2[:, q * 512 + k:q * 512 + k + 512]) for k in range(4)], rd, bps2)
                        _stop_at(3.06)
                        act(u2[:, qs], ps2[:, :], AF.Identity, (bps2, bSV), (b_u2,), bias=lcb_t[:, j:j + 1])
                        _stop_at(3.07)
                        cp("dve", ubf[:, qs], u2[:, qs], (b_u2,), (b_ubf,))
                        _stop_at(3.08)
                        ps, bps = next_pf()
                        mm_group(ps[:, :], [(wab[:, j, :], ubf[:, qs])], (b_wab, b_ubf), bps)
                        act(r_t[:, qs], ps[:, :], AF.Tanh, (bps, b_hv), (b_r,), bias=hba[:, j:j + 1], scale=0.5)
                        ps, bps = next_pf()
                        mm_group(ps[:, :], [(wxb[:, j, :], ubf[:, qs])], (b_wab, b_ubf), bps)
                        act(i_t[:, qs], ps[:, :], AF.Tanh, (bps, b_hv), (b_i,), bias=hbx[:, j:j + 1], scale=0.5)
                    proj_q(0); proj_q(1); conv_q(0); proj_q(2); conv_q(1); proj_q(3); conv_q(2); conv_q(3)
                    _stop_at(3.1)
                    ps, bps = next_pf()
                    mm_group(ps[:, 0:NS], [(wX[:, k, co:co + 128], hT[:, k, T:TP]) for k in range(KC)], (bHTs, wXb), bps)
                    dma_in("sp", lxs[:, :, 0:3], lru_cvT[j * 128:(j + 1) * 128, :, :], (), (b_lxs,))
                    cp("act", lxs[:, :, 3], ps[:, 0:NS], (bps,), (b_lxs,))
                    dma_out(o_lc_s[:, j, :, :], lxs[:, :, 1:4], (b_lxs,))
                    ts("dve", lus, lxs[:, :, 0], cw4[:, 0:1], lcb_t[:, j:j + 1], ALU.mult, ALU.add, (b_lxs, bSV), (b_lus,))
                    for k in range(1, 4):
                        stt(lus, lxs[:, :, k], cw4[:, k:k + 1], lus, ALU.mult, ALU.add, (b_lxs, bSV, b_lus), (b_lus,))
                    cp("dve", lusb, lus, (b_lus,), (b_lus,))
                    ps, bps = next_pf()
                    mm_group(ps[:, 0:NS], [(wab[:, j, :], lusb)], (b_wab, b_lus), bps)
                    mm_group(ps[:, 32:32 + NS], [(wxb[:, j, :], lusb)], (b_wab, b_lus), bps)
                    act(lr, ps[:, 0:NS], AF.Tanh, (bps, b_hv), (b_lsm,), bias=hba[:, j:j + 1], scale=0.5)
                    act(li, ps[:, 32:32 + NS], AF.Tanh, (bps, b_hv), (b_lsm,), bias=hbx[:, j:j + 1], scale=0.5)
                    _stop_at(3.2)
                    act(a_t, r_t, AF.Exp, (b_r, b_hv), (b_a,), scale=hcv[:, j:j + 1], bias=hcv[:, j:j + 1])
                    act(m_t, r_t, AF.Exp, (b_r, bSV), (b_m,), scale=cvec_t[:, j:j + 1], bias=cvec_t[:, j:j + 1])
                    act(la, lr, AF.Exp, (b_lsm, b_hv), (b_lsm,), scale=hcv[:, j:j + 1], bias=hcv[:, j:j + 1])
                    act(lm, lr, AF.Exp, (b_lsm, bSV), (b_lsm,), scale=cvec_t[:, j:j + 1], bias=cvec_t[:, j:j + 1])
                    act(m_t, m_t, AF.Sqrt, (b_m, b_hv), (b_m,), scale=-0.25, bias=q25[:, 0:1])
                    act(lm, lm, AF.Sqrt, (b_lsm, b_hv), (b_lsm,), scale=-0.25, bias=q25[:, 0:1])
                    _stop_at(3.3)
                    stt(i_t, i_t, 1.0, u2, ALU.add, ALU.mult, (b_i, b_u2), (b_i,))
                    tt("dve", m_t[:, 1:T], m_t[:, 1:T], i_t[:, 1:T], ALU.mult, (b_m, b_i), (b_m,))
                    ts("dve", m_t[:, 0:1], i_t[:, 0:1], 0.5, None, ALU.mult, None, (b_m, b_i), (b_m,))
                    S.op("dve", lambda e: e.tensor_tensor_scan(out=r_t, data0=a_t, data1=m_t, initial=0.0, op0=ALU.mult, op1=ALU.add),
                         (b_a, b_m, b_r), (b_r,))
                    cp("dve", lhp[:, j:j + 1], r_t[:, T - 1:T], (b_r,), (b_lhp,))
                    _stop_at(3.4)
                    stt(li, li, 1.0, lus, ALU.add, ALU.mult, (b_lsm, b_lus), (b_lsm,))
                    tt("dve", lm, lm, li, ALU.mult, (b_lsm,), (b_lsm,))
                    tt("dve", la, la, lh0[:, j, :], ALU.mult, (b_lsm, b_lh0), (b_lsm,))
                    tt("dve", lhn[:, j, :], la, lm, ALU.add, (b_lsm,), (b_lhn,))
                    _stop_at(3.5)
                    for q in range(4):
                        qs = slice(q * 512, (q + 1) * 512)
                        ps, bps = next_pf()
                        mm_group(ps[:, :], [(wZ[:, k, co:co + 128], hT[:, k, qs]) for k in range(KC)], tuple(bHT) + (wZb,), bps)
                        act(a_t[:, qs], ps[:, :], AF.Tanh, (bps, b_a), (b_a,), scale=0.5)
                        stt(a_t[:, qs], a_t[:, qs], 1.0, ps[:, :], ALU.add, ALU.mult, (b_a, bps), (b_a,))
                    stt(ylru[:, j, 0:T], a_t, 0.5, r_t, ALU.mult, ALU.mult, (b_r, b_a), (bYL[j],))
                    ps, bps = next_pf()
                    mm_group(ps[:, 0:NS], [(wZ[:, k, co:co + 128], hT[:, k, T:TP]) for k in range(KC)], (bHTs, wZb), bps)
                    act(lr, ps[:, 0:NS], AF.Tanh, (bps,), (b_lsm,), scale=0.5)
                    stt(lr, lr, 1.0, ps[:, 0:NS], ALU.add, ALU.mult, (b_lsm, bps), (b_lsm,))
                    stt(ylru[:, j, T:TP], lr, 0.5, lhn[:, j, :], ALU.mult, ALU.mult, (b_lsm, b_lhn), (bYLs,))
            dma_out(o_lh_p[:, :], lhp, (b_lhp,))
            dma_out(o_lh_s[:, :, :], lhn, (b_lhn,))

            _stop_at(4)
            S.barrier([b for b in allbufs if b not in bYL and b is not bYLs] + [bWTbig[0], bWTbig[1]] + bWTsm)
            P3 = Carver((8 * TP) // 2)
            merged = v3(P3.bf16(8 * TP), 8)
            sgA = P3.f32(512); sgB = P3.f32(512); tA = P3.f32(512); tB = P3.f32(512)
            yo_t = [P3.f32(1024), P3.f32(1024)]; gts = P3.f32(1024)
            bnst_d = [P3.f32(16), P3.f32(16)]; mh3 = P3.f32(1); junk3 = P3.bf16(1024); cbb = v3(P3.bf16(8 * 128), 8); cf2 = v3(P3.f32(8 * 17), 8); cb2 = v3(P3.bf16(8 * 17), 8)
            P3b = Carver(0)
            gate_b = P3b.f32(1024); lng_b = P3b.f32(1024); lnb_b = P3b.f32(1024); bg_b = P3b.f32(1024)
            xtk = [P3b.f32(1024), P3b.f32(1024)]; resid_d = [P3b.f32(1024), scr[:, (8 * TP) // 2 + 8 * TP // 2:(8 * TP) // 2 + 8 * TP // 2 + 1024]]; xn_d = [P3b.f32(1024), scr[:, (8 * TP) // 2 + 8 * TP // 2 + 1024:(8 * TP) // 2 + 8 * TP // 2 + 2048]]
            assert P3b.pos <= (8 * TP) // 2
            bMG = [mk("merged%d" % k) for k in range(8)]; bMGs = mk("merged_s")
            b_sgA = mk("sgA"); b_sgB = mk("sgB"); b_tA = mk("tA"); b_tB = mk("tB"); b_gateb = mk("gate_b"); b_ln = mk("lnbc")
            b_xtk = [mk("xtk0"), mk("xtk1")]; b_res_d = [mk("resid0"), mk("resid1")]; b_xn_d = [mk("xn0"), mk("xn1")]; b_bn_d = [mk("bn0"), mk("bn1")]; b_yo = [mk("yo0"), mk("yo1")]
            b_cbb = mk("cbb"); b_c2 = mk("c2"); b_gts = mk("gts"); b_j3 = mk("junk3")
            colsets = [(slice(q * 512, (q + 1) * 512), 512) for q in range(4)] + [(slice(T, TP), NS)]
            for jo in range(8):
                wA, wAb = load_wsm(w_lp, 0, jo * 128)
                wB0, wB0b = load_wsm(w_sp, 0, jo * 128)
                wB1, wB1b = load_wsm(w_sp, 1024, jo * 128)
                wgA, wgAb = load_wsm(w_in, 0, C_MA + jo * 128)
                wgB, wgBb = load_wsm(w_in, 0, C_MB + jo * 128)
                for qi, (cs, n) in enumerate(colsets):
                    smp = qi == 4
                    rH = (bHTs,) if smp else tuple(bHT)
                    rYL = (bYLs,) if smp else tuple(bYL)
                    rYS = (bYSs,) if smp else tuple(bYS)
                    pA, bpA = next_pf()
                    mm_group(pA[:, 0:n], [(wA[:, k, :], ylru[:, k, cs]) for k in range(8)], rYL + (wAb,), bpA)
                    pB, bpB = next_pf()
                    mm_group(pB[:, 0:n], [(wB0[:, k, :], yssd[:, k, cs]) for k in range(8)] + [(wB1[:, k, :], yssd[:, 8 + k, cs]) for k in range(8)],
                             rYS + (wB0b, wB1b), bpB)
                    pgA, bpgA = next_pf()
                    mm_group(pgA[:, 0:n], [(wgA[:, k, :], hT[:, k, cs]) for k in range(8)], rH + (wgAb,), bpgA)
                    pgB, bpgB = next_pf()
                    mm_group(pgB[:, 0:n], [(wgB[:, k, :], hT[:, k, cs]) for k in range(8)], rH + (wgBb,), bpgB)
                    act(sgA[:, 0:n], pgA[:, 0:n], AF.Sigmoid, (bpgA,), (b_sgA,))
                    act(sgB[:, 0:n], pgB[:, 0:n], AF.Sigmoid, (bpgB,), (b_sgB,))
                    tt("dve", tA[:, 0:n], sgA[:, 0:n], pA[:, 0:n], ALU.mult, (b_sgA, bpA), (b_tA,))
                    tt("dve", tB[:, 0:n], sgB[:, 0:n], pB[:, 0:n], ALU.mult, (b_sgB, bpB), (b_tB,))
                    tt("dve", merged[:, jo, cs], tA[:, 0:n], tB[:, 0:n], ALU.add, (b_tA, b_tB), (bMGs if smp else bMG[jo],))
            S.barrier(bWTsm + bWTbig + bYL + [bYLs, b_sgA, b_sgB, b_tA, b_tB])
            dma_in("sp", cf2, cT.rearrange("(k p) n -> p k n", p=128), (), (b_c2,))
            cp("dve", cb2, cf2, (b_c2,), (b_c2,))
            cp("dve", cbb, cf2[:, :, 0:1].to_broadcast([128, 8, 128]), (b_c2,), (b_cbb,))
            S.op("dve", lambda e: e.memset(mh3, -0.5), (), (b_ln,))
            dma_in("sp", bg_b, b_gate.partition_broadcast(128), (), (b_ln,))
            dma_in("sp", lng_b, lng_row.partition_broadcast(128), (), (b_ln,))
            dma_in("sp", lnb_b, lnb_row.partition_broadcast(128), (), (b_ln,))
            for hf in range(2):
                wg, wgb = load_wbig(w_cond, 0, 2048 + hf * 512, 512)
                ps, bps = next_pf()
                mm_group(ps[:, :], [(cbb[:, k, :], wg[:, k, :]) for k in range(8)], (b_cbb, wgb), bps)
                tt("dve", gate_b[:, hf * 512:(hf + 1) * 512], ps[:, :], bg_b[:, hf * 512:(hf + 1) * 512], ALU.add, (bps, b_ln), (b_gateb,))
                ps, bps = next_pf()
                mm_group(ps[0:NS, :], [(cb2[:, k, 1:17], wg[:, k, :]) for k in range(8)], (b_c2, wgb), bps)
                tt("dve", gts[0:NS, hf * 512:(hf + 1) * 512], ps[0:NS, :], bg_b[0:NS, hf * 512:(hf + 1) * 512], ALU.add, (bps, b_ln), (b_gts,))
            wo0, wo0b = load_wbig(w_out, 0, 0, 512)
            wo1, wo1b = load_wbig(w_out, 0, 512, 512)
            def tile_ctx(ti):
                smp = ti == NCH
                np_ = NS if smp else 128
                cs = slice(T, TP) if smp else slice(ti * 128, (ti + 1) * 128)
                sl = ti % 2
                return smp, np_, cs, sl

            def ln_stageP(ti):
                smp, np_, cs, sl = tile_ctx(ti)
                resid, bnst, b_res, b_bn = resid_d[sl], bnst_d[sl], b_res_d[sl], b_bn_d[sl]
                rM = (bMGs,) if smp else tuple(bMG)
                dma_in("act", xtk[sl][0:np_, :], xs_tok[:, :] if smp else x_tok[ti * 128:(ti + 1) * 128, :], (), (b_xtk[sl],))
                gsrc = gts if smp else gate_b
                bgs = b_gts if smp else b_gateb
                for hf, (wo, wob) in enumerate(((wo0, wo0b), (wo1, wo1b))):
                    ps, bps = next_pf()
                    mm_group(ps[0:np_, :], [(merged[:, k, cs], wo[:, k, :]) for k in range(8)], rM + (wob,), bps)
                    tt("dve", resid[0:np_, hf * 512:(hf + 1) * 512], ps[0:np_, :], gsrc[0:np_, hf * 512:(hf + 1) * 512], ALU.mult,
                       (bps, bgs), (b_res,))
                stt(resid[0:np_, :], xtk[sl][0:np_, :], ALPHA, resid[0:np_, :], ALU.mult, ALU.add, (b_xtk[sl], b_res), (b_res,))
                act(junk3[0:np_, :], resid[0:np_, :], AF.Copy, (b_res,), (b_j3, b_bn), accum_out=bnst[0:np_, 0:1])
                act(junk3[0:np_, :], resid[0:np_, :], AF.Square, (b_res,), (b_j3, b_bn), accum_out=bnst[0:np_, 1:2])
                ts("pool", bnst[0:np_, 12:13], bnst[0:np_, 0:1], 1.0 / D, None, ALU.mult, None, (b_bn,), (b_bn,))
                tt("pool", bnst[0:np_, 2:3], bnst[0:np_, 12:13], bnst[0:np_, 12:13], ALU.mult, (b_bn,), (b_bn,))
                ts("pool", bnst[0:np_, 3:4], bnst[0:np_, 1:2], 1.0 / D, LN_EPS, ALU.mult, ALU.add, (b_bn,), (b_bn,))
                tt("pool", bnst[0:np_, 14:15], bnst[0:np_, 3:4], bnst[0:np_, 2:3], ALU.subtract, (b_bn,), (b_bn,))
                tt("pool", bnst[0:np_, 15:16], bnst[0:np_, 14:15], mh3[0:np_, 0:1], ALU.pow, (b_bn, b_ln), (b_bn,))

            def ln_stageQ(ti):
                smp, np_, cs, sl = tile_ctx(ti)
                resid, xn, bnst, b_res, b_xn, b_bn = resid_d[sl], xn_d[sl], bnst_d[sl], b_res_d[sl], b_xn_d[sl], b_bn_d[sl]
                stt(xn[0:np_, :], resid[0:np_, :], bnst[0:np_, 12:13], lng_b[0:np_, :], ALU.subtract, ALU.mult, (b_res, b_bn, b_ln), (b_xn,))
                stt(yo_t[sl][0:np_, :], xn[0:np_, :], bnst[0:np_, 15:16], lnb_b[0:np_, :], ALU.mult, ALU.add, (b_xn, b_bn, b_ln), (b_yo[sl],))
                dma_out(y_s[:, :] if smp else y_p[ti * 128:(ti + 1) * 128, :], yo_t[sl][0:np_, :], (b_yo[sl],))

            for ti in range(NCH + 2):
                if ti <= NCH:
                    ln_stageP(ti)
                if ti >= 1:
                    ln_stageQ(ti - 1)

        except _StopRec:
            pass
        S.finalize()

        @block.sync
        def _(e):
            S.emit("sp", e)

        @block.gpsimd
        def _(e):
            S.emit("pool", e)

        @block.scalar
        def _(e):
            S.emit("act", e)

        @block.vector
        def _(e):
            S.emit("dve", e)

        @block.tensor
        def _(e):
            S.emit("pe", e)
    return nc


_NC_CACHE = {}


def _vec_pk(v, k):
    return np.ascontiguousarray(np.asarray(v, np.float32).reshape(k, 128).T)


def kernel(x_prompt, x_sample, state_lru_h, state_lru_conv, state_ssd_h, state_ssd_conv,
           c_prompt, c_sample, w_cond, b_cond, w_in, lru_conv_w, lru_conv_b, lru_wa, lru_ba,
           lru_wx, lru_bx, lru_lambda, ssd_conv_w, ssd_conv_b, ssd_dt_bias, ssd_a_log, ssd_d,
           ssd_norm_w, w_lru_proj, w_ssd_proj, w_out, ln_g, ln_b):
    f = lambda a: np.ascontiguousarray(np.asarray(a, np.float32))
    x_prompt, x_sample = f(x_prompt), f(x_sample)
    if "nc" not in _NC_CACHE:
        _NC_CACHE["nc"] = build_program()
    nc = _NC_CACHE["nc"]
    shared = {
        "w_cond": f(w_cond[0]), "b_condT": _vec_pk(b_cond[0], 24), "b_gate": f(np.asarray(b_cond)[0:1, 2048:3072]),
        "w_in": f(w_in[0]),
        "lcw": f(np.asarray(lru_conv_w)[0].reshape(4, 8, 128).transpose(2, 1, 0)), "lcb": _vec_pk(lru_conv_b[0], 8),
        "lwa": f(lru_wa[0]), "lwx": f(lru_wx[0]),
        "lba": _vec_pk(lru_ba[0], 8), "lbx": _vec_pk(lru_bx[0], 8), "llam": _vec_pk(lru_lambda[0], 8),
        "scw": f(np.asarray(ssd_conv_w)[0].reshape(4, 24, 128).transpose(2, 1, 0)), "scb": _vec_pk(ssd_conv_b[0], 24),
        "dtb_row": f(np.asarray(ssd_dt_bias)[0:1]), "alog_row": f(np.asarray(ssd_a_log)[0:1]), "d_row": f(np.asarray(ssd_d)[0:1]),
        "dtb_col": f(np.asarray(ssd_dt_bias)[0].reshape(32, 1)), "alog_col": f(np.asarray(ssd_a_log)[0].reshape(32, 1)),
        "d_x": _vec_pk(np.repeat(np.asarray(ssd_d, np.float32)[0], 64), 16),
        "normw_row": f(np.asarray(ssd_norm_w)[0:1]), "normwT": _vec_pk(ssd_norm_w[0], 16),
        "w_lp": f(w_lru_proj[0]), "w_sp": f(w_ssd_proj[0]), "w_out": f(w_out[0]),
        "lng_row": f(np.asarray(ln_g)[0:1]), "lnb_row": f(np.asarray(ln_b)[0:1]),
        "c_ident": np.eye(128, dtype=np.float32), "c_tri": np.triu(np.ones((128, 128), np.float32)),
        "c_esel": f((np.arange(128)[:, None] == (np.arange(2048)[None, :] // 64)).astype(np.float32)),
    }
    in_maps = []
    for i in range(NCORES):
        ss = slice(NS * i, NS * (i + 1))
        cT = np.concatenate([np.asarray(c_prompt, np.float32)[i][:, None], np.asarray(c_sample, np.float32)[ss].T], axis=1)
        m = dict(shared)
        m.update({
            "xT": f(x_prompt[i].T), "x_tok": f(x_prompt[i]),
            "xsT": f(x_sample[ss, 0, :].T), "xs_tok": f(x_sample[ss, 0, :]),
            "cT": f(cT),
            "lru_h0T": f(np.asarray(state_lru_h)[0, ss].T),
            "lru_cvT": f(np.asarray(state_lru_conv)[0, ss].transpose(2, 0, 1)),
            "ssd_h0": f(np.asarray(state_ssd_h)[0, ss]),
            "ssd_cvT": f(np.asarray(state_ssd_conv)[0, ss].transpose(2, 0, 1)),
        })
        in_maps.append(m)
    res = run_bass_kernel_spmd(nc, in_maps, core_ids=list(range(NCORES)))
    R = res.results
    y_prompt = np.stack([R[i]["y_p"] for i in range(NCORES)])
    y_sample = np.concatenate([R[i]["y_s"] for i in range(NCORES)])[:, None, :]
    lh_p = np.stack([R[i]["o_lh_p"].T.reshape(1024) for i in range(NCORES)])[None]
    lc_p = np.stack([R[i]["o_lc_p"].transpose(2, 1, 0).reshape(3, 1024) for i in range(NCORES)])[None]
    sh_p = np.stack([R[i]["o_sh_p"].T.reshape(32, 64, 128) for i in range(NCORES)])[None]
    sc_p = np.stack([R[i]["o_sc_p"].transpose(2, 1, 0).reshape(3, 3072) for i in range(NCORES)])[None]
    lh_s = np.concatenate([R[i]["o_lh_s"].transpose(2, 1, 0).reshape(NS, 1024) for i in range(NCORES)])[None]
    lc_s = np.concatenate([R[i]["o_lc_s"].transpose(2, 3, 1, 0).reshape(NS, 3, 1024) for i in range(NCORES)])[None]
    sh_s = np.concatenate([R[i]["o_sh_s"] for i in range(NCORES)])[None]
    sc_s = np.concatenate([R[i]["o_sc_s"].transpose(2, 3, 1, 0).reshape(NS, 3, 3072) for i in range(NCORES)])[None]
    c32 = lambda a: np.ascontiguousarray(a, dtype=np.float32)
    return (c32(y_prompt), c32(y_sample), c32(lh_p), c32(lc_p), c32(sh_p), c32(sc_p),
            c32(lh_s), c32(lc_s), c32(sh_s), c32(sc_s))
```

```python
import numpy as np
import concourse.bass as bass
import concourse.mybir as mybir
from concourse.bass_utils import run_bass_kernel_spmd

F32 = mybir.dt.float32
BF16 = mybir.dt.bfloat16
AF = mybir.ActivationFunctionType
ALU = mybir.AluOpType
AX = mybir.AxisListType

NCORES = 8
D = 1024
T = 2048
NS = 16
TP = T + NS
KC = 8
NCH = 16
L = 128
N_IN = 9248
C_LX, C_LZ, C_SZ, C_SX, C_SB, C_SC, C_DT, C_MA, C_MB = 0, 1024, 2048, 4096, 6144, 6656, 7168, 7200, 8224
ALPHA = 2.0 ** 0.25
LN_EPS = 1e-5
RMS_EPS = 1e-5
SAME_ENGINE_SYNC = True
RAW_ONLY = True
import os as _os
KSTOP = float(_os.environ.get('KSTOP', '99'))


class _StopRec(Exception):
    pass


def _stop_at(n):
    if KSTOP <= n:
        raise _StopRec()


class Buf:
    __slots__ = ("name", "last_w", "readers")

    def __init__(self, name):
        self.name = name
        self.last_w = None
        self.readers = []


class Op:
    __slots__ = ("eng", "fn", "deps", "is_dma", "sem", "semval", "prevval", "signal", "sig", "raw")

    def __init__(self, eng, fn, deps, is_dma):
        self.eng, self.fn, self.deps, self.is_dma = eng, fn, deps, is_dma
        self.sem = None
        self.semval = 0
        self.prevval = 0
        self.signal = False
        self.sig = 0
        self.raw = set()


class Sched:
    ENGS = ("sp", "pool", "act", "dve", "pe")

    def __init__(self, nc, esems, dma_sems):
        self.nc = nc
        self.ops = []
        self.esems = esems
        self.dma_pool = dma_sems
        self.dma_idx = {q: 0 for q in dma_sems}
        self.dma_val = {}
        self.store_ops = []

    def _deps(self, reads, writes):
        deps = set()
        self._last_raw = set()
        for b in list(reads) + list(writes):
            if b.last_w is not None:
                deps.add(b.last_w)
        for b in reads:
            if b.last_w is not None:
                self._last_raw.add(b.last_w)
        for b in writes:
            for r in b.readers:
                deps.add(r)
        return deps

    def _update(self, idx, reads, writes):
        for b in reads:
            b.readers.append(idx)
        for b in writes:
            b.last_w = idx
            b.readers = []

    def op(self, eng, fn, reads=(), writes=()):
        idx = len(self.ops)
        o = Op(eng, fn, self._deps(reads, writes), False)
        o.raw = self._last_raw
        self.ops.append(o)
        self._update(idx, reads, writes)
        return idx

    def dma(self, q, fn, n, reads=(), writes=(), store=False):
        idx = len(self.ops)
        o = Op(q, fn, self._deps(reads, writes), True)
        o.raw = self._last_raw
        pool = self.dma_pool[q]
        sem = pool[self.dma_idx[q] % len(pool)]
        self.dma_idx[q] += 1
        o.sem = sem
        o.prevval = self.dma_val.get(id(sem), 0)
        o.semval = o.prevval + 16 * n
        self.dma_val[id(sem)] = o.semval
        self.ops.append(o)
        self._update(idx, reads, writes)
        if store:
            self.store_ops.append(idx)
        return idx

    def barrier(self, bufs):
        allidx = len(self.ops)
        last = {}
        for i, o in enumerate(self.ops):
            if o.fn is not None:
                last[o.eng] = i
        dmas = [i for i, o in enumerate(self.ops) if o.is_dma]
        deps = set(last.values()) | set(dmas[-64:])
        for e in self.ENGS:
            o = Op(e, None, set(deps), False)
            self.ops.append(o)
        for b in bufs:
            b.last_w = None
            b.readers = []

    def finalize(self):
        for i, o in enumerate(self.ops):
            for d in o.deps:
                p = self.ops[d]
                if p.is_dma or p.fn is None:
                    continue
                same_ok = SAME_ENGINE_SYNC and p.eng != "pe" and (not RAW_ONLY or d in o.raw)
                if p.eng != o.eng or same_ok or o.is_dma:
                    p.signal = True
        cnt = {e: 0 for e in self.ENGS}
        for o in self.ops:
            if o.is_dma:
                continue
            if o.fn is None:
                continue
            if o.signal:
                cnt[o.eng] += 1
                o.sig = cnt[o.eng]

    def emit(self, eng_name, e):
        water = {}
        esems = self.esems

        def wait(sem, val):
            k = id(sem)
            if water.get(k, 0) >= val:
                return
            water[k] = val
            e.wait_ge(sem, val)

        for i, o in enumerate(self.ops):
            if o.eng != eng_name:
                continue
            for d in sorted(o.deps):
                p = self.ops[d]
                if p.is_dma:
                    wait(p.sem, p.semval)
                elif p.fn is None:
                    continue
                elif p.eng != eng_name or (SAME_ENGINE_SYNC and eng_name != "pe" and (not RAW_ONLY or d in o.raw)) or o.is_dma:
                    wait(esems[p.eng], p.sig)
            if o.fn is None:
                continue
            if o.is_dma:
                if o.prevval > 0:
                    wait(o.sem, o.prevval)
                o.fn(e, o.sem)
            else:
                ins = o.fn(e)
                if o.signal:
                    ins.then_inc(esems[eng_name], 1)
        if eng_name == "sp":
            for idx in self.store_ops:
                o = self.ops[idx]
                wait(o.sem, o.semval)


def build_program():
    nc = bass.Bass("TRN2", target_bir_lowering=False)
    din, dout = {}, {}

    def inp(name, shape):
        din[name] = nc.dram_tensor(name, list(shape), F32, kind="ExternalInput").ap()
        return din[name]

    def outp(name, shape):
        dout[name] = nc.dram_tensor(name, list(shape), F32, kind="ExternalOutput").ap()
        return dout[name]

    xT = inp("xT", [D, T]); x_tok = inp("x_tok", [T, D])
    xsT = inp("xsT", [D, NS]); xs_tok = inp("xs_tok", [NS, D])
    cT = inp("cT", [D, 17])
    lru_h0T = inp("lru_h0T", [D, NS]); lru_cvT = inp("lru_cvT", [D, NS, 3])
    ssd_h0 = inp("ssd_h0", [NS, 32, 64, 128]); ssd_cvT = inp("ssd_cvT", [3072, NS, 3])
    w_cond = inp("w_cond", [D, 3072]); b_condT = inp("b_condT", [128, 24]); b_gate = inp("b_gate", [1, 1024])
    w_in = inp("w_in", [D, N_IN])
    lcw = inp("lcw", [128, 8, 4]); lcb = inp("lcb", [128, 8])
    lwa = inp("lwa", [16, 64, 64]); lwx = inp("lwx", [16, 64, 64])
    lba = inp("lba", [128, 8]); lbx = inp("lbx", [128, 8]); llam = inp("llam", [128, 8])
    scw = inp("scw", [128, 24, 4]); scb = inp("scb", [128, 24])
    dtb_row = inp("dtb_row", [1, 32]); alog_row = inp("alog_row", [1, 32]); d_row = inp("d_row", [1, 32])
    dtb_col = inp("dtb_col", [32, 1]); alog_col = inp("alog_col", [32, 1])
    d_x = inp("d_x", [128, 16]); normw_row = inp("normw_row", [1, 2048]); normwT = inp("normwT", [128, 16])
    w_lp = inp("w_lp", [D, D]); w_sp = inp("w_sp", [2048, D]); w_out = inp("w_out", [D, D])
    lng_row = inp("lng_row", [1, D]); lnb_row = inp("lnb_row", [1, D])
    c_ident = inp("c_ident", [128, 128]); c_tri = inp("c_tri", [128, 128]); c_esel = inp("c_esel", [128, 2048])

    y_p = outp("y_p", [T, D]); y_s = outp("y_s", [NS, D])
    o_lh_p = outp("o_lh_p", [128, 8]); o_lc_p = outp("o_lc_p", [128, 8, 3])
    o_sh_p = outp("o_sh_p", [128, 2048]); o_sc_p = outp("o_sc_p", [128, 24, 3])
    o_lh_s = outp("o_lh_s", [128, 8, NS]); o_lc_s = outp("o_lc_s", [128, 8, NS, 3])
    o_sh_s = outp("o_sh_s", [NS, 32, 64, 128]); o_sc_s = outp("o_sc_s", [128, 24, NS, 3])

    SCRN = 23040
    from contextlib import ExitStack
    with ExitStack() as _es:
        _en = _es.enter_context
        hT = _en(nc.sbuf_tensor("hT", [128, KC, TP], BF16))
        yssd = _en(nc.sbuf_tensor("yssd", [128, 16, TP], BF16))
        wt = _en(nc.sbuf_tensor("wt", [128, 8192], BF16))
        scr = _en(nc.sbuf_tensor("scr", [128, SCRN], F32))
        identF = _en(nc.sbuf_tensor("identF", [128, 128], F32))
        identB = _en(nc.sbuf_tensor("identB", [128, 128], BF16))
        triF = _en(nc.sbuf_tensor("triF", [128, 128], F32))
        onesF = _en(nc.sbuf_tensor("onesF", [128, 128], F32))
        smallv = _en(nc.sbuf_tensor("smallv", [128, 512], F32))
        modT = _en(nc.sbuf_tensor("modT", [128, 16, 17], F32))
        pF0, pF1, pF2, pF3, pF4, pF5 = [_en(nc.psum_tensor("pF%d" % i, [128, 512], F32)) for i in range(6)]
        pB0 = _en(nc.psum_tensor("pB0", [128, 1024], BF16))
        pB1 = _en(nc.psum_tensor("pB1", [128, 1024], BF16))
        s_pool, s_act, s_dve, s_pe, s_sp = [_en(nc.semaphore(n)) for n in ("s_pool", "s_act", "s_dve", "s_pe", "s_sp")]
        dq0, dq1, dq2, dq3, dq4, dq5, dq6, dq7 = [_en(nc.semaphore("dq%d" % i)) for i in range(8)]
        dg0, dg1, dg2, dg3, dg4, dg5 = [_en(nc.semaphore("dg%d" % i)) for i in range(6)]
        da0, da1, da2, da3 = [_en(nc.semaphore("da%d" % i)) for i in range(4)]
        block = _en(nc.Block())
        S = Sched(nc, {"sp": s_sp, "pool": s_pool, "act": s_act, "dve": s_dve, "pe": s_pe},
                  {"sp": [dq0, dq1, dq2, dq3, dq4, dq5, dq6, dq7], "pool": [dg0, dg1, dg2, dg3, dg4, dg5], "act": [da0, da1, da2, da3]})
        PF = [pF0, pF1, pF2, pF3, pF4, pF5]
        bPF = [Buf("pF%d" % i) for i in range(6)]
        bPB = [Buf("pB0"), Buf("pB1")]

        class Carver:
            def __init__(self, start=0):
                self.pos = start

            def f32(self, n):
                a = scr[:, self.pos:self.pos + n]
                self.pos += n
                assert self.pos <= SCRN, self.pos
                return a

            def bf16(self, n):
                n32 = (n + 1) // 2
                a = scr[:, self.pos:self.pos + n32].bitcast(BF16)
                self.pos += n32
                assert self.pos <= SCRN, self.pos
                return a

        def v3(ap, a):
            return ap.rearrange("p (a b) -> p a b", a=a)

        sv = [0]

        def svec(n):
            a = smallv[:, sv[0]:sv[0] + n]
            sv[0] += n
            assert sv[0] <= 512
            return a

        lcw_t = svec(32); lcb_t = svec(8); lba_t = svec(8); lbx_t = svec(8); cvec_t = svec(8); cvec2_t = svec(8)
        scw_t = svec(96); scb_t = svec(24); bcond_t = svec(24); dx_t = svec(16); nwT_t = svec(16)
        dtbc_t = svec(1); alogc_t = svec(1)
        bSV = Buf("smallv")
        bCONST = Buf("consts")
        bMOD = Buf("modT")
        bHT = [Buf("hT%d" % k) for k in range(KC)]
        bHTs = Buf("hTs")
        bYS = [Buf("yssd%d" % k) for k in range(16)]
        bYSs = Buf("yssd_s")
        bWTbig = [Buf("wtbig0"), Buf("wtbig1")]
        bWTsm = [Buf("wtsm%d" % i) for i in range(8)]
        wt_big = [wt[:, i * 4096:(i + 1) * 4096].rearrange("p (k c) -> p k c", k=8) for i in range(2)]
        wt_sm = [wt[:, i * 1024:(i + 1) * 1024].rearrange("p (k c) -> p k c", k=8) for i in range(8)]
        allbufs = []

        def mk(name):
            b = Buf(name)
            allbufs.append(b)
            return b

        def dma_in(q, out_ap, in_ap, reads=(), writes=(), n=1):
            def fn(e, sem, out_ap=out_ap, in_ap=in_ap):
                e.dma_start(out=out_ap, in_=in_ap).then_inc(sem, 16)
            return S.dma(q, fn, 1, reads, writes)

        def dma_out(out_ap, in_ap, reads=()):
            def fn(e, sem, out_ap=out_ap, in_ap=in_ap):
                e.dma_start(out=out_ap, in_=in_ap).then_inc(sem, 16)
            return S.dma("sp", fn, 1, reads, (), store=True)

        wbig_i = [0]

        def load_wbig(src, r0, c0, ncols):
            s = wbig_i[0] % 2
            wbig_i[0] += 1
            dst = wt_big[s][:, :, 0:ncols]
            srcv = src[r0:r0 + 1024, c0:c0 + ncols].rearrange("(k p) n -> p k n", p=128)
            dma_in("pool", dst, srcv, (), (bWTbig[s],))
            return wt_big[s], bWTbig[s]

        wsm_i = [0]

        def load_wsm(src, r0, c0):
            s = wsm_i[0] % 8
            wsm_i[0] += 1
            srcv = src[r0:r0 + 1024, c0:c0 + 128].rearrange("(k p) n -> p k n", p=128)
            dma_in("pool", wt_sm[s], srcv, (), (bWTsm[s],))
            return wt_sm[s], bWTsm[s]

        pf_i = [0]

        def next_pf():
            i = pf_i[0] % 6
            pf_i[0] += 1
            return PF[i], bPF[i]

        def mm_group(out_ap, pairs, reads, wbuf):
            n = len(pairs)

            def fn(e, out_ap=out_ap, pairs=pairs):
                ins = None
                for i, (l, r) in enumerate(pairs):
                    ins = e.matmul(out_ap, lhsT=l, rhs=r, start=(i == 0), stop=(i == n - 1))
                return ins
            return S.op("pe", fn, reads, (wbuf,))

        def act(out, in_, func, reads, writes, bias=None, scale=None, accum_out=None):
            kw = {}
            if bias is not None:
                kw["bias"] = bias
            if scale is not None:
                kw["scale"] = scale
            if accum_out is not None:
                kw["accum_out"] = accum_out
            return S.op("act", lambda e, kw=kw: e.activation(out=out, in_=in_, func=func, **kw), reads, writes)

        def tt(eng, out, in0, in1, op, reads, writes):
            return S.op(eng, lambda e: e.tensor_tensor(out=out, in0=in0, in1=in1, op=op), reads, writes)

        def ts(eng, out, in0, s1, s2, op0, op1, reads, writes):
            if s2 is None:
                return S.op(eng, lambda e: e.tensor_scalar(out=out, in0=in0, scalar1=s1, scalar2=None, op0=op0), reads, writes)
            return S.op(eng, lambda e: e.tensor_scalar(out=out, in0=in0, scalar1=s1, scalar2=s2, op0=op0, op1=op1), reads, writes)

        def stt(out, in0, scalar, in1, op0, op1, reads, writes):
            return S.op("dve", lambda e: e.scalar_tensor_tensor(out=out, in0=in0, scalar=scalar, in1=in1, op0=op0, op1=op1), reads, writes)

        def cp(eng, out, in_, reads, writes):
            if eng == "act":
                return act(out, in_, AF.Copy, reads, writes)
            return S.op(eng, lambda e: e.tensor_copy(out=out, in_=in_), reads, writes)

        try:
            S.op("dve", lambda e: e.memset(smallv[:], 0.0), (), (bSV,))
            dma_in("sp", identF[:], c_ident[:, :], (), (bCONST,))
            dma_in("sp", triF[:], c_tri[:, :], (), (bCONST,))
            dma_in("pool", identB[:], c_ident[:, :], (), (bCONST,))
            S.op("pool", lambda e: e.memset(onesF[:], 1.0), (), (bCONST,))
            for (t_, src_) in ((lcw_t, lcw.rearrange("p a b -> p (a b)")), (lcb_t, lcb), (lba_t, lba), (lbx_t, lbx), (cvec_t, llam),
                               (scw_t, scw.rearrange("p a b -> p (a b)")), (scb_t, scb), (bcond_t, b_condT), (dx_t, d_x), (nwT_t, normwT)):
                dma_in("sp", t_, src_, (), (bSV,))
            dma_in("sp", dtbc_t[0:32, :], dtb_col[:, :], (), (bSV,))
            dma_in("sp", alogc_t[0:32, :], alog_col[:, :], (), (bSV,))
            _stop_at(-3)
            act(cvec_t, cvec_t, AF.Exp, (bSV,), (bSV,), scale=-1.0)
            act(cvec_t, cvec_t, AF.Ln, (bSV,), (bSV,), bias=1.0)
            ts("dve", cvec2_t, cvec_t, -16.0, None, ALU.mult, None, (bSV,), (bSV,))
            ts("dve", cvec_t, cvec_t, -8.0, None, ALU.mult, None, (bSV,), (bSV,))
            act(alogc_t, alogc_t, AF.Exp, (bSV,), (bSV,))
            ts("dve", alogc_t, alogc_t, -1.0, None, ALU.mult, None, (bSV,), (bSV,))

            _stop_at(-2)
            P0 = Carver(0)
            cf = P0.f32(8 * 17); cb_ = P0.bf16(8 * 17)
            xin = [P0.f32(T), P0.f32(T)]
            xs_f = P0.f32(8 * NS); hs_f = P0.f32(8 * NS)
            b_cf, b_cb = mk("cf"), mk("cb")
            b_xin = [mk("xin0"), mk("xin1")]
            b_xs = mk("xs_f")
            cf3 = v3(cf, 8); cb3 = v3(cb_, 8)
            dma_in("sp", cf3, cT.rearrange("(k p) n -> p k n", p=128), (), (b_cf,))
            cp("dve", cb_, cf, (b_cf,), (b_cb,))
            modps, bmodps = PF[0], bPF[0]
            for pc in range(4):
                wtile, wb = load_wbig(w_cond, 0, pc * 512, 512)
                for i in range(4):
                    mc = pc * 4 + i
                    pairs = [(wtile[:, k, i * 128:(i + 1) * 128], cb3[:, k, :]) for k in range(KC)]
                    mm_group(modps[:, mc * 17:(mc + 1) * 17], pairs, (wb, b_cb), bmodps)
            tt("dve", modT[:], v3(modps[:, 0:16 * 17], 16), bcond_t[:, 0:16].unsqueeze(2).to_broadcast([128, 16, 17]),
               ALU.add, (bmodps, bSV), (bMOD,))
            ts("dve", modT[:, 8:16, :], modT[:, 8:16, :], 1.0, None, ALU.add, None, (bMOD,), (bMOD,))
            _stop_at(-1)
            xTv = xT.rearrange("(k p) t -> p k t", p=128)
            for k in range(KC):
                s = k % 2
                dma_in("sp", xin[s], xTv[:, k, :], (), (b_xin[s],))
                act(hT[:, k, 0:T], xin[s], AF.Identity, (b_xin[s], bMOD), (bHT[k],), bias=modT[:, k, 0:1], scale=modT[:, 8 + k, 0:1])
            _stop_at(-0.5)
            dma_in("sp", v3(xs_f, 8), xsT.rearrange("(k p) n -> p k n", p=128), (), (b_xs,))
            _stop_at(-0.4)
            tt("dve", v3(hs_f, 8), v3(xs_f, 8), modT[:, 8:16, 1:17], ALU.mult, (b_xs, bMOD), (b_xs,))
            _stop_at(-0.3)
            tt("dve", v3(hs_f, 8), v3(hs_f, 8), modT[:, 0:8, 1:17], ALU.add, (b_xs, bMOD), (b_xs,))
            _stop_at(-0.2)
            cp("act", hT[:, :, T:TP], v3(hs_f, 8), (b_xs,), (bHTs,))
            bHTall = bHT + [bHTs]
            _stop_at(0)

            S.barrier(allbufs)
            P1 = Carver(0)
            xTg = v3(P1.bf16(4 * T), 4); BTg = P1.bf16(T); CTg = P1.bf16(T)
            xbuf = P1.bf16(T + 8); dg = v3(P1.bf16(512), 4); tail3 = P1.f32(4)
            wsmB = [v3(P1.bf16(1024), 8), v3(P1.bf16(1024), 8)]; wsmC = [v3(P1.bf16(1024), 8), v3(P1.bf16(1024), 8)]
            dt_t = P1.f32(512); adt_t = P1.f32(512); acs_t = P1.f32(512); nacs_t = P1.f32(512)
            e_t = P1.f32(512); w1_t = P1.f32(512); cdec_t = P1.f32(512); tmp_t = P1.f32(512)
            negI = P1.bf16(128); Lmask = P1.bf16(512)
            persist_start = P1.pos
            nw_b = P1.f32(512); D_b = P1.f32(32); dtb_b = P1.f32(32); nA_b = P1.f32(32)
            xsT_all = v3(P1.f32(16 * NS), 16)
            BsT = v3(P1.f32(4 * NS), 4); CsT = v3(P1.f32(4 * NS), 4)
            szs = v3(P1.f32(16 * NS), 16)
            xsb = v3(P1.f32(NS * 4), NS); us_t = P1.f32(NS)
            dtT_s = P1.f32(NS); adtT_s = P1.f32(NS); dAT_s = P1.f32(NS)
            persist_end = P1.pos
            x_tok_d = [P1.bf16(512), P1.bf16(512)]; xs_d = [None, None]; xsc_d = [P1.bf16(512), P1.bf16(512)]
            B_tok_d = [P1.bf16(128), P1.bf16(128)]; MT_d = [v3(P1.bf16(1024), 8), v3(P1.bf16(1024), 8)]
            sz_d = [P1.f32(512), P1.f32(512)]
            CBm = P1.f32(128); _ex = P1.f32(512); ex_t = [_ex, P1.f32(512)]
            wdt = v3(_ex.bitcast(BF16), 8)
            t_a_d = [P1.f32(512), P1.f32(512)]; t_b = tmp_t; junk = P1.bf16(512); mhalf = P1.f32(1)
            yn_t = P1.bf16(512); hst = P1.f32(512); hst_bf = P1.bf16(512); st8_d = [P1.f32(8), P1.f32(8)]
            print("P1 end", P1.pos, "of", SCRN)
            b_xTg = [mk("xTg%d" % i) for i in range(4)]; b_BT = mk("BTg"); b_CT = mk("CTg")
            b_wsmB = [mk("wsmB0"), mk("wsmB1")]; b_wsmC = [mk("wsmC0"), mk("wsmC1")]
            b_xbuf = mk("xbuf"); b_xbufq = [mk("xbufq%d" % q) for q in range(4)]; b_dg = mk("dg"); b_tail = mk("tail3"); b_dtf = mk("dtfam"); b_bc = mk("bcasts")
            b_xsT = mk("xsT_all"); b_BsT = mk("BsT"); b_CsT = mk("CsT"); b_szs = mk("szs"); b_xsb = mk("xsb"); b_us = mk("us")
            b_msk = mk("maskconsts")
            b_dts = mk("dts"); b_wdt = mk("wdt"); b_wz = mk("wzs")
            bd_xtok = [mk("x_tok0"), mk("x_tok1")]; bd_xs = [mk("xs0"), mk("xs1")]; bd_xsc = [mk("xsc0"), mk("xsc1")]
            bd_Btok = [mk("B_tok0"), mk("B_tok1")]; bd_MT = [mk("MT0"), mk("MT1")]; bd_sz = [mk("sz0"), mk("sz1")]
            b_CBm = mk("CBm"); b_ex = [mk("ex0"), mk("ex1")]; bd_ta = [mk("t_a0"), mk("t_a1")]; b_tb = mk("t_b"); b_junk = mk("junk")
            b_yn = mk("yn"); b_hst = mk("hst"); b_hbf = mk("hst_bf"); bd_st8 = [mk("st8a"), mk("st8b")]
            S.op("dve", lambda e: e.memset(xbuf[:, 0:3], 0.0), (), (b_xbuf,))
            S.op("dve", lambda e: e.memset(mhalf, -0.5), (), (b_bc,))
            ts("pool", negI, identF[:], -32768.0, None, ALU.mult, None, (bCONST,), (b_msk,))
            ts("dve", v3(Lmask, 4), triF[:].unsqueeze(1).to_broadcast([128, 4, 128]), -1.0, 1.0, ALU.mult, ALU.add, (bCONST,), (b_msk,))
            dma_in("sp", dtb_b, dtb_row.partition_broadcast(128), (), (b_bc,))
            dma_in("sp", nA_b, alog_row.partition_broadcast(128), (), (b_bc,))
            dma_in("sp", D_b, d_row.partition_broadcast(128), (), (b_bc,))
            act(nA_b, nA_b, AF.Exp, (b_bc,), (b_bc,))
            ts("dve", nA_b, nA_b, -1.0, None, ALU.mult, None, (b_bc,), (b_bc,))
            S.op("pool", lambda e: e.memset(wdt, 0.0), (), (b_wdt,))
            dma_in("pool", wdt[:, :, 0:32], w_in[:, C_DT:C_DT + 32].rearrange("(k p) n -> p k n", p=128), (), (b_wdt,))
            dtps, bdtps = PF[1], bPF[1]
            for c in range(NCH):
                pairs = [(hT[:, k, c * L:(c + 1) * L], wdt[:, k, 0:32]) for k in range(KC)]
                mm_group(dtps[:, c * 32:(c + 1) * 32], pairs, tuple(bHT) + (b_wdt,), bdtps)
            dsps, bdsps = PF[2], bPF[2]
            mm_group(dsps[:, 0:NS], [(wdt[:, k, :], hT[:, k, T:TP]) for k in range(KC)], (bHTs, b_wdt), bdsps)
            act(dtT_s, dsps[:, 0:NS], AF.Exp, (bdsps, bSV), (b_dts,), bias=dtbc_t)
            act(dtT_s, dtT_s, AF.Ln, (b_dts,), (b_dts,), bias=1.0)
            ts("dve", adtT_s, dtT_s, alogc_t, None, ALU.mult, None, (b_dts, bSV), (b_dts,))
            act(dAT_s, adtT_s, AF.Exp, (b_dts,), (b_dts,))
            tt("dve", v3(tmp_t, 16), v3(dtps[:, :], 16), dtb_b.unsqueeze(1).to_broadcast([128, 16, 32]), ALU.add, (bdtps, b_bc), (b_dtf,))
            act(tmp_t, tmp_t, AF.Exp, (b_dtf,), (b_dtf,))
            act(dt_t, tmp_t, AF.Ln, (b_dtf,), (b_dtf,), bias=1.0)
            tt("dve", v3(adt_t, 16), v3(dt_t, 16), nA_b.unsqueeze(1).to_broadcast([128, 16, 32]), ALU.mult, (b_dtf, b_bc), (b_dtf,))
            acsps, bacsps = PF[3], bPF[3]
            totps, btotps = PF[4], bPF[4]
            mm_group(acsps[:, :], [(triF[:], adt_t)], (b_dtf, bCONST), bacsps)
            mm_group(totps[:, :], [(onesF[:], adt_t)], (b_dtf, bCONST), btotps)
            cp("act", acs_t, acsps[:, :], (bacsps,), (b_dtf,))
            act(e_t, acs_t, AF.Exp, (b_dtf,), (b_dtf,))
            tt("dve", tmp_t, totps[:, :], acs_t, ALU.subtract, (btotps, b_dtf), (b_dtf,))
            act(tmp_t, tmp_t, AF.Exp, (b_dtf,), (b_dtf,))
            tt("dve", w1_t, tmp_t, dt_t, ALU.mult, (b_dtf,), (b_dtf,))
            act(tmp_t, dt_t, AF.Ln, (b_dtf,), (b_dtf,))
            tt("dve", nacs_t, tmp_t, acs_t, ALU.subtract, (b_dtf,), (b_dtf,))
            act(cdec_t, totps[:, :], AF.Exp, (btotps,), (b_dtf,))
            _stop_at(1)

            def conv_chunk(wtile, wb, coff, cw4, cbias, out_bf, b_out, tail_dst, s_state_src, s_out, b_s_out, s_state_dst):
                tt("pool", dg, identF[:].unsqueeze(1).to_broadcast([128, 4, 128]), cw4.unsqueeze(2).to_broadcast([128, 4, 128]), ALU.mult,
                   (bCONST, bSV), (b_dg,))
                def proj_q(q):
                    ps, bps = next_pf()
                    pairs = [(wtile[:, k, coff:coff + 128], hT[:, k, q * 512:(q + 1) * 512]) for k in range(KC)]
                    mm_group(ps[:, :], pairs, tuple(bHT) + (wb,), bps)
                    cp("dve", xbuf[:, 3 + q * 512:3 + (q + 1) * 512], ps[:, :], (bps,), (b_xbufq[q],))
                    if q == 3:
                        cp("dve", tail3[:, 0:3], ps[:, 509:512], (bps,), (b_tail,))
                        dma_out(tail_dst, tail3[:, 0:3], (b_tail,))

                def conv_q(q):
                    ps2, bps2 = next_pf()
                    rd = (b_dg, b_xbufq[q]) + ((b_xbufq[q - 1],) if q > 0 else (b_xbuf,))
                    mm_group(ps2[:, :], [(dg[:, k, :], xbuf[:, q * 512 + k:q * 512 + k + 512]) for k in range(4)], rd, bps2)
                    act(out_bf[:, q * 512:(q + 1) * 512], ps2[:, :], AF.Silu, (bps2, bSV), (b_out,), bias=cbias)
                proj_q(0); proj_q(1); conv_q(0); proj_q(2); conv_q(1); proj_q(3); conv_q(2); conv_q(3)
                ps, bps = next_pf()
                mm_group(ps[:, 0:NS], [(wtile[:, k, coff:coff + 128], hT[:, k, T:TP]) for k in range(KC)], (bHTs, wb), bps)
                dma_in("sp", xsb[:, :, 0:3], s_state_src, (), (b_xsb,))
                cp("act", xsb[:, :, 3], ps[:, 0:NS], (bps,), (b_xsb,))
                dma_out(s_state_dst, xsb[:, :, 1:4], (b_xsb,))
                ts("dve", us_t, xsb[:, :, 0], cw4[:, 0:1], cbias, ALU.mult, ALU.add, (b_xsb, bSV), (b_us,))
                for k in range(1, 4):
                    stt(us_t, xsb[:, :, k], cw4[:, k:k + 1], us_t, ALU.mult, ALU.add, (b_xsb, bSV, b_us), (b_us,))
                act(s_out, us_t, AF.Silu, (b_us,), (b_s_out,))

            def load_big_slot(slot, src, c0, ncols=512):
                srcv = src[0:1024, c0:c0 + ncols].rearrange("(k p) n -> p k n", p=128)
                dma_in("pool", wt_big[slot][:, :, 0:ncols], srcv, (), (bWTbig[slot],))
                return wt_big[slot], bWTbig[slot]

            def load_bc(g):
                par = g % 2
                for (tile_, buf_, c0) in ((wsmB[par], b_wsmB[par], C_SB + g * 128), (wsmC[par], b_wsmC[par], C_SC + g * 128)):
                    dma_in("pool", tile_, w_in[0:1024, c0:c0 + 128].rearrange("(k p) n -> p k n", p=128), (), (buf_,))

            wX, wXb_ = load_big_slot(0, w_in, C_SX)
            load_bc(0)
            for g in range(4):
                wzs, b_wz = load_big_slot(1, w_in, C_SZ + g * 512)
                if g + 1 < 4:
                    load_bc(g + 1)
                par = g % 2
                ch = 16 + g
                conv_chunk(wsmB[par], b_wsmB[par], 0, scw_t[:, ch * 4:(ch + 1) * 4], scb_t[:, ch:ch + 1], BTg, b_BT,
                           o_sc_p[:, ch, :], ssd_cvT[ch * 128:(ch + 1) * 128, :, :], BsT[:, g, :], b_BsT, o_sc_s[:, ch, :, :])
                ch = 20 + g
                conv_chunk(wsmC[par], b_wsmC[par], 0, scw_t[:, ch * 4:(ch + 1) * 4], scb_t[:, ch:ch + 1], CTg, b_CT,
                           o_sc_p[:, ch, :], ssd_cvT[ch * 128:(ch + 1) * 128, :, :], CsT[:, g, :], b_CsT, o_sc_s[:, ch, :, :])
                wtile, wb = wt_big[0], bWTbig[0]
                for i in range(4):
                    ch = g * 4 + i
                    conv_chunk(wtile, wb, i * 128, scw_t[:, ch * 4:(ch + 1) * 4], scb_t[:, ch:ch + 1], xTg[:, i, :], b_xTg[i],
                               o_sc_p[:, ch, :], ssd_cvT[ch * 128:(ch + 1) * 128, :, :], xsT_all[:, ch, :], b_xsT, o_sc_s[:, ch, :, :])
                if g + 1 < 4:
                    load_big_slot(0, w_in, C_SX + (g + 1) * 512)
                dma_in("sp", nw_b, normw_row[:, g * 512:(g + 1) * 512].partition_broadcast(128), (), (b_bc,))
                for i in range(4):
                    ps, bps = next_pf()
                    mm_group(ps[:, 0:NS], [(wzs[:, k, i * 128:(i + 1) * 128], hT[:, k, T:TP]) for k in range(KC)], (bHTs, b_wz), bps)
                    act(szs[:, g * 4 + i, :], ps[:, 0:NS], AF.Silu, (bps,), (b_szs,))
                def bufs(c):
                    par = c % 2
                    return (x_tok_d[par], xs_d[par], xsc_d[par], B_tok_d[par], MT_d[par], sz_d[par],
                            bd_xtok[par], bd_xs[par], bd_xsc[par], bd_Btok[par], bd_MT[par], bd_sz[par],
                            t_a_d[par], bd_ta[par], st8_d[par], bd_st8[par])

                def S4(c, g=g):
                    (x_tok_t, xs_t, xsc_t, B_tok, MT, sz_t, b_xtok, b_xs_, b_xsc, b_Btok, b_MT, b_sz, t_a, b_ta, st8, b_st8) = bufs(c)
                    cs = slice(c * L, (c + 1) * L)

                    def fnY(e):
                        ins = None
                        for h8 in range(8):
                            ins = e.matmul(PF[3][:, h8 * 64:(h8 + 1) * 64], lhsT=MT[:, h8, :], rhs=x_tok_t[:, h8 * 64:(h8 + 1) * 64],
                                           start=True, stop=True)
                        return ins
                    if c > 0:
                        mm_group(PF[4][:, :], [(CTg[:, cs], hst_bf)], (b_CT, b_hbf), bPF[4])
                    mm_group(PF[5][:, :], [(B_tok, xsc_t)], (b_Btok, b_xsc), bPF[5])
                    S.op("pe", fnY, (b_MT, b_xtok), (bPF[3],))

                def Dskip(c, g=g):
                    (x_tok_t, xs_t, xsc_t, B_tok, MT, sz_t, b_xtok, b_xs_, b_xsc, b_Btok, b_MT, b_sz, t_a, b_ta, st8, b_st8) = bufs(c)
                    tt("pool", v3(t_b, 8), v3(x_tok_t, 8), D_b[:, g * 8:(g + 1) * 8].unsqueeze(2).to_broadcast([128, 8, 64]), ALU.mult,
                       (b_xtok, b_bc), (b_tb,))

                def S7(c, g=g):
                    (x_tok_t, xs_t, xsc_t, B_tok, MT, sz_t, b_xtok, b_xs_, b_xsc, b_Btok, b_MT, b_sz, t_a, b_ta, st8, b_st8) = bufs(c)
                    stt(yn_t, t_a, st8[:, 3:4], nw_b, ALU.mult, ALU.mult, (b_ta, b_st8, b_bc), (b_yn,))

                def S5a(c, g=g):
                    hb = c * 32 + g * 8
                    if c > 0:
                        tt("dve", v3(hst, 8), v3(hst, 8), cdec_t[:, hb:hb + 8].unsqueeze(2).to_broadcast([128, 8, 64]), ALU.mult,
                           (b_hst, b_dtf), (b_hst,))
                        tt("dve", hst, hst, PF[5][:, :], ALU.add, (b_hst, bPF[5]), (b_hst,))
                    else:
                        cp("dve", hst, PF[5][:, :], (bPF[5],), (b_hst,))
                    if c == NCH - 1:
                        dma_out(o_sh_p[:, g * 512:(g + 1) * 512], hst, (b_hst,))

                def S5b(c, g=g):
                    (x_tok_t, xs_t, xsc_t, B_tok, MT, sz_t, b_xtok, b_xs_, b_xsc, b_Btok, b_MT, b_sz, t_a, b_ta, st8, b_st8) = bufs(c)
                    hb = c * 32 + g * 8
                    if c > 0:
                        tt("dve", v3(t_a, 8), v3(PF[4][:, :], 8), e_t[:, hb:hb + 8].unsqueeze(2).to_broadcast([128, 8, 64]), ALU.mult,
                           (bPF[4], b_dtf), (b_ta,))
                        tt("dve", t_a, t_a, PF[3][:, :], ALU.add, (b_ta, bPF[3]), (b_ta,))
                    else:
                        cp("dve", t_a, PF[3][:, :], (bPF[3],), (b_ta,))
                    tt("dve", t_a, t_a, t_b, ALU.add, (b_ta, b_tb), (b_ta,))
                    tt("dve", t_a, t_a, sz_t, ALU.mult, (b_ta, b_sz), (b_ta,))

                def S1(c, g=g):
                    cs = slice(c * L, (c + 1) * L)

                    def fnT(e, cs=cs):
                        ins = None
                        for i in range(4):
                            ins = e.transpose(pB0[:, i * 128:(i + 1) * 128], xTg[:, i, cs], identB[:])
                        ins = e.transpose(pB0[:, 512:640], BTg[:, cs], identB[:])
                        return ins
                    S.op("pe", fnT, tuple(b_xTg) + (b_BT, bCONST), (bPB[0],))

                def S1b(c, g=g):
                    cs = slice(c * L, (c + 1) * L)
                    mm_group(PF[0][:, 0:128], [(BTg[:, cs], CTg[:, cs])], (b_BT, b_CT), bPF[0])

                def S1c(c, g=g):
                    cs = slice(c * L, (c + 1) * L)
                    mm_group(PF[0][:, :], [(hT[:, k, cs], wzs[:, k, :]) for k in range(KC)], tuple(bHT) + (b_wz,), bPF[0])

                def S1d(c, half, g=g):
                    hb = c * 32 + g * 8
                    accp, baccp = PF[1 + half], bPF[1 + half]

                    def fnA(e, half=half, hb=hb, accp=accp):
                        ins = e.matmul(accp[:, :], lhsT=negI, rhs=Lmask, start=True, stop=False)
                        for hh in range(4):
                            col = hb + half * 4 + hh
                            ins = e.matmul(accp[:, hh * 128:(hh + 1) * 128], lhsT=adt_t[:, col:col + 1].to_broadcast([128, 128]),
                                           rhs=triF[:], start=False, stop=(hh == 3))
                        return ins
                    S.op("pe", fnA, (b_dtf, bCONST, b_msk), (baccp,))

                def copies(c, g=g):
                    (x_tok_t, xs_t, xsc_t, B_tok, MT, sz_t, b_xtok, b_xs_, b_xsc, b_Btok, b_MT, b_sz, t_a, b_ta, st8, b_st8) = bufs(c)
                    cp("act", x_tok_t, pB0[:, 0:512], (bPB[0],), (b_xtok,))
                    cp("act", B_tok, pB0[:, 512:640], (bPB[0],), (b_Btok,))

                def poolx(c, g=g):
                    (x_tok_t, xs_t, xsc_t, B_tok, MT, sz_t, b_xtok, b_xs_, b_xsc, b_Btok, b_MT, b_sz, t_a, b_ta, st8, b_st8) = bufs(c)
                    hb = c * 32 + g * 8
                    tt("pool", v3(xsc_t, 8), v3(x_tok_t, 8), w1_t[:, hb:hb + 8].unsqueeze(2).to_broadcast([128, 8, 64]), ALU.mult,
                       (b_xtok, b_dtf), (b_xsc,))

                def tanhz(c, g=g):
                    (x_tok_t, xs_t, xsc_t, B_tok, MT, sz_t, b_xtok, b_xs_, b_xsc, b_Btok, b_MT, b_sz, t_a, b_ta, st8, b_st8) = bufs(c)
                    act(sz_t, PF[0][:, :], AF.Tanh, (bPF[0],), (b_sz,), scale=0.5)

                def exps(c, half, g=g):
                    hb = c * 32 + g * 8
                    accp, baccp = PF[1 + half], bPF[1 + half]
                    for hh in range(4):
                        col = hb + half * 4 + hh
                        act(ex_t[half][:, hh * 128:(hh + 1) * 128], accp[:, hh * 128:(hh + 1) * 128], AF.Exp,
                            (baccp, b_dtf), (b_ex[half],), bias=nacs_t[:, col:col + 1])

                def S3a(c, g=g):
                    tt("dve", CBm, PF[0][:, 0:128], triF[:], ALU.mult, (bPF[0], bCONST), (b_CBm,))

                def S3b(c, g=g):
                    (x_tok_t, xs_t, xsc_t, B_tok, MT, sz_t, b_xtok, b_xs_, b_xsc, b_Btok, b_MT, b_sz, t_a, b_ta, st8, b_st8) = bufs(c)
                    stt(sz_t, sz_t, 1.0, PF[0][:, :], ALU.add, ALU.mult, (b_sz, bPF[0]), (b_sz,))

                def S3c(c, half, g=g):
                    (x_tok_t, xs_t, xsc_t, B_tok, MT, sz_t, b_xtok, b_xs_, b_xsc, b_Btok, b_MT, b_sz, t_a, b_ta, st8, b_st8) = bufs(c)
                    stt(MT[:, half * 4:(half + 1) * 4, :], v3(ex_t[half], 4), 1.0e30, CBm.unsqueeze(1).to_broadcast([128, 4, 128]),
                        ALU.min, ALU.mult, (b_ex[half], b_CBm), (b_MT,))

                def S6(c, g=g):
                    (x_tok_t, xs_t, xsc_t, B_tok, MT, sz_t, b_xtok, b_xs_, b_xsc, b_Btok, b_MT, b_sz, t_a, b_ta, st8, b_st8) = bufs(c)
                    act(junk, t_a, AF.Square, (b_ta,), (b_junk, b_st8), accum_out=st8[:, 0:1])
                    ts("pool", st8[:, 1:2], st8[:, 0:1], 1.0 / 512.0, 4.0 * RMS_EPS, ALU.mult, ALU.add, (b_st8,), (b_st8,))
                    tt("pool", st8[:, 3:4], st8[:, 1:2], mhalf[:, 0:1], ALU.pow, (b_st8, b_bc), (b_st8,))

                def S8(c, g=g):
                    cs = slice(c * L, (c + 1) * L)

                    def fnT2(e):
                        ins = None
                        for i in range(4):
                            ins = e.transpose(pB1[:, i * 128:(i + 1) * 128], yn_t[:, i * 128:(i + 1) * 128], identB[:])
                        return ins
                    S.op("pe", fnT2, (b_yn, bCONST), (bPB[1],))
                    cp("act", yssd[:, g * 4:(g + 1) * 4, cs], v3(pB1[:, 0:512], 4), (bPB[1],), tuple(bYS[g * 4:(g + 1) * 4]))

                for s_ in range(-1, NCH + 1):
                    cur, nxt, prv = s_, s_ + 1, s_ - 1
                    hc = 0 <= cur < NCH
                    hn = 0 <= nxt < NCH
                    hp = 0 <= prv < NCH
                    if hc:
                        S4(cur)
                        Dskip(cur)
                    if hc:
                        S5a(cur)
                    if hp:
                        S7(prv)
                    if hn:
                        S1(nxt)
                        S1d(nxt, 0)
                        S1d(nxt, 1)
                        S1b(nxt)
                        S3a(nxt)
                        copies(nxt)
                        poolx(nxt)
                        exps(nxt, 0)
                    if hn:
                        S1c(nxt)
                        exps(nxt, 1)
                    if hc and cur < NCH - 1:
                        cp("act", hst_bf, hst, (b_hst,), (b_hbf,))
                    if hc:
                        S5b(cur)
                    if hn:
                        S3c(nxt, 0)
                        tanhz(nxt)
                        S3c(nxt, 1)
                        S3b(nxt)
                    if hc:
                        S6(cur)
                    if hp:
                        S8(prv)

            _stop_at(2)
            S.barrier(allbufs)
            P1s = Carver(0)
            H0 = [v3(P1s.f32(2048), 16), v3(P1s.f32(2048), 16)]
            HN = [v3(P1s.f32(2048), 16), v3(P1s.f32(2048), 16)]
            T1 = v3(P1s.f32(2048), 16); T2 = v3(P1s.f32(2048), 16)
            assert P1s.pos <= persist_start, (P1s.pos, persist_start)
            P1s2 = Carver(persist_end)
            Bb = v3(P1s2.f32(512), 4); Cb = v3(P1s2.f32(512), 4)
            T3 = v3(P1s2.f32(2048), 16)
            esel = P1s2.f32(2048)
            dAx = v3(P1s2.f32(256), 16); xdt = v3(P1s2.f32(256), 16); ysT = v3(P1s2.f32(256), 16)
            gys = v3(P1s2.f32(256), 16); sqs = v3(P1s2.f32(256), 16); rs_t = v3(P1s2.f32(64), 4)
            b_H0 = [mk("H0a"), mk("H0b")]; b_HN = [mk("HNa"), mk("HNb")]; b_T1 = mk("T1"); b_T2 = mk("T2"); b_T3 = mk("T3")
            b_Bb = mk("Bb"); b_Cb = mk("Cb"); b_esel = mk("esel"); b_dAx = mk("dAx"); b_xdt = mk("xdt"); b_ysT = mk("ysT")
            b_gys = mk("gys"); b_sqs = mk("sqs"); b_rs = mk("rs")
            dma_in("sp", esel, c_esel[:, :], (), (b_esel,))
            exps, bexps = PF[0], bPF[0]
            _stop_at(2.05)

            def fnE(e):
                ins = None
                for j in range(16):
                    ins = e.matmul(exps[:, j * 16:(j + 1) * 16], lhsT=esel[:, j * 128:(j + 1) * 128], rhs=dAT_s, start=True, stop=True)
                for j in range(16):
                    ins = e.matmul(exps[:, 256 + j * 16:256 + (j + 1) * 16], lhsT=esel[:, j * 128:(j + 1) * 128], rhs=dtT_s,
                                   start=True, stop=True)
                return ins
            S.op("pe", fnE, (b_esel, b_dts), (bexps,))
            _stop_at(2.07)
            cp("act", dAx, v3(exps[:, 0:256], 16), (bexps,), (b_dAx,))
            _stop_at(2.08)
            cp("act", xdt, v3(exps[:, 256:512], 16), (bexps,), (b_xdt,))
            tt("dve", xdt, xdt, xsT_all, ALU.mult, (b_xdt, b_xsT), (b_xdt,))
            _stop_at(2.1)
            h0v = ssd_h0.rearrange("s (j e) p n -> s (e p) j n", e=2)
            ohv = o_sh_s.rearrange("s (j e) p n -> s (e p) j n", e=2)
            for s in range(NS):
                sl = s % 2
                dma_in("act", H0[sl], h0v[s], (), (b_H0[sl],))
                bps_, bbps_ = PF[1], bPF[1]
                cps_, bcps_ = PF[2], bPF[2]

                def fnB(e, s=s):
                    ins = None
                    for g in range(4):
                        ins = e.matmul(PF[1][:, g * 128:(g + 1) * 128], lhsT=BsT[:, g, s:s + 1].to_broadcast([128, 128]), rhs=identF[:],
                                       start=True, stop=True)
                    return ins

                def fnC(e, s=s):
                    ins = None
                    for g in range(4):
                        ins = e.matmul(PF[2][:, g * 128:(g + 1) * 128], lhsT=CsT[:, g, s:s + 1].to_broadcast([128, 128]), rhs=identF[:],
                                       start=True, stop=True)
                    return ins
                S.op("pe", fnB, (b_BsT, bCONST), (bbps_,))
                S.op("pe", fnC, (b_CsT, bCONST), (bcps_,))
                cp("act", Bb, v3(PF[1][:, :], 4), (bbps_,), (b_Bb,))
                cp("act", Cb, v3(PF[2][:, :], 4), (bcps_,), (b_Cb,))
                _stop_at(2.2)
                _stop_at(2.3)
                xdt4 = xdt[:, :, s:s + 1].rearrange("p (g j) o -> p g j o", g=4).to_broadcast([128, 4, 4, 128])
                Bb4 = Bb.unsqueeze(2).to_broadcast([128, 4, 4, 128])
                Cb4 = Cb.unsqueeze(2).to_broadcast([128, 4, 4, 128])
                tt("pool", T2.rearrange("p (g j) n -> p g j n", g=4), xdt4, Bb4, ALU.mult, (b_xdt, b_Bb), (b_T2,))
                _stop_at(2.4)
                for j_ in range(16):
                    stt(HN[sl][:, j_, :], H0[sl][:, j_, :], dAx[:, j_, s:s + 1], T2[:, j_, :], ALU.mult, ALU.add,
                        (b_H0[sl], b_dAx, b_T2), (b_HN[sl],))
                dma_out(ohv[s], HN[sl], (b_HN[sl],))
                tt("dve", T3.rearrange("p (g j) n -> p g j n", g=4), HN[sl].rearrange("p (g j) n -> p g j n", g=4), Cb4, ALU.mult,
                   (b_HN[sl], b_Cb), (b_T3,))
                _stop_at(2.5)
                S.op("dve", lambda e, s=s: e.tensor_reduce(out=ysT[:, :, s], in_=T3, axis=AX.X, op=ALU.add), (b_T3,), (b_ysT,))
                _stop_at(2.6)
            _stop_at(2.7)
            tt("dve", gys, xsT_all, dx_t.unsqueeze(2).to_broadcast([128, 16, NS]), ALU.mult, (b_xsT, bSV), (b_gys,))
            tt("dve", ysT, ysT, gys, ALU.add, (b_ysT, b_gys), (b_ysT,))
            tt("dve", gys, ysT, szs, ALU.mult, (b_ysT, b_szs), (b_gys,))
            tt("dve", sqs, gys, gys, ALU.mult, (b_gys,), (b_sqs,))
            ssp, bssp = PF[3], bPF[3]

            def fnS(e):
                ins = None
                for g in range(4):
                    for i in range(4):
                        ins = e.matmul(ssp[:, g * 16:(g + 1) * 16], lhsT=onesF[:], rhs=sqs[:, g * 4 + i, :], start=(i == 0), stop=(i == 3))
                return ins
            S.op("pe", fnS, (b_sqs, bCONST), (bssp,))
            ts("dve", rs_t, v3(ssp[:, 0:64], 4), 1.0 / 512.0, RMS_EPS, ALU.mult, ALU.add, (bssp,), (b_rs,))
            act(rs_t, rs_t, AF.Sqrt, (b_rs,), (b_rs,))
            S.op("dve", lambda e: e.reciprocal(out=rs_t, in_=rs_t), (b_rs,), (b_rs,))
            tt("dve", gys.rearrange("p (g j) s -> p g j s", g=4), gys.rearrange("p (g j) s -> p g j s", g=4),
               rs_t.unsqueeze(2).to_broadcast([128, 4, 4, NS]), ALU.mult, (b_gys, b_rs), (b_gys,))
            tt("dve", gys, gys, nwT_t.unsqueeze(2).to_broadcast([128, 16, NS]), ALU.mult, (b_gys, bSV), (b_gys,))
            cp("act", yssd[:, :, T:TP], gys, (b_gys,), (bYSs,))

            _stop_at(3)
            S.barrier(allbufs)
            P2 = Carver(0)
            ylru = v3(P2.bf16(8 * TP), 8)
            xbuf2 = P2.bf16(T + 8); dg2 = v3(P2.bf16(512), 4); tail2 = P2.f32(4)
            u2 = P2.f32(T); ubf = P2.bf16(T)
            r_t = P2.f32(T); i_t = P2.f32(T); a_t = P2.f32(T); m_t = P2.f32(T)
            lxs = v3(P2.f32(NS * 4), NS); lus = P2.f32(NS); lusb = P2.bf16(NS); lr = P2.f32(NS); li = P2.f32(NS); la = P2.f32(NS)
            lm = P2.f32(NS); lh0 = v3(P2.f32(8 * NS), 8); lhn = v3(P2.f32(8 * NS), 8); lhp = P2.f32(8)
            hba = P2.f32(8); hbx = P2.f32(8); hcv = P2.f32(8); q25 = P2.f32(1)
            wab = v3(P2.bf16(8 * 128), 8); wxb = v3(P2.bf16(8 * 128), 8)
            bYL = [mk("ylru%d" % k) for k in range(8)]; bYLs = mk("ylru_s")
            b_x2 = mk("xbuf2"); b_x2q = [mk("xbuf2q%d" % q) for q in range(4)]; b_dg2 = mk("dg2"); b_tail2 = mk("tail2")
            b_u2 = mk("u2"); b_ubf = mk("ubf"); b_r = mk("r"); b_i = mk("i"); b_a = mk("a"); b_m = mk("m")
            b_lxs = mk("lxs"); b_lus = mk("lus"); b_lsm = mk("lsm"); b_lh0 = mk("lh0"); b_lhn = mk("lhn"); b_lhp = mk("lhp"); b_wab = mk("wab")
            b_hv = mk("halfvecs")
            S.op("dve", lambda e: e.memset(xbuf2[:, 0:3], 0.0), (), (b_x2,))
            ts("dve", hba, lba_t, 0.5, None, ALU.mult, None, (bSV,), (b_hv,))
            ts("dve", hbx, lbx_t, 0.5, None, ALU.mult, None, (bSV,), (b_hv,))
            ts("dve", hcv, cvec_t, 0.5, None, ALU.mult, None, (bSV,), (b_hv,))
            S.op("dve", lambda e: e.memset(q25, 0.25), (), (b_hv,))
            S.op("pool", lambda e: e.memset(wab, 0.0), (), (b_wab,))
            S.op("pool", lambda e: e.memset(wxb, 0.0), (), (b_wab,))
            for (dst_, src_) in ((wab, lwa), (wxb, lwx)):
                sv_ = src_.rearrange("(j e) k m -> e k j m", e=2)
                for e_ in range(2):
                    dma_in("pool", dst_[e_ * 64:(e_ + 1) * 64, :, e_ * 64:(e_ + 1) * 64], sv_[e_], (), (b_wab,))
            dma_in("sp", lh0, lru_h0T.rearrange("(k p) n -> p k n", p=128), (), (b_lh0,))
            _stop_at(3.05)
            for pp in range(2):
                wX, wXb = load_wbig(w_in, 0, C_LX + pp * 512, 512)
                wZ, wZb = load_wbig(w_in, 0, C_LZ + pp * 512, 512)
                for j4 in range(4):
                    j = pp * 4 + j4
                    co = j4 * 128
                    cw4 = lcw_t[:, j * 4:(j + 1) * 4]
                    tt("pool", dg2, identF[:].unsqueeze(1).to_broadcast([128, 4, 128]), cw4.unsqueeze(2).to_broadcast([128, 4, 128]), ALU.mult,
                       (bCONST, bSV), (b_dg2,))

                    def proj_q(q, j=j, co=co, wX=wX, wXb=wXb):
                        ps, bps = next_pf()
                        mm_group(ps[:, :], [(wX[:, k, co:co + 128], hT[:, k, q * 512:(q + 1) * 512]) for k in range(KC)], tuple(bHT) + (wXb,), bps)
                        cp("act", xbuf2[:, 3 + q * 512:3 + (q + 1) * 512], ps[:, :], (bps,), (b_x2q[q],))
                        if q == 3:
                            cp("act", tail2[:, 0:3], ps[:, 509:512], (bps,), (b_tail2,))
                            dma_out(o_lc_p[:, j, :], tail2[:, 0:3], (b_tail2,))

                    def conv_q(q, j=j):
                        qs = slice(q * 512, (q + 1) * 512)
                        ps2, bps2 = next_pf()
                        rd = (b_dg2, b_x2q[q]) + ((b_x2q[q - 1],) if q > 0 else (b_x2,))
                        mm_group(ps2[:, :], [(dg2[:, k, :], xbuf2[:, q * 512 + k:q * 512 + k + 512]) for k in range(4)], rd, bps2)
                        _stop_at(3.06)
                        act(u2[:, qs], ps2[:, :], AF.Identity, (bps2, bSV), (b_u2,), bias=lcb_t[:, j:j + 1])
                        _stop_at(3.07)
                        cp("dve", ubf[:, qs], u2[:, qs], (b_u2,), (b_ubf,))
                        _stop_at(3.08)
                        ps, bps = next_pf()
                        mm_group(ps[:, :], [(wab[:, j, :], ubf[:, qs])], (b_wab, b_ubf), bps)
                        act(r_t[:, qs], ps[:, :], AF.Tanh, (bps, b_hv), (b_r,), bias=hba[:, j:j + 1], scale=0.5)
                        ps, bps = next_pf()
                        mm_group(ps[:, :], [(wxb[:, j, :], ubf[:, qs])], (b_wab, b_ubf), bps)
                        act(i_t[:, qs], ps[:, :], AF.Tanh, (bps, b_hv), (b_i,), bias=hbx[:, j:j + 1], scale=0.5)
                    proj_q(0); proj_q(1); conv_q(0); proj_q(2); conv_q(1); proj_q(3); conv_q(2); conv_q(3)
                    _stop_at(3.1)
                    ps, bps = next_pf()
                    mm_group(ps[:, 0:NS], [(wX[:, k, co:co + 128], hT[:, k, T:TP]) for k in range(KC)], (bHTs, wXb), bps)
                    dma_in("sp", lxs[:, :, 0:3], lru_cvT[j * 128:(j + 1) * 128, :, :], (), (b_lxs,))
                    cp("act", lxs[:, :, 3], ps[:, 0:NS], (bps,), (b_lxs,))
                    dma_out(o_lc_s[:, j, :, :], lxs[:, :, 1:4], (b_lxs,))
                    ts("dve", lus, lxs[:, :, 0], cw4[:, 0:1], lcb_t[:, j:j + 1], ALU.mult, ALU.add, (b_lxs, bSV), (b_lus,))
                    for k in range(1, 4):
                        stt(lus, lxs[:, :, k], cw4[:, k:k + 1], lus, ALU.mult, ALU.add, (b_lxs, bSV, b_lus), (b_lus,))
                    cp("dve", lusb, lus, (b_lus,), (b_lus,))
                    ps, bps = next_pf()
                    mm_group(ps[:, 0:NS], [(wab[:, j, :], lusb)], (b_wab, b_lus), bps)
                    mm_group(ps[:, 32:32 + NS], [(wxb[:, j, :], lusb)], (b_wab, b_lus), bps)
                    act(lr, ps[:, 0:NS], AF.Tanh, (bps, b_hv), (b_lsm,), bias=hba[:, j:j + 1], scale=0.5)
                    act(li, ps[:, 32:32 + NS], AF.Tanh, (bps, b_hv), (b_lsm,), bias=hbx[:, j:j + 1], scale=0.5)
                    _stop_at(3.2)
                    act(a_t, r_t, AF.Exp, (b_r, b_hv), (b_a,), scale=hcv[:, j:j + 1], bias=hcv[:, j:j + 1])
                    act(m_t, r_t, AF.Exp, (b_r, bSV), (b_m,), scale=cvec_t[:, j:j + 1], bias=cvec_t[:, j:j + 1])
                    act(la, lr, AF.Exp, (b_lsm, b_hv), (b_lsm,), scale=hcv[:, j:j + 1], bias=hcv[:, j:j + 1])
                    act(lm, lr, AF.Exp, (b_lsm, bSV), (b_lsm,), scale=cvec_t[:, j:j + 1], bias=cvec_t[:, j:j + 1])
                    act(m_t, m_t, AF.Sqrt, (b_m, b_hv), (b_m,), scale=-0.25, bias=q25[:, 0:1])
                    act(lm, lm, AF.Sqrt, (b_lsm, b_hv), (b_lsm,), scale=-0.25, bias=q25[:, 0:1])
                    _stop_at(3.3)
                    stt(i_t, i_t, 1.0, u2, ALU.add, ALU.mult, (b_i, b_u2), (b_i,))
                    tt("dve", m_t[:, 1:T], m_t[:, 1:T], i_t[:, 1:T], ALU.mult, (b_m, b_i), (b_m,))
                    ts("dve", m_t[:, 0:1], i_t[:, 0:1], 0.5, None, ALU.mult, None, (b_m, b_i), (b_m,))
                    S.op("dve", lambda e: e.tensor_tensor_scan(out=r_t, data0=a_t, data1=m_t, initial=0.0, op0=ALU.mult, op1=ALU.add),
                         (b_a, b_m, b_r), (b_r,))
                    cp("dve", lhp[:, j:j + 1], r_t[:, T - 1:T], (b_r,), (b_lhp,))
                    _stop_at(3.4)
                    stt(li, li, 1.0, lus, ALU.add, ALU.mult, (b_lsm, b_lus), (b_lsm,))
                    tt("dve", lm, lm, li, ALU.mult, (b_lsm,), (b_lsm,))
                    tt("dve", la, la, lh0[:, j, :], ALU.mult, (b_lsm, b_lh0), (b_lsm,))
                    tt("dve", lhn[:, j, :], la, lm, ALU.add, (b_lsm,), (b_lhn,))
                    _stop_at(3.5)
                    for q in range(4):
                        qs = slice(q * 512, (q + 1) * 512)
                        ps, bps = next_pf()
                        mm_group(ps[:, :], [(wZ[:, k, co:co + 128], hT[:, k, qs]) for k in range(KC)], tuple(bHT) + (wZb,), bps)
                        act(a_t[:, qs], ps[:, :], AF.Tanh, (bps, b_a), (b_a,), scale=0.5)
                        stt(a_t[:, qs], a_t[:, qs], 1.0, ps[:, :], ALU.add, ALU.mult, (b_a, bps), (b_a,))
                    stt(ylru[:, j, 0:T], a_t, 0.5, r_t, ALU.mult, ALU.mult, (b_r, b_a), (bYL[j],))
                    ps, bps = next_pf()
                    mm_group(ps[:, 0:NS], [(wZ[:, k, co:co + 128], hT[:, k, T:TP]) for k in range(KC)], (bHTs, wZb), bps)
                    act(lr, ps[:, 0:NS], AF.Tanh, (bps,), (b_lsm,), scale=0.5)
                    stt(lr, lr, 1.0, ps[:, 0:NS], ALU.add, ALU.mult, (b_lsm, bps), (b_lsm,))
                    stt(ylru[:, j, T:TP], lr, 0.5, lhn[:, j, :], ALU.mult, ALU.mult, (b_lsm, b_lhn), (bYLs,))
            dma_out(o_lh_p[:, :], lhp, (b_lhp,))
            dma_out(o_lh_s[:, :, :], lhn, (b_lhn,))

            _stop_at(4)
            S.barrier([b for b in allbufs if b not in bYL and b is not bYLs] + [bWTbig[0], bWTbig[1]] + bWTsm)
            P3 = Carver((8 * TP) // 2)
            merged = v3(P3.bf16(8 * TP), 8)
            sgA = P3.f32(512); sgB = P3.f32(512); tA = P3.f32(512); tB = P3.f32(512)
            yo_t = [P3.f32(1024), P3.f32(1024)]; gts = P3.f32(1024)
            bnst_d = [P3.f32(16), P3.f32(16)]; mh3 = P3.f32(1); junk3 = P3.bf16(1024); cbb = v3(P3.bf16(8 * 128), 8); cf2 = v3(P3.f32(8 * 17), 8); cb2 = v3(P3.bf16(8 * 17), 8)
            P3b = Carver(0)
            gate_b = P3b.f32(1024); lng_b = P3b.f32(1024); lnb_b = P3b.f32(1024); bg_b = P3b.f32(1024)
            xtk = [P3b.f32(1024), P3b.f32(1024)]; resid_d = [P3b.f32(1024), scr[:, (8 * TP) // 2 + 8 * TP // 2:(8 * TP) // 2 + 8 * TP // 2 + 1024]]; xn_d = [P3b.f32(1024), scr[:, (8 * TP) // 2 + 8 * TP // 2 + 1024:(8 * TP) // 2 + 8 * TP // 2 + 2048]]
            assert P3b.pos <= (8 * TP) // 2
            bMG = [mk("merged%d" % k) for k in range(8)]; bMGs = mk("merged_s")
            b_sgA = mk("sgA"); b_sgB = mk("sgB"); b_tA = mk("tA"); b_tB = mk("tB"); b_gateb = mk("gate_b"); b_ln = mk("lnbc")
            b_xtk = [mk("xtk0"), mk("xtk1")]; b_res_d = [mk("resid0"), mk("resid1")]; b_xn_d = [mk("xn0"), mk("xn1")]; b_bn_d = [mk("bn0"), mk("bn1")]; b_yo = [mk("yo0"), mk("yo1")]
            b_cbb = mk("cbb"); b_c2 = mk("c2"); b_gts = mk("gts"); b_j3 = mk("junk3")
            colsets = [(slice(q * 512, (q + 1) * 512), 512) for q in range(4)] + [(slice(T, TP), NS)]
            for jo in range(8):
                wA, wAb = load_wsm(w_lp, 0, jo * 128)
                wB0, wB0b = load_wsm(w_sp, 0, jo * 128)
                wB1, wB1b = load_wsm(w_sp, 1024, jo * 128)
                wgA, wgAb = load_wsm(w_in, 0, C_MA + jo * 128)
                wgB, wgBb = load_wsm(w_in, 0, C_MB + jo * 128)
                for qi, (cs, n) in enumerate(colsets):
                    smp = qi == 4
                    rH = (bHTs,) if smp else tuple(bHT)
                    rYL = (bYLs,) if smp else tuple(bYL)
                    rYS = (bYSs,) if smp else tuple(bYS)
                    pA, bpA = next_pf()
                    mm_group(pA[:, 0:n], [(wA[:, k, :], ylru[:, k, cs]) for k in range(8)], rYL + (wAb,), bpA)
                    pB, bpB = next_pf()
                    mm_group(pB[:, 0:n], [(wB0[:, k, :], yssd[:, k, cs]) for k in range(8)] + [(wB1[:, k, :], yssd[:, 8 + k, cs]) for k in range(8)],
                             rYS + (wB0b, wB1b), bpB)
                    pgA, bpgA = next_pf()
                    mm_group(pgA[:, 0:n], [(wgA[:, k, :], hT[:, k, cs]) for k in range(8)], rH + (wgAb,), bpgA)
                    pgB, bpgB = next_pf()
                    mm_group(pgB[:, 0:n], [(wgB[:, k, :], hT[:, k, cs]) for k in range(8)], rH + (wgBb,), bpgB)
                    act(sgA[:, 0:n], pgA[:, 0:n], AF.Sigmoid, (bpgA,), (b_sgA,))
                    act(sgB[:, 0:n], pgB[:, 0:n], AF.Sigmoid, (bpgB,), (b_sgB,))
                    tt("dve", tA[:, 0:n], sgA[:, 0:n], pA[:, 0:n], ALU.mult, (b_sgA, bpA), (b_tA,))
                    tt("dve", tB[:, 0:n], sgB[:, 0:n], pB[:, 0:n], ALU.mult, (b_sgB, bpB), (b_tB,))
                    tt("dve", merged[:, jo, cs], tA[:, 0:n], tB[:, 0:n], ALU.add, (b_tA, b_tB), (bMGs if smp else bMG[jo],))
            S.barrier(bWTsm + bWTbig + bYL + [bYLs, b_sgA, b_sgB, b_tA, b_tB])
            dma_in("sp", cf2, cT.rearrange("(k p) n -> p k n", p=128), (), (b_c2,))
            cp("dve", cb2, cf2, (b_c2,), (b_c2,))
            cp("dve", cbb, cf2[:, :, 0:1].to_broadcast([128, 8, 128]), (b_c2,), (b_cbb,))
            S.op("dve", lambda e: e.memset(mh3, -0.5), (), (b_ln,))
            dma_in("sp", bg_b, b_gate.partition_broadcast(128), (), (b_ln,))
            dma_in("sp", lng_b, lng_row.partition_broadcast(128), (), (b_ln,))
            dma_in("sp", lnb_b, lnb_row.partition_broadcast(128), (), (b_ln,))
            for hf in range(2):
                wg, wgb = load_wbig(w_cond, 0, 2048 + hf * 512, 512)
                ps, bps = next_pf()
                mm_group(ps[:, :], [(cbb[:, k, :], wg[:, k, :]) for k in range(8)], (b_cbb, wgb), bps)
                tt("dve", gate_b[:, hf * 512:(hf + 1) * 512], ps[:, :], bg_b[:, hf * 512:(hf + 1) * 512], ALU.add, (bps, b_ln), (b_gateb,))
                ps, bps = next_pf()
                mm_group(ps[0:NS, :], [(cb2[:, k, 1:17], wg[:, k, :]) for k in range(8)], (b_c2, wgb), bps)
                tt("dve", gts[0:NS, hf * 512:(hf + 1) * 512], ps[0:NS, :], bg_b[0:NS, hf * 512:(hf + 1) * 512], ALU.add, (bps, b_ln), (b_gts,))
            wo0, wo0b = load_wbig(w_out, 0, 0, 512)
            wo1, wo1b = load_wbig(w_out, 0, 512, 512)
            def tile_ctx(ti):
                smp = ti == NCH
                np_ = NS if smp else 128
                cs = slice(T, TP) if smp else slice(ti * 128, (ti + 1) * 128)
                sl = ti % 2
                return smp, np_, cs, sl

            def ln_stageP(ti):
                smp, np_, cs, sl = tile_ctx(ti)
                resid, bnst, b_res, b_bn = resid_d[sl], bnst_d[sl], b_res_d[sl], b_bn_d[sl]
                rM = (bMGs,) if smp else tuple(bMG)
                dma_in("act", xtk[sl][0:np_, :], xs_tok[:, :] if smp else x_tok[ti * 128:(ti + 1) * 128, :], (), (b_xtk[sl],))
                gsrc = gts if smp else gate_b
                bgs = b_gts if smp else b_gateb
                for hf, (wo, wob) in enumerate(((wo0, wo0b), (wo1, wo1b))):
                    ps, bps = next_pf()
                    mm_group(ps[0:np_, :], [(merged[:, k, cs], wo[:, k, :]) for k in range(8)], rM + (wob,), bps)
                    tt("dve", resid[0:np_, hf * 512:(hf + 1) * 512], ps[0:np_, :], gsrc[0:np_, hf * 512:(hf + 1) * 512], ALU.mult,
                       (bps, bgs), (b_res,))
                stt(resid[0:np_, :], xtk[sl][0:np_, :], ALPHA, resid[0:np_, :], ALU.mult, ALU.add, (b_xtk[sl], b_res), (b_res,))
                act(junk3[0:np_, :], resid[0:np_, :], AF.Copy, (b_res,), (b_j3, b_bn), accum_out=bnst[0:np_, 0:1])
                act(junk3[0:np_, :], resid[0:np_, :], AF.Square, (b_res,), (b_j3, b_bn), accum_out=bnst[0:np_, 1:2])
                ts("pool", bnst[0:np_, 12:13], bnst[0:np_, 0:1], 1.0 / D, None, ALU.mult, None, (b_bn,), (b_bn,))
                tt("pool", bnst[0:np_, 2:3], bnst[0:np_, 12:13], bnst[0:np_, 12:13], ALU.mult, (b_bn,), (b_bn,))
                ts("pool", bnst[0:np_, 3:4], bnst[0:np_, 1:2], 1.0 / D, LN_EPS, ALU.mult, ALU.add, (b_bn,), (b_bn,))
                tt("pool", bnst[0:np_, 14:15], bnst[0:np_, 3:4], bnst[0:np_, 2:3], ALU.subtract, (b_bn,), (b_bn,))
                tt("pool", bnst[0:np_, 15:16], bnst[0:np_, 14:15], mh3[0:np_, 0:1], ALU.pow, (b_bn, b_ln), (b_bn,))

            def ln_stageQ(ti):
                smp, np_, cs, sl = tile_ctx(ti)
                resid, xn, bnst, b_res, b_xn, b_bn = resid_d[sl], xn_d[sl], bnst_d[sl], b_res_d[sl], b_xn_d[sl], b_bn_d[sl]
                stt(xn[0:np_, :], resid[0:np_, :], bnst[0:np_, 12:13], lng_b[0:np_, :], ALU.subtract, ALU.mult, (b_res, b_bn, b_ln), (b_xn,))
                stt(yo_t[sl][0:np_, :], xn[0:np_, :], bnst[0:np_, 15:16], lnb_b[0:np_, :], ALU.mult, ALU.add, (b_xn, b_bn, b_ln), (b_yo[sl],))
                dma_out(y_s[:, :] if smp else y_p[ti * 128:(ti + 1) * 128, :], yo_t[sl][0:np_, :], (b_yo[sl],))

            for ti in range(NCH + 2):
                if ti <= NCH:
                    ln_stageP(ti)
                if ti >= 1:
                    ln_stageQ(ti - 1)

        except _StopRec:
            pass
        S.finalize()

        @block.sync
        def _(e):
            S.emit("sp", e)

        @block.gpsimd
        def _(e):
            S.emit("pool", e)

        @block.scalar
        def _(e):
            S.emit("act", e)

        @block.vector
        def _(e):
            S.emit("dve", e)

        @block.tensor
        def _(e):
            S.emit("pe", e)
    return nc


_NC_CACHE = {}


def _vec_pk(v, k):
    return np.ascontiguousarray(np.asarray(v, np.float32).reshape(k, 128).T)


def kernel(x_prompt, x_sample, state_lru_h, state_lru_conv, state_ssd_h, state_ssd_conv,
           c_prompt, c_sample, w_cond, b_cond, w_in, lru_conv_w, lru_conv_b, lru_wa, lru_ba,
           lru_wx, lru_bx, lru_lambda, ssd_conv_w, ssd_conv_b, ssd_dt_bias, ssd_a_log, ssd_d,
           ssd_norm_w, w_lru_proj, w_ssd_proj, w_out, ln_g, ln_b):
    f = lambda a: np.ascontiguousarray(np.asarray(a, np.float32))
    x_prompt, x_sample = f(x_prompt), f(x_sample)
    if "nc" not in _NC_CACHE:
        _NC_CACHE["nc"] = build_program()
    nc = _NC_CACHE["nc"]
    shared = {
        "w_cond": f(w_cond[0]), "b_condT": _vec_pk(b_cond[0], 24), "b_gate": f(np.asarray(b_cond)[0:1, 2048:3072]),
        "w_in": f(w_in[0]),
        "lcw": f(np.asarray(lru_conv_w)[0].reshape(4, 8, 128).transpose(2, 1, 0)), "lcb": _vec_pk(lru_conv_b[0], 8),
        "lwa": f(lru_wa[0]), "lwx": f(lru_wx[0]),
        "lba": _vec_pk(lru_ba[0], 8), "lbx": _vec_pk(lru_bx[0], 8), "llam": _vec_pk(lru_lambda[0], 8),
        "scw": f(np.asarray(ssd_conv_w)[0].reshape(4, 24, 128).transpose(2, 1, 0)), "scb": _vec_pk(ssd_conv_b[0], 24),
        "dtb_row": f(np.asarray(ssd_dt_bias)[0:1]), "alog_row": f(np.asarray(ssd_a_log)[0:1]), "d_row": f(np.asarray(ssd_d)[0:1]),
        "dtb_col": f(np.asarray(ssd_dt_bias)[0].reshape(32, 1)), "alog_col": f(np.asarray(ssd_a_log)[0].reshape(32, 1)),
        "d_x": _vec_pk(np.repeat(np.asarray(ssd_d, np.float32)[0], 64), 16),
        "normw_row": f(np.asarray(ssd_norm_w)[0:1]), "normwT": _vec_pk(ssd_norm_w[0], 16),
        "w_lp": f(w_lru_proj[0]), "w_sp": f(w_ssd_proj[0]), "w_out": f(w_out[0]),
        "lng_row": f(np.asarray(ln_g)[0:1]), "lnb_row": f(np.asarray(ln_b)[0:1]),
        "c_ident": np.eye(128, dtype=np.float32), "c_tri": np.triu(np.ones((128, 128), np.float32)),
        "c_esel": f((np.arange(128)[:, None] == (np.arange(2048)[None, :] // 64)).astype(np.float32)),
    }
    in_maps = []
    for i in range(NCORES):
        ss = slice(NS * i, NS * (i + 1))
        cT = np.concatenate([np.asarray(c_prompt, np.float32)[i][:, None], np.asarray(c_sample, np.float32)[ss].T], axis=1)
        m = dict(shared)
        m.update({
            "xT": f(x_prompt[i].T), "x_tok": f(x_prompt[i]),
            "xsT": f(x_sample[ss, 0, :].T), "xs_tok": f(x_sample[ss, 0, :]),
            "cT": f(cT),
            "lru_h0T": f(np.asarray(state_lru_h)[0, ss].T),
            "lru_cvT": f(np.asarray(state_lru_conv)[0, ss].transpose(2, 0, 1)),
            "ssd_h0": f(np.asarray(state_ssd_h)[0, ss]),
            "ssd_cvT": f(np.asarray(state_ssd_conv)[0, ss].transpose(2, 0, 1)),
        })
        in_maps.append(m)
    res = run_bass_kernel_spmd(nc, in_maps, core_ids=list(range(NCORES)))
    R = res.results
    y_prompt = np.stack([R[i]["y_p"] for i in range(NCORES)])
    y_sample = np.concatenate([R[i]["y_s"] for i in range(NCORES)])[:, None, :]
    lh_p = np.stack([R[i]["o_lh_p"].T.reshape(1024) for i in range(NCORES)])[None]
    lc_p = np.stack([R[i]["o_lc_p"].transpose(2, 1, 0).reshape(3, 1024) for i in range(NCORES)])[None]
    sh_p = np.stack([R[i]["o_sh_p"].T.reshape(32, 64, 128) for i in range(NCORES)])[None]
    sc_p = np.stack([R[i]["o_sc_p"].transpose(2, 1, 0).reshape(3, 3072) for i in range(NCORES)])[None]
    lh_s = np.concatenate([R[i]["o_lh_s"].transpose(2, 1, 0).reshape(NS, 1024) for i in range(NCORES)])[None]
    lc_s = np.concatenate([R[i]["o_lc_s"].transpose(2, 3, 1, 0).reshape(NS, 3, 1024) for i in range(NCORES)])[None]
    sh_s = np.concatenate([R[i]["o_sh_s"] for i in range(NCORES)])[None]
    sc_s = np.concatenate([R[i]["o_sc_s"].transpose(2, 3, 1, 0).reshape(NS, 3, 3072) for i in range(NCORES)])[None]
    c32 = lambda a: np.ascontiguousarray(a, dtype=np.float32)
    return (c32(y_prompt), c32(y_sample), c32(lh_p), c32(lc_p), c32(sh_p), c32(sc_p),
            c32(lh_s), c32(lc_s), c32(sh_s), c32(sc_s))
```

```python
import numpy as np
import concourse.bass as bass
import concourse.mybir as mybir
from concourse.bass_utils import run_bass_kernel_spmd

F32 = mybir.dt.float32
BF16 = mybir.dt.bfloat16
AF = mybir.ActivationFunctionType
ALU = mybir.AluOpType
AX = mybir.AxisListType

NCORES = 8
D = 1024
T = 2048
NS = 16
TP = T + NS
KC = 8
NCH = 16
L = 128
N_IN = 9248
C_LX, C_LZ, C_SZ, C_SX, C_SB, C_SC, C_DT, C_MA, C_MB = 0, 1024, 2048, 4096, 6144, 6656, 7168, 7200, 8224
ALPHA = 2.0 ** 0.25
LN_EPS = 1e-5
RMS_EPS = 1e-5
SAME_ENGINE_SYNC = True
RAW_ONLY = True
import os as _os
KSTOP = float(_os.environ.get('KSTOP', '99'))


class _StopRec(Exception):
    pass


def _stop_at(n):
    if KSTOP <= n:
        raise _StopRec()


class Buf:
    __slots__ = ("name", "last_w", "readers")

    def __init__(self, name):
        self.name = name
        self.last_w = None
        self.readers = []


class Op:
    __slots__ = ("eng", "fn", "deps", "is_dma", "sem", "semval", "prevval", "signal", "sig", "raw")

    def __init__(self, eng, fn, deps, is_dma):
        self.eng, self.fn, self.deps, self.is_dma = eng, fn, deps, is_dma
        self.sem = None
        self.semval = 0
        self.prevval = 0
        self.signal = False
        self.sig = 0
        self.raw = set()


class Sched:
    ENGS = ("sp", "pool", "act", "dve", "pe")

    def __init__(self, nc, esems, dma_sems):
        self.nc = nc
        self.ops = []
        self.esems = esems
        self.dma_pool = dma_sems
        self.dma_idx = {q: 0 for q in dma_sems}
        self.dma_val = {}
        self.store_ops = []

    def _deps(self, reads, writes):
        deps = set()
        self._last_raw = set()
        for b in list(reads) + list(writes):
            if b.last_w is not None:
                deps.add(b.last_w)
        for b in reads:
            if b.last_w is not None:
                self._last_raw.add(b.last_w)
        for b in writes:
            for r in b.readers:
                deps.add(r)
        return deps

    def _update(self, idx, reads, writes):
        for b in reads:
            b.readers.append(idx)
        for b in writes:
            b.last_w = idx
            b.readers = []

    def op(self, eng, fn, reads=(), writes=()):
        idx = len(self.ops)
        o = Op(eng, fn, self._deps(reads, writes), False)
        o.raw = self._last_raw
        self.ops.append(o)
        self._update(idx, reads, writes)
        return idx

    def dma(self, q, fn, n, reads=(), writes=(), store=False):
        idx = len(self.ops)
        o = Op(q, fn, self._deps(reads, writes), True)
        o.raw = self._last_raw
        pool = self.dma_pool[q]
        sem = pool[self.dma_idx[q] % len(pool)]
        self.dma_idx[q] += 1
        o.sem = sem
        o.prevval = self.dma_val.get(id(sem), 0)
        o.semval = o.prevval + 16 * n
        self.dma_val[id(sem)] = o.semval
        self.ops.append(o)
        self._update(idx, reads, writes)
        if store:
            self.store_ops.append(idx)
        return idx

    def barrier(self, bufs):
        allidx = len(self.ops)
        last = {}
        for i, o in enumerate(self.ops):
            if o.fn is not None:
                last[o.eng] = i
        dmas = [i for i, o in enumerate(self.ops) if o.is_dma]
        deps = set(last.values()) | set(dmas[-64:])
        for e in self.ENGS:
            o = Op(e, None, set(deps), False)
            self.ops.append(o)
        for b in bufs:
            b.last_w = None
            b.readers = []

    def finalize(self):
        for i, o in enumerate(self.ops):
            for d in o.deps:
                p = self.ops[d]
                if p.is_dma or p.fn is None:
                    continue
                same_ok = SAME_ENGINE_SYNC and p.eng != "pe" and (not RAW_ONLY or d in o.raw)
                if p.eng != o.eng or same_ok or o.is_dma:
                    p.signal = True
        cnt = {e: 0 for e in self.ENGS}
        for o in self.ops:
            if o.is_dma:
                continue
            if o.fn is None:
                continue
            if o.signal:
                cnt[o.eng] += 1
                o.sig = cnt[o.eng]

    def emit(self, eng_name, e):
        water = {}
        esems = self.esems

        def wait(sem, val):
            k = id(sem)
            if water.get(k, 0) >= val:
                return
            water[k] = val
            e.wait_ge(sem, val)

        for i, o in enumerate(self.ops):
            if o.eng != eng_name:
                continue
            for d in sorted(o.deps):
                p = self.ops[d]
                if p.is_dma:
                    wait(p.sem, p.semval)
                elif p.fn is None:
                    continue
                elif p.eng != eng_name or (SAME_ENGINE_SYNC and eng_name != "pe" and (not RAW_ONLY or d in o.raw)) or o.is_dma:
                    wait(esems[p.eng], p.sig)
            if o.fn is None:
                continue
            if o.is_dma:
                if o.prevval > 0:
                    wait(o.sem, o.prevval)
                o.fn(e, o.sem)
            else:
                ins = o.fn(e)
                if o.signal:
                    ins.then_inc(esems[eng_name], 1)
        if eng_name == "sp":
            for idx in self.store_ops:
                o = self.ops[idx]
                wait(o.sem, o.semval)


def build_program():
    nc = bass.Bass("TRN2", target_bir_lowering=False)
    din, dout = {}, {}

    def inp(name, shape):
        din[name] = nc.dram_tensor(name, list(shape), F32, kind="ExternalInput").ap()
        return din[name]

    def outp(name, shape):
        dout[name] = nc.dram_tensor(name, list(shape), F32, kind="ExternalOutput").ap()
        return dout[name]

    xT = inp("xT", [D, T]); x_tok = inp("x_tok", [T, D])
    xsT = inp("xsT", [D, NS]); xs_tok = inp("xs_tok", [NS, D])
    cT = inp("cT", [D, 17])
    lru_h0T = inp("lru_h0T", [D, NS]); lru_cvT = inp("lru_cvT", [D, NS, 3])
    ssd_h0 = inp("ssd_h0", [NS, 32, 64, 128]); ssd_cvT = inp("ssd_cvT", [3072, NS, 3])
    w_cond = inp("w_cond", [D, 3072]); b_condT = inp("b_condT", [128, 24]); b_gate = inp("b_gate", [1, 1024])
    w_in = inp("w_in", [D, N_IN])
    lcw = inp("lcw", [128, 8, 4]); lcb = inp("lcb", [128, 8])
    lwa = inp("lwa", [16, 64, 64]); lwx = inp("lwx", [16, 64, 64])
    lba = inp("lba", [128, 8]); lbx = inp("lbx", [128, 8]); llam = inp("llam", [128, 8])
    scw = inp("scw", [128, 24, 4]); scb = inp("scb", [128, 24])
    dtb_row = inp("dtb_row", [1, 32]); alog_row = inp("alog_row", [1, 32]); d_row = inp("d_row", [1, 32])
    dtb_col = inp("dtb_col", [32, 1]); alog_col = inp("alog_col", [32, 1])
    d_x = inp("d_x", [128, 16]); normw_row = inp("normw_row", [1, 2048]); normwT = inp("normwT", [128, 16])
    w_lp = inp("w_lp", [D, D]); w_sp = inp("w_sp", [2048, D]); w_out = inp("w_out", [D, D])
    lng_row = inp("lng_row", [1, D]); lnb_row = inp("lnb_row", [1, D])
    c_ident = inp("c_ident", [128, 128]); c_tri = inp("c_tri", [128, 128]); c_esel = inp("c_esel", [128, 2048])

    y_p = outp("y_p", [T, D]); y_s = outp("y_s", [NS, D])
    o_lh_p = outp("o_lh_p", [128, 8]); o_lc_p = outp("o_lc_p", [128, 8, 3])
    o_sh_p = outp("o_sh_p", [128, 2048]); o_sc_p = outp("o_sc_p", [128, 24, 3])
    o_lh_s = outp("o_lh_s", [128, 8, NS]); o_lc_s = outp("o_lc_s", [128, 8, NS, 3])
    o_sh_s = outp("o_sh_s", [NS, 32, 64, 128]); o_sc_s = outp("o_sc_s", [128, 24, NS, 3])

    SCRN = 23040
    from contextlib import ExitStack
    with ExitStack() as _es:
        _en = _es.enter_context
        hT = _en(nc.sbuf_tensor("hT", [128, KC, TP], BF16))
        yssd = _en(nc.sbuf_tensor("yssd", [128, 16, TP], BF16))
        wt = _en(nc.sbuf_tensor("wt", [128, 8192], BF16))
        scr = _en(nc.sbuf_tensor("scr", [128, SCRN], F32))
        identF = _en(nc.sbuf_tensor("identF", [128, 128], F32))
        identB = _en(nc.sbuf_tensor("identB", [128, 128], BF16))
        triF = _en(nc.sbuf_tensor("triF", [128, 128], F32))
        onesF = _en(nc.sbuf_tensor("onesF", [128, 128], F32))
        smallv = _en(nc.sbuf_tensor("smallv", [128, 512], F32))
        modT = _en(nc.sbuf_tensor("modT", [128, 16, 17], F32))
        pF0, pF1, pF2, pF3, pF4, pF5 = [_en(nc.psum_tensor("pF%d" % i, [128, 512], F32)) for i in range(6)]
        pB0 = _en(nc.psum_tensor("pB0", [128, 1024], BF16))
        pB1 = _en(nc.psum_tensor("pB1", [128, 1024], BF16))
        s_pool, s_act, s_dve, s_pe, s_sp = [_en(nc.semaphore(n)) for n in ("s_pool", "s_act", "s_dve", "s_pe", "s_sp")]
        dq0, dq1, dq2, dq3, dq4, dq5, dq6, dq7 = [_en(nc.semaphore("dq%d" % i)) for i in range(8)]
        dg0, dg1, dg2, dg3, dg4, dg5 = [_en(nc.semaphore("dg%d" % i)) for i in range(6)]
        da0, da1, da2, da3 = [_en(nc.semaphore("da%d" % i)) for i in range(4)]
        block = _en(nc.Block())
        S = Sched(nc, {"sp": s_sp, "pool": s_pool, "act": s_act, "dve": s_dve, "pe": s_pe},
                  {"sp": [dq0, dq1, dq2, dq3, dq4, dq5, dq6, dq7], "pool": [dg0, dg1, dg2, dg3, dg4, dg5], "act": [da0, da1, da2, da3]})
        PF = [pF0, pF1, pF2, pF3, pF4, pF5]
        bPF = [Buf("pF%d" % i) for i in range(6)]
        bPB = [Buf("pB0"), Buf("pB1")]

        class Carver:
            def __init__(self, start=0):
                self.pos = start

            def f32(self, n):
                a = scr[:, self.pos:self.pos + n]
                self.pos += n
                assert self.pos <= SCRN, self.pos
                return a

            def bf16(self, n):
                n32 = (n + 1) // 2
                a = scr[:, self.pos:self.pos + n32].bitcast(BF16)
                self.pos += n32
                assert self.pos <= SCRN, self.pos
                return a

        def v3(ap, a):
            return ap.rearrange("p (a b) -> p a b", a=a)

        sv = [0]

        def svec(n):
            a = smallv[:, sv[0]:sv[0] + n]
            sv[0] += n
            assert sv[0] <= 512
            return a

        lcw_t = svec(32); lcb_t = svec(8); lba_t = svec(8); lbx_t = svec(8); cvec_t = svec(8); cvec2_t = svec(8)
        scw_t = svec(96); scb_t = svec(24); bcond_t = svec(24); dx_t = svec(16); nwT_t = svec(16)
        dtbc_t = svec(1); alogc_t = svec(1)
        bSV = Buf("smallv")
        bCONST = Buf("consts")
        bMOD = Buf("modT")
        bHT = [Buf("hT%d" % k) for k in range(KC)]
        bHTs = Buf("hTs")
        bYS = [Buf("yssd%d" % k) for k in range(16)]
        bYSs = Buf("yssd_s")
        bWTbig = [Buf("wtbig0"), Buf("wtbig1")]
        bWTsm = [Buf("wtsm%d" % i) for i in range(8)]
        wt_big = [wt[:, i * 4096:(i + 1) * 4096].rearrange("p (k c) -> p k c", k=8) for i in range(2)]
        wt_sm = [wt[:, i * 1024:(i + 1) * 1024].rearrange("p (k c) -> p k c", k=8) for i in range(8)]
        allbufs = []

        def mk(name):
            b = Buf(name)
            allbufs.append(b)
            return b

        def dma_in(q, out_ap, in_ap, reads=(), writes=(), n=1):
            def fn(e, sem, out_ap=out_ap, in_ap=in_ap):
                e.dma_start(out=out_ap, in_=in_ap).then_inc(sem, 16)
            return S.dma(q, fn, 1, reads, writes)

        def dma_out(out_ap, in_ap, reads=()):
            def fn(e, sem, out_ap=out_ap, in_ap=in_ap):
                e.dma_start(out=out_ap, in_=in_ap).then_inc(sem, 16)
            return S.dma("sp", fn, 1, reads, (), store=True)

        wbig_i = [0]

        def load_wbig(src, r0, c0, ncols):
            s = wbig_i[0] % 2
            wbig_i[0] += 1
            dst = wt_big[s][:, :, 0:ncols]
            srcv = src[r0:r0 + 1024, c0:c0 + ncols].rearrange("(k p) n -> p k n", p=128)
            dma_in("pool", dst, srcv, (), (bWTbig[s],))
            return wt_big[s], bWTbig[s]

        wsm_i = [0]

        def load_wsm(src, r0, c0):
            s = wsm_i[0] % 8
            wsm_i[0] += 1
            srcv = src[r0:r0 + 1024, c0:c0 + 128].rearrange("(k p) n -> p k n", p=128)
            dma_in("pool", wt_sm[s], srcv, (), (bWTsm[s],))
            return wt_sm[s], bWTsm[s]

        pf_i = [0]

        def next_pf():
            i = pf_i[0] % 6
            pf_i[0] += 1
            return PF[i], bPF[i]

        def mm_group(out_ap, pairs, reads, wbuf):
            n = len(pairs)

            def fn(e, out_ap=out_ap, pairs=pairs):
                ins = None
                for i, (l, r) in enumerate(pairs):
                    ins = e.matmul(out_ap, lhsT=l, rhs=r, start=(i == 0), stop=(i == n - 1))
                return ins
            return S.op("pe", fn, reads, (wbuf,))

        def act(out, in_, func, reads, writes, bias=None, scale=None, accum_out=None):
            kw = {}
            if bias is not None:
                kw["bias"] = bias
            if scale is not None:
                kw["scale"] = scale
            if accum_out is not None:
                kw["accum_out"] = accum_out
            return S.op("act", lambda e, kw=kw: e.activation(out=out, in_=in_, func=func, **kw), reads, writes)

        def tt(eng, out, in0, in1, op, reads, writes):
            return S.op(eng, lambda e: e.tensor_tensor(out=out, in0=in0, in1=in1, op=op), reads, writes)

        def ts(eng, out, in0, s1, s2, op0, op1, reads, writes):
            if s2 is None:
                return S.op(eng, lambda e: e.tensor_scalar(out=out, in0=in0, scalar1=s1, scalar2=None, op0=op0), reads, writes)
            return S.op(eng, lambda e: e.tensor_scalar(out=out, in0=in0, scalar1=s1, scalar2=s2, op0=op0, op1=op1), reads, writes)

        def stt(out, in0, scalar, in1, op0, op1, reads, writes):
            return S.op("dve", lambda e: e.scalar_tensor_tensor(out=out, in0=in0, scalar=scalar, in1=in1, op0=op0, op1=op1), reads, writes)

        def cp(eng, out, in_, reads, writes):
            if eng == "act":
                return act(out, in_, AF.Copy, reads, writes)
            return S.op(eng, lambda e: e.tensor_copy(out=out, in_=in_), reads, writes)

        try:
            S.op("dve", lambda e: e.memset(smallv[:], 0.0), (), (bSV,))
            dma_in("sp", identF[:], c_ident[:, :], (), (bCONST,))
            dma_in("sp", triF[:], c_tri[:, :], (), (bCONST,))
            dma_in("pool", identB[:], c_ident[:, :], (), (bCONST,))
            S.op("pool", lambda e: e.memset(onesF[:], 1.0), (), (bCONST,))
            for (t_, src_) in ((lcw_t, lcw.rearrange("p a b -> p (a b)")), (lcb_t, lcb), (lba_t, lba), (lbx_t, lbx), (cvec_t, llam),
                               (scw_t, scw.rearrange("p a b -> p (a b)")), (scb_t, scb), (bcond_t, b_condT), (dx_t, d_x), (nwT_t, normwT)):
                dma_in("sp", t_, src_, (), (bSV,))
            dma_in("sp", dtbc_t[0:32, :], dtb_col[:, :], (), (bSV,))
            dma_in("sp", alogc_t[0:32, :], alog_col[:, :], (), (bSV,))
            _stop_at(-3)
            act(cvec_t, cvec_t, AF.Exp, (bSV,), (bSV,), scale=-1.0)
            act(cvec_t, cvec_t, AF.Ln, (bSV,), (bSV,), bias=1.0)
            ts("dve", cvec2_t, cvec_t, -16.0, None, ALU.mult, None, (bSV,), (bSV,))
            ts("dve", cvec_t, cvec_t, -8.0, None, ALU.mult, None, (bSV,), (bSV,))
            act(alogc_t, alogc_t, AF.Exp, (bSV,), (bSV,))
            ts("dve", alogc_t, alogc_t, -1.0, None, ALU.mult, None, (bSV,), (bSV,))

            _stop_at(-2)
            P0 = Carver(0)
            cf = P0.f32(8 * 17); cb_ = P0.bf16(8 * 17)
            xin = [P0.f32(T), P0.f32(T)]
            xs_f = P0.f32(8 * NS); hs_f = P0.f32(8 * NS)
            b_cf, b_cb = mk("cf"), mk("cb")
            b_xin = [mk("xin0"), mk("xin1")]
            b_xs = mk("xs_f")
            cf3 = v3(cf, 8); cb3 = v3(cb_, 8)
            dma_in("sp", cf3, cT.rearrange("(k p) n -> p k n", p=128), (), (b_cf,))
            cp("dve", cb_, cf, (b_cf,), (b_cb,))
            modps, bmodps = PF[0], bPF[0]
            for pc in range(4):
                wtile, wb = load_wbig(w_cond, 0, pc * 512, 512)
                for i in range(4):
                    mc = pc * 4 + i
                    pairs = [(wtile[:, k, i * 128:(i + 1) * 128], cb3[:, k, :]) for k in range(KC)]
                    mm_group(modps[:, mc * 17:(mc + 1) * 17], pairs, (wb, b_cb), bmodps)
            tt("dve", modT[:], v3(modps[:, 0:16 * 17], 16), bcond_t[:, 0:16].unsqueeze(2).to_broadcast([128, 16, 17]),
               ALU.add, (bmodps, bSV), (bMOD,))
            ts("dve", modT[:, 8:16, :], modT[:, 8:16, :], 1.0, None, ALU.add, None, (bMOD,), (bMOD,))
            _stop_at(-1)
            xTv = xT.rearrange("(k p) t -> p k t", p=128)
            for k in range(KC):
                s = k % 2
                dma_in("sp", xin[s], xTv[:, k, :], (), (b_xin[s],))
                act(hT[:, k, 0:T], xin[s], AF.Identity, (b_xin[s], bMOD), (bHT[k],), bias=modT[:, k, 0:1], scale=modT[:, 8 + k, 0:1])
            _stop_at(-0.5)
            dma_in("sp", v3(xs_f, 8), xsT.rearrange("(k p) n -> p k n", p=128), (), (b_xs,))
            _stop_at(-0.4)
            tt("dve", v3(hs_f, 8), v3(xs_f, 8), modT[:, 8:16, 1:17], ALU.mult, (b_xs, bMOD), (b_xs,))
            _stop_at(-0.3)
            tt("dve", v3(hs_f, 8), v3(hs_f, 8), modT[:, 0:8, 1:17], ALU.add, (b_xs, bMOD), (b_xs,))
            _stop_at(-0.2)
            cp("act", hT[:, :, T:TP], v3(hs_f, 8), (b_xs,), (bHTs,))
            bHTall = bHT + [bHTs]
            _stop_at(0)

            S.barrier(allbufs)
            P1 = Carver(0)
            xTg = v3(P1.bf16(4 * T), 4); BTg = P1.bf16(T); CTg = P1.bf16(T)
            xbuf = P1.bf16(T + 8); dg = v3(P1.bf16(512), 4); tail3 = P1.f32(4)
            wsmB = [v3(P1.bf16(1024), 8), v3(P1.bf16(1024), 8)]; wsmC = [v3(P1.bf16(1024), 8), v3(P1.bf16(1024), 8)]
            dt_t = P1.f32(512); adt_t = P1.f32(512); acs_t = P1.f32(512); nacs_t = P1.f32(512)
            e_t = P1.f32(512); w1_t = P1.f32(512); cdec_t = P1.f32(512); tmp_t = P1.f32(512)
            negI = P1.bf16(128); Lmask = P1.bf16(512)
            persist_start = P1.pos
            nw_b = P1.f32(512); D_b = P1.f32(32); dtb_b = P1.f32(32); nA_b = P1.f32(32)
            xsT_all = v3(P1.f32(16 * NS), 16)
            BsT = v3(P1.f32(4 * NS), 4); CsT = v3(P1.f32(4 * NS), 4)
            szs = v3(P1.f32(16 * NS), 16)
            xsb = v3(P1.f32(NS * 4), NS); us_t = P1.f32(NS)
            dtT_s = P1.f32(NS); adtT_s = P1.f32(NS); dAT_s = P1.f32(NS)
            persist_end = P1.pos
            x_tok_d = [P1.bf16(512), P1.bf16(512)]; xs_d = [None, None]; xsc_d = [P1.bf16(512), P1.bf16(512)]
            B_tok_d = [P1.bf16(128), P1.bf16(128)]; MT_d = [v3(P1.bf16(1024), 8), v3(P1.bf16(1024), 8)]
            sz_d = [P1.f32(512), P1.f32(512)]
            CBm = P1.f32(128); _ex = P1.f32(512); ex_t = [_ex, P1.f32(512)]
            wdt = v3(_ex.bitcast(BF16), 8)
            t_a_d = [P1.f32(512), P1.f32(512)]; t_b = tmp_t; junk = P1.bf16(512); mhalf = P1.f32(1)
            yn_t = P1.bf16(512); hst = P1.f32(512); hst_bf = P1.bf16(512); st8_d = [P1.f32(8), P1.f32(8)]
            print("P1 end", P1.pos, "of", SCRN)
            b_xTg = [mk("xTg%d" % i) for i in range(4)]; b_BT = mk("BTg"); b_CT = mk("CTg")
            b_wsmB = [mk("wsmB0"), mk("wsmB1")]; b_wsmC = [mk("wsmC0"), mk("wsmC1")]
            b_xbuf = mk("xbuf"); b_xbufq = [mk("xbufq%d" % q) for q in range(4)]; b_dg = mk("dg"); b_tail = mk("tail3"); b_dtf = mk("dtfam"); b_bc = mk("bcasts")
            b_xsT = mk("xsT_all"); b_BsT = mk("BsT"); b_CsT = mk("CsT"); b_szs = mk("szs"); b_xsb = mk("xsb"); b_us = mk("us")
            b_msk = mk("maskconsts")
            b_dts = mk("dts"); b_wdt = mk("wdt"); b_wz = mk("wzs")
            bd_xtok = [mk("x_tok0"), mk("x_tok1")]; bd_xs = [mk("xs0"), mk("xs1")]; bd_xsc = [mk("xsc0"), mk("xsc1")]
            bd_Btok = [mk("B_tok0"), mk("B_tok1")]; bd_MT = [mk("MT0"), mk("MT1")]; bd_sz = [mk("sz0"), mk("sz1")]
            b_CBm = mk("CBm"); b_ex = [mk("ex0"), mk("ex1")]; bd_ta = [mk("t_a0"), mk("t_a1")]; b_tb = mk("t_b"); b_junk = mk("junk")
            b_yn = mk("yn"); b_hst = mk("hst"); b_hbf = mk("hst_bf"); bd_st8 = [mk("st8a"), mk("st8b")]
            S.op("dve", lambda e: e.memset(xbuf[:, 0:3], 0.0), (), (b_xbuf,))
            S.op("dve", lambda e: e.memset(mhalf, -0.5), (), (b_bc,))
            ts("pool", negI, identF[:], -32768.0, None, ALU.mult, None, (bCONST,), (b_msk,))
            ts("dve", v3(Lmask, 4), triF[:].unsqueeze(1).to_broadcast([128, 4, 128]), -1.0, 1.0, ALU.mult, ALU.add, (bCONST,), (b_msk,))
            dma_in("sp", dtb_b, dtb_row.partition_broadcast(128), (), (b_bc,))
            dma_in("sp", nA_b, alog_row.partition_broadcast(128), (), (b_bc,))
            dma_in("sp", D_b, d_row.partition_broadcast(128), (), (b_bc,))
            act(nA_b, nA_b, AF.Exp, (b_bc,), (b_bc,))
            ts("dve", nA_b, nA_b, -1.0, None, ALU.mult, None, (b_bc,), (b_bc,))
            S.op("pool", lambda e: e.memset(wdt, 0.0), (), (b_wdt,))
            dma_in("pool", wdt[:, :, 0:32], w_in[:, C_DT:C_DT + 32].rearrange("(k p) n -> p k n", p=128), (), (b_wdt,))
            dtps, bdtps = PF[1], bPF[1]
            for c in range(NCH):
                pairs = [(hT[:, k, c * L:(c + 1) * L], wdt[:, k, 0:32]) for k in range(KC)]
                mm_group(dtps[:, c * 32:(c + 1) * 32], pairs, tuple(bHT) + (b_wdt,), bdtps)
            dsps, bdsps = PF[2], bPF[2]
            mm_group(dsps[:, 0:NS], [(wdt[:, k, :], hT[:, k, T:TP]) for k in range(KC)], (bHTs, b_wdt), bdsps)
            act(dtT_s, dsps[:, 0:NS], AF.Exp, (bdsps, bSV), (b_dts,), bias=dtbc_t)
            act(dtT_s, dtT_s, AF.Ln, (b_dts,), (b_dts,), bias=1.0)
            ts("dve", adtT_s, dtT_s, alogc_t, None, ALU.mult, None, (b_dts, bSV), (b_dts,))
            act(dAT_s, adtT_s, AF.Exp, (b_dts,), (b_dts,))
            tt("dve", v3(tmp_t, 16), v3(dtps[:, :], 16), dtb_b.unsqueeze(1).to_broadcast([128, 16, 32]), ALU.add, (bdtps, b_bc), (b_dtf,))
            act(tmp_t, tmp_t, AF.Exp, (b_dtf,), (b_dtf,))
            act(dt_t, tmp_t, AF.Ln, (b_dtf,), (b_dtf,), bias=1.0)
            tt("dve", v3(adt_t, 16), v3(dt_t, 16), nA_b.unsqueeze(1).to_broadcast([128, 16, 32]), ALU.mult, (b_dtf, b_bc), (b_dtf,))
            acsps, bacsps = PF[3], bPF[3]
            totps, btotps = PF[4], bPF[4]
            mm_group(acsps[:, :], [(triF[:], adt_t)], (b_dtf, bCONST), bacsps)
            mm_group(totps[:, :], [(onesF[:], adt_t)], (b_dtf, bCONST), btotps)
            cp("act", acs_t, acsps[:, :], (bacsps,), (b_dtf,))
            act(e_t, acs_t, AF.Exp, (b_dtf,), (b_dtf,))
            tt("dve", tmp_t, totps[:, :], acs_t, ALU.subtract, (btotps, b_dtf), (b_dtf,))
            act(tmp_t, tmp_t, AF.Exp, (b_dtf,), (b_dtf,))
            tt("dve", w1_t, tmp_t, dt_t, ALU.mult, (b_dtf,), (b_dtf,))
            act(tmp_t, dt_t, AF.Ln, (b_dtf,), (b_dtf,))
            tt("dve", nacs_t, tmp_t, acs_t, ALU.subtract, (b_dtf,), (b_dtf,))
            act(cdec_t, totps[:, :], AF.Exp, (btotps,), (b_dtf,))
            _stop_at(1)

            def conv_chunk(wtile, wb, coff, cw4, cbias, out_bf, b_out, tail_dst, s_state_src, s_out, b_s_out, s_state_dst):
                tt("pool", dg, identF[:].unsqueeze(1).to_broadcast([128, 4, 128]), cw4.unsqueeze(2).to_broadcast([128, 4, 128]), ALU.mult,
                   (bCONST, bSV), (b_dg,))
                def proj_q(q):
                    ps, bps = next_pf()
                    pairs = [(wtile[:, k, coff:coff + 128], hT[:, k, q * 512:(q + 1) * 512]) for k in range(KC)]
                    mm_group(ps[:, :], pairs, tuple(bHT) + (wb,), bps)
                    cp("dve", xbuf[:, 3 + q * 512:3 + (q + 1) * 512], ps[:, :], (bps,), (b_xbufq[q],))
                    if q == 3:
                        cp("dve", tail3[:, 0:3], ps[:, 509:512], (bps,), (b_tail,))
                        dma_out(tail_dst, tail3[:, 0:3], (b_tail,))

                def conv_q(q):
                    ps2, bps2 = next_pf()
                    rd = (b_dg, b_xbufq[q]) + ((b_xbufq[q - 1],) if q > 0 else (b_xbuf,))
                    mm_group(ps2[:, :], [(dg[:, k, :], xbuf[:, q * 512 + k:q * 512 + k + 512]) for k in range(4)], rd, bps2)
                    act(out_bf[:, q * 512:(q + 1) * 512], ps2[:, :], AF.Silu, (bps2, bSV), (b_out,), bias=cbias)
                proj_q(0); proj_q(1); conv_q(0); proj_q(2); conv_q(1); proj_q(3); conv_q(2); conv_q(3)
                ps, bps = next_pf()
                mm_group(ps[:, 0:NS], [(wtile[:, k, coff:coff + 128], hT[:, k, T:TP]) for k in range(KC)], (bHTs, wb), bps)
                dma_in("sp", xsb[:, :, 0:3], s_state_src, (), (b_xsb,))
                cp("act", xsb[:, :, 3], ps[:, 0:NS], (bps,), (b_xsb,))
                dma_out(s_state_dst, xsb[:, :, 1:4], (b_xsb,))
                ts("dve", us_t, xsb[:, :, 0], cw4[:, 0:1], cbias, ALU.mult, ALU.add, (b_xsb, bSV), (b_us,))
                for k in range(1, 4):
                    stt(us_t, xsb[:, :, k], cw4[:, k:k + 1], us_t, ALU.mult, ALU.add, (b_xsb, bSV, b_us), (b_us,))
                act(s_out, us_t, AF.Silu, (b_us,), (b_s_out,))

            def load_big_slot(slot, src, c0, ncols=512):
                srcv = src[0:1024, c0:c0 + ncols].rearrange("(k p) n -> p k n", p=128)
                dma_in("pool", wt_big[slot][:, :, 0:ncols], srcv, (), (bWTbig[slot],))
                return wt_big[slot], bWTbig[slot]

            def load_bc(g):
                par = g % 2
                for (tile_, buf_, c0) in ((wsmB[par], b_wsmB[par], C_SB + g * 128), (wsmC[par], b_wsmC[par], C_SC + g * 128)):
                    dma_in("pool", tile_, w_in[0:1024, c0:c0 + 128].rearrange("(k p) n -> p k n", p=128), (), (buf_,))

            wX, wXb_ = load_big_slot(0, w_in, C_SX)
            load_bc(0)
            for g in range(4):
                wzs, b_wz = load_big_slot(1, w_in, C_SZ + g * 512)
                if g + 1 < 4:
                    load_bc(g + 1)
                par = g % 2
                ch = 16 + g
                conv_chunk(wsmB[par], b_wsmB[par], 0, scw_t[:, ch * 4:(ch + 1) * 4], scb_t[:, ch:ch + 1], BTg, b_BT,
                           o_sc_p[:, ch, :], ssd_cvT[ch * 128:(ch + 1) * 128, :, :], BsT[:, g, :], b_BsT, o_sc_s[:, ch, :, :])
                ch = 20 + g
                conv_chunk(wsmC[par], b_wsmC[par], 0, scw_t[:, ch * 4:(ch + 1) * 4], scb_t[:, ch:ch + 1], CTg, b_CT,
                           o_sc_p[:, ch, :], ssd_cvT[ch * 128:(ch + 1) * 128, :, :], CsT[:, g, :], b_CsT, o_sc_s[:, ch, :, :])
                wtile, wb = wt_big[0], bWTbig[0]
                for i in range(4):
                    ch = g * 4 + i
                    conv_chunk(wtile, wb, i * 128, scw_t[:, ch * 4:(ch + 1) * 4], scb_t[:, ch:ch + 1], xTg[:, i, :], b_xTg[i],
                               o_sc_p[:, ch, :], ssd_cvT[ch * 128:(ch + 1) * 128, :, :], xsT_all[:, ch, :], b_xsT, o_sc_s[:, ch, :, :])
                if g + 1 < 4:
                    load_big_slot(0, w_in, C_SX + (g + 1) * 512)
                dma_in("sp", nw_b, normw_row[:, g * 512:(g + 1) * 512].partition_broadcast(128), (), (b_bc,))
                for i in range(4):
                    ps, bps = next_pf()
                    mm_group(ps[:, 0:NS], [(wzs[:, k, i * 128:(i + 1) * 128], hT[:, k, T:TP]) for k in range(KC)], (bHTs, b_wz), bps)
                    act(szs[:, g * 4 + i, :], ps[:, 0:NS], AF.Silu, (bps,), (b_szs,))
                def bufs(c):
                    par = c % 2
                    return (x_tok_d[par], xs_d[par], xsc_d[par], B_tok_d[par], MT_d[par], sz_d[par],
                            bd_xtok[par], bd_xs[par], bd_xsc[par], bd_Btok[par], bd_MT[par], bd_sz[par],
                            t_a_d[par], bd_ta[par], st8_d[par], bd_st8[par])

                def S4(c, g=g):
                    (x_tok_t, xs_t, xsc_t, B_tok, MT, sz_t, b_xtok, b_xs_, b_xsc, b_Btok, b_MT, b_sz, t_a, b_ta, st8, b_st8) = bufs(c)
                    cs = slice(c * L, (c + 1) * L)

                    def fnY(e):
                        ins = None
                        for h8 in range(8):
                            ins = e.matmul(PF[3][:, h8 * 64:(h8 + 1) * 64], lhsT=MT[:, h8, :], rhs=x_tok_t[:, h8 * 64:(h8 + 1) * 64],
                                           start=True, stop=True)
                        return ins
                    mm_group(PF[5][:, :], [(B_tok, xsc_t)], (b_Btok, b_xsc), bPF[5])
                    if c > 0:
                        mm_group(PF[4][:, :], [(CTg[:, cs], hst_bf)], (b_CT, b_hbf), bPF[4])
                    S.op("pe", fnY, (b_MT, b_xtok), (bPF[3],))

                def Dskip(c, g=g):
                    (x_tok_t, xs_t, xsc_t, B_tok, MT, sz_t, b_xtok, b_xs_, b_xsc, b_Btok, b_MT, b_sz, t_a, b_ta, st8, b_st8) = bufs(c)
                    tt("pool", v3(t_b, 8), v3(x_tok_t, 8), D_b[:, g * 8:(g + 1) * 8].unsqueeze(2).to_broadcast([128, 8, 64]), ALU.mult,
                       (b_xtok, b_bc), (b_tb,))

                def S7(c, g=g):
                    (x_tok_t, xs_t, xsc_t, B_tok, MT, sz_t, b_xtok, b_xs_, b_xsc, b_Btok, b_MT, b_sz, t_a, b_ta, st8, b_st8) = bufs(c)
                    stt(yn_t, t_a, st8[:, 3:4], nw_b, ALU.mult, ALU.mult, (b_ta, b_st8, b_bc), (b_yn,))

                def S5a(c, g=g):
                    hb = c * 32 + g * 8
                    if c > 0:
                        tt("dve", v3(hst, 8), v3(hst, 8), cdec_t[:, hb:hb + 8].unsqueeze(2).to_broadcast([128, 8, 64]), ALU.mult,
                           (b_hst, b_dtf), (b_hst,))
                        tt("dve", hst, hst, PF[5][:, :], ALU.add, (b_hst, bPF[5]), (b_hst,))
                    else:
                        cp("dve", hst, PF[5][:, :], (bPF[5],), (b_hst,))
                    if c == NCH - 1:
                        dma_out(o_sh_p[:, g * 512:(g + 1) * 512], hst, (b_hst,))

                def S5b(c, g=g):
                    (x_tok_t, xs_t, xsc_t, B_tok, MT, sz_t, b_xtok, b_xs_, b_xsc, b_Btok, b_MT, b_sz, t_a, b_ta, st8, b_st8) = bufs(c)
                    hb = c * 32 + g * 8
                    if c > 0:
                        tt("dve", v3(t_a, 8), v3(PF[4][:, :], 8), e_t[:, hb:hb + 8].unsqueeze(2).to_broadcast([128, 8, 64]), ALU.mult,
                           (bPF[4], b_dtf), (b_ta,))
                        tt("dve", t_a, t_a, PF[3][:, :], ALU.add, (b_ta, bPF[3]), (b_ta,))
                    else:
                        cp("dve", t_a, PF[3][:, :], (bPF[3],), (b_ta,))
                    tt("dve", t_a, t_a, t_b, ALU.add, (b_ta, b_tb), (b_ta,))
                    tt("dve", t_a, t_a, sz_t, ALU.mult, (b_ta, b_sz), (b_ta,))

                def S1(c, g=g):
                    cs = slice(c * L, (c + 1) * L)

                    def fnT(e, cs=cs):
                        ins = None
                        for i in range(4):
                            ins = e.transpose(pB0[:, i * 128:(i + 1) * 128], xTg[:, i, cs], identB[:])
                        ins = e.transpose(pB0[:, 512:640], BTg[:, cs], identB[:])
                        return ins
                    S.op("pe", fnT, tuple(b_xTg) + (b_BT, bCONST), (bPB[0],))

                def S1b(c, g=g):
                    cs = slice(c * L, (c + 1) * L)
                    mm_group(PF[0][:, 0:128], [(BTg[:, cs], CTg[:, cs])], (b_BT, b_CT), bPF[0])

                def S1c(c, g=g):
                    cs = slice(c * L, (c + 1) * L)
                    mm_group(PF[0][:, :], [(hT[:, k, cs], wzs[:, k, :]) for k in range(KC)], tuple(bHT) + (b_wz,), bPF[0])

                def S1d(c, half, g=g):
                    hb = c * 32 + g * 8
                    accp, baccp = PF[1 + half], bPF[1 + half]

                    def fnA(e, half=half, hb=hb, accp=accp):
                        ins = e.matmul(accp[:, :], lhsT=negI, rhs=Lmask, start=True, stop=False)
                        for hh in range(4):
                            col = hb + half * 4 + hh
                            ins = e.matmul(accp[:, hh * 128:(hh + 1) * 128], lhsT=adt_t[:, col:col + 1].to_broadcast([128, 128]),
                                           rhs=triF[:], start=False, stop=(hh == 3))
                        return ins
                    S.op("pe", fnA, (b_dtf, bCONST, b_msk), (baccp,))

                def copies(c, g=g):
                    (x_tok_t, xs_t, xsc_t, B_tok, MT, sz_t, b_xtok, b_xs_, b_xsc, b_Btok, b_MT, b_sz, t_a, b_ta, st8, b_st8) = bufs(c)
                    cp("act", x_tok_t, pB0[:, 0:512], (bPB[0],), (b_xtok,))
                    cp("act", B_tok, pB0[:, 512:640], (bPB[0],), (b_Btok,))

                def poolx(c, g=g):
                    (x_tok_t, xs_t, xsc_t, B_tok, MT, sz_t, b_xtok, b_xs_, b_xsc, b_Btok, b_MT, b_sz, t_a, b_ta, st8, b_st8) = bufs(c)
                    hb = c * 32 + g * 8
                    tt("pool", v3(xsc_t, 8), v3(x_tok_t, 8), w1_t[:, hb:hb + 8].unsqueeze(2).to_broadcast([128, 8, 64]), ALU.mult,
                       (b_xtok, b_dtf), (b_xsc,))

                def tanhz(c, g=g):
                    (x_tok_t, xs_t, xsc_t, B_tok, MT, sz_t, b_xtok, b_xs_, b_xsc, b_Btok, b_MT, b_sz, t_a, b_ta, st8, b_st8) = bufs(c)
                    act(sz_t, PF[0][:, :], AF.Tanh, (bPF[0],), (b_sz,), scale=0.5)

                def exps(c, half, g=g):
                    hb = c * 32 + g * 8
                    accp, baccp = PF[1 + half], bPF[1 + half]
                    for hh in range(4):
                        col = hb + half * 4 + hh
                        act(ex_t[half][:, hh * 128:(hh + 1) * 128], accp[:, hh * 128:(hh + 1) * 128], AF.Exp,
                            (baccp, b_dtf), (b_ex[half],), bias=nacs_t[:, col:col + 1])

                def S3a(c, g=g):
                    tt("dve", CBm, PF[0][:, 0:128], triF[:], ALU.mult, (bPF[0], bCONST), (b_CBm,))

                def S3b(c, g=g):
                    (x_tok_t, xs_t, xsc_t, B_tok, MT, sz_t, b_xtok, b_xs_, b_xsc, b_Btok, b_MT, b_sz, t_a, b_ta, st8, b_st8) = bufs(c)
                    stt(sz_t, sz_t, 1.0, PF[0][:, :], ALU.add, ALU.mult, (b_sz, bPF[0]), (b_sz,))

                def S3c(c, half, g=g):
                    (x_tok_t, xs_t, xsc_t, B_tok, MT, sz_t, b_xtok, b_xs_, b_xsc, b_Btok, b_MT, b_sz, t_a, b_ta, st8, b_st8) = bufs(c)
                    stt(MT[:, half * 4:(half + 1) * 4, :], v3(ex_t[half], 4), 1.0e30, CBm.unsqueeze(1).to_broadcast([128, 4, 128]),
                        ALU.min, ALU.mult, (b_ex[half], b_CBm), (b_MT,))

                def S6(c, g=g):
                    (x_tok_t, xs_t, xsc_t, B_tok, MT, sz_t, b_xtok, b_xs_, b_xsc, b_Btok, b_MT, b_sz, t_a, b_ta, st8, b_st8) = bufs(c)
                    act(junk, t_a, AF.Square, (b_ta,), (b_junk, b_st8), accum_out=st8[:, 0:1])
                    ts("pool", st8[:, 1:2], st8[:, 0:1], 1.0 / 512.0, 4.0 * RMS_EPS, ALU.mult, ALU.add, (b_st8,), (b_st8,))
                    tt("pool", st8[:, 3:4], st8[:, 1:2], mhalf[:, 0:1], ALU.pow, (b_st8, b_bc), (b_st8,))

                def S8(c, g=g):
                    cs = slice(c * L, (c + 1) * L)

                    def fnT2(e):
                        ins = None
                        for i in range(4):
                            ins = e.transpose(pB1[:, i * 128:(i + 1) * 128], yn_t[:, i * 128:(i + 1) * 128], identB[:])
                        return ins
                    S.op("pe", fnT2, (b_yn, bCONST), (bPB[1],))
                    cp("act", yssd[:, g * 4:(g + 1) * 4, cs], v3(pB1[:, 0:512], 4), (bPB[1],), tuple(bYS[g * 4:(g + 1) * 4]))

                for s_ in range(-1, NCH + 1):
                    cur, nxt, prv = s_, s_ + 1, s_ - 1
                    hc = 0 <= cur < NCH
                    hn = 0 <= nxt < NCH
                    hp = 0 <= prv < NCH
                    if hc:
                        S4(cur)
                        Dskip(cur)
                    if hc:
                        S5a(cur)
                    if hp:
                        S7(prv)
                    if hn:
                        S1(nxt)
                        S1d(nxt, 0)
                        S1d(nxt, 1)
                        S1b(nxt)
                        S3a(nxt)
                        copies(nxt)
                        poolx(nxt)
                        exps(nxt, 0)
                    if hn:
                        S1c(nxt)
                        exps(nxt, 1)
                    if hc and cur < NCH - 1:
                        cp("act", hst_bf, hst, (b_hst,), (b_hbf,))
                    if hc:
                        S5b(cur)
                    if hn:
                        S3c(nxt, 0)
                        tanhz(nxt)
                        S3c(nxt, 1)
                        S3b(nxt)
                    if hc:
                        S6(cur)
                    if hp:
                        S8(prv)

            _stop_at(2)
            S.barrier(allbufs)
            P1s = Carver(0)
            H0 = [v3(P1s.f32(2048), 16), v3(P1s.f32(2048), 16)]
            HN = [v3(P1s.f32(2048), 16), v3(P1s.f32(2048), 16)]
            T1 = v3(P1s.f32(2048), 16); T2 = v3(P1s.f32(2048), 16)
            assert P1s.pos <= persist_start, (P1s.pos, persist_start)
            P1s2 = Carver(persist_end)
            Bb = v3(P1s2.f32(512), 4); Cb = v3(P1s2.f32(512), 4)
            T3 = v3(P1s2.f32(2048), 16)
            esel = P1s2.f32(2048)
            dAx = v3(P1s2.f32(256), 16); xdt = v3(P1s2.f32(256), 16); ysT = v3(P1s2.f32(256), 16)
            gys = v3(P1s2.f32(256), 16); sqs = v3(P1s2.f32(256), 16); rs_t = v3(P1s2.f32(64), 4)
            b_H0 = [mk("H0a"), mk("H0b")]; b_HN = [mk("HNa"), mk("HNb")]; b_T1 = mk("T1"); b_T2 = mk("T2"); b_T3 = mk("T3")
            b_Bb = mk("Bb"); b_Cb = mk("Cb"); b_esel = mk("esel"); b_dAx = mk("dAx"); b_xdt = mk("xdt"); b_ysT = mk("ysT")
            b_gys = mk("gys"); b_sqs = mk("sqs"); b_rs = mk("rs")
            dma_in("sp", esel, c_esel[:, :], (), (b_esel,))
            exps, bexps = PF[0], bPF[0]
            _stop_at(2.05)

            def fnE(e):
                ins = None
                for j in range(16):
                    ins = e.matmul(exps[:, j * 16:(j + 1) * 16], lhsT=esel[:, j * 128:(j + 1) * 128], rhs=dAT_s, start=True, stop=True)
                for j in range(16):
                    ins = e.matmul(exps[:, 256 + j * 16:256 + (j + 1) * 16], lhsT=esel[:, j * 128:(j + 1) * 128], rhs=dtT_s,
                                   start=True, stop=True)
                return ins
            S.op("pe", fnE, (b_esel, b_dts), (bexps,))
            _stop_at(2.07)
            cp("act", dAx, v3(exps[:, 0:256], 16), (bexps,), (b_dAx,))
            _stop_at(2.08)
            cp("act", xdt, v3(exps[:, 256:512], 16), (bexps,), (b_xdt,))
            tt("dve", xdt, xdt, xsT_all, ALU.mult, (b_xdt, b_xsT), (b_xdt,))
            _stop_at(2.1)
            h0v = ssd_h0.rearrange("s (j e) p n -> s (e p) j n", e=2)
            ohv = o_sh_s.rearrange("s (j e) p n -> s (e p) j n", e=2)
            for s in range(NS):
                sl = s % 2
                dma_in("act", H0[sl], h0v[s], (), (b_H0[sl],))
                bps_, bbps_ = PF[1], bPF[1]
                cps_, bcps_ = PF[2], bPF[2]

                def fnB(e, s=s):
                    ins = None
                    for g in range(4):
                        ins = e.matmul(PF[1][:, g * 128:(g + 1) * 128], lhsT=BsT[:, g, s:s + 1].to_broadcast([128, 128]), rhs=identF[:],
                                       start=True, stop=True)
                    return ins

                def fnC(e, s=s):
                    ins = None
                    for g in range(4):
                        ins = e.matmul(PF[2][:, g * 128:(g + 1) * 128], lhsT=CsT[:, g, s:s + 1].to_broadcast([128, 128]), rhs=identF[:],
                                       start=True, stop=True)
                    return ins
                S.op("pe", fnB, (b_BsT, bCONST), (bbps_,))
                S.op("pe", fnC, (b_CsT, bCONST), (bcps_,))
                cp("act", Bb, v3(PF[1][:, :], 4), (bbps_,), (b_Bb,))
                cp("act", Cb, v3(PF[2][:, :], 4), (bcps_,), (b_Cb,))
                _stop_at(2.2)
                _stop_at(2.3)
                xdt4 = xdt[:, :, s:s + 1].rearrange("p (g j) o -> p g j o", g=4).to_broadcast([128, 4, 4, 128])
                Bb4 = Bb.unsqueeze(2).to_broadcast([128, 4, 4, 128])
                Cb4 = Cb.unsqueeze(2).to_broadcast([128, 4, 4, 128])
                tt("pool", T2.rearrange("p (g j) n -> p g j n", g=4), xdt4, Bb4, ALU.mult, (b_xdt, b_Bb), (b_T2,))
                _stop_at(2.4)
                for j_ in range(16):
                    stt(HN[sl][:, j_, :], H0[sl][:, j_, :], dAx[:, j_, s:s + 1], T2[:, j_, :], ALU.mult, ALU.add,
                        (b_H0[sl], b_dAx, b_T2), (b_HN[sl],))
                dma_out(ohv[s], HN[sl], (b_HN[sl],))
                tt("dve", T3.rearrange("p (g j) n -> p g j n", g=4), HN[sl].rearrange("p (g j) n -> p g j n", g=4), Cb4, ALU.mult,
                   (b_HN[sl], b_Cb), (b_T3,))
                _stop_at(2.5)
                S.op("dve", lambda e, s=s: e.tensor_reduce(out=ysT[:, :, s], in_=T3, axis=AX.X, op=ALU.add), (b_T3,), (b_ysT,))
                _stop_at(2.6)
            _stop_at(2.7)
            tt("dve", gys, xsT_all, dx_t.unsqueeze(2).to_broadcast([128, 16, NS]), ALU.mult, (b_xsT, bSV), (b_gys,))
            tt("dve", ysT, ysT, gys, ALU.add, (b_ysT, b_gys), (b_ysT,))
            tt("dve", gys, ysT, szs, ALU.mult, (b_ysT, b_szs), (b_gys,))
            tt("dve", sqs, gys, gys, ALU.mult, (b_gys,), (b_sqs,))
            ssp, bssp = PF[3], bPF[3]

            def fnS(e):
                ins = None
                for g in range(4):
                    for i in range(4):
                        ins = e.matmul(ssp[:, g * 16:(g + 1) * 16], lhsT=onesF[:], rhs=sqs[:, g * 4 + i, :], start=(i == 0), stop=(i == 3))
                return ins
            S.op("pe", fnS, (b_sqs, bCONST), (bssp,))
            ts("dve", rs_t, v3(ssp[:, 0:64], 4), 1.0 / 512.0, RMS_EPS, ALU.mult, ALU.add, (bssp,), (b_rs,))
            act(rs_t, rs_t, AF.Sqrt, (b_rs,), (b_rs,))
            S.op("dve", lambda e: e.reciprocal(out=rs_t, in_=rs_t), (b_rs,), (b_rs,))
            tt("dve", gys.rearrange("p (g j) s -> p g j s", g=4), gys.rearrange("p (g j) s -> p g j s", g=4),
               rs_t.unsqueeze(2).to_broadcast([128, 4, 4, NS]), ALU.mult, (b_gys, b_rs), (b_gys,))
            tt("dve", gys, gys, nwT_t.unsqueeze(2).to_broadcast([128, 16, NS]), ALU.mult, (b_gys, bSV), (b_gys,))
            cp("act", yssd[:, :, T:TP], gys, (b_gys,), (bYSs,))

            _stop_at(3)
            S.barrier(allbufs)
            P2 = Carver(0)
            ylru = v3(P2.bf16(8 * TP), 8)
            xbuf2 = P2.bf16(T + 8); dg2 = v3(P2.bf16(512), 4); tail2 = P2.f32(4)
            u2 = P2.f32(T); ubf = P2.bf16(T)
            r_t = P2.f32(T); i_t = P2.f32(T); a_t = P2.f32(T); m_t = P2.f32(T)
            lxs = v3(P2.f32(NS * 4), NS); lus = P2.f32(NS); lusb = P2.bf16(NS); lr = P2.f32(NS); li = P2.f32(NS); la = P2.f32(NS)
            lm = P2.f32(NS); lh0 = v3(P2.f32(8 * NS), 8); lhn = v3(P2.f32(8 * NS), 8); lhp = P2.f32(8)
            hba = P2.f32(8); hbx = P2.f32(8); hcv = P2.f32(8); q25 = P2.f32(1)
            wab = v3(P2.bf16(8 * 128), 8); wxb = v3(P2.bf16(8 * 128), 8)
            bYL = [mk("ylru%d" % k) for k in range(8)]; bYLs = mk("ylru_s")
            b_x2 = mk("xbuf2"); b_x2q = [mk("xbuf2q%d" % q) for q in range(4)]; b_dg2 = mk("dg2"); b_tail2 = mk("tail2")
            b_u2 = mk("u2"); b_ubf = mk("ubf"); b_r = mk("r"); b_i = mk("i"); b_a = mk("a"); b_m = mk("m")
            b_lxs = mk("lxs"); b_lus = mk("lus"); b_lsm = mk("lsm"); b_lh0 = mk("lh0"); b_lhn = mk("lhn"); b_lhp = mk("lhp"); b_wab = mk("wab")
            b_hv = mk("halfvecs")
            S.op("dve", lambda e: e.memset(xbuf2[:, 0:3], 0.0), (), (b_x2,))
            ts("dve", hba, lba_t, 0.5, None, ALU.mult, None, (bSV,), (b_hv,))
            ts("dve", hbx, lbx_t, 0.5, None, ALU.mult, None, (bSV,), (b_hv,))
            ts("dve", hcv, cvec_t, 0.5, None, ALU.mult, None, (bSV,), (b_hv,))
            S.op("dve", lambda e: e.memset(q25, 0.25), (), (b_hv,))
            S.op("pool", lambda e: e.memset(wab, 0.0), (), (b_wab,))
            S.op("pool", lambda e: e.memset(wxb, 0.0), (), (b_wab,))
            for (dst_, src_) in ((wab, lwa), (wxb, lwx)):
                sv_ = src_.rearrange("(j e) k m -> e k j m", e=2)
                for e_ in range(2):
                    dma_in("pool", dst_[e_ * 64:(e_ + 1) * 64, :, e_ * 64:(e_ + 1) * 64], sv_[e_], (), (b_wab,))
            dma_in("sp", lh0, lru_h0T.rearrange("(k p) n -> p k n", p=128), (), (b_lh0,))
            _stop_at(3.05)
            for pp in range(2):
                wX, wXb = load_wbig(w_in, 0, C_LX + pp * 512, 512)
                wZ, wZb = load_wbig(w_in, 0, C_LZ + pp * 512, 512)
                for j4 in range(4):
                    j = pp * 4 + j4
                    co = j4 * 128
                    cw4 = lcw_t[:, j * 4:(j + 1) * 4]
                    tt("pool", dg2, identF[:].unsqueeze(1).to_broadcast([128, 4, 128]), cw4.unsqueeze(2).to_broadcast([128, 4, 128]), ALU.mult,
                       (bCONST, bSV), (b_dg2,))

                    def proj_q(q, j=j, co=co, wX=wX, wXb=wXb):
                        ps, bps = next_pf()
                        mm_group(ps[:, :], [(wX[:, k, co:co + 128], hT[:, k, q * 512:(q + 1) * 512]) for k in range(KC)], tuple(bHT) + (wXb,), bps)
                        cp("act", xbuf2[:, 3 + q * 512:3 + (q + 1) * 512], ps[:, :], (bps,), (b_x2q[q],))
                        if q == 3:
                            cp("act", tail2[:, 0:3], ps[:, 509:512], (bps,), (b_tail2,))
                            dma_out(o_lc_p[:, j, :], tail2[:, 0:3], (b_tail2,))

                    def conv_q(q, j=j):
                        qs = slice(q * 512, (q + 1) * 512)
                        ps2, bps2 = next_pf()
                        rd = (b_dg2, b_x2q[q]) + ((b_x2q[q - 1],) if q > 0 else (b_x2,))
                        mm_group(ps2[:, :], [(dg2[:, k, :], xbuf2[:, q * 512 + k:q * 512 + k + 512]) for k in range(4)], rd, bps2)
                        _stop_at(3.06)
                        act(u2[:, qs], ps2[:, :], AF.Identity, (bps2, bSV), (b_u2,), bias=lcb_t[:, j:j + 1])
                        _stop_at(3.07)
                        cp("dve", ubf[:, qs], u2[:, qs], (b_u2,), (b_ubf,))
                        _stop_at(3.08)
                        ps, bps = next_pf()
                        mm_group(ps[:, :], [(wab[:, j, :], ubf[:, qs])], (b_wab, b_ubf), bps)
                        act(r_t[:, qs], ps[:, :], AF.Tanh, (bps, b_hv), (b_r,), bias=hba[:, j:j + 1], scale=0.5)
                        ps, bps = next_pf()
                        mm_group(ps[:, :], [(wxb[:, j, :], ubf[:, qs])], (b_wab, b_ubf), bps)
                        act(i_t[:, qs], ps[:, :], AF.Tanh, (bps, b_hv), (b_i,), bias=hbx[:, j:j + 1], scale=0.5)
                    proj_q(0); proj_q(1); conv_q(0); proj_q(2); conv_q(1); proj_q(3); conv_q(2); conv_q(3)
                    _stop_at(3.1)
                    ps, bps = next_pf()
                    mm_group(ps[:, 0:NS], [(wX[:, k, co:co + 128], hT[:, k, T:TP]) for k in range(KC)], (bHTs, wXb), bps)
                    dma_in("sp", lxs[:, :, 0:3], lru_cvT[j * 128:(j + 1) * 128, :, :], (), (b_lxs,))
                    cp("act", lxs[:, :, 3], ps[:, 0:NS], (bps,), (b_lxs,))
                    dma_out(o_lc_s[:, j, :, :], lxs[:, :, 1:4], (b_lxs,))
                    ts("dve", lus, lxs[:, :, 0], cw4[:, 0:1], lcb_t[:, j:j + 1], ALU.mult, ALU.add, (b_lxs, bSV), (b_lus,))
                    for k in range(1, 4):
                        stt(lus, lxs[:, :, k], cw4[:, k:k + 1], lus, ALU.mult, ALU.add, (b_lxs, bSV, b_lus), (b_lus,))
                    cp("dve", lusb, lus, (b_lus,), (b_lus,))
                    ps, bps = next_pf()
                    mm_group(ps[:, 0:NS], [(wab[:, j, :], lusb)], (b_wab, b_lus), bps)
                    mm_group(ps[:, 32:32 + NS], [(wxb[:, j, :], lusb)], (b_wab, b_lus), bps)
                    act(lr, ps[:, 0:NS], AF.Tanh, (bps, b_hv), (b_lsm,), bias=hba[:, j:j + 1], scale=0.5)
                    act(li, ps[:, 32:32 + NS], AF.Tanh, (bps, b_hv), (b_lsm,), bias=hbx[:, j:j + 1], scale=0.5)
                    _stop_at(3.2)
                    act(a_t, r_t, AF.Exp, (b_r, b_hv), (b_a,), scale=hcv[:, j:j + 1], bias=hcv[:, j:j + 1])
                    act(m_t, r_t, AF.Exp, (b_r, bSV), (b_m,), scale=cvec_t[:, j:j + 1], bias=cvec_t[:, j:j + 1])
                    act(la, lr, AF.Exp, (b_lsm, b_hv), (b_lsm,), scale=hcv[:, j:j + 1], bias=hcv[:, j:j + 1])
                    act(lm, lr, AF.Exp, (b_lsm, bSV), (b_lsm,), scale=cvec_t[:, j:j + 1], bias=cvec_t[:, j:j + 1])
                    act(m_t, m_t, AF.Sqrt, (b_m, b_hv), (b_m,), scale=-0.25, bias=q25[:, 0:1])
                    act(lm, lm, AF.Sqrt, (b_lsm, b_hv), (b_lsm,), scale=-0.25, bias=q25[:, 0:1])
                    _stop_at(3.3)
                    stt(i_t, i_t, 1.0, u2, ALU.add, ALU.mult, (b_i, b_u2), (b_i,))
                    tt("dve", m_t[:, 1:T], m_t[:, 1:T], i_t[:, 1:T], ALU.mult, (b_m, b_i), (b_m,))
                    ts("dve", m_t[:, 0:1], i_t[:, 0:1], 0.5, None, ALU.mult, None, (b_m, b_i), (b_m,))
                    S.op("dve", lambda e: e.tensor_tensor_scan(out=r_t, data0=a_t, data1=m_t, initial=0.0, op0=ALU.mult, op1=ALU.add),
                         (b_a, b_m, b_r), (b_r,))
                    cp("dve", lhp[:, j:j + 1], r_t[:, T - 1:T], (b_r,), (b_lhp,))
                    _stop_at(3.4)
                    stt(li, li, 1.0, lus, ALU.add, ALU.mult, (b_lsm, b_lus), (b_lsm,))
                    tt("dve", lm, lm, li, ALU.mult, (b_lsm,), (b_lsm,))
                    tt("dve", la, la, lh0[:, j, :], ALU.mult, (b_lsm, b_lh0), (b_lsm,))
                    tt("dve", lhn[:, j, :], la, lm, ALU.add, (b_lsm,), (b_lhn,))
                    _stop_at(3.5)
                    for q in range(4):
                        qs = slice(q * 512, (q + 1) * 512)
                        ps, bps = next_pf()
                        mm_group(ps[:, :], [(wZ[:, k, co:co + 128], hT[:, k, qs]) for k in range(KC)], tuple(bHT) + (wZb,), bps)
                        act(a_t[:, qs], ps[:, :], AF.Tanh, (bps, b_a), (b_a,), scale=0.5)
                        stt(a_t[:, qs], a_t[:, qs], 1.0, ps[:, :], ALU.add, ALU.mult, (b_a, bps), (b_a,))
                    stt(ylru[:, j, 0:T], a_t, 0.5, r_t, ALU.mult, ALU.mult, (b_r, b_a), (bYL[j],))
                    ps, bps = next_pf()
                    mm_group(ps[:, 0:NS], [(wZ[:, k, co:co + 128], hT[:, k, T:TP]) for k in range(KC)], (bHTs, wZb), bps)
                    act(lr, ps[:, 0:NS], AF.Tanh, (bps,), (b_lsm,), scale=0.5)
                    stt(lr, lr, 1.0, ps[:, 0:NS], ALU.add, ALU.mult, (b_lsm, bps), (b_lsm,))
                    stt(ylru[:, j, T:TP], lr, 0.5, lhn[:, j, :], ALU.mult, ALU.mult, (b_lsm, b_lhn), (bYLs,))
            dma_out(o_lh_p[:, :], lhp, (b_lhp,))
            dma_out(o_lh_s[:, :, :], lhn, (b_lhn,))

            _stop_at(4)
            S.barrier([b for b in allbufs if b not in bYL and b is not bYLs] + [bWTbig[0], bWTbig[1]] + bWTsm)
            P3 = Carver((8 * TP) // 2)
            merged = v3(P3.bf16(8 * TP), 8)
            sgA = P3.f32(512); sgB = P3.f32(512); tA = P3.f32(512); tB = P3.f32(512)
            yo_t = [P3.f32(1024), P3.f32(1024)]; gts = P3.f32(1024)
            bnst_d = [P3.f32(16), P3.f32(16)]; mh3 = P3.f32(1); junk3 = P3.bf16(1024); cbb = v3(P3.bf16(8 * 128), 8); cf2 = v3(P3.f32(8 * 17), 8); cb2 = v3(P3.bf16(8 * 17), 8)
            P3b = Carver(0)
            gate_b = P3b.f32(1024); lng_b = P3b.f32(1024); lnb_b = P3b.f32(1024); bg_b = P3b.f32(1024)
            xtk = [P3b.f32(1024), P3b.f32(1024)]; resid_d = [P3b.f32(1024), scr[:, (8 * TP) // 2 + 8 * TP // 2:(8 * TP) // 2 + 8 * TP // 2 + 1024]]; xn_d = [P3b.f32(1024), scr[:, (8 * TP) // 2 + 8 * TP // 2 + 1024:(8 * TP) // 2 + 8 * TP // 2 + 2048]]
            assert P3b.pos <= (8 * TP) // 2
            bMG = [mk("merged%d" % k) for k in range(8)]; bMGs = mk("merged_s")
            b_sgA = mk("sgA"); b_sgB = mk("sgB"); b_tA = mk("tA"); b_tB = mk("tB"); b_gateb = mk("gate_b"); b_ln = mk("lnbc")
            b_xtk = [mk("xtk0"), mk("xtk1")]; b_res_d = [mk("resid0"), mk("resid1")]; b_xn_d = [mk("xn0"), mk("xn1")]; b_bn_d = [mk("bn0"), mk("bn1")]; b_yo = [mk("yo0"), mk("yo1")]
            b_cbb = mk("cbb"); b_c2 = mk("c2"); b_gts = mk("gts"); b_j3 = mk("junk3")
            colsets = [(slice(q * 512, (q + 1) * 512), 512) for q in range(4)] + [(slice(T, TP), NS)]
            for jo in range(8):
                wA, wAb = load_wsm(w_lp, 0, jo * 128)
                wB0, wB0b = load_wsm(w_sp, 0, jo * 128)
                wB1, wB1b = load_wsm(w_sp, 1024, jo * 128)
                wgA, wgAb = load_wsm(w_in, 0, C_MA + jo * 128)
                wgB, wgBb = load_wsm(w_in, 0, C_MB + jo * 128)
                for qi, (cs, n) in enumerate(colsets):
                    smp = qi == 4
                    rH = (bHTs,) if smp else tuple(bHT)
                    rYL = (bYLs,) if smp else tuple(bYL)
                    rYS = (bYSs,) if smp else tuple(bYS)
                    pA, bpA = next_pf()
                    mm_group(pA[:, 0:n], [(wA[:, k, :], ylru[:, k, cs]) for k in range(8)], rYL + (wAb,), bpA)
                    pB, bpB = next_pf()
                    mm_group(pB[:, 0:n], [(wB0[:, k, :], yssd[:, k, cs]) for k in range(8)] + [(wB1[:, k, :], yssd[:, 8 + k, cs]) for k in range(8)],
                             rYS + (wB0b, wB1b), bpB)
                    pgA, bpgA = next_pf()
                    mm_group(pgA[:, 0:n], [(wgA[:, k, :], hT[:, k, cs]) for k in range(8)], rH + (wgAb,), bpgA)
                    pgB, bpgB = next_pf()
                    mm_group(pgB[:, 0:n], [(wgB[:, k, :], hT[:, k, cs]) for k in range(8)], rH + (wgBb,), bpgB)
                    act(sgA[:, 0:n], pgA[:, 0:n], AF.Sigmoid, (bpgA,), (b_sgA,))
                    act(sgB[:, 0:n], pgB[:, 0:n], AF.Sigmoid, (bpgB,), (b_sgB,))
                    tt("dve", tA[:, 0:n], sgA[:, 0:n], pA[:, 0:n], ALU.mult, (b_sgA, bpA), (b_tA,))
                    tt("dve", tB[:, 0:n], sgB[:, 0:n], pB[:, 0:n], ALU.mult, (b_sgB, bpB), (b_tB,))
                    tt("dve", merged[:, jo, cs], tA[:, 0:n], tB[:, 0:n], ALU.add, (b_tA, b_tB), (bMGs if smp else bMG[jo],))
            S.barrier(bWTsm + bWTbig + bYL + [bYLs, b_sgA, b_sgB, b_tA, b_tB])
            dma_in("sp", cf2, cT.rearrange("(k p) n -> p k n", p=128), (), (b_c2,))
            cp("dve", cb2, cf2, (b_c2,), (b_c2,))
            cp("dve", cbb, cf2[:, :, 0:1].to_broadcast([128, 8, 128]), (b_c2,), (b_cbb,))
            S.op("dve", lambda e: e.memset(mh3, -0.5), (), (b_ln,))
            dma_in("sp", bg_b, b_gate.partition_broadcast(128), (), (b_ln,))
            dma_in("sp", lng_b, lng_row.partition_broadcast(128), (), (b_ln,))
            dma_in("sp", lnb_b, lnb_row.partition_broadcast(128), (), (b_ln,))
            for hf in range(2):
                wg, wgb = load_wbig(w_cond, 0, 2048 + hf * 512, 512)
                ps, bps = next_pf()
                mm_group(ps[:, :], [(cbb[:, k, :], wg[:, k, :]) for k in range(8)], (b_cbb, wgb), bps)
                tt("dve", gate_b[:, hf * 512:(hf + 1) * 512], ps[:, :], bg_b[:, hf * 512:(hf + 1) * 512], ALU.add, (bps, b_ln), (b_gateb,))
                ps, bps = next_pf()
                mm_group(ps[0:NS, :], [(cb2[:, k, 1:17], wg[:, k, :]) for k in range(8)], (b_c2, wgb), bps)
                tt("dve", gts[0:NS, hf * 512:(hf + 1) * 512], ps[0:NS, :], bg_b[0:NS, hf * 512:(hf + 1) * 512], ALU.add, (bps, b_ln), (b_gts,))
            wo0, wo0b = load_wbig(w_out, 0, 0, 512)
            wo1, wo1b = load_wbig(w_out, 0, 512, 512)
            def tile_ctx(ti):
                smp = ti == NCH
                np_ = NS if smp else 128
                cs = slice(T, TP) if smp else slice(ti * 128, (ti + 1) * 128)
                sl = ti % 2
                return smp, np_, cs, sl

            def ln_stageP(ti):
                smp, np_, cs, sl = tile_ctx(ti)
                resid, bnst, b_res, b_bn = resid_d[sl], bnst_d[sl], b_res_d[sl], b_bn_d[sl]
                rM = (bMGs,) if smp else tuple(bMG)
                dma_in("act", xtk[sl][0:np_, :], xs_tok[:, :] if smp else x_tok[ti * 128:(ti + 1) * 128, :], (), (b_xtk[sl],))
                gsrc = gts if smp else gate_b
                bgs = b_gts if smp else b_gateb
                for hf, (wo, wob) in enumerate(((wo0, wo0b), (wo1, wo1b))):
                    ps, bps = next_pf()
                    mm_group(ps[0:np_, :], [(merged[:, k, cs], wo[:, k, :]) for k in range(8)], rM + (wob,), bps)
                    tt("dve", resid[0:np_, hf * 512:(hf + 1) * 512], ps[0:np_, :], gsrc[0:np_, hf * 512:(hf + 1) * 512], ALU.mult,
                       (bps, bgs), (b_res,))
                stt(resid[0:np_, :], xtk[sl][0:np_, :], ALPHA, resid[0:np_, :], ALU.mult, ALU.add, (b_xtk[sl], b_res), (b_res,))
                act(junk3[0:np_, :], resid[0:np_, :], AF.Copy, (b_res,), (b_j3, b_bn), accum_out=bnst[0:np_, 0:1])
                act(junk3[0:np_, :], resid[0:np_, :], AF.Square, (b_res,), (b_j3, b_bn), accum_out=bnst[0:np_, 1:2])
                ts("pool", bnst[0:np_, 12:13], bnst[0:np_, 0:1], 1.0 / D, None, ALU.mult, None, (b_bn,), (b_bn,))
                tt("pool", bnst[0:np_, 2:3], bnst[0:np_, 12:13], bnst[0:np_, 12:13], ALU.mult, (b_bn,), (b_bn,))
                ts("pool", bnst[0:np_, 3:4], bnst[0:np_, 1:2], 1.0 / D, LN_EPS, ALU.mult, ALU.add, (b_bn,), (b_bn,))
                tt("pool", bnst[0:np_, 14:15], bnst[0:np_, 3:4], bnst[0:np_, 2:3], ALU.subtract, (b_bn,), (b_bn,))
                tt("pool", bnst[0:np_, 15:16], bnst[0:np_, 14:15], mh3[0:np_, 0:1], ALU.pow, (b_bn, b_ln), (b_bn,))

            def ln_stageQ(ti):
                smp, np_, cs, sl = tile_ctx(ti)
                resid, xn, bnst, b_res, b_xn, b_bn = resid_d[sl], xn_d[sl], bnst_d[sl], b_res_d[sl], b_xn_d[sl], b_bn_d[sl]
                stt(xn[0:np_, :], resid[0:np_, :], bnst[0:np_, 12:13], lng_b[0:np_, :], ALU.subtract, ALU.mult, (b_res, b_bn, b_ln), (b_xn,))
                stt(yo_t[sl][0:np_, :], xn[0:np_, :], bnst[0:np_, 15:16], lnb_b[0:np_, :], ALU.mult, ALU.add, (b_xn, b_bn, b_ln), (b_yo[sl],))
                dma_out(y_s[:, :] if smp else y_p[ti * 128:(ti + 1) * 128, :], yo_t[sl][0:np_, :], (b_yo[sl],))

            for ti in range(NCH + 2):
                if ti <= NCH:
                    ln_stageP(ti)
                if ti >= 1:
                    ln_stageQ(ti - 1)

        except _StopRec:
            pass
        S.finalize()

        @block.sync
        def _(e):
            S.emit("sp", e)

        @block.gpsimd
        def _(e):
            S.emit("pool", e)

        @block.scalar
        def _(e):
            S.emit("act", e)

        @block.vector
        def _(e):
            S.emit("dve", e)

        @block.tensor
        def _(e):
            S.emit("pe", e)
    return nc


_NC_CACHE = {}


def _vec_pk(v, k):
    return np.ascontiguousarray(np.asarray(v, np.float32).reshape(k, 128).T)


def kernel(x_prompt, x_sample, state_lru_h, state_lru_conv, state_ssd_h, state_ssd_conv,
           c_prompt, c_sample, w_cond, b_cond, w_in, lru_conv_w, lru_conv_b, lru_wa, lru_ba,
           lru_wx, lru_bx, lru_lambda, ssd_conv_w, ssd_conv_b, ssd_dt_bias, ssd_a_log, ssd_d,
           ssd_norm_w, w_lru_proj, w_ssd_proj, w_out, ln_g, ln_b):
    f = lambda a: np.ascontiguousarray(np.asarray(a, np.float32))
    x_prompt, x_sample = f(x_prompt), f(x_sample)
    if "nc" not in _NC_CACHE:
        _NC_CACHE["nc"] = build_program()
    nc = _NC_CACHE["nc"]
    shared = {
        "w_cond": f(w_cond[0]), "b_condT": _vec_pk(b_cond[0], 24), "b_gate": f(np.asarray(b_cond)[0:1, 2048:3072]),
        "w_in": f(w_in[0]),
        "lcw": f(np.asarray(lru_conv_w)[0].reshape(4, 8, 128).transpose(2, 1, 0)), "lcb": _vec_pk(lru_conv_b[0], 8),
        "lwa": f(lru_wa[0]), "lwx": f(lru_wx[0]),
        "lba": _vec_pk(lru_ba[0], 8), "lbx": _vec_pk(lru_bx[0], 8), "llam": _vec_pk(lru_lambda[0], 8),
        "scw": f(np.asarray(ssd_conv_w)[0].reshape(4, 24, 128).transpose(2, 1, 0)), "scb": _vec_pk(ssd_conv_b[0], 24),
        "dtb_row": f(np.asarray(ssd_dt_bias)[0:1]), "alog_row": f(np.asarray(ssd_a_log)[0:1]), "d_row": f(np.asarray(ssd_d)[0:1]),
        "dtb_col": f(np.asarray(ssd_dt_bias)[0].reshape(32, 1)), "alog_col": f(np.asarray(ssd_a_log)[0].reshape(32, 1)),
        "d_x": _vec_pk(np.repeat(np.asarray(ssd_d, np.float32)[0], 64), 16),
        "normw_row": f(np.asarray(ssd_norm_w)[0:1]), "normwT": _vec_pk(ssd_norm_w[0], 16),
        "w_lp": f(w_lru_proj[0]), "w_sp": f(w_ssd_proj[0]), "w_out": f(w_out[0]),
        "lng_row": f(np.asarray(ln_g)[0:1]), "lnb_row": f(np.asarray(ln_b)[0:1]),
        "c_ident": np.eye(128, dtype=np.float32), "c_tri": np.triu(np.ones((128, 128), np.float32)),
        "c_esel": f((np.arange(128)[:, None] == (np.arange(2048)[None, :] // 64)).astype(np.float32)),
    }
    in_maps = []
    for i in range(NCORES):
        ss = slice(NS * i, NS * (i + 1))
        cT = np.concatenate([np.asarray(c_prompt, np.float32)[i][:, None], np.asarray(c_sample, np.float32)[ss].T], axis=1)
        m = dict(shared)
        m.update({
            "xT": f(x_prompt[i].T), "x_tok": f(x_prompt[i]),
            "xsT": f(x_sample[ss, 0, :].T), "xs_tok": f(x_sample[ss, 0, :]),
            "cT": f(cT),
            "lru_h0T": f(np.asarray(state_lru_h)[0, ss].T),
            "lru_cvT": f(np.asarray(state_lru_conv)[0, ss].transpose(2, 0, 1)),
            "ssd_h0": f(np.asarray(state_ssd_h)[0, ss]),
            "ssd_cvT": f(np.asarray(state_ssd_conv)[0, ss].transpose(2, 0, 1)),
        })
        in_maps.append(m)
    res = run_bass_kernel_spmd(nc, in_maps, core_ids=list(range(NCORES)))
    R = res.results
    y_prompt = np.stack([R[i]["y_p"] for i in range(NCORES)])
    y_sample = np.concatenate([R[i]["y_s"] for i in range(NCORES)])[:, None, :]
    lh_p = np.stack([R[i]["o_lh_p"].T.reshape(1024) for i in range(NCORES)])[None]
    lc_p = np.stack([R[i]["o_lc_p"].transpose(2, 1, 0).reshape(3, 1024) for i in range(NCORES)])[None]
    sh_p = np.stack([R[i]["o_sh_p"].T.reshape(32, 64, 128) for i in range(NCORES)])[None]
    sc_p = np.stack([R[i]["o_sc_p"].transpose(2, 1, 0).reshape(3, 3072) for i in range(NCORES)])[None]
    lh_s = np.concatenate([R[i]["o_lh_s"].transpose(2, 1, 0).reshape(NS, 1024) for i in range(NCORES)])[None]
    lc_s = np.concatenate([R[i]["o_lc_s"].transpose(2, 3, 1, 0).reshape(NS, 3, 1024) for i in range(NCORES)])[None]
    sh_s = np.concatenate([R[i]["o_sh_s"] for i in range(NCORES)])[None]
    sc_s = np.concatenate([R[i]["o_sc_s"].transpose(2, 3, 1, 0).reshape(NS, 3, 3072) for i in range(NCORES)])[None]
    c32 = lambda a: np.ascontiguousarray(a, dtype=np.float32)
    return (c32(y_prompt), c32(y_sample), c32(lh_p), c32(lc_p), c32(sh_p), c32(sc_p),
            c32(lh_s), c32(lc_s), c32(sh_s), c32(sc_s))
```
